# Optimizing a Trainium2 kernel written in Bass

```python
import jax, jax.numpy as jnp
from jax import lax
import numpy as np

D_MODEL = 1024
BATCH = 2
SEQ = 8192
DEPTH = 2
DEC_BATCH = 16
DEC_SEQ = 2048
PAST_LEN = 128

HEAD_DIM = 64
GRID_W = 64
EPS = 1e-6
A_HEADS = 6
A_PATTERNS = ((128, 1), (512, 4), (2048, 16))
A_BLOCK = 64
ROPE_THETA = 500000.0
ROPE_DIMS = HEAD_DIM // 4
B_HEADS = 4
B_KV_HEADS = 2
B_QBLOCK = 128
AXIAL_THETA = 10000.0
C_HEADS = 6
NA_ROWS = 8
NA_COLS = 16

A_W = A_HEADS * HEAD_DIM
B_W = B_HEADS * HEAD_DIM
B_KV_W = B_KV_HEADS * HEAD_DIM
C_W = C_HEADS * HEAD_DIM
MIX_W = A_W + B_W + C_W
IN_W = 4 * A_W + 2 * B_W + 2 * B_KV_W + 4 * C_W

kernel_name = "hybrid_dilated_gqa_neighbourhood_encoder"


def _rmsnorm(x, g):
    xf = x.astype(jnp.float32)
    r = lax.rsqrt(jnp.mean(xf * xf, axis=-1, keepdims=True) + EPS)
    return (xf * r * g.astype(jnp.float32)).astype(x.dtype)


def _freqs(n_dims, theta):
    return theta ** (-jnp.arange(0, n_dims, 2, dtype=jnp.float32) / n_dims)


def _rotate(x, ang):
    n = ang.shape[-1]
    cos = jnp.cos(ang)[None, :, None, :].astype(x.dtype)
    sin = jnp.sin(ang)[None, :, None, :].astype(x.dtype)
    x1, x2 = x[..., :n], x[..., n:]
    return jnp.concatenate([x1 * cos - x2 * sin, x2 * cos + x1 * sin], axis=-1)


def _window_attn(q, k, v, radius):
    N, L, H, dh = q.shape
    nb = -(-L // A_BLOCK)
    Lp = nb * A_BLOCK
    nk = A_BLOCK + 2 * radius
    qb = jnp.pad(q, ((0, 0), (0, Lp - L), (0, 0), (0, 0))).reshape(N, nb, A_BLOCK, H, dh)
    pad = ((0, 0), (radius, Lp - L + radius), (0, 0), (0, 0))
    kidx = (jnp.arange(nb) * A_BLOCK)[:, None] + jnp.arange(nk)[None, :]
    kb = jnp.pad(k, pad)[:, kidx]
    vb = jnp.pad(v, pad)[:, kidx]
    qpos = (jnp.arange(nb) * A_BLOCK)[:, None] + jnp.arange(A_BLOCK)[None, :]
    kpos = (kidx - radius)[:, None, :]
    valid = (jnp.abs(kpos - qpos[:, :, None]) <= radius) & (kpos >= 0) & (kpos < L)
    s = jnp.einsum('nbqhd,nbkhd->nbhqk', qb, kb).astype(jnp.float32) * (dh ** -0.5)
    s = jnp.where(valid[None, :, None], s, -jnp.inf)
    m = jnp.max(s, axis=-1, keepdims=True)
    p = jnp.exp(s - m)
    l = jnp.sum(p, axis=-1, keepdims=True)
    o = jnp.einsum('nbhqk,nbkhd->nbqhd', (p / l).astype(v.dtype), vb)
    lse = (m + jnp.log(l))[..., 0]
    o = o.reshape(N, Lp, H, dh)[:, :L]
    lse = lse.transpose(0, 1, 3, 2).reshape(N, Lp, H)[:, :L]
    return o, lse


def _to_residues(t, dil):
    B, S, H, dh = t.shape
    return t.reshape(B, S // dil, dil, H, dh).transpose(0, 2, 1, 3, 4).reshape(B * dil, S // dil, H, dh)


def _from_residues(t, B, dil):
    N, L = t.shape[0], t.shape[1]
    rest = t.shape[2:]
    t = t.reshape((B, dil, L) + rest)
    t = jnp.swapaxes(t, 1, 2)
    return t.reshape((B, L * dil) + rest)


def _dilated_attention(q, k, v):
    B, S, H, dh = q.shape
    outs, lses = [], []
    for window, dil in A_PATTERNS:
        radius = window // (2 * dil)
        o, lse = _window_attn(_to_residues(q, dil), _to_residues(k, dil), _to_residues(v, dil), radius)
        outs.append(_from_residues(o, B, dil))
        lses.append(_from_residues(lse, B, dil))
    w = jax.nn.softmax(jnp.stack(lses, axis=-1), axis=-1)
    out = sum(w[..., i, None].astype(q.dtype) * outs[i] for i in range(len(A_PATTERNS)))
    return out.reshape(B, S, H * dh)


def _gqa_axial(q, k, v, q_norm, k_norm):
    B, S, Hq, dh = q.shape
    q = _rmsnorm(q, q_norm)
    k = _rmsnorm(k, k_norm)
    t = jnp.arange(S)
    f = _freqs(dh // 2, AXIAL_THETA)
    ang_row = (t // GRID_W).astype(jnp.float32)[:, None] * f
    ang_col = (t % GRID_W).astype(jnp.float32)[:, None] * f
    half = dh // 2
    q = jnp.concatenate([_rotate(q[..., :half], ang_row), _rotate(q[..., half:], ang_col)], axis=-1)
    k = jnp.concatenate([_rotate(k[..., :half], ang_row), _rotate(k[..., half:], ang_col)], axis=-1)
    rep = Hq // B_KV_HEADS
    nqb = S // B_QBLOCK
    qb = q.reshape(B, nqb, B_QBLOCK, B_KV_HEADS, rep, dh).transpose(1, 0, 2, 3, 4, 5)
    scale = dh ** -0.5

    def block(qi):
        s = jnp.einsum('bqgrd,bkgd->bgrqk', qi, k).astype(jnp.float32) * scale
        p = jax.nn.softmax(s, axis=-1).astype(v.dtype)
        return jnp.einsum('bgrqk,bkgd->bqgrd', p, v)

    o = lax.map(block, qb)
    return o.transpose(1, 0, 2, 3, 4, 5).reshape(B, S, Hq * dh)


def _neighbourhood_attn(q, k, v, rel_bias):
    B, S, H, dh = q.shape
    R = S // GRID_W
    kh = min(NA_ROWS, R)
    rows = jnp.arange(R)
    rs = jnp.clip(rows - NA_ROWS // 2, 0, R - kh)
    krow = rs[:, None] + jnp.arange(kh)[None, :]
    cols = jnp.arange(GRID_W)
    cs = jnp.clip(cols - NA_COLS // 2, 0, GRID_W - NA_COLS)
    col_valid = (cols[None, :] >= cs[:, None]) & (cols[None, :] < cs[:, None] + NA_COLS)
    qg = q.reshape(B, R, GRID_W, H, dh)
    kg = k.reshape(B, R, GRID_W, H, dh)[:, krow]
    vg = v.reshape(B, R, GRID_W, H, dh)[:, krow]
    s = jnp.einsum('brqhd,brikhd->brhqik', qg, kg).astype(jnp.float32) * (dh ** -0.5)
    ro = krow - rows[:, None] + (NA_ROWS - 1)
    co = jnp.clip(cols[None, :] - cols[:, None] + (NA_COLS - 1), 0, 2 * NA_COLS - 2)
    bias = rel_bias[:, ro[:, None, :, None], co[None, :, None, :]]
    s = s + bias.transpose(1, 0, 2, 3, 4)[None].astype(jnp.float32)
    s = jnp.where(col_valid[:, None, :], s, -jnp.inf)
    p = jax.nn.softmax(s.reshape(s.shape[:4] + (kh * GRID_W,)), axis=-1)
    p = p.reshape(s.shape).astype(v.dtype)
    o = jnp.einsum('brhqik,brikhd->brqhd', p, vg)
    return o.reshape(B, S, H * dh)


def _layer(x, norm_pre, w_in, q_norm, k_norm, rel_bias, branch_gain, w_out, norm_post):
    B, S, _ = x.shape
    h = _rmsnorm(x, norm_pre)
    proj = h @ w_in
    widths = [A_W] * 4 + [B_W, B_KV_W, B_KV_W, B_W] + [C_W] * 4
    cuts = [int(c) for c in np.cumsum(widths)[:-1]]
    aq, ak, av, az, bq, bk, bv, bz, cq, ck, cv, cz = jnp.split(proj, cuts, axis=-1)
    heads = lambda t: t.reshape(B, S, t.shape[-1] // HEAD_DIM, HEAD_DIM)

    ang = jnp.arange(S, dtype=jnp.float32)[:, None] * _freqs(ROPE_DIMS, ROPE_THETA)
    aq, ak = heads(aq), heads(ak)
    aq = jnp.concatenate([_rotate(aq[..., :ROPE_DIMS], ang), aq[..., ROPE_DIMS:]], axis=-1)
    ak = jnp.concatenate([_rotate(ak[..., :ROPE_DIMS], ang), ak[..., ROPE_DIMS:]], axis=-1)
    ya = _dilated_attention(aq, ak, heads(av))
    yb = _gqa_axial(heads(bq), heads(bk), heads(bv), q_norm, k_norm)
    yc = _neighbourhood_attn(heads(cq), heads(ck), heads(cv), rel_bias)

    ga, gb, gc = jnp.split(branch_gain, [A_W, A_W + B_W])
    y = jnp.concatenate([
        _rmsnorm(ya * jax.nn.silu(az), ga),
        _rmsnorm(yb * jax.nn.silu(bz), gb),
        _rmsnorm(yc * jax.nn.silu(cz), gc),
    ], axis=-1)
    return x + _rmsnorm(y @ w_out, norm_post)


def setup_inputs(seed: int = 0) -> dict:
    key = jax.random.key(seed)
    ks = jax.random.split(key, 10)
    f32 = jnp.float32
    nrm = jax.random.normal
    return {
        "x_prompt": nrm(ks[0], (BATCH, SEQ, D_MODEL), f32),
        "x_sample": nrm(ks[1], (DEC_BATCH, DEC_SEQ, D_MODEL), f32),
        "norm_pre": 1.0 + 0.05 * nrm(ks[2], (DEPTH, D_MODEL), f32),
        "w_in": nrm(ks[3], (DEPTH, D_MODEL, IN_W), f32) * (D_MODEL ** -0.5),
        "q_norm": 1.0 + 0.05 * nrm(ks[4], (DEPTH, HEAD_DIM), f32),
        "k_norm": 1.0 + 0.05 * nrm(ks[5], (DEPTH, HEAD_DIM), f32),
        "rel_bias": 0.1 * nrm(ks[6], (DEPTH, C_HEADS, 2 * NA_ROWS - 1, 2 * NA_COLS - 1), f32),
        "branch_gain": 1.0 + 0.05 * nrm(ks[7], (DEPTH, MIX_W), f32),
        "w_out": nrm(ks[8], (DEPTH, MIX_W, D_MODEL), f32) * (MIX_W ** -0.5),
        "norm_post": 1.0 + 0.05 * nrm(ks[9], (DEPTH, D_MODEL), f32),
    }


def reference(x_prompt, x_sample, norm_pre, w_in, q_norm, k_norm, rel_bias, branch_gain, w_out, norm_post):
    y_prompt = x_prompt
    y_sample = x_sample
    for l in range(DEPTH):
        params = (norm_pre[l], w_in[l], q_norm[l], k_norm[l], rel_bias[l], branch_gain[l], w_out[l], norm_post[l])
        y_prompt = _layer(y_prompt, *params)
        y_sample = _layer(y_sample, *params)
    return (y_prompt, y_sample)
```

```python
import numpy as np
from contextlib import ExitStack
import concourse.bass as bass
import concourse.mybir as mybir
from concourse.bass_utils import run_bass_kernel_spmd

F32 = mybir.dt.float32
BF16 = mybir.dt.bfloat16
AF = mybir.ActivationFunctionType
ALU = mybir.AluOpType
AX = mybir.AxisListType

D = 1024
INW = 3840
EPS = 1e-6
NEG = -30000.0
STOP = None
QUARTER = True
LIMIT = None
AQ, AK, AV, AZ, BQ, BK, BV, BZ, CQ, CK, CV, CZ = 0, 384, 768, 1152, 1536, 1792, 1920, 2048, 2304, 2688, 3072, 3456

COMPUTE = ("pe", "act", "dve", "pool")
ISSUERS = ("pe", "act", "dve", "pool", "sp")
ENGOBJ = {"pe": "tensor", "act": "scalar", "dve": "vector", "pool": "gpsimd", "sp": "sync"}


class Op:
    __slots__ = ("eng", "fn", "is_dma", "sem", "ticket", "signal", "waits")

    def __init__(self, eng, fn, is_dma, sem):
        self.eng = eng
        self.fn = fn
        self.is_dma = is_dma
        self.sem = sem
        self.ticket = None
        self.signal = is_dma
        self.waits = []


class Prog:
    def __init__(self, nc, stack, block):
        self.nc = nc
        self.stack = stack
        self.block = block
        self.streams = {e: [] for e in ISSUERS}
        self.esem = {e: self.new_sem("sem_" + e) for e in COMPUTE}
        self.bar = self.new_sem("sem_bar")
        self.bar_count = 0
        self.ecount = {e: 0 for e in COMPUTE}
        self.last_w = {}
        self.readers = {}
        self.dma_sems = {}
        self.dma_count = {}
        self.pending = None
        self.alias = {}
        self.n_ins = 0

    def new_sem(self, name):
        return self.stack.enter_context(self.nc.semaphore(name))

    def _dep(self, op, prod):
        if prod is None or prod is op:
            return
        if (not op.is_dma) and (not prod.is_dma) and prod.eng == op.eng and op.eng == "pe":
            return
        prod.signal = True
        op.waits.append(prod)

    def op(self, eng, fn, reads=(), writes=(), dma=None):
        self.nrec = getattr(self, "nrec", 0) + 1
        if LIMIT is not None and self.nrec > LIMIT:
            return None
        is_dma = dma is not None
        ps_reads = [r for r in reads if r.startswith("ps")]
        if ps_reads:
            reads = [r for r in reads if not r.startswith("ps")]
            writes = list(writes) + ps_reads
        o = Op(eng, fn, is_dma, dma)
        if is_dma:
            if dma not in self.alias:
                self.alias[dma] = f"g{len(self.alias)}"
            dma = self.alias[dma]
            o.sem = dma
            if dma not in self.dma_sems:
                self.dma_sems[dma] = self.new_sem("d_" + dma)
                self.dma_count[dma] = 0
            self.dma_count[dma] += 16
            o.ticket = self.dma_count[dma]
        for r in reads:
            self._dep(o, self.last_w.get(r))
        for w in writes:
            self._dep(o, self.last_w.get(w))
            for rd in self.readers.get(w, ()):
                self._dep(o, rd)
        for r in reads:
            self.readers.setdefault(r, []).append(o)
        for w in writes:
            self.last_w[w] = o
            self.readers[w] = []
        self.streams[eng].append(o)
        return o

    def flush(self, final=False):
        for e in COMPUTE:
            ops = [o for o in self.streams[e] if not o.is_dma]
            if ops:
                ops[-1].signal = True
            for o in ops:
                if o.signal:
                    self.ecount[e] += 1
                    o.ticket = self.ecount[e]
        pending = self.pending
        dma_final = dict(self.dma_count)
        self.bar_count += 1
        bar_val = self.bar_count
        ecount = dict(self.ecount)

        def make(ename):
            ops = self.streams[ename]

            def body(eng):
                waited = {}
                if pending is not None:
                    for key, val in pending.items():
                        if val > 0:
                            sem = self.bar if key == "bar" else self.esem[key]
                            if key != ename:
                                eng.wait_ge(sem, val)
                for o in ops:
                    need = {}
                    for p in o.waits:
                        key = ("d", p.sem) if p.is_dma else ("e", p.eng)
                        if p.ticket > need.get(key, 0):
                            need[key] = p.ticket
                    for key, val in need.items():
                        if waited.get(key, 0) >= val:
                            continue
                        waited[key] = val
                        sem = self.dma_sems[key[1]] if key[0] == "d" else self.esem[key[1]]
                        eng.wait_ge(sem, val)
                    ins = o.fn(eng)
                    self.n_ins += 1
                    if o.is_dma:
                        ins.then_inc(self.dma_sems[o.sem], 16)
                    elif o.signal:
                        ins.then_inc(self.esem[o.eng], 1)
                if ename == "sp":
                    for s, v in dma_final.items():
                        if v > 0:
                            eng.wait_ge(self.dma_sems[s], v)
                    eng.sem_inc(self.bar, 1)

            return body

        for ename in ISSUERS:
            if ename != "sp" and not self.streams[ename] and pending is None:
                continue
            getattr(self.block, ENGOBJ[ename])(make(ename))
        self.pending = dict(ecount)
        self.pending["bar"] = bar_val
        self.streams = {e: [] for e in ISSUERS}
        self.last_w = {}
        self.readers = {}
        self.alias = {}
        if final:
            pend = self.pending

            def fin(eng):
                eng.wait_ge(self.bar, pend["bar"])

            for ename in ("pe", "act", "dve", "pool"):
                getattr(self.block, ENGOBJ[ename])(fin)


_UID = [0]


def U(name):
    _UID[0] += 1
    return f"{name}_u{_UID[0]}"


class Ring:
    def __init__(self, P, st, name, n, shape, dt, psum=False):
        self.n = n
        self.name = name
        self.i = -1
        alloc = P.nc.psum_tensor if psum else P.nc.sbuf_tensor
        self.bufs = [st.enter_context(alloc(U(f"{name}{k}"), list(shape), dt)) for k in range(n)]

    def next(self):
        self.i += 1
        k = self.i % self.n
        return self.bufs[k], f"{self.name}{k}"

    def cur(self):
        k = self.i % self.n
        return self.bufs[k], f"{self.name}{k}"


def _const_tables():
    SMAX = 8192
    t = np.arange(SMAX, dtype=np.float32)
    fa = (500000.0 ** (-np.arange(0, 16, 2, dtype=np.float32) / 16)).astype(np.float32)
    anga = (t[:, None] * fa[None, :]).astype(np.float32).astype(np.float64)
    fb = (10000.0 ** (-np.arange(0, 32, 2, dtype=np.float32) / 32)).astype(np.float32)
    row = (np.arange(SMAX) // 64).astype(np.float32)
    col = (np.arange(SMAX) % 64).astype(np.float32)
    angr = (row[:, None] * fb[None, :]).astype(np.float32).astype(np.float64)
    angc = (col[:, None] * fb[None, :]).astype(np.float32).astype(np.float64)
    angb = np.concatenate([angr, angc], axis=1)

    def tm(a):
        return np.ascontiguousarray(a.reshape(SMAX // 128, 128, -1).transpose(1, 0, 2)).astype(np.float32)

    ropea = np.stack([tm(np.cos(anga)), tm(np.sin(anga))], axis=1)
    ropeb = np.stack([tm(np.cos(angb)), tm(np.sin(angb))], axis=1)
    kk = np.arange(128)[:, None, None]
    dl = (np.arange(5) - 2)[None, :, None]
    ii = np.arange(128)[None, None, :]
    dd = 128 * dl + kk - ii
    ad = np.abs(dd)
    mult = (ad <= 64).astype(np.float32) + ((dd % 4 == 0) & (ad <= 256))
    du = 128 * (np.arange(3) - 1)[None, :, None] + kk - ii
    m16 = (np.abs(du) <= 64).astype(np.float32)
    ea = np.ascontiguousarray(np.concatenate([mult, m16], axis=1).astype(np.float32))
    ident = np.eye(128, dtype=np.float32)
    return ropea, ropeb, ea, ident


def _c_bias_tables(rel_bias):
    L = rel_bias.shape[0]
    krl = (np.arange(128) // 64)[:, None, None]
    kc = (np.arange(128) % 64)[:, None, None]
    dlt = (np.arange(7) - 3)[None, :, None]
    rl = (np.arange(128) // 64)[None, None, :]
    qc = (np.arange(128) % 64)[None, None, :]
    dr = 2 * dlt + krl - rl
    ro = dr + 7
    co = np.clip(kc - qc + 15, 0, 30)
    cs = np.clip(qc - 8, 0, 48)
    valid = (kc >= cs) & (kc < cs + 16) & (ro >= 0) & (ro <= 14)
    ro_c = np.clip(ro, 0, 14)
    ro_b, co_b, valid_b = np.broadcast_arrays(ro_c, co, valid)
    out = np.empty((L, 6, 128, 7, 128), dtype=np.float32)
    for l in range(L):
        for h in range(6):
            g = rel_bias[l, h][ro_b, co_b]
            out[l, h] = np.where(valid_b, g, np.float32(NEG))
    return out


def build(parts, n_layers, debug=False):
    nc = bass.Bass("TRN2", target_bir_lowering=False)
    dr = {}

    def din(name, shape, dt=F32):
        dr[name] = nc.dram_tensor(name, list(shape), dt, kind="ExternalInput").ap()
        return dr[name]

    def dout(name, shape, dt=F32):
        dr[name] = nc.dram_tensor(name, list(shape), dt, kind="ExternalOutput").ap()
        return dr[name]

    def dscr(name, shape, dt):
        if debug:
            dr[name] = nc.dram_tensor(name, list(shape), dt, kind="ExternalOutput").ap()
        else:
            dr[name] = nc.dram_tensor(name, list(shape), dt).ap()
        return dr[name]

    QPN = "p" if (QUARTER and any(pn == "p" for (pn, _, _) in parts)) else None
    if QPN is not None:
        dscr("oq", [2048, D], F32)
        dscr("szq", [2048, D // 2], F32)
        dscr("xq", [2048, D], F32)
        dscr("obq", [2048, 256], F32)
    for (pn, S, src) in parts:
        din("x_" + pn, [S, D])
        if pn == QPN:
            dout("yq_" + pn, [2048, D])
        else:
            dout("y_" + pn, [S, D])
        dscr("qt_" + pn, [1024, S], BF16)
        dscr("kt_" + pn, [896, S], BF16)
        dscr("v_" + pn, [S, 8 * 192], BF16)
        dscr("sz_" + pn, [S, D // 2], F32)
        dscr("o_" + pn, [S, D], F32)
        if n_layers > 1:
            dscr("y1_" + pn, [S, D], F32)
    din("w_in", [n_layers, D, INW])
    din("w_out", [n_layers, D, D])
    din("norm_pre", [n_layers, D])
    din("norm_post", [n_layers, D])
    din("branch_gain", [n_layers, D])
    din("q_norm", [n_layers, 64])
    din("k_norm", [n_layers, 64])
    din("ropea", [128, 2, 64, 8])
    din("ropeb", [128, 2, 64, 32])
    din("ea", [128, 8, 128])
    din("ident", [128, 128])
    din("efraw", [n_layers, 6, 128, 7, 128])
    if QPN is not None:
        dr["qoff"] = nc.dram_tensor("qoff", [1, 4], mybir.dt.int32, kind="ExternalInput").ap()

    with ExitStack() as top:
        block = top.enter_context(nc.Block())
        P = Prog(nc, top, block)
        identf = top.enter_context(nc.sbuf_tensor("identf", [128, 128], F32))
        identb = top.enter_context(nc.sbuf_tensor("identb", [128, 128], BF16))
        epsc = top.enter_context(nc.sbuf_tensor("epsc", [128, 8], F32))
        nhalf = top.enter_context(nc.sbuf_tensor("nhalf", [128, 8], F32))
        P.op("sp", lambda e: e.dma_start(out=identf[:], in_=dr["ident"]), writes=["identf"], dma="c0")
        P.op("dve", lambda e: e.tensor_copy(out=identb[:], in_=identf[:]), reads=["identf"], writes=["identb"])
        P.op("dve", lambda e: e.memset(epsc[:], EPS), writes=["epsc"])
        P.op("dve", lambda e: e.memset(nhalf[:], -0.5), writes=["nhalf"])
        if QPN is not None:
            qs = top.enter_context(nc.sbuf_tensor("qs", [1, 4], mybir.dt.int32))
            rq0 = top.enter_context(nc.sync.register("rq0"))
            rrow = top.enter_context(nc.sync.register("rrow"))
            rrow2 = top.enter_context(nc.sync.register("rrow2"))
            rtmp = [top.enter_context(nc.sync.register(f"rtmp{i}")) for i in range(4)]
            rti = [0]
            P.op("sp", lambda e: e.dma_start(out=qs[:], in_=dr["qoff"]), writes=["qs"], dma="c1")

            def setregs(e):
                e.reg_load(rq0, qs[0:1, 0:1])
                e.reg_load(rrow2, qs[0:1, 2:3])
                return e.reg_load(rrow, qs[0:1, 1:2])
            P.op("sp", setregs, reads=["qs"], writes=["regs"])

            def dyn_ap(e, base_reg, const, tensor_ap, pattern):
                t = rtmp[rti[0] % 4]
                rti[0] += 1
                e.reg_add(t, base_reg, int(const))
                return bass.AP(tensor_ap.tensor, t, pattern)
        P.flush(final=(STOP == "pre"))
        if STOP == "pre":
            return nc

        def rstd_ops(v_ap, key, n, scale):
            P.op("dve", lambda e: e.tensor_scalar(out=v_ap, in0=v_ap, scalar1=float(scale), scalar2=float(EPS),
                                                  op0=ALU.mult, op1=ALU.add), reads=[key], writes=[key])
            P.op("pool", lambda e: e.tensor_tensor(out=v_ap, in0=v_ap, in1=nhalf[:, 0:n], op=ALU.pow),
                 reads=[key], writes=[key])

        for l in range(n_layers):
            last = l == n_layers - 1
            with ExitStack() as st:
                wsb = st.enter_context(nc.sbuf_tensor(U("wsb"), [128, 8, INW], BF16))
                wst = Ring(P, st, "wst", 2, [128, 8, 120], F32)
                gpre = st.enter_context(nc.sbuf_tensor(U("gpre"), [128, 8], F32))
                gqk = st.enter_context(nc.sbuf_tensor(U("gqk"), [128, 6, 64], F32))
                ropa = st.enter_context(nc.sbuf_tensor(U("ropa"), [128, 2, 64, 8], F32))
                ropb = st.enter_context(nc.sbuf_tensor(U("ropb"), [128, 2, 64, 32], F32))
                xin = Ring(P, st, "xin", 5, [128, D], F32)
                junk = st.enter_context(nc.sbuf_tensor(U("junk"), [128, D], BF16))
                ssr = Ring(P, st, "ssr", 5, [128, 1], F32)
                xnr = Ring(P, st, "xnr", 2, [128, D], BF16)
                qfr = Ring(P, st, "qfr", 2, [128, 6, 64], F32)
                bfr = Ring(P, st, "bfr", 2, [128, 512], F32)
                hTr = Ring(P, st, "hTr", 2, [128, 8, 512], BF16)
                qts = Ring(P, st, "qts", 2, [128, 8, 512], BF16)
                kts = Ring(P, st, "kts", 2, [128, 7, 512], BF16)
                vsr = Ring(P, st, "vsr", 4, [128, 8, 192], BF16)
                szr = Ring(P, st, "szr", 4, [128, D], BF16)
                qar = Ring(P, st, "qar", 3, [128, 6, 64], BF16)
                kar = Ring(P, st, "kar", 3, [128, 6, 64], BF16)
                qkbr = Ring(P, st, "qkbr", 3, [128, 6, 64], BF16)
                sqb = st.enter_context(nc.sbuf_tensor(U("sqb"), [128, 6, 64], F32))
                xgb = st.enter_context(nc.sbuf_tensor(U("xgb"), [128, 6, 64], F32))
                ss6 = st.enter_context(nc.sbuf_tensor(U("ss6"), [128, 6], F32))
                tA = [st.enter_context(nc.sbuf_tensor(U(f"tA{i}"), [128, 6, 8], F32)) for i in range(4)]
                tB = [st.enter_context(nc.sbuf_tensor(U(f"tB{i}"), [128, 6, 2, 16], F32)) for i in range(4)]
                psT = Ring(P, st, "psT", 2, [128, 8, 128], BF16, psum=True)
                psT2 = Ring(P, st, "psT2", 2, [128, 8, 128], BF16, psum=True)
                psM = Ring(P, st, "psM", 3, [128, 512], F32, psum=True)
                psF = Ring(P, st, "psF", 1, [128, 512], F32, psum=True)

                P.op("sp", lambda e: e.dma_start(out=gpre[:], in_=dr["norm_pre"][l].rearrange("(k p) -> p k", p=128),
                                                 allow_slow_non_contiguous=True), writes=["gpre"], dma="c0")
                P.op("sp", lambda e: e.dma_start(out=gqk[:, 0:4, :], in_=dr["q_norm"][l].partition_broadcast(128).unsqueeze(1).to_broadcast([128, 4, 64]),
                                                 allow_slow_non_contiguous=True), writes=["gqk_q"], dma="c1")
                P.op("sp", lambda e: e.dma_start(out=gqk[:, 4:6, :], in_=dr["k_norm"][l].partition_broadcast(128).unsqueeze(1).to_broadcast([128, 2, 64]),
                                                 allow_slow_non_contiguous=True), writes=["gqk_k"], dma="c2")
                P.op("sp", lambda e: e.dma_start(out=ropa[:], in_=dr["ropea"]), writes=["ropa"], dma="c3")
                P.op("sp", lambda e: e.dma_start(out=ropb[:], in_=dr["ropeb"]), writes=["ropb"], dma="c4")
                for ci in range(32):
                    wbuf, wkey = wst.next()
                    c0 = ci * 120
                    P.op("sp", lambda e, wbuf=wbuf, c0=c0: e.dma_start(
                        out=wbuf[:], in_=dr["w_in"][l][:, c0:c0 + 120].rearrange("(k p) c -> p k c", p=128)),
                        writes=[wkey], dma=wkey)
                    eng = ("pool", "dve", "act")[ci % 3]
                    if eng == "act":
                        P.op("act", lambda e, wbuf=wbuf, c0=c0: e.activation(out=wsb[:, :, c0:c0 + 120], in_=wbuf[:], func=AF.Copy),
                             reads=[wkey], writes=[f"wsb{ci}"])
                    else:
                        P.op(eng, lambda e, wbuf=wbuf, c0=c0: e.tensor_copy(out=wsb[:, :, c0:c0 + 120], in_=wbuf[:]),
                             reads=[wkey], writes=[f"wsb{ci}"])
                WALL = [f"wsb{ci}" for ci in range(32)]

                def wkeys(c0, c1):
                    return [f"wsb{ci}" for ci in range(c0 // 120, (c1 - 1) // 120 + 1)]

                from collections import deque
                blocks1 = []
                for (pn, S, src) in parts:
                    xsrc = dr["x_" + pn] if l == 0 else dr["y1_" + pn]
                    for blk in range(S // 128):
                        blocks1.append(dict(pn=pn, blk=blk, j=blk % 4, tg=blk // 4, xsrc=xsrc))
                pend = deque()

                def run_pend(keep):
                    while len(pend) > keep:
                        pend.popleft()()

                def s1_load(c):
                    xb, xkey = xin.next()
                    c["x"] = (xb, xkey)
                    t0, xsrc = c["blk"] * 128, c["xsrc"]
                    P.op("sp", lambda e: e.dma_start(out=xb[:], in_=xsrc[t0:t0 + 128, :]), writes=[xkey], dma=xkey)

                grp = {}

                def s1_fa(c):
                    xb, xkey = c["x"]
                    ss, sskey = ssr.next()
                    c["ss"] = (ss, sskey)
                    P.op("act", lambda e: e.activation(out=junk[:], in_=xb[:], func=AF.Square, accum_out=ss[:]), reads=[xkey], writes=[sskey, "junk"])
                    rstd_ops(ss[:], sskey, 1, 1.0 / D)

                def s1_front(c):
                    j = c["j"]
                    if j == 0:
                        grp["hT"] = hTr.next()
                        grp["qt"] = qts.next()
                        grp["kt"] = kts.next()
                    c["hT"], c["qt"], c["kt"] = grp["hT"], grp["qt"], grp["kt"]
                    hT, hkey = c["hT"]
                    xb, xkey = c["x"]
                    ss, sskey = c["ss"]
                    xn, xnkey = xnr.next()
                    P.op("act", lambda e: e.activation(out=xn[:], in_=xb[:], func=AF.Copy, scale=ss[:]), reads=[xkey, sskey], writes=[xnkey])
                    pT, pTkey = psT.next()
                    for kc in range(8):
                        P.op("pe", lambda e, kc=kc: e.transpose(out=pT[:, kc, :], in_=xn[:, kc * 128:(kc + 1) * 128], identity=identb[:]),
                             reads=[xnkey, "identb"], writes=[pTkey])
                    P.op("dve", lambda e: e.tensor_tensor(
                        out=hT[:, :, j * 128:(j + 1) * 128], in0=pT[:], in1=gpre[:].unsqueeze(2).to_broadcast([128, 8, 128]), op=ALU.mult),
                        reads=[pTkey, "gpre"], writes=[hkey + f"_{j}"])

                def s1_main(c):
                    j, blk, pn = c["j"], c["blk"], c["pn"]
                    hT, hkey = c["hT"]
                    qt_s, qkey = c["qt"]
                    kt_s, kkey = c["kt"]
                    hk = hkey + f"_{j}"

                    def tok_mm(c0, c1):
                        run_pend(2)
                        pm, pmkey = psM.next()
                        for kc in range(8):
                            P.op("pe", lambda e, kc=kc: e.matmul(pm[:, 0:c1 - c0], lhsT=hT[:, kc, j * 128:(j + 1) * 128], rhs=wsb[:, kc, c0:c1],
                                                                 start=(kc == 0), stop=(kc == 7)),
                                 reads=[hk] + wkeys(c0, c1), writes=[pmkey])
                        return pm, pmkey

                    for which, c0, ring, dst in (("q", AQ, qar, qt_s), ("k", AK, kar, kt_s)):
                        pm, pmkey = tok_mm(c0, c0 + 384)
                        qf, qfkey = qfr.next()
                        P.op("act", lambda e, qf=qf, pm=pm: e.activation(out=qf[:].rearrange("p h d -> p (h d)"), in_=pm[:, 0:384], func=AF.Copy), reads=[pmkey], writes=[qfkey])
                        ob, okey = ring.next()
                        P.op("pool", lambda e, ob=ob, qf=qf: e.tensor_copy(out=ob[:], in_=qf[:]), reads=[qfkey], writes=[okey])
                        cosb = ropa[:, 0, blk, :].unsqueeze(1).to_broadcast([128, 6, 8])
                        sinb = ropa[:, 1, blk, :].unsqueeze(1).to_broadcast([128, 6, 8])
                        x1 = qf[:, :, 0:8]
                        x2 = qf[:, :, 8:16]
                        P.op("dve", lambda e, x1=x1, cosb=cosb: e.tensor_tensor(out=tA[0][:], in0=x1, in1=cosb, op=ALU.mult), reads=[qfkey, "ropa"], writes=["tA0"])
                        P.op("dve", lambda e, x2=x2, sinb=sinb: e.tensor_tensor(out=tA[1][:], in0=x2, in1=sinb, op=ALU.mult), reads=[qfkey, "ropa"], writes=["tA1"])
                        P.op("dve", lambda e, x2=x2, cosb=cosb: e.tensor_tensor(out=tA[2][:], in0=x2, in1=cosb, op=ALU.mult), reads=[qfkey, "ropa"], writes=["tA2"])
                        P.op("dve", lambda e, x1=x1, sinb=sinb: e.tensor_tensor(out=tA[3][:], in0=x1, in1=sinb, op=ALU.mult), reads=[qfkey, "ropa"], writes=["tA3"])
                        P.op("dve", lambda e, ob=ob: e.tensor_tensor(out=ob[:, :, 0:8], in0=tA[0][:], in1=tA[1][:], op=ALU.subtract), reads=["tA0", "tA1", okey], writes=[okey])
                        P.op("dve", lambda e, ob=ob: e.tensor_tensor(out=ob[:, :, 8:16], in0=tA[2][:], in1=tA[3][:], op=ALU.add), reads=["tA2", "tA3", okey], writes=[okey])

                        def fin_a(ob=ob, okey=okey, dst=dst, which=which):
                            pT2, pT2key = psT2.next()
                            obf = ob[:].rearrange("p h d -> p (h d)")
                            for tt in range(3):
                                P.op("pe", lambda e, tt=tt: e.transpose(out=pT2[:, tt, :], in_=obf[:, tt * 128:(tt + 1) * 128], identity=identb[:]),
                                     reads=[okey, "identb"], writes=[pT2key])
                            dkey = (qkey if which == "q" else kkey) + f"_{j}a"
                            P.op("act", lambda e: e.activation(out=dst[:, 0:3, j * 128:(j + 1) * 128], in_=pT2[:, 0:3, :], func=AF.Copy),
                                 reads=[pT2key], writes=[dkey])
                        pend.append(fin_a)
                    vs, vkey = vsr.next()
                    c["vs"] = (vs, vkey)
                    pm, pmkey = tok_mm(AV, AV + 384)
                    P.op("dve", lambda e, pm=pm: e.tensor_copy(out=vs[:, 0:3, :].rearrange("p a (s d) -> p a s d", d=64)[:, :, 0::2, :], in_=pm[:, 0:384].rearrange("p (a s d) -> p a s d", s=2, d=64)),
                         reads=[pmkey], writes=[vkey + "a"])
                    P.op("pool", lambda e: e.memset(vs[:, :, 64:128], 1.0), writes=[vkey + "one"])
                    sz, szkey = szr.next()
                    c["sz"] = (sz, szkey)
                    pm, pmkey = tok_mm(AZ, AZ + 384)
                    P.op("act", lambda e, pm=pm: e.activation(out=sz[:, 0:384], in_=pm[:, 0:384], func=AF.Silu), reads=[pmkey], writes=[szkey + "a"])
                    pm, pmkey = tok_mm(BQ, BQ + 512)
                    bf, bfkey = bfr.next()
                    P.op("act", lambda e, pm=pm, bf=bf: e.activation(out=bf[:], in_=pm[:], func=AF.Copy), reads=[pmkey], writes=[bfkey])
                    bf6 = bf[:, 0:384].rearrange("p (h d) -> p h d", d=64)
                    P.op("pool", lambda e, bf6=bf6: e.tensor_tensor(out=sqb[:], in0=bf6, in1=bf6, op=ALU.mult), reads=[bfkey], writes=["sqb"])
                    P.op("dve", lambda e: e.tensor_reduce(out=ss6[:], in_=sqb[:], axis=AX.X, op=ALU.add), reads=["sqb"], writes=["ss6"])
                    rstd_ops(ss6[:], "ss6", 6, 1.0 / 64)
                    P.op("dve", lambda e, bf6=bf6: e.tensor_tensor(out=xgb[:], in0=bf6, in1=ss6[:].unsqueeze(2).to_broadcast([128, 6, 64]), op=ALU.mult),
                         reads=[bfkey, "ss6"], writes=["xgb"])
                    P.op("pool", lambda e: e.tensor_tensor(out=xgb[:], in0=xgb[:], in1=gqk[:], op=ALU.mult), reads=["xgb", "gqk_q", "gqk_k"], writes=["xgb"])
                    qkb, qkbkey = qkbr.next()
                    xv = xgb[:].rearrange("p h (a b c) -> p h a b c", a=2, b=2)
                    ov = qkb[:].rearrange("p h (a b c) -> p h a b c", a=2, b=2)
                    cb = ropb[:, 0, blk, :].rearrange("p (a c) -> p a c", a=2).unsqueeze(1).to_broadcast([128, 6, 2, 16])
                    sb_ = ropb[:, 1, blk, :].rearrange("p (a c) -> p a c", a=2).unsqueeze(1).to_broadcast([128, 6, 2, 16])
                    x1 = xv[:, :, :, 0, :]
                    x2 = xv[:, :, :, 1, :]
                    P.op("pool", lambda e, x1=x1, cb=cb: e.tensor_tensor(out=tB[0][:], in0=x1, in1=cb, op=ALU.mult), reads=["xgb", "ropb"], writes=["tB0"])
                    P.op("pool", lambda e, x2=x2, sb_=sb_: e.tensor_tensor(out=tB[1][:], in0=x2, in1=sb_, op=ALU.mult), reads=["xgb", "ropb"], writes=["tB1"])
                    P.op("dve", lambda e, x2=x2, cb=cb: e.tensor_tensor(out=tB[2][:], in0=x2, in1=cb, op=ALU.mult), reads=["xgb", "ropb"], writes=["tB2"])
                    P.op("dve", lambda e, x1=x1, sb_=sb_: e.tensor_tensor(out=tB[3][:], in0=x1, in1=sb_, op=ALU.mult), reads=["xgb", "ropb"], writes=["tB3"])
                    P.op("pool", lambda e, ov=ov: e.tensor_tensor(out=ov[:, :, :, 0, :], in0=tB[0][:], in1=tB[1][:], op=ALU.subtract), reads=["tB0", "tB1"], writes=[qkbkey + "x"])
                    P.op("dve", lambda e, ov=ov: e.tensor_tensor(out=ov[:, :, :, 1, :], in0=tB[2][:], in1=tB[3][:], op=ALU.add), reads=["tB2", "tB3"], writes=[qkbkey + "y"])
                    bfv = bf[:, 384:512].rearrange("p (h d) -> p h d", d=64)
                    P.op("pool", lambda e, bfv=bfv: e.tensor_copy(out=vs[:, 3:5, 0:64], in_=bfv), reads=[bfkey], writes=[vkey + "b"])
                    P.op("pool", lambda e, bfv=bfv: e.tensor_copy(out=vs[:, 3:5, 128:192], in_=bfv), reads=[bfkey], writes=[vkey + "b2"])

                    def fin_b(qkb=qkb, qkbkey=qkbkey):
                        pT2, pT2key = psT2.next()
                        qkbf = qkb[:].rearrange("p h d -> p (h d)")
                        for tt in range(3):
                            P.op("pe", lambda e, tt=tt: e.transpose(out=pT2[:, tt, :], in_=qkbf[:, tt * 128:(tt + 1) * 128], identity=identb[:]),
                                 reads=[qkbkey + "x", qkbkey + "y", "identb"], writes=[pT2key])
                        P.op("act", lambda e: e.activation(out=qt_s[:, 3:5, j * 128:(j + 1) * 128], in_=pT2[:, 0:2, :], func=AF.Copy),
                             reads=[pT2key], writes=[qkey + f"_{j}b"])
                        P.op("act", lambda e: e.activation(out=kt_s[:, 3, j * 128:(j + 1) * 128], in_=pT2[:, 2, :], func=AF.Copy),
                             reads=[pT2key], writes=[kkey + f"_{j}b"])
                    pend.append(fin_b)
                    pm, pmkey = tok_mm(BZ, BZ + 256)
                    P.op("act", lambda e, pm=pm: e.activation(out=sz[:, 384:640], in_=pm[:, 0:256], func=AF.Silu), reads=[pmkey], writes=[szkey + "b"])
                    pm, pmkey = tok_mm(CV, CV + 384)
                    P.op("dve", lambda e, pm=pm: e.tensor_copy(out=vs[:, 5:8, :].rearrange("p a (s d) -> p a s d", d=64)[:, :, 0::2, :], in_=pm[:, 0:384].rearrange("p (a s d) -> p a s d", s=2, d=64)),
                         reads=[pmkey], writes=[vkey + "c"])
                    pm, pmkey = tok_mm(CZ, CZ + 384)
                    P.op("act", lambda e, pm=pm: e.activation(out=sz[:, 640:1024], in_=pm[:, 0:384], func=AF.Silu), reads=[pmkey], writes=[szkey + "c"])
                    if j == 3:
                        hks = [hkey + f"_{jj}" for jj in range(4)]
                        for ti in range(6):
                            run_pend(2)
                            c0 = (CQ if ti < 3 else CK) + (ti % 3) * 128
                            pf, pfkey = psF.next()
                            for kc in range(8):
                                P.op("pe", lambda e, pf=pf, kc=kc, c0=c0: e.matmul(pf[:], lhsT=wsb[:, kc, c0:c0 + 128], rhs=hT[:, kc, :], start=(kc == 0), stop=(kc == 7)),
                                     reads=hks + wkeys(c0, c0 + 128), writes=[pfkey])
                            if ti < 3:
                                P.op("dve", lambda e, pf=pf, ti=ti: e.tensor_copy(out=qt_s[:, 5 + ti, :], in_=pf[:]), reads=[pfkey], writes=[qkey + f"_c{ti}"])
                            else:
                                P.op("act", lambda e, pf=pf, ti=ti: e.activation(out=kt_s[:, 4 + ti - 3, :], in_=pf[:], func=AF.Copy), reads=[pfkey], writes=[kkey + f"_c{ti}"])

                def s1_store(c):
                    j, blk, pn = c["j"], c["blk"], c["pn"]
                    t0 = blk * 128
                    vs, vkey = c["vs"]
                    sz, szkey = c["sz"]
                    qt_s, qkey = c["qt"]
                    kt_s, kkey = c["kt"]
                    P.op("sp", lambda e: e.dma_start(out=dr["v_" + pn][t0:t0 + 128, :], in_=vs[:].rearrange("p h d -> p (h d)")),
                         reads=[vkey + "a", vkey + "b", vkey + "b2", vkey + "c", vkey + "one"], dma=vkey)
                    P.op("sp", lambda e: e.dma_start(out=dr["sz_" + pn][t0:t0 + 128, :], in_=sz[:].bitcast(F32)),
                         reads=[szkey + "a", szkey + "b", szkey + "c"], dma=szkey)
                    if j == 3:
                        tt0 = c["tg"] * 512
                        qr = [qkey + f"_{jj}a" for jj in range(4)] + [qkey + f"_{jj}b" for jj in range(4)] + [qkey + f"_c{ti}" for ti in range(3)]
                        kr = [kkey + f"_{jj}a" for jj in range(4)] + [kkey + f"_{jj}b" for jj in range(4)] + [kkey + f"_c{ti}" for ti in range(3, 6)]
                        P.op("sp", lambda e: e.dma_start(out=dr["qt_" + pn][:, tt0:tt0 + 512].rearrange("(k p) t -> p k t", p=128), in_=qt_s[:]), reads=qr, dma=qkey)
                        P.op("sp", lambda e: e.dma_start(out=dr["kt_" + pn][:, tt0:tt0 + 512].rearrange("(k p) t -> p k t", p=128), in_=kt_s[:]), reads=kr, dma=kkey)

                nb1 = len(blocks1)
                for i in range(-3, nb1 + 2):
                    if 0 <= i + 3 < nb1:
                        s1_load(blocks1[i + 3])
                    if 0 <= i + 2 < nb1:
                        s1_fa(blocks1[i + 2])
                    if 0 <= i + 1 < nb1:
                        s1_front(blocks1[i + 1])
                    if 0 <= i < nb1:
                        s1_main(blocks1[i])
                        if blocks1[i]["j"] == 3:
                            run_pend(0)
                    if 0 <= i - 2 < nb1:
                        s1_store(blocks1[i - 2])
                run_pend(0)
                P.flush(final=(STOP == "s1"))
            if STOP == "s1":
                return nc

            with ExitStack() as st:
                SMAXP = max(S for (_, S, _) in parts)
                qpr = Ring(P, st, "qpr", 2, [128, SMAXP], BF16)
                kpr = Ring(P, st, "kpr", 2, [128, SMAXP], BF16)
                vbr = Ring(P, st, "vbr", 2, [128, SMAXP // 128, 192], BF16)
                eab = st.enter_context(nc.sbuf_tensor(U("eab"), [128, 5, 128], BF16))
                m16 = st.enter_context(nc.sbuf_tensor(U("m16"), [128, 3, 128], BF16))
                accs = [(st.enter_context(nc.sbuf_tensor(U(f"acc{hh}"), [128, 2048], F32)), f"acc{hh}") for hh in range(2)]
                v16r = Ring(P, st, "v16r", 1, [128, SMAXP // 128, 192], BF16)
                est = Ring(P, st, "est", 2, [128, 8 * 128], F32)
                efb = st.enter_context(nc.sbuf_tensor(U("efb"), [128, 6, 7, 128], BF16))
                eib = st.enter_context(nc.sbuf_tensor(U("eib"), [128, 6, 5, 128], BF16))
                ptr = Ring(P, st, "ptr", 6, [128, 512], BF16)
                osr = Ring(P, st, "osr", 2, [128, 512], F32)
                ogr = Ring(P, st, "ogr", 2, [128, 4, 128], F32)
                rlr = Ring(P, st, "rlr", 2, [128, 4], F32)
                psS = Ring(P, st, "psS", 4, [128, 512], F32, psum=True)
                psO = Ring(P, st, "psO", 2, [128, 512], F32, psum=True)
                psR = Ring(P, st, "psR", 2, [128, 4, 128], F32, psum=True)

                eb, ekey = est.next()
                P.op("sp", lambda e, eb=eb: e.dma_start(out=eb[:], in_=dr["ea"].rearrange("p a b -> p (a b)")), writes=[ekey], dma=ekey)
                P.op("dve", lambda e, eb=eb: e.tensor_copy(out=eab[:].rearrange("p a b -> p (a b)"), in_=eb[:, 0:5 * 128]), reads=[ekey], writes=["eab"])
                P.op("dve", lambda e, eb=eb: e.tensor_copy(out=m16[:].rearrange("p a b -> p (a b)"), in_=eb[:, 5 * 128:8 * 128]), reads=[ekey], writes=["m16"])
                for h in range(6):
                    eb, ekey = est.next()
                    P.op("sp", lambda e, eb=eb, h=h: e.dma_start(out=eb[:, 0:7 * 128], in_=dr["efraw"][l, h].rearrange("p a b -> p (a b)")), writes=[ekey], dma=ekey)
                    P.op("act", lambda e, eb=eb, h=h: e.activation(out=efb[:, h].rearrange("p a b -> p (a b)"), in_=eb[:, 0:7 * 128], func=AF.Exp),
                         reads=[ekey], writes=[f"efb{h}"])
                    P.op("dve", lambda e, h=h: e.tensor_copy(out=eib[:, h], in_=efb[:, h, 1:6, :]), reads=[f"efb{h}"], writes=[f"eib{h}"])
                    P.op("dve", lambda e, h=h: e.memset(eib[0:64, h, 0, 64:128], 0.0), reads=[f"eib{h}"], writes=[f"eib{h}"])
                    P.op("dve", lambda e, h=h: e.memset(eib[64:128, h, 4, :], 0.0), reads=[f"eib{h}"], writes=[f"eib{h}"])
                    P.op("dve", lambda e, h=h: e.memset(eib[0:64, h, 4, 0:64], 0.0), reads=[f"eib{h}"], writes=[f"eib{h}"])

                jobs = []
                for (pn, S, src) in parts:
                    for gp in range(8):
                        jobs.append(dict(pn=pn, S=S, gp=gp, dyn=(last and pn == QPN and 3 <= gp < 5)))

                def load_job(jb):
                    pn, S, gp = jb["pn"], jb["S"], jb["gp"]
                    NB = S // 128
                    qtd, ktd, vd = dr["qt_" + pn], dr["kt_" + pn], dr["v_" + pn]
                    if gp < 3:
                        pr = gp
                        qrow, krows = pr * 128, [pr * 128, pr * 128 + 64]
                    elif gp < 5:
                        pr = gp - 3
                        qrow, krows = 384 + pr * 128, [384 + pr * 64, 384 + pr * 64]
                    else:
                        pr = gp - 5
                        qrow, krows = 640 + pr * 128, [512 + pr * 128, 512 + pr * 128 + 64]
                    vb, vbkey = vbr.next()
                    nch = 4 if S > 2048 else 1
                    for ch in range(nch):
                        b0 = ch * (NB // nch)
                        b1 = (ch + 1) * (NB // nch)
                        P.op("sp", lambda e, b0=b0, b1=b1: e.dma_start(
                            out=vb[:, b0:b1, :], in_=vd[b0 * 128:b1 * 128, gp * 192:(gp + 1) * 192].rearrange("(b p) c -> p b c", p=128)),
                            writes=[vbkey + f"_{ch}"], dma=f"{vbkey}_{ch}")
                    jb["vb"] = (vb, vbkey, [vbkey + f"_{ch}" for ch in range(nch)])

                    qp, qpkey = qpr.next()
                    kp, kpkey = kpr.next()
                    if jb.get("dyn"):
                        P.op("sp", lambda e: e.dma_start(out=qp[:, 0:2048], in_=dyn_ap(e, rq0, qrow * S, qtd, [[S, 128], [1, 2048]])), writes=[qpkey], dma=qpkey)
                    else:
                        P.op("sp", lambda e: e.dma_start(out=qp[:, 0:S], in_=qtd[qrow:qrow + 128, :]), writes=[qpkey], dma=qpkey)
                    for hh in range(2):
                        P.op("sp", lambda e, hh=hh: e.dma_start(out=kp[hh * 64:(hh + 1) * 64, 0:S], in_=ktd[krows[hh]:krows[hh] + 64, :]),
                             writes=[kpkey + f"_{hh}"], dma=f"{kpkey}_{hh}")
                    jb["qp"] = (qp, qpkey)
                    jb["kp"] = (kp, kpkey)

                load_job(jobs[0])
                for ji, jb in enumerate(jobs):
                    if ji + 1 < len(jobs):
                        load_job(jobs[ji + 1])
                    pn, S, gp = jb["pn"], jb["S"], jb["gp"]
                    NB = S // 128
                    NW = S // 512
                    od = dr["o_" + pn]
                    if True:
                        if gp < 3:
                            br, pr = "A", gp
                            ocol = pr * 128
                        elif gp < 5:
                            br, pr = "B", gp - 3
                            ocol = 384 + pr * 128
                        else:
                            br, pr = "C", gp - 5
                            ocol = 640 + pr * 128
                        vb, vbkey, vkeys = jb["vb"]
                        qp, qpkey = jb["qp"]
                        kp, kpkey = jb["kp"]
                        groups = []

                        def dense_groups(w):
                            for qb in range(4):
                                b = w * 4 + qb
                                if br == "A":
                                    alist = list(range(max(0, b - 2), min(NB, b + 3)))
                                    kind, off = "ea", 2
                                elif b <= 1:
                                    alist, kind, off = list(range(0, 4)), "ef", 3
                                elif b >= NB - 2:
                                    alist, kind, off = list(range(NB - 4, NB)), "ef", 3
                                else:
                                    alist, kind, off = list(range(b - 2, b + 3)), "ei", 2
                                nu = len(alist)
                                gi = 0
                                while gi < nu:
                                    gn = min(4, nu - gi)
                                    if nu - gi == 5:
                                        gn = 3
                                    us = alist[gi:gi + gn]
                                    groups.append(dict(units=[dict(k=(a * 128, 1), q=(b * 128, 1, 128), pc0=ui * 128, v=("nat", a), oc0=qb * 128,
                                                                   st=(gi + ui == 0), sp=(gi + ui == nu - 1)) for ui, a in enumerate(us)],
                                                       e=(kind, us[0] - b + off, gn, False), tail=("win" if (qb == 3 and gi + gn == nu) else None), w=w))
                                    gi += gn

                        dyn = jb.get("dyn")
                        if br == "B":
                            for w in range(4 if dyn else NW):
                                for kb in range(NB):
                                    groups.append(dict(units=[dict(k=(kb * 128, 1), q=(w * 512, 1, 512), pc0=0, v=("nat", kb), oc0=0, st=(kb == 0), sp=(kb == NB - 1))],
                                                       e=None, tail=("win" if kb == NB - 1 else None), w=w))
                        elif br == "C":
                            for w in range(NW):
                                dense_groups(w)
                        else:
                            nj = NB // 16
                            for sw in range(nj):
                                for rq in range(4):
                                    ulist = [u for u in (-1, 0, 1) if 0 <= sw + u < nj]
                                    if len(ulist) == 1:
                                        units = []
                                        for ri in range(4):
                                            r = 4 * rq + ri
                                            units.append(dict(k=(sw * 2048 + r, 16), q=(sw * 2048 + r, 16, 128), pc0=ri * 128, v=("v16", r * nj + sw), oc0=ri * 128, st=True, sp=True))
                                        groups.append(dict(units=units, e=("m16", 1, 4, True), tail="quad", sw=sw, rq=rq))
                                    else:
                                        for ri in range(4):
                                            r = 4 * rq + ri
                                            units = []
                                            for ui, u in enumerate(ulist):
                                                units.append(dict(k=((sw + u) * 2048 + r, 16), q=(sw * 2048 + r, 16, 128), pc0=ui * 128, v=("v16", r * nj + sw + u), oc0=ri * 128,
                                                                  st=(ui == 0), sp=(ui == len(ulist) - 1)))
                                            groups.append(dict(units=units, e=("m16", ulist[0] + 1, len(ulist), False), tail=("quad" if ri == 3 else None), sw=sw, rq=rq))
                                for w in range(sw * 4, sw * 4 + 4):
                                    dense_groups(w)
                        ng = len(groups)
                        pos = [psO.next(), psO.next()]
                        state = {}
                        v16 = None
                        if br == "A":
                            v16b, v16key = v16r.next()
                            nj_ = NB // 16
                            vsrc = dr["v_" + pn][:, gp * 192:(gp + 1) * 192].rearrange("(jj i r) c -> r i jj c", i=128, r=16)
                            for r in range(16):
                                P.op("sp", lambda e, r=r, v16b=v16b, vsrc=vsrc, nj_=nj_: e.dma_start(out=v16b[:, r * nj_:(r + 1) * nj_, :], in_=vsrc[r]),
                                     writes=[v16key + f"_{r}"], dma=f"{v16key}_{r}")
                            v16 = (v16b, [v16key + f"_{r}" for r in range(16)])

                        def sl(start, stride, n):
                            return slice(start, start + (n - 1) * stride + 1, stride) if stride != 1 else slice(start, start + n)

                        def emit_front2(g, qp=qp, kp=kp, qpkey=qpkey, kpkey=kpkey, pr=pr):
                            pss = [psS.next(), psS.next()]
                            ncols = 0
                            for u in g["units"]:
                                ks, kst = u["k"]
                                qs, qst, n = u["q"]
                                pc0 = u["pc0"]
                                for hh in range(2):
                                    ps, pskey = pss[hh]
                                    P.op("pe", lambda e, ps=ps, ks=ks, kst=kst, qs=qs, qst=qst, n=n, pc0=pc0, hh=hh: e.matmul(
                                        ps[:, pc0:pc0 + n], lhsT=kp[hh * 64:(hh + 1) * 64, sl(ks, kst, 128)], rhs=qp[hh * 64:(hh + 1) * 64, sl(qs, qst, n)], start=True, stop=True),
                                        reads=[qpkey, kpkey + f"_{hh}"], writes=[pskey])
                                ncols = max(ncols, pc0 + n)
                            for hh in range(2):
                                ps, pskey = pss[hh]
                                pt, ptkey = ptr.next()
                                P.op("act", lambda e, ps=ps, pt=pt, ncols=ncols: e.activation(out=pt[:, 0:ncols], in_=ps[:, 0:ncols], func=AF.Exp, scale=0.125),
                                     reads=[pskey], writes=[ptkey])
                                if g["e"] is not None:
                                    kind, i0_, gn, bc = g["e"]
                                    hglob = 2 * pr + hh
                                    if kind == "ea":
                                        e_ap, ekeys = eab[:, i0_:i0_ + gn, :], ["eab"]
                                    elif kind == "m16":
                                        if bc:
                                            e_ap, ekeys = m16[:, i0_:i0_ + 1, :].to_broadcast([128, gn, 128]), ["m16"]
                                        else:
                                            e_ap, ekeys = m16[:, i0_:i0_ + gn, :], ["m16"]
                                    elif kind == "ef":
                                        e_ap, ekeys = efb[:, hglob, i0_:i0_ + gn, :], [f"efb{hglob}"]
                                    else:
                                        e_ap, ekeys = eib[:, hglob, i0_:i0_ + gn, :], [f"eib{hglob}"]
                                    P.op("dve", lambda e, pt=pt, e_ap=e_ap, gn=gn: e.tensor_tensor(
                                        out=pt[:, 0:gn * 128].rearrange("p (a b) -> p a b", b=128), in0=pt[:, 0:gn * 128].rearrange("p (a b) -> p a b", b=128), in1=e_ap, op=ALU.mult),
                                        reads=[ptkey] + ekeys, writes=[ptkey])
                                g["pt%d" % hh] = (pt, ptkey)

                        def emit_pv2(g, vb=vb, vkeys=vkeys):
                            for u in g["units"]:
                                vkind, vblk = u["v"]
                                pc0, oc0, st_, sp_ = u["pc0"], u["oc0"], u["st"], u["sp"]
                                n = u["q"][2]
                                if vkind == "nat":
                                    vt, vks = vb, vkeys
                                else:
                                    vt, vks = v16[0], v16[1]
                                for hh in range(2):
                                    pt, ptkey = g["pt%d" % hh]
                                    po, pokey = pos[hh]
                                    P.op("pe", lambda e, po=po, vt=vt, vblk=vblk, pc0=pc0, n=n, oc0=oc0, st_=st_, sp_=sp_, pt=pt, hh=hh: e.matmul(
                                        po[:, oc0:oc0 + n], lhsT=vt[:, vblk, hh * 64:hh * 64 + 128], rhs=pt[:, pc0:pc0 + n], start=st_, stop=sp_),
                                        reads=[ptkey] + vks, writes=[pokey])

                        def emit_back(g, hh, ocol=ocol, od=od, br=br, dyn=dyn):
                            po, pokey = pos[hh]
                            if g["tail"] == "quad":
                                rq = g["rq"]
                                ac, ackey = accs[hh]
                                dst = ac[:].rearrange("p (l r) -> p r l", r=16)[:, 4 * rq:4 * rq + 4, :]
                                P.op("act", lambda e, po=po, dst=dst: e.activation(out=dst, in_=po[:].rearrange("p (a b) -> p a b", b=128), func=AF.Copy),
                                     reads=[pokey], writes=[ackey + f"_{rq}"])
                            if g["tail"] == "win":
                                osb, oskey = osr.next()
                                w = g["w"]
                                if br == "A":
                                    ac, ackey = accs[hh]
                                    wl = (w % 4) * 512
                                    P.op("dve", lambda e, osb=osb, po=po, ac=ac, wl=wl: e.tensor_tensor(out=osb[:], in0=po[:], in1=ac[:, wl:wl + 512], op=ALU.add),
                                         reads=[pokey] + [ackey + f"_{q}" for q in range(4)], writes=[oskey])
                                else:
                                    P.op("act", lambda e, osb=osb, po=po: e.activation(out=osb[:], in_=po[:], func=AF.Copy), reads=[pokey], writes=[oskey])
                                prr, prkey = psR.next()
                                for jj in range(4):
                                    P.op("pe", lambda e, prr=prr, osb=osb, jj=jj: e.transpose(out=prr[:, jj, :], in_=osb[:, jj * 128:(jj + 1) * 128], identity=identf[:]),
                                         reads=[oskey, "identf"], writes=[prkey])
                                rl, rlkey = rlr.next()
                                lcol = 64 if hh == 0 else 0
                                P.op("dve", lambda e, rl=rl, prr=prr, lcol=lcol: e.reciprocal(out=rl[:], in_=prr[:, :, lcol]), reads=[prkey], writes=[rlkey])
                                if hh == 0:
                                    state["og"] = ogr.next()
                                og, ogkey = state["og"]
                                P.op("dve", lambda e, og=og, prr=prr, rl=rl, hh=hh: e.tensor_tensor(
                                    out=og[:, :, hh * 64:(hh + 1) * 64], in0=prr[:, :, hh * 64:(hh + 1) * 64], in1=rl[:].unsqueeze(2).to_broadcast([128, 4, 64]), op=ALU.mult),
                                    reads=[prkey, rlkey], writes=[ogkey + f"_{hh}"])
                                if hh == 1 and dyn:
                                    P.op("sp", lambda e, og=og, w=w: e.dma_start(
                                        out=dr["obq"][w * 512:(w + 1) * 512, ocol - 384:ocol - 384 + 128].rearrange("(b p) c -> p b c", p=128), in_=og[:]),
                                        reads=[ogkey + "_0", ogkey + "_1"], dma=ogkey)
                                elif hh == 1:
                                    P.op("sp", lambda e, og=og, w=w: e.dma_start(
                                        out=od[w * 512:(w + 1) * 512, ocol:ocol + 128].rearrange("(b p) c -> p b c", p=128), in_=og[:]),
                                        reads=[ogkey + "_0", ogkey + "_1"], dma=ogkey)

                        LOOK = 1
                        for i in range(ng + LOOK):
                            if i < ng:
                                emit_front2(groups[i])
                            if i - LOOK >= 0:
                                emit_pv2(groups[i - LOOK])
                                emit_back(groups[i - LOOK], 0)
                                emit_back(groups[i - LOOK], 1)
                P.flush(final=(STOP == "s2"))
            if STOP == "s2":
                return nc

            with ExitStack() as st:
                wo = st.enter_context(nc.sbuf_tensor(U("wo"), [128, 8, D], BF16))
                wst3 = Ring(P, st, "wst3", 2, [128, 8, 256], F32)
                gbr = st.enter_context(nc.sbuf_tensor(U("gbr"), [128, D], F32))
                gpo = st.enter_context(nc.sbuf_tensor(U("gpo"), [128, D], F32))
                invw = st.enter_context(nc.sbuf_tensor(U("invw"), [128, 3], F32))
                o_r = Ring(P, st, "o_r", 4, [128, D], F32)
                z_r = Ring(P, st, "z_r", 4, [128, D // 2], F32)
                x_r = Ring(P, st, "x_r", 5, [128, D], F32)
                g_r = Ring(P, st, "g_r", 6, [128, D], F32)
                pc_r = Ring(P, st, "pc_r", 6, [128, D], F32)
                junk3 = st.enter_context(nc.sbuf_tensor(U("junk3"), [128, D], BF16))
                s3r = Ring(P, st, "s3r", 6, [128, 3], F32)
                s2r = Ring(P, st, "s2r", 6, [128, 2], F32)
                ybr = Ring(P, st, "ybr", 3, [128, D], BF16)
                yTr = Ring(P, st, "yTr", 3, [128, 8, 128], BF16)
                t_r = Ring(P, st, "t_r", 5, [128, D], F32)
                psT3 = Ring(P, st, "psT3", 2, [128, 8, 128], BF16, psum=True)
                psY = Ring(P, st, "psY", 3, [128, D], F32, psum=True)

                for ci in range(4):
                    wbuf, wkey = wst3.next()
                    c0 = ci * 256
                    P.op("sp", lambda e, wbuf=wbuf, c0=c0: e.dma_start(out=wbuf[:], in_=dr["w_out"][l][:, c0:c0 + 256].rearrange("(k p) c -> p k c", p=128)),
                         writes=[wkey], dma=wkey)
                    P.op("pool" if ci % 2 == 0 else "dve", lambda e, wbuf=wbuf, c0=c0: e.tensor_copy(out=wo[:, :, c0:c0 + 256], in_=wbuf[:]), reads=[wkey], writes=[f"wo{ci}"])
                WO = [f"wo{ci}" for ci in range(4)]
                P.op("sp", lambda e: e.dma_start(out=gbr[:], in_=dr["branch_gain"][l].partition_broadcast(128), allow_slow_non_contiguous=True), writes=["gbr"], dma="c5")
                P.op("sp", lambda e: e.dma_start(out=gpo[:], in_=dr["norm_post"][l].partition_broadcast(128), allow_slow_non_contiguous=True), writes=["gpo"], dma="c6")
                P.op("dve", lambda e: e.memset(invw[:, 0:1], 1.0 / 384), writes=["invw"])
                P.op("dve", lambda e: e.memset(invw[:, 1:2], 1.0 / 256), writes=["invw"])
                P.op("dve", lambda e: e.memset(invw[:, 2:3], 1.0 / 384), writes=["invw"])
                BR = ((0, 384), (384, 640), (640, 1024))
                blocks3 = []
                qblocks = []
                for (pn, S, src) in parts:
                    xsrc = dr["x_" + pn] if l == 0 else dr["y1_" + pn]
                    if last and pn == QPN:
                        P.op("sp", lambda e, pn=pn: e.dma_start(out=dr["oq"][:, 0:384], in_=dyn_ap(e, rrow, 0, dr["o_" + pn], [[D, 2048], [1, 384]])), writes=["oq"], dma="cq0")
                        P.op("sp", lambda e, pn=pn: e.dma_start(out=dr["oq"][:, 640:1024], in_=dyn_ap(e, rrow, 640, dr["o_" + pn], [[D, 2048], [1, 384]])), reads=["oq"], writes=["oq"], dma="cq4")
                        P.op("sp", lambda e, pn=pn: e.dma_start(out=dr["szq"], in_=dyn_ap(e, rrow2, 0, dr["sz_" + pn], [[D // 2, 2048], [1, D // 2]])), writes=["szq"], dma="cq1")
                        P.op("sp", lambda e, xsrc=xsrc: e.dma_start(out=dr["xq"], in_=dyn_ap(e, rrow, 0, xsrc, [[D, 2048], [1, D]])), writes=["xq"], dma="cq2")
                        P.op("sp", lambda e: e.dma_start(out=dr["oq"][:, 384:640], in_=dr["obq"]), reads=["oq"], writes=["oq"], dma="cq3")
                        qblocks = [dict(pn=pn, t0=blk * 128, osrc=dr["oq"], zsrc=dr["szq"], xsrc=dr["xq"], ydst=dr["yq_" + pn], keys=["oq", "szq", "xq"]) for blk in range(16)]
                        continue
                    ydst = dr["y_" + pn] if last else dr["y1_" + pn]
                    for blk in range(S // 128):
                        blocks3.append(dict(pn=pn, t0=blk * 128, osrc=dr["o_" + pn], zsrc=dr["sz_" + pn], xsrc=xsrc, ydst=ydst, keys=[]))
                blocks3 = blocks3 + qblocks

                def p_load(c):
                    pn, t0 = c["pn"], c["t0"]
                    ob, okey = o_r.next()
                    zb, zkey = z_r.next()
                    c["o"], c["z"] = (ob, okey), (zb, zkey)
                    osrc, zsrc = c["osrc"], c["zsrc"]
                    P.op("sp", lambda e: e.dma_start(out=ob[:], in_=osrc[t0:t0 + 128, :]), reads=c["keys"][0:1], writes=[okey], dma=okey)
                    P.op("sp", lambda e: e.dma_start(out=zb[:], in_=zsrc[t0:t0 + 128, :]), reads=c["keys"][1:2], writes=[zkey], dma=zkey)

                def p_g(c):
                    ob, okey = c["o"]
                    zb, zkey = c["z"]
                    gb, gkey = g_r.next()
                    c["g"] = (gb, gkey)
                    P.op("pool", lambda e: e.tensor_tensor(out=gb[:], in0=ob[:], in1=zb[:].bitcast(BF16), op=ALU.mult), reads=[okey, zkey], writes=[gkey])

                def p_sq(c):
                    gb, gkey = c["g"]
                    s3, s3key = s3r.next()
                    c["s3"] = (s3, s3key)
                    for bi, (c0, c1) in enumerate(BR):
                        P.op("act", lambda e, bi=bi, c0=c0, c1=c1: e.activation(out=junk3[:, c0:c1], in_=gb[:, c0:c1], func=AF.Square, accum_out=s3[:, bi:bi + 1]),
                             reads=[gkey], writes=[s3key, "junk3"])

                def p_r1(c):
                    s3, s3key = c["s3"]
                    P.op("dve", lambda e: e.tensor_tensor(out=s3[:], in0=s3[:], in1=invw[:], op=ALU.mult), reads=[s3key, "invw"], writes=[s3key])
                    P.op("dve", lambda e: e.tensor_scalar(out=s3[:], in0=s3[:], scalar1=1.0, scalar2=float(EPS), op0=ALU.mult, op1=ALU.add), reads=[s3key], writes=[s3key])

                def p_r2(c):
                    s3, s3key = c["s3"]
                    P.op("pool", lambda e: e.tensor_tensor(out=s3[:], in0=s3[:], in1=nhalf[:, 0:3], op=ALU.pow), reads=[s3key], writes=[s3key])

                def p_y(c):
                    gb, gkey = c["g"]
                    s3, s3key = c["s3"]
                    yb, ykey = ybr.next()
                    c["y"] = (yb, ykey)
                    for bi, (c0, c1) in enumerate(BR):
                        P.op("dve", lambda e, bi=bi, c0=c0, c1=c1: e.scalar_tensor_tensor(
                            out=yb[:, c0:c1], in0=gb[:, c0:c1], scalar=s3[:, bi:bi + 1], in1=gbr[:, c0:c1], op0=ALU.mult, op1=ALU.mult),
                            reads=[gkey, s3key, "gbr"], writes=[ykey])

                def p_T(c):
                    yb, ykey = c["y"]
                    pT, pTkey = psT3.next()
                    c["pT"] = (pT, pTkey)
                    for kc in range(8):
                        P.op("pe", lambda e, kc=kc: e.transpose(out=pT[:, kc, :], in_=yb[:, kc * 128:(kc + 1) * 128], identity=identb[:]),
                             reads=[ykey, "identb"], writes=[pTkey])

                def p_yT(c):
                    pT, pTkey = c["pT"]
                    yT, yTkey = yTr.next()
                    c["yT"] = (yT, yTkey)
                    P.op("act", lambda e: e.activation(out=yT[:], in_=pT[:], func=AF.Copy), reads=[pTkey], writes=[yTkey])

                def p_mm(c):
                    yT, yTkey = c["yT"]
                    py, pykey = psY.next()
                    c["py"] = (py, pykey)
                    for n in range(2):
                        for kc in range(8):
                            P.op("pe", lambda e, n=n, kc=kc: e.matmul(py[:, n * 512:(n + 1) * 512], lhsT=yT[:, kc, :], rhs=wo[:, kc, n * 512:(n + 1) * 512],
                                                                      start=(kc == 0), stop=(kc == 7)),
                                 reads=[yTkey] + WO, writes=[pykey])

                def p_ev(c):
                    py, pykey = c["py"]
                    s2, s2key = s2r.next()
                    c["s2"] = (s2, s2key)
                    pc, pckey = pc_r.next()
                    c["pc"] = (pc, pckey)
                    for n in range(2):
                        P.op("act", lambda e, n=n: e.activation(out=junk3[:, n * 512:(n + 1) * 512], in_=py[:, n * 512:(n + 1) * 512], func=AF.Square, accum_out=s2[:, n:n + 1]),
                             reads=[pykey], writes=[s2key, "junk3"])
                    P.op("act", lambda e: e.activation(out=pc[:], in_=py[:], func=AF.Copy), reads=[pykey], writes=[pckey])

                def p_r3(c):
                    s2, s2key = c["s2"]
                    P.op("dve", lambda e: e.tensor_tensor(out=s2[:, 0:1], in0=s2[:, 0:1], in1=s2[:, 1:2], op=ALU.add), reads=[s2key], writes=[s2key])
                    P.op("dve", lambda e: e.tensor_scalar(out=s2[:, 0:1], in0=s2[:, 0:1], scalar1=1.0 / D, scalar2=float(EPS), op0=ALU.mult, op1=ALU.add), reads=[s2key], writes=[s2key])

                def p_r4(c):
                    s2, s2key = c["s2"]
                    P.op("pool", lambda e: e.tensor_tensor(out=s2[:, 0:1], in0=s2[:, 0:1], in1=nhalf[:, 0:1], op=ALU.pow), reads=[s2key], writes=[s2key])
                    t0, xsrc = c["t0"], c["xsrc"]
                    xb, xkey = x_r.next()
                    c["x"] = (xb, xkey)
                    P.op("sp", lambda e: e.dma_start(out=xb[:], in_=xsrc[t0:t0 + 128, :]), reads=c["keys"][2:3], writes=[xkey], dma=xkey)

                def p_stt(c):
                    pc, pckey = c["pc"]
                    s2, s2key = c["s2"]
                    tb, tkey = t_r.next()
                    c["t"] = (tb, tkey)
                    P.op("dve", lambda e: e.scalar_tensor_tensor(out=tb[:], in0=pc[:], scalar=s2[:, 0:1], in1=gpo[:], op0=ALU.mult, op1=ALU.mult),
                         reads=[pckey, s2key, "gpo"], writes=[tkey])

                def p_add(c):
                    tb, tkey = c["t"]
                    xb, xkey = c["x"]
                    P.op("pool", lambda e: e.tensor_tensor(out=tb[:], in0=tb[:], in1=xb[:], op=ALU.add), reads=[tkey, xkey], writes=[tkey])

                def p_st(c):
                    tb, tkey = c["t"]
                    t0, ydst = c["t0"], c["ydst"]
                    P.op("sp", lambda e: e.dma_start(out=ydst[t0:t0 + 128, :], in_=tb[:]), reads=[tkey], dma=tkey)

                phases = [(p_load, 0), (p_g, 2), (p_sq, 3), (p_r1, 4), (p_r2, 5), (p_y, 6), (p_T, 7), (p_yT, 8), (p_mm, 9), (p_ev, 10),
                          (p_r3, 11), (p_r4, 12), (p_stt, 14), (p_add, 15), (p_st, 17)]
                nb3 = len(blocks3)
                for i in range(nb3 + 18):
                    for fn, dly in phases:
                        if 0 <= i - dly < nb3:
                            fn(blocks3[i - dly])
                P.flush(final=last)
        print(f"[build] instructions: {P.n_ins}", flush=True)
    return nc


_PARTS = [("p", 8192, "xp"), ("s0", 2048, "xs0"), ("s1", 2048, "xs1")]


def kernel(x_prompt, x_sample, norm_pre, w_in, q_norm, k_norm, rel_bias, branch_gain, w_out, norm_post):
    f = lambda a: np.ascontiguousarray(np.asarray(a, dtype=np.float32))
    x_prompt, x_sample = f(x_prompt), f(x_sample)
    ropea, ropeb, ea, ident = _const_tables()
    efraw = _c_bias_tables(f(rel_bias))
    nc = build(_PARTS, 2)
    shared = dict(w_in=f(w_in), w_out=f(w_out), norm_pre=f(norm_pre), norm_post=f(norm_post), branch_gain=f(branch_gain),
                  q_norm=f(q_norm), k_norm=f(k_norm), ropea=ropea, ropeb=ropeb, ea=ea, ident=ident, efraw=efraw)
    in_maps = []
    for c in range(8):
        m = dict(shared)
        q0 = (c % 4) * 2048
        m["qoff"] = np.array([[q0, q0 * D, q0 * (D // 2), 0]], dtype=np.int32)
        m["x_p"] = x_prompt[c // 4]
        m["x_s0"] = x_sample[2 * c]
        m["x_s1"] = x_sample[2 * c + 1]
        in_maps.append(m)
    res = run_bass_kernel_spmd(nc, in_maps, core_ids=list(range(8)))
    r = res.results
    y_prompt = np.stack([np.concatenate([np.asarray(r[4 * b + q]["yq_p"], dtype=np.float32) for q in range(4)], axis=0) for b in range(2)], axis=0)
    ys = []
    for c in range(8):
        ys.append(np.asarray(r[c]["y_s0"], dtype=np.float32))
        ys.append(np.asarray(r[c]["y_s1"], dtype=np.float32))
    y_sample = np.stack(ys, axis=0)
    return (y_prompt, y_sample)
```

```python
import numpy as np
from contextlib import ExitStack
import concourse.bass as bass
import concourse.mybir as mybir
from concourse.bass_utils import run_bass_kernel_spmd

F32 = mybir.dt.float32
BF16 = mybir.dt.bfloat16
AF = mybir.ActivationFunctionType
ALU = mybir.AluOpType
AX = mybir.AxisListType

D = 1024
INW = 3840
EPS = 1e-6
NEG = -30000.0
STOP = None
QUARTER = True
LIMIT = None
AQ, AK, AV, AZ, BQ, BK, BV, BZ, CQ, CK, CV, CZ = 0, 384, 768, 1152, 1536, 1792, 1920, 2048, 2304, 2688, 3072, 3456

COMPUTE = ("pe", "act", "dve", "pool")
ISSUERS = ("pe", "act", "dve", "pool", "sp")
ENGOBJ = {"pe": "tensor", "act": "scalar", "dve": "vector", "pool": "gpsimd", "sp": "sync"}


class Op:
    __slots__ = ("eng", "fn", "is_dma", "sem", "ticket", "signal", "waits")

    def __init__(self, eng, fn, is_dma, sem):
        self.eng = eng
        self.fn = fn
        self.is_dma = is_dma
        self.sem = sem
        self.ticket = None
        self.signal = is_dma
        self.waits = []


class Prog:
    def __init__(self, nc, stack, block):
        self.nc = nc
        self.stack = stack
        self.block = block
        self.streams = {e: [] for e in ISSUERS}
        self.esem = {e: self.new_sem("sem_" + e) for e in COMPUTE}
        self.bar = self.new_sem("sem_bar")
        self.bar_count = 0
        self.ecount = {e: 0 for e in COMPUTE}
        self.last_w = {}
        self.readers = {}
        self.dma_sems = {}
        self.dma_count = {}
        self.pending = None
        self.alias = {}
        self.n_ins = 0

    def new_sem(self, name):
        return self.stack.enter_context(self.nc.semaphore(name))

    def _dep(self, op, prod):
        if prod is None or prod is op:
            return
        if (not op.is_dma) and (not prod.is_dma) and prod.eng == op.eng and op.eng == "pe":
            return
        prod.signal = True
        op.waits.append(prod)

    def op(self, eng, fn, reads=(), writes=(), dma=None):
        self.nrec = getattr(self, "nrec", 0) + 1
        if LIMIT is not None and self.nrec > LIMIT:
            return None
        is_dma = dma is not None
        ps_reads = [r for r in reads if r.startswith("ps")]
        if ps_reads:
            reads = [r for r in reads if not r.startswith("ps")]
            writes = list(writes) + ps_reads
        o = Op(eng, fn, is_dma, dma)
        if is_dma:
            if dma not in self.alias:
                self.alias[dma] = f"g{len(self.alias)}"
            dma = self.alias[dma]
            o.sem = dma
            if dma not in self.dma_sems:
                self.dma_sems[dma] = self.new_sem("d_" + dma)
                self.dma_count[dma] = 0
            self.dma_count[dma] += 16
            o.ticket = self.dma_count[dma]
        for r in reads:
            self._dep(o, self.last_w.get(r))
        for w in writes:
            self._dep(o, self.last_w.get(w))
            for rd in self.readers.get(w, ()):
                self._dep(o, rd)
        for r in reads:
            self.readers.setdefault(r, []).append(o)
        for w in writes:
            self.last_w[w] = o
            self.readers[w] = []
        self.streams[eng].append(o)
        return o

    def flush(self, final=False):
        for e in COMPUTE:
            ops = [o for o in self.streams[e] if not o.is_dma]
            if ops:
                ops[-1].signal = True
            for o in ops:
                if o.signal:
                    self.ecount[e] += 1
                    o.ticket = self.ecount[e]
        pending = self.pending
        dma_final = dict(self.dma_count)
        self.bar_count += 1
        bar_val = self.bar_count
        ecount = dict(self.ecount)

        def make(ename):
            ops = self.streams[ename]

            def body(eng):
                waited = {}
                if pending is not None:
                    for key, val in pending.items():
                        if val > 0:
                            sem = self.bar if key == "bar" else self.esem[key]
                            if key != ename:
                                eng.wait_ge(sem, val)
                for o in ops:
                    need = {}
                    for p in o.waits:
                        key = ("d", p.sem) if p.is_dma else ("e", p.eng)
                        if p.ticket > need.get(key, 0):
                            need[key] = p.ticket
                    for key, val in need.items():
                        if waited.get(key, 0) >= val:
                            continue
                        waited[key] = val
                        sem = self.dma_sems[key[1]] if key[0] == "d" else self.esem[key[1]]
                        eng.wait_ge(sem, val)
                    ins = o.fn(eng)
                    self.n_ins += 1
                    if o.is_dma:
                        ins.then_inc(self.dma_sems[o.sem], 16)
                    elif o.signal:
                        ins.then_inc(self.esem[o.eng], 1)
                if ename == "sp":
                    for s, v in dma_final.items():
                        if v > 0:
                            eng.wait_ge(self.dma_sems[s], v)
                    eng.sem_inc(self.bar, 1)

            return body

        for ename in ISSUERS:
            if ename != "sp" and not self.streams[ename] and pending is None:
                continue
            getattr(self.block, ENGOBJ[ename])(make(ename))
        self.pending = dict(ecount)
        self.pending["bar"] = bar_val
        self.streams = {e: [] for e in ISSUERS}
        self.last_w = {}
        self.readers = {}
        self.alias = {}
        if final:
            pend = self.pending

            def fin(eng):
                eng.wait_ge(self.bar, pend["bar"])

            for ename in ("pe", "act", "dve", "pool"):
                getattr(self.block, ENGOBJ[ename])(fin)


_UID = [0]


def U(name):
    _UID[0] += 1
    return f"{name}_u{_UID[0]}"


class Ring:
    def __init__(self, P, st, name, n, shape, dt, psum=False):
        self.n = n
        self.name = name
        self.i = -1
        alloc = P.nc.psum_tensor if psum else P.nc.sbuf_tensor
        self.bufs = [st.enter_context(alloc(U(f"{name}{k}"), list(shape), dt)) for k in range(n)]

    def next(self):
        self.i += 1
        k = self.i % self.n
        return self.bufs[k], f"{self.name}{k}"

    def cur(self):
        k = self.i % self.n
        return self.bufs[k], f"{self.name}{k}"


def _const_tables():
    SMAX = 8192
    t = np.arange(SMAX, dtype=np.float32)
    fa = (500000.0 ** (-np.arange(0, 16, 2, dtype=np.float32) / 16)).astype(np.float32)
    anga = (t[:, None] * fa[None, :]).astype(np.float32).astype(np.float64)
    fb = (10000.0 ** (-np.arange(0, 32, 2, dtype=np.float32) / 32)).astype(np.float32)
    row = (np.arange(SMAX) // 64).astype(np.float32)
    col = (np.arange(SMAX) % 64).astype(np.float32)
    angr = (row[:, None] * fb[None, :]).astype(np.float32).astype(np.float64)
    angc = (col[:, None] * fb[None, :]).astype(np.float32).astype(np.float64)
    angb = np.concatenate([angr, angc], axis=1)

    def tm(a):
        return np.ascontiguousarray(a.reshape(SMAX // 128, 128, -1).transpose(1, 0, 2)).astype(np.float32)

    ropea = np.stack([tm(np.cos(anga)), tm(np.sin(anga))], axis=1)
    ropeb = np.stack([tm(np.cos(angb)), tm(np.sin(angb))], axis=1)
    kk = np.arange(128)[:, None, None]
    dl = (np.arange(5) - 2)[None, :, None]
    ii = np.arange(128)[None, None, :]
    dd = 128 * dl + kk - ii
    ad = np.abs(dd)
    mult = (ad <= 64).astype(np.float32) + ((dd % 4 == 0) & (ad <= 256))
    du = 128 * (np.arange(3) - 1)[None, :, None] + kk - ii
    m16 = (np.abs(du) <= 64).astype(np.float32)
    kk2 = np.arange(128)[:, None]
    ii2 = np.arange(128)[None, :]
    m16q = np.stack([(kk2 >= ii2), (kk2 <= ii2)], axis=1).astype(np.float32)
    ea = np.ascontiguousarray(np.concatenate([mult, m16, m16q], axis=1).astype(np.float32))
    ident = np.eye(128, dtype=np.float32)
    return ropea, ropeb, ea, ident


def _c_bias_tables(rel_bias):
    L = rel_bias.shape[0]
    krl = (np.arange(128) // 64)[:, None, None]
    kc = (np.arange(128) % 64)[:, None, None]
    dlt = (np.arange(7) - 3)[None, :, None]
    rl = (np.arange(128) // 64)[None, None, :]
    qc = (np.arange(128) % 64)[None, None, :]
    dr = 2 * dlt + krl - rl
    ro = dr + 7
    co = np.clip(kc - qc + 15, 0, 30)
    cs = np.clip(qc - 8, 0, 48)
    valid = (kc >= cs) & (kc < cs + 16) & (ro >= 0) & (ro <= 14)
    ro_c = np.clip(ro, 0, 14)
    ro_b, co_b, valid_b = np.broadcast_arrays(ro_c, co, valid)
    out = np.empty((L, 6, 128, 7, 128), dtype=np.float32)
    for l in range(L):
        for h in range(6):
            g = rel_bias[l, h][ro_b, co_b]
            out[l, h] = np.where(valid_b, g, np.float32(NEG))
    return out


def build(parts, n_layers, debug=False):
    nc = bass.Bass("TRN2", target_bir_lowering=False)
    dr = {}

    def din(name, shape, dt=F32):
        dr[name] = nc.dram_tensor(name, list(shape), dt, kind="ExternalInput").ap()
        return dr[name]

    def dout(name, shape, dt=F32):
        dr[name] = nc.dram_tensor(name, list(shape), dt, kind="ExternalOutput").ap()
        return dr[name]

    def dscr(name, shape, dt):
        if debug:
            dr[name] = nc.dram_tensor(name, list(shape), dt, kind="ExternalOutput").ap()
        else:
            dr[name] = nc.dram_tensor(name, list(shape), dt).ap()
        return dr[name]

    QPN = "p" if (QUARTER and any(pn == "p" for (pn, _, _) in parts)) else None
    if QPN is not None:
        dscr("oq", [2048, D], F32)
        dscr("szq", [2048, D // 2], F32)
        dscr("xq", [2048, D], F32)
        dscr("obq", [2048, 256], F32)
    for (pn, S, src) in parts:
        din("x_" + pn, [S, D])
        if pn == QPN:
            dout("yq_" + pn, [2048, D])
        else:
            dout("y_" + pn, [S, D])
        dscr("qt_" + pn, [1024, S], BF16)
        if pn == QPN:
            dscr("ktpad", [896, S + 2048], BF16)
            dscr("vpad", [S + 2048, 8 * 192], BF16)
            dr["kt_" + pn] = dr["ktpad"][:, 1024:1024 + S]
            dr["v_" + pn] = dr["vpad"][1024:1024 + S, :]
            dscr("ktc", [384, 4096], BF16)
            dscr("vc", [4096, 576], BF16)
            dscr("oaq", [2048, 384], F32)
        else:
            dscr("kt_" + pn, [896, S], BF16)
            dscr("v_" + pn, [S, 8 * 192], BF16)
        dscr("sz_" + pn, [S, D // 2], F32)
        dscr("o_" + pn, [S, D], F32)
        if n_layers > 1:
            dscr("y1_" + pn, [S, D], F32)
    din("w_in", [n_layers, D, INW])
    din("w_out", [n_layers, D, D])
    din("norm_pre", [n_layers, D])
    din("norm_post", [n_layers, D])
    din("branch_gain", [n_layers, D])
    din("q_norm", [n_layers, 64])
    din("k_norm", [n_layers, 64])
    din("ropea", [128, 2, 64, 8])
    din("ropeb", [128, 2, 64, 32])
    din("ea", [128, 10, 128])
    din("ident", [128, 128])
    din("efraw", [n_layers, 6, 128, 7, 128])
    if QPN is not None:
        dr["qoff"] = nc.dram_tensor("qoff", [1, 4], mybir.dt.int32, kind="ExternalInput").ap()

    with ExitStack() as top:
        block = top.enter_context(nc.Block())
        P = Prog(nc, top, block)
        identf = top.enter_context(nc.sbuf_tensor("identf", [128, 128], F32))
        identb = top.enter_context(nc.sbuf_tensor("identb", [128, 128], BF16))
        epsc = top.enter_context(nc.sbuf_tensor("epsc", [128, 8], F32))
        nhalf = top.enter_context(nc.sbuf_tensor("nhalf", [128, 8], F32))
        P.op("sp", lambda e: e.dma_start(out=identf[:], in_=dr["ident"]), writes=["identf"], dma="c0")
        P.op("dve", lambda e: e.tensor_copy(out=identb[:], in_=identf[:]), reads=["identf"], writes=["identb"])
        P.op("dve", lambda e: e.memset(epsc[:], EPS), writes=["epsc"])
        P.op("dve", lambda e: e.memset(nhalf[:], -0.5), writes=["nhalf"])
        if QPN is not None:
            qs = top.enter_context(nc.sbuf_tensor("qs", [1, 4], mybir.dt.int32))
            rq0 = top.enter_context(nc.sync.register("rq0"))
            rrow = top.enter_context(nc.sync.register("rrow"))
            rrow2 = top.enter_context(nc.sync.register("rrow2"))
            rv = top.enter_context(nc.sync.register("rv"))
            rtmp = [top.enter_context(nc.sync.register(f"rtmp{i}")) for i in range(4)]
            rti = [0]
            P.op("sp", lambda e: e.dma_start(out=qs[:], in_=dr["qoff"]), writes=["qs"], dma="c1")

            def setregs(e):
                e.reg_load(rq0, qs[0:1, 0:1])
                e.reg_load(rrow2, qs[0:1, 2:3])
                e.reg_load(rv, qs[0:1, 3:4])
                return e.reg_load(rrow, qs[0:1, 1:2])
            P.op("sp", setregs, reads=["qs"], writes=["regs"])

            SQ = [S for (pn_, S, _) in parts if pn_ == QPN][0]
            zt = top.enter_context(nc.sbuf_tensor("zt", [128, 1536], BF16))
            P.op("pool", lambda e: e.memset(zt[:], 0.0), writes=["zt"])
            zi = 0
            for side in (0, 1024 + SQ):
                for k in range(7):
                    P.op("sp", lambda e, k=k, side=side: e.dma_start(out=dr["ktpad"][k * 128:(k + 1) * 128, side:side + 1024], in_=zt[:, 0:1024]), reads=["zt"], dma=f"zp{zi}")
                    zi += 1
                for k in range(8):
                    P.op("sp", lambda e, k=k, side=side: e.dma_start(out=dr["vpad"][side + k * 128:side + (k + 1) * 128, :], in_=zt[:]), reads=["zt"], dma=f"zp{zi}")
                    zi += 1

            def dyn_ap(e, base_reg, const, tensor_ap, pattern):
                t = rtmp[rti[0] % 4]
                rti[0] += 1
                e.reg_add(t, base_reg, int(const))
                return bass.AP(tensor_ap.tensor, t, pattern)
        P.flush(final=(STOP == "pre"))
        if STOP == "pre":
            return nc

        def rstd_ops(v_ap, key, n, scale):
            P.op("dve", lambda e: e.tensor_scalar(out=v_ap, in0=v_ap, scalar1=float(scale), scalar2=float(EPS),
                                                  op0=ALU.mult, op1=ALU.add), reads=[key], writes=[key])
            P.op("pool", lambda e: e.tensor_tensor(out=v_ap, in0=v_ap, in1=nhalf[:, 0:n], op=ALU.pow),
                 reads=[key], writes=[key])

        for l in range(n_layers):
            last = l == n_layers - 1
            with ExitStack() as st:
                wsb = st.enter_context(nc.sbuf_tensor(U("wsb"), [128, 8, INW], BF16))
                wst = Ring(P, st, "wst", 2, [128, 8, 120], F32)
                gpre = st.enter_context(nc.sbuf_tensor(U("gpre"), [128, 8], F32))
                gqk = st.enter_context(nc.sbuf_tensor(U("gqk"), [128, 6, 64], F32))
                ropa = st.enter_context(nc.sbuf_tensor(U("ropa"), [128, 2, 64, 8], F32))
                ropb = st.enter_context(nc.sbuf_tensor(U("ropb"), [128, 2, 64, 32], F32))
                xin = Ring(P, st, "xin", 5, [128, D], F32)
                junk = st.enter_context(nc.sbuf_tensor(U("junk"), [128, D], BF16))
                ssr = Ring(P, st, "ssr", 5, [128, 1], F32)
                xnr = Ring(P, st, "xnr", 2, [128, D], BF16)
                qfr = Ring(P, st, "qfr", 2, [128, 6, 64], F32)
                bfr = Ring(P, st, "bfr", 2, [128, 512], F32)
                hTr = Ring(P, st, "hTr", 2, [128, 8, 512], BF16)
                qts = Ring(P, st, "qts", 2, [128, 8, 512], BF16)
                kts = Ring(P, st, "kts", 2, [128, 7, 512], BF16)
                vsr = Ring(P, st, "vsr", 4, [128, 8, 192], BF16)
                szr = Ring(P, st, "szr", 4, [128, D], BF16)
                qar = Ring(P, st, "qar", 3, [128, 6, 64], BF16)
                kar = Ring(P, st, "kar", 3, [128, 6, 64], BF16)
                qkbr = Ring(P, st, "qkbr", 3, [128, 6, 64], BF16)
                sqb = st.enter_context(nc.sbuf_tensor(U("sqb"), [128, 6, 64], F32))
                xgb = st.enter_context(nc.sbuf_tensor(U("xgb"), [128, 6, 64], F32))
                ss6 = st.enter_context(nc.sbuf_tensor(U("ss6"), [128, 6], F32))
                tA = [st.enter_context(nc.sbuf_tensor(U(f"tA{i}"), [128, 6, 8], F32)) for i in range(4)]
                tB = [st.enter_context(nc.sbuf_tensor(U(f"tB{i}"), [128, 6, 2, 16], F32)) for i in range(4)]
                psT = Ring(P, st, "psT", 2, [128, 8, 128], BF16, psum=True)
                psT2 = Ring(P, st, "psT2", 2, [128, 8, 128], BF16, psum=True)
                psM = Ring(P, st, "psM", 3, [128, 512], F32, psum=True)
                psF = Ring(P, st, "psF", 1, [128, 512], F32, psum=True)

                P.op("sp", lambda e: e.dma_start(out=gpre[:], in_=dr["norm_pre"][l].rearrange("(k p) -> p k", p=128),
                                                 allow_slow_non_contiguous=True), writes=["gpre"], dma="c0")
                P.op("sp", lambda e: e.dma_start(out=gqk[:, 0:4, :], in_=dr["q_norm"][l].partition_broadcast(128).unsqueeze(1).to_broadcast([128, 4, 64]),
                                                 allow_slow_non_contiguous=True), writes=["gqk_q"], dma="c1")
                P.op("sp", lambda e: e.dma_start(out=gqk[:, 4:6, :], in_=dr["k_norm"][l].partition_broadcast(128).unsqueeze(1).to_broadcast([128, 2, 64]),
                                                 allow_slow_non_contiguous=True), writes=["gqk_k"], dma="c2")
                P.op("sp", lambda e: e.dma_start(out=ropa[:], in_=dr["ropea"]), writes=["ropa"], dma="c3")
                P.op("sp", lambda e: e.dma_start(out=ropb[:], in_=dr["ropeb"]), writes=["ropb"], dma="c4")
                for ci in range(32):
                    wbuf, wkey = wst.next()
                    c0 = ci * 120
                    P.op("sp", lambda e, wbuf=wbuf, c0=c0: e.dma_start(
                        out=wbuf[:], in_=dr["w_in"][l][:, c0:c0 + 120].rearrange("(k p) c -> p k c", p=128)),
                        writes=[wkey], dma=wkey)
                    eng = ("pool", "dve", "act")[ci % 3]
                    if eng == "act":
                        P.op("act", lambda e, wbuf=wbuf, c0=c0: e.activation(out=wsb[:, :, c0:c0 + 120], in_=wbuf[:], func=AF.Copy),
                             reads=[wkey], writes=[f"wsb{ci}"])
                    else:
                        P.op(eng, lambda e, wbuf=wbuf, c0=c0: e.tensor_copy(out=wsb[:, :, c0:c0 + 120], in_=wbuf[:]),
                             reads=[wkey], writes=[f"wsb{ci}"])
                WALL = [f"wsb{ci}" for ci in range(32)]

                def wkeys(c0, c1):
                    return [f"wsb{ci}" for ci in range(c0 // 120, (c1 - 1) // 120 + 1)]

                from collections import deque
                blocks1 = []
                for (pn, S, src) in parts:
                    xsrc = dr["x_" + pn] if l == 0 else dr["y1_" + pn]
                    for blk in range(S // 128):
                        blocks1.append(dict(pn=pn, blk=blk, j=blk % 4, tg=blk // 4, xsrc=xsrc))
                pend = deque()

                def run_pend(keep):
                    while len(pend) > keep:
                        pend.popleft()()

                def s1_load(c):
                    xb, xkey = xin.next()
                    c["x"] = (xb, xkey)
                    t0, xsrc = c["blk"] * 128, c["xsrc"]
                    P.op("sp", lambda e: e.dma_start(out=xb[:], in_=xsrc[t0:t0 + 128, :]), writes=[xkey], dma=xkey)

                grp = {}

                def s1_fa(c):
                    xb, xkey = c["x"]
                    ss, sskey = ssr.next()
                    c["ss"] = (ss, sskey)
                    P.op("act", lambda e: e.activation(out=junk[:], in_=xb[:], func=AF.Square, accum_out=ss[:]), reads=[xkey], writes=[sskey, "junk"])
                    rstd_ops(ss[:], sskey, 1, 1.0 / D)

                def s1_front(c):
                    j = c["j"]
                    if j == 0:
                        grp["hT"] = hTr.next()
                        grp["qt"] = qts.next()
                        grp["kt"] = kts.next()
                    c["hT"], c["qt"], c["kt"] = grp["hT"], grp["qt"], grp["kt"]
                    hT, hkey = c["hT"]
                    xb, xkey = c["x"]
                    ss, sskey = c["ss"]
                    xn, xnkey = xnr.next()
                    P.op("act", lambda e: e.activation(out=xn[:], in_=xb[:], func=AF.Copy, scale=ss[:]), reads=[xkey, sskey], writes=[xnkey])
                    pT, pTkey = psT.next()
                    for kc in range(8):
                        P.op("pe", lambda e, kc=kc: e.transpose(out=pT[:, kc, :], in_=xn[:, kc * 128:(kc + 1) * 128], identity=identb[:]),
                             reads=[xnkey, "identb"], writes=[pTkey])
                    P.op("dve", lambda e: e.tensor_tensor(
                        out=hT[:, :, j * 128:(j + 1) * 128], in0=pT[:], in1=gpre[:].unsqueeze(2).to_broadcast([128, 8, 128]), op=ALU.mult),
                        reads=[pTkey, "gpre"], writes=[hkey + f"_{j}"])

                def s1_main(c):
                    j, blk, pn = c["j"], c["blk"], c["pn"]
                    hT, hkey = c["hT"]
                    qt_s, qkey = c["qt"]
                    kt_s, kkey = c["kt"]
                    hk = hkey + f"_{j}"

                    def tok_mm(c0, c1):
                        run_pend(2)
                        pm, pmkey = psM.next()
                        for kc in range(8):
                            P.op("pe", lambda e, kc=kc: e.matmul(pm[:, 0:c1 - c0], lhsT=hT[:, kc, j * 128:(j + 1) * 128], rhs=wsb[:, kc, c0:c1],
                                                                 start=(kc == 0), stop=(kc == 7)),
                                 reads=[hk] + wkeys(c0, c1), writes=[pmkey])
                        return pm, pmkey

                    for which, c0, ring, dst in (("q", AQ, qar, qt_s), ("k", AK, kar, kt_s)):
                        pm, pmkey = tok_mm(c0, c0 + 384)
                        qf, qfkey = qfr.next()
                        P.op("act", lambda e, qf=qf, pm=pm: e.activation(out=qf[:].rearrange("p h d -> p (h d)"), in_=pm[:, 0:384], func=AF.Copy), reads=[pmkey], writes=[qfkey])
                        ob, okey = ring.next()
                        P.op("pool", lambda e, ob=ob, qf=qf: e.tensor_copy(out=ob[:], in_=qf[:]), reads=[qfkey], writes=[okey])
                        cosb = ropa[:, 0, blk, :].unsqueeze(1).to_broadcast([128, 6, 8])
                        sinb = ropa[:, 1, blk, :].unsqueeze(1).to_broadcast([128, 6, 8])
                        x1 = qf[:, :, 0:8]
                        x2 = qf[:, :, 8:16]
                        P.op("dve", lambda e, x1=x1, cosb=cosb: e.tensor_tensor(out=tA[0][:], in0=x1, in1=cosb, op=ALU.mult), reads=[qfkey, "ropa"], writes=["tA0"])
                        P.op("dve", lambda e, x2=x2, sinb=sinb: e.tensor_tensor(out=tA[1][:], in0=x2, in1=sinb, op=ALU.mult), reads=[qfkey, "ropa"], writes=["tA1"])
                        P.op("dve", lambda e, x2=x2, cosb=cosb: e.tensor_tensor(out=tA[2][:], in0=x2, in1=cosb, op=ALU.mult), reads=[qfkey, "ropa"], writes=["tA2"])
                        P.op("dve", lambda e, x1=x1, sinb=sinb: e.tensor_tensor(out=tA[3][:], in0=x1, in1=sinb, op=ALU.mult), reads=[qfkey, "ropa"], writes=["tA3"])
                        P.op("dve", lambda e, ob=ob: e.tensor_tensor(out=ob[:, :, 0:8], in0=tA[0][:], in1=tA[1][:], op=ALU.subtract), reads=["tA0", "tA1", okey], writes=[okey])
                        P.op("dve", lambda e, ob=ob: e.tensor_tensor(out=ob[:, :, 8:16], in0=tA[2][:], in1=tA[3][:], op=ALU.add), reads=["tA2", "tA3", okey], writes=[okey])

                        def fin_a(ob=ob, okey=okey, dst=dst, which=which):
                            pT2, pT2key = psT2.next()
                            obf = ob[:].rearrange("p h d -> p (h d)")
                            for tt in range(3):
                                P.op("pe", lambda e, tt=tt: e.transpose(out=pT2[:, tt, :], in_=obf[:, tt * 128:(tt + 1) * 128], identity=identb[:]),
                                     reads=[okey, "identb"], writes=[pT2key])
                            dkey = (qkey if which == "q" else kkey) + f"_{j}a"
                            P.op("act", lambda e: e.activation(out=dst[:, 0:3, j * 128:(j + 1) * 128], in_=pT2[:, 0:3, :], func=AF.Copy),
                                 reads=[pT2key], writes=[dkey])
                        pend.append(fin_a)
                    vs, vkey = vsr.next()
                    c["vs"] = (vs, vkey)
                    pm, pmkey = tok_mm(AV, AV + 384)
                    P.op("dve", lambda e, pm=pm: e.tensor_copy(out=vs[:, 0:3, :].rearrange("p a (s d) -> p a s d", d=64)[:, :, 0::2, :], in_=pm[:, 0:384].rearrange("p (a s d) -> p a s d", s=2, d=64)),
                         reads=[pmkey], writes=[vkey + "a"])
                    P.op("pool", lambda e: e.memset(vs[:, :, 64:128], 1.0), writes=[vkey + "one"])
                    sz, szkey = szr.next()
                    c["sz"] = (sz, szkey)
                    pm, pmkey = tok_mm(AZ, AZ + 384)
                    P.op("act", lambda e, pm=pm: e.activation(out=sz[:, 0:384], in_=pm[:, 0:384], func=AF.Silu), reads=[pmkey], writes=[szkey + "a"])
                    pm, pmkey = tok_mm(BQ, BQ + 512)
                    bf, bfkey = bfr.next()
                    P.op("act", lambda e, pm=pm, bf=bf: e.activation(out=bf[:], in_=pm[:], func=AF.Copy), reads=[pmkey], writes=[bfkey])
                    bf6 = bf[:, 0:384].rearrange("p (h d) -> p h d", d=64)
                    P.op("pool", lambda e, bf6=bf6: e.tensor_tensor(out=sqb[:], in0=bf6, in1=bf6, op=ALU.mult), reads=[bfkey], writes=["sqb"])
                    P.op("dve", lambda e: e.tensor_reduce(out=ss6[:], in_=sqb[:], axis=AX.X, op=ALU.add), reads=["sqb"], writes=["ss6"])
                    rstd_ops(ss6[:], "ss6", 6, 1.0 / 64)
                    P.op("dve", lambda e, bf6=bf6: e.tensor_tensor(out=xgb[:], in0=bf6, in1=ss6[:].unsqueeze(2).to_broadcast([128, 6, 64]), op=ALU.mult),
                         reads=[bfkey, "ss6"], writes=["xgb"])
                    P.op("pool", lambda e: e.tensor_tensor(out=xgb[:], in0=xgb[:], in1=gqk[:], op=ALU.mult), reads=["xgb", "gqk_q", "gqk_k"], writes=["xgb"])
                    qkb, qkbkey = qkbr.next()
                    xv = xgb[:].rearrange("p h (a b c) -> p h a b c", a=2, b=2)
                    ov = qkb[:].rearrange("p h (a b c) -> p h a b c", a=2, b=2)
                    cb = ropb[:, 0, blk, :].rearrange("p (a c) -> p a c", a=2).unsqueeze(1).to_broadcast([128, 6, 2, 16])
                    sb_ = ropb[:, 1, blk, :].rearrange("p (a c) -> p a c", a=2).unsqueeze(1).to_broadcast([128, 6, 2, 16])
                    x1 = xv[:, :, :, 0, :]
                    x2 = xv[:, :, :, 1, :]
                    P.op("pool", lambda e, x1=x1, cb=cb: e.tensor_tensor(out=tB[0][:], in0=x1, in1=cb, op=ALU.mult), reads=["xgb", "ropb"], writes=["tB0"])
                    P.op("pool", lambda e, x2=x2, sb_=sb_: e.tensor_tensor(out=tB[1][:], in0=x2, in1=sb_, op=ALU.mult), reads=["xgb", "ropb"], writes=["tB1"])
                    P.op("dve", lambda e, x2=x2, cb=cb: e.tensor_tensor(out=tB[2][:], in0=x2, in1=cb, op=ALU.mult), reads=["xgb", "ropb"], writes=["tB2"])
                    P.op("dve", lambda e, x1=x1, sb_=sb_: e.tensor_tensor(out=tB[3][:], in0=x1, in1=sb_, op=ALU.mult), reads=["xgb", "ropb"], writes=["tB3"])
                    P.op("pool", lambda e, ov=ov: e.tensor_tensor(out=ov[:, :, :, 0, :], in0=tB[0][:], in1=tB[1][:], op=ALU.subtract), reads=["tB0", "tB1"], writes=[qkbkey + "x"])
                    P.op("dve", lambda e, ov=ov: e.tensor_tensor(out=ov[:, :, :, 1, :], in0=tB[2][:], in1=tB[3][:], op=ALU.add), reads=["tB2", "tB3"], writes=[qkbkey + "y"])
                    bfv = bf[:, 384:512].rearrange("p (h d) -> p h d", d=64)
                    P.op("pool", lambda e, bfv=bfv: e.tensor_copy(out=vs[:, 3:5, 0:64], in_=bfv), reads=[bfkey], writes=[vkey + "b"])
                    P.op("pool", lambda e, bfv=bfv: e.tensor_copy(out=vs[:, 3:5, 128:192], in_=bfv), reads=[bfkey], writes=[vkey + "b2"])

                    def fin_b(qkb=qkb, qkbkey=qkbkey):
                        pT2, pT2key = psT2.next()
                        qkbf = qkb[:].rearrange("p h d -> p (h d)")
                        for tt in range(3):
                            P.op("pe", lambda e, tt=tt: e.transpose(out=pT2[:, tt, :], in_=qkbf[:, tt * 128:(tt + 1) * 128], identity=identb[:]),
                                 reads=[qkbkey + "x", qkbkey + "y", "identb"], writes=[pT2key])
                        P.op("act", lambda e: e.activation(out=qt_s[:, 3:5, j * 128:(j + 1) * 128], in_=pT2[:, 0:2, :], func=AF.Copy),
                             reads=[pT2key], writes=[qkey + f"_{j}b"])
                        P.op("act", lambda e: e.activation(out=kt_s[:, 3, j * 128:(j + 1) * 128], in_=pT2[:, 2, :], func=AF.Copy),
                             reads=[pT2key], writes=[kkey + f"_{j}b"])
                    pend.append(fin_b)
                    pm, pmkey = tok_mm(BZ, BZ + 256)
                    P.op("act", lambda e, pm=pm: e.activation(out=sz[:, 384:640], in_=pm[:, 0:256], func=AF.Silu), reads=[pmkey], writes=[szkey + "b"])
                    pm, pmkey = tok_mm(CV, CV + 384)
                    P.op("dve", lambda e, pm=pm: e.tensor_copy(out=vs[:, 5:8, :].rearrange("p a (s d) -> p a s d", d=64)[:, :, 0::2, :], in_=pm[:, 0:384].rearrange("p (a s d) -> p a s d", s=2, d=64)),
                         reads=[pmkey], writes=[vkey + "c"])
                    pm, pmkey = tok_mm(CZ, CZ + 384)
                    P.op("act", lambda e, pm=pm: e.activation(out=sz[:, 640:1024], in_=pm[:, 0:384], func=AF.Silu), reads=[pmkey], writes=[szkey + "c"])
                    if j == 3:
                        hks = [hkey + f"_{jj}" for jj in range(4)]
                        for ti in range(6):
                            run_pend(2)
                            c0 = (CQ if ti < 3 else CK) + (ti % 3) * 128
                            pf, pfkey = psF.next()
                            for kc in range(8):
                                P.op("pe", lambda e, pf=pf, kc=kc, c0=c0: e.matmul(pf[:], lhsT=wsb[:, kc, c0:c0 + 128], rhs=hT[:, kc, :], start=(kc == 0), stop=(kc == 7)),
                                     reads=hks + wkeys(c0, c0 + 128), writes=[pfkey])
                            if ti < 3:
                                P.op("dve", lambda e, pf=pf, ti=ti: e.tensor_copy(out=qt_s[:, 5 + ti, :], in_=pf[:]), reads=[pfkey], writes=[qkey + f"_c{ti}"])
                            else:
                                P.op("act", lambda e, pf=pf, ti=ti: e.activation(out=kt_s[:, 4 + ti - 3, :], in_=pf[:], func=AF.Copy), reads=[pfkey], writes=[kkey + f"_c{ti}"])

                def s1_store(c):
                    j, blk, pn = c["j"], c["blk"], c["pn"]
                    t0 = blk * 128
                    vs, vkey = c["vs"]
                    sz, szkey = c["sz"]
                    qt_s, qkey = c["qt"]
                    kt_s, kkey = c["kt"]
                    P.op("sp", lambda e: e.dma_start(out=dr["v_" + pn][t0:t0 + 128, :], in_=vs[:].rearrange("p h d -> p (h d)")),
                         reads=[vkey + "a", vkey + "b", vkey + "b2", vkey + "c", vkey + "one"], dma=vkey)
                    P.op("sp", lambda e: e.dma_start(out=dr["sz_" + pn][t0:t0 + 128, :], in_=sz[:].bitcast(F32)),
                         reads=[szkey + "a", szkey + "b", szkey + "c"], dma=szkey)
                    if j == 3:
                        tt0 = c["tg"] * 512
                        qr = [qkey + f"_{jj}a" for jj in range(4)] + [qkey + f"_{jj}b" for jj in range(4)] + [qkey + f"_c{ti}" for ti in range(3)]
                        kr = [kkey + f"_{jj}a" for jj in range(4)] + [kkey + f"_{jj}b" for jj in range(4)] + [kkey + f"_c{ti}" for ti in range(3, 6)]
                        P.op("sp", lambda e: e.dma_start(out=dr["qt_" + pn][:, tt0:tt0 + 512].rearrange("(k p) t -> p k t", p=128), in_=qt_s[:]), reads=qr, dma=qkey)
                        P.op("sp", lambda e: e.dma_start(out=dr["kt_" + pn][:, tt0:tt0 + 512].rearrange("(k p) t -> p k t", p=128), in_=kt_s[:]), reads=kr, dma=kkey)

                nb1 = len(blocks1)
                for i in range(-3, nb1 + 2):
                    if 0 <= i + 3 < nb1:
                        s1_load(blocks1[i + 3])
                    if 0 <= i + 2 < nb1:
                        s1_fa(blocks1[i + 2])
                    if 0 <= i + 1 < nb1:
                        s1_front(blocks1[i + 1])
                    if 0 <= i < nb1:
                        s1_main(blocks1[i])
                        if blocks1[i]["j"] == 3:
                            run_pend(0)
                    if 0 <= i - 2 < nb1:
                        s1_store(blocks1[i - 2])
                run_pend(0)
                P.flush(final=(STOP == "s1"))
            if STOP == "s1":
                return nc

            with ExitStack() as st:
                SMAXP = max(S for (_, S, _) in parts)
                qpr = Ring(P, st, "qpr", 2, [128, SMAXP], BF16)
                kpr = Ring(P, st, "kpr", 2, [128, SMAXP], BF16)
                vbr = Ring(P, st, "vbr", 2, [128, SMAXP // 128, 192], BF16)
                eab = st.enter_context(nc.sbuf_tensor(U("eab"), [128, 5, 128], BF16))
                m16 = st.enter_context(nc.sbuf_tensor(U("m16"), [128, 5, 128], BF16))
                accs = [(st.enter_context(nc.sbuf_tensor(U(f"acc{hh}"), [128, 2048], F32)), f"acc{hh}") for hh in range(2)]
                v16r = Ring(P, st, "v16r", 1, [128, SMAXP // 128, 192], BF16)
                est = Ring(P, st, "est", 2, [128, 10 * 128], F32)
                efb = st.enter_context(nc.sbuf_tensor(U("efb"), [128, 6, 7, 128], BF16))
                eib = st.enter_context(nc.sbuf_tensor(U("eib"), [128, 6, 5, 128], BF16))
                ptr = Ring(P, st, "ptr", 6, [128, 512], BF16)
                osr = Ring(P, st, "osr", 2, [128, 512], F32)
                ogr = Ring(P, st, "ogr", 2, [128, 4, 128], F32)
                rlr = Ring(P, st, "rlr", 2, [128, 4], F32)
                psS = Ring(P, st, "psS", 4, [128, 512], F32, psum=True)
                psO = Ring(P, st, "psO", 2, [128, 512], F32, psum=True)
                psR = Ring(P, st, "psR", 2, [128, 4, 128], F32, psum=True)

                eb, ekey = est.next()
                P.op("sp", lambda e, eb=eb: e.dma_start(out=eb[:], in_=dr["ea"].rearrange("p a b -> p (a b)")), writes=[ekey], dma=ekey)
                P.op("dve", lambda e, eb=eb: e.tensor_copy(out=eab[:].rearrange("p a b -> p (a b)"), in_=eb[:, 0:5 * 128]), reads=[ekey], writes=["eab"])
                P.op("dve", lambda e, eb=eb: e.tensor_copy(out=m16[:].rearrange("p a b -> p (a b)"), in_=eb[:, 5 * 128:10 * 128]), reads=[ekey], writes=["m16"])
                for h in range(6):
                    eb, ekey = est.next()
                    P.op("sp", lambda e, eb=eb, h=h: e.dma_start(out=eb[:, 0:7 * 128], in_=dr["efraw"][l, h].rearrange("p a b -> p (a b)")), writes=[ekey], dma=ekey)
                    P.op("act", lambda e, eb=eb, h=h: e.activation(out=efb[:, h].rearrange("p a b -> p (a b)"), in_=eb[:, 0:7 * 128], func=AF.Exp),
                         reads=[ekey], writes=[f"efb{h}"])
                    P.op("dve", lambda e, h=h: e.tensor_copy(out=eib[:, h], in_=efb[:, h, 1:6, :]), reads=[f"efb{h}"], writes=[f"eib{h}"])
                    P.op("dve", lambda e, h=h: e.memset(eib[0:64, h, 0, 64:128], 0.0), reads=[f"eib{h}"], writes=[f"eib{h}"])
                    P.op("dve", lambda e, h=h: e.memset(eib[64:128, h, 4, :], 0.0), reads=[f"eib{h}"], writes=[f"eib{h}"])
                    P.op("dve", lambda e, h=h: e.memset(eib[0:64, h, 4, 0:64], 0.0), reads=[f"eib{h}"], writes=[f"eib{h}"])

                jobs = []
                for (pn, S, src) in parts:
                    for gp in range(8):
                        jobs.append(dict(pn=pn, S=S, gp=gp, dyn=(last and pn == QPN and 3 <= gp < 5), dynA=(last and pn == QPN and gp < 3)))
                if last and QPN is not None:
                    P.op("sp", lambda e: e.dma_start(out=dr["ktc"], in_=dyn_ap(e, rq0, 0, dr["ktpad"], [[SQ + 2048, 384], [1, 4096]])), writes=["ktc"], dma="cq5")
                    P.op("sp", lambda e: e.dma_start(out=dr["vc"], in_=dyn_ap(e, rv, 0, dr["vpad"], [[1536, 4096], [1, 576]])), writes=["vc"], dma="cq6")

                def load_job(jb):
                    pn, S, gp = jb["pn"], jb["S"], jb["gp"]
                    NB = S // 128
                    qtd, ktd, vd = dr["qt_" + pn], dr["kt_" + pn], dr["v_" + pn]
                    if gp < 3:
                        pr = gp
                        qrow, krows = pr * 128, [pr * 128, pr * 128 + 64]
                    elif gp < 5:
                        pr = gp - 3
                        qrow, krows = 384 + pr * 128, [384 + pr * 64, 384 + pr * 64]
                    else:
                        pr = gp - 5
                        qrow, krows = 640 + pr * 128, [512 + pr * 128, 512 + pr * 128 + 64]
                    vb, vbkey = vbr.next()
                    dynA = jb.get("dynA")
                    if dynA:
                        nch = 1
                        P.op("sp", lambda e: e.dma_start(out=vb[:, 0:32, :], in_=dr["vc"][:, gp * 192:(gp + 1) * 192].rearrange("(b p) c -> p b c", p=128)),
                             reads=["vc"], writes=[vbkey + "_0"], dma=f"{vbkey}_0")
                    else:
                        nch = 4 if S > 2048 else 1
                        for ch in range(nch):
                            b0 = ch * (NB // nch)
                            b1 = (ch + 1) * (NB // nch)
                            P.op("sp", lambda e, b0=b0, b1=b1: e.dma_start(
                                out=vb[:, b0:b1, :], in_=vd[b0 * 128:b1 * 128, gp * 192:(gp + 1) * 192].rearrange("(b p) c -> p b c", p=128)),
                                writes=[vbkey + f"_{ch}"], dma=f"{vbkey}_{ch}")
                    jb["vb"] = (vb, vbkey, [vbkey + f"_{ch}" for ch in range(nch)])

                    qp, qpkey = qpr.next()
                    kp, kpkey = kpr.next()
                    if jb.get("dyn") or dynA:
                        P.op("sp", lambda e: e.dma_start(out=qp[:, 0:2048], in_=dyn_ap(e, rq0, qrow * S, qtd, [[S, 128], [1, 2048]])), writes=[qpkey], dma=qpkey)
                    else:
                        P.op("sp", lambda e: e.dma_start(out=qp[:, 0:S], in_=qtd[qrow:qrow + 128, :]), writes=[qpkey], dma=qpkey)
                    for hh in range(2):
                        if dynA:
                            P.op("sp", lambda e, hh=hh: e.dma_start(out=kp[hh * 64:(hh + 1) * 64, 0:4096], in_=dr["ktc"][krows[hh]:krows[hh] + 64, :]),
                                 reads=["ktc"], writes=[kpkey + f"_{hh}"], dma=f"{kpkey}_{hh}")
                        else:
                            P.op("sp", lambda e, hh=hh: e.dma_start(out=kp[hh * 64:(hh + 1) * 64, 0:S], in_=ktd[krows[hh]:krows[hh] + 64, :]),
                                 writes=[kpkey + f"_{hh}"], dma=f"{kpkey}_{hh}")
                    jb["qp"] = (qp, qpkey)
                    jb["kp"] = (kp, kpkey)

                load_job(jobs[0])
                for ji, jb in enumerate(jobs):
                    if ji + 1 < len(jobs):
                        load_job(jobs[ji + 1])
                    pn, S, gp = jb["pn"], jb["S"], jb["gp"]
                    NB = S // 128
                    NW = S // 512
                    od = dr["o_" + pn]
                    if True:
                        if gp < 3:
                            br, pr = "A", gp
                            ocol = pr * 128
                        elif gp < 5:
                            br, pr = "B", gp - 3
                            ocol = 384 + pr * 128
                        else:
                            br, pr = "C", gp - 5
                            ocol = 640 + pr * 128
                        vb, vbkey, vkeys = jb["vb"]
                        qp, qpkey = jb["qp"]
                        kp, kpkey = jb["kp"]
                        groups = []

                        def dense_groups(w):
                            for qb in range(4):
                                b = w * 4 + qb
                                if br == "A":
                                    alist = list(range(max(0, b - 2), min(NB, b + 3)))
                                    kind, off = "ea", 2
                                elif b <= 1:
                                    alist, kind, off = list(range(0, 4)), "ef", 3
                                elif b >= NB - 2:
                                    alist, kind, off = list(range(NB - 4, NB)), "ef", 3
                                else:
                                    alist, kind, off = list(range(b - 2, b + 3)), "ei", 2
                                nu = len(alist)
                                gi = 0
                                while gi < nu:
                                    gn = min(4, nu - gi)
                                    if nu - gi == 5:
                                        gn = 3
                                    us = alist[gi:gi + gn]
                                    groups.append(dict(units=[dict(k=(a * 128, 1), q=(b * 128, 1, 128), pc0=ui * 128, v=("nat", a), oc0=qb * 128,
                                                                   st=(gi + ui == 0), sp=(gi + ui == nu - 1)) for ui, a in enumerate(us)],
                                                       e=(kind, us[0] - b + off, gn, False), tail=("win" if (qb == 3 and gi + gn == nu) else None), w=w))
                                    gi += gn

                        dyn = jb.get("dyn")
                        if br == "B":
                            for w in range(4 if dyn else NW):
                                for kb in range(NB):
                                    groups.append(dict(units=[dict(k=(kb * 128, 1), q=(w * 512, 1, 512), pc0=0, v=("nat", kb), oc0=0, st=(kb == 0), sp=(kb == NB - 1))],
                                                       e=None, tail=("win" if kb == NB - 1 else None), w=w))
                        elif br == "C":
                            for w in range(NW):
                                dense_groups(w)
                        elif jb.get("dynA"):
                            for rq in range(4):
                                for ri in range(4):
                                    r = 4 * rq + ri
                                    units = [dict(k=(jj * 2048 + r, 16), q=(r, 16, 128), pc0=jj * 128, v=("v16", r * 2 + jj), oc0=ri * 128, st=(jj == 0), sp=(jj == 1)) for jj in range(2)]
                                    groups.append(dict(units=units, e=("m16", 3, 2, False), tail=("quad" if ri == 3 else None), sw=0, rq=rq))
                            for w in range(4):
                                for qb in range(4):
                                    b = w * 4 + qb
                                    alist = [b + 8 + d_ for d_ in range(-2, 3)]
                                    for (gi, gn) in ((0, 3), (3, 2)):
                                        us = alist[gi:gi + gn]
                                        groups.append(dict(units=[dict(k=(a * 128, 1), q=(b * 128, 1, 128), pc0=ui * 128, v=("nat", a), oc0=qb * 128,
                                                                       st=(gi + ui == 0), sp=(gi + ui == 4)) for ui, a in enumerate(us)],
                                                           e=("ea", gi, gn, False), tail=("win" if (qb == 3 and gi == 3) else None), w=w))
                        else:
                            nj = NB // 16
                            for sw in range(nj):
                                for rq in range(4):
                                    ulist = [u for u in (-1, 0, 1) if 0 <= sw + u < nj]
                                    if len(ulist) == 1:
                                        units = []
                                        for ri in range(4):
                                            r = 4 * rq + ri
                                            units.append(dict(k=(sw * 2048 + r, 16), q=(sw * 2048 + r, 16, 128), pc0=ri * 128, v=("v16", r * nj + sw), oc0=ri * 128, st=True, sp=True))
                                        groups.append(dict(units=units, e=("m16", 1, 4, True), tail="quad", sw=sw, rq=rq))
                                    else:
                                        for ri in range(4):
                                            r = 4 * rq + ri
                                            units = []
                                            for ui, u in enumerate(ulist):
                                                units.append(dict(k=((sw + u) * 2048 + r, 16), q=(sw * 2048 + r, 16, 128), pc0=ui * 128, v=("v16", r * nj + sw + u), oc0=ri * 128,
                                                                  st=(ui == 0), sp=(ui == len(ulist) - 1)))
                                            groups.append(dict(units=units, e=("m16", ulist[0] + 1, len(ulist), False), tail=("quad" if ri == 3 else None), sw=sw, rq=rq))
                                for w in range(sw * 4, sw * 4 + 4):
                                    dense_groups(w)
                        ng = len(groups)
                        pos = [psO.next(), psO.next()]
                        state = {}
                        v16 = None
                        if br == "A":
                            v16b, v16key = v16r.next()
                            if jb.get("dynA"):
                                nj_ = 2
                                vsrc = dr["vc"][:, gp * 192:(gp + 1) * 192].rearrange("(jj i r) c -> r i jj c", i=128, r=16)
                                v16reads = ["vc"]
                            else:
                                nj_ = NB // 16
                                vsrc = dr["v_" + pn][:, gp * 192:(gp + 1) * 192].rearrange("(jj i r) c -> r i jj c", i=128, r=16)
                                v16reads = []
                            for r in range(16):
                                P.op("sp", lambda e, r=r, v16b=v16b, vsrc=vsrc, nj_=nj_: e.dma_start(out=v16b[:, r * nj_:(r + 1) * nj_, :], in_=vsrc[r]),
                                     reads=v16reads, writes=[v16key + f"_{r}"], dma=f"{v16key}_{r}")
                            v16 = (v16b, [v16key + f"_{r}" for r in range(16)])

                        def sl(start, stride, n):
                            return slice(start, start + (n - 1) * stride + 1, stride) if stride != 1 else slice(start, start + n)

                        def emit_front2(g, qp=qp, kp=kp, qpkey=qpkey, kpkey=kpkey, pr=pr):
                            pss = [psS.next(), psS.next()]
                            ncols = 0
                            for u in g["units"]:
                                ks, kst = u["k"]
                                qs, qst, n = u["q"]
                                pc0 = u["pc0"]
                                for hh in range(2):
                                    ps, pskey = pss[hh]
                                    P.op("pe", lambda e, ps=ps, ks=ks, kst=kst, qs=qs, qst=qst, n=n, pc0=pc0, hh=hh: e.matmul(
                                        ps[:, pc0:pc0 + n], lhsT=kp[hh * 64:(hh + 1) * 64, sl(ks, kst, 128)], rhs=qp[hh * 64:(hh + 1) * 64, sl(qs, qst, n)], start=True, stop=True),
                                        reads=[qpkey, kpkey + f"_{hh}"], writes=[pskey])
                                ncols = max(ncols, pc0 + n)
                            for hh in range(2):
                                ps, pskey = pss[hh]
                                pt, ptkey = ptr.next()
                                P.op("act", lambda e, ps=ps, pt=pt, ncols=ncols: e.activation(out=pt[:, 0:ncols], in_=ps[:, 0:ncols], func=AF.Exp, scale=0.125),
                                     reads=[pskey], writes=[ptkey])
                                if g["e"] is not None:
                                    kind, i0_, gn, bc = g["e"]
                                    hglob = 2 * pr + hh
                                    if kind == "ea":
                                        e_ap, ekeys = eab[:, i0_:i0_ + gn, :], ["eab"]
                                    elif kind == "m16":
                                        if bc:
                                            e_ap, ekeys = m16[:, i0_:i0_ + 1, :].to_broadcast([128, gn, 128]), ["m16"]
                                        else:
                                            e_ap, ekeys = m16[:, i0_:i0_ + gn, :], ["m16"]
                                    elif kind == "ef":
                                        e_ap, ekeys = efb[:, hglob, i0_:i0_ + gn, :], [f"efb{hglob}"]
                                    else:
                                        e_ap, ekeys = eib[:, hglob, i0_:i0_ + gn, :], [f"eib{hglob}"]
                                    P.op("dve", lambda e, pt=pt, e_ap=e_ap, gn=gn: e.tensor_tensor(
                                        out=pt[:, 0:gn * 128].rearrange("p (a b) -> p a b", b=128), in0=pt[:, 0:gn * 128].rearrange("p (a b) -> p a b", b=128), in1=e_ap, op=ALU.mult),
                                        reads=[ptkey] + ekeys, writes=[ptkey])
                                g["pt%d" % hh] = (pt, ptkey)

                        def emit_pv2(g, vb=vb, vkeys=vkeys):
                            for u in g["units"]:
                                vkind, vblk = u["v"]
                                pc0, oc0, st_, sp_ = u["pc0"], u["oc0"], u["st"], u["sp"]
                                n = u["q"][2]
                                if vkind == "nat":
                                    vt, vks = vb, vkeys
                                else:
                                    vt, vks = v16[0], v16[1]
                                for hh in range(2):
                                    pt, ptkey = g["pt%d" % hh]
                                    po, pokey = pos[hh]
                                    P.op("pe", lambda e, po=po, vt=vt, vblk=vblk, pc0=pc0, n=n, oc0=oc0, st_=st_, sp_=sp_, pt=pt, hh=hh: e.matmul(
                                        po[:, oc0:oc0 + n], lhsT=vt[:, vblk, hh * 64:hh * 64 + 128], rhs=pt[:, pc0:pc0 + n], start=st_, stop=sp_),
                                        reads=[ptkey] + vks, writes=[pokey])

                        def emit_back(g, hh, ocol=ocol, od=od, br=br, dyn=dyn, dynA=jb.get("dynA")):
                            po, pokey = pos[hh]
                            if g["tail"] == "quad":
                                rq = g["rq"]
                                ac, ackey = accs[hh]
                                dst = ac[:].rearrange("p (l r) -> p r l", r=16)[:, 4 * rq:4 * rq + 4, :]
                                P.op("act", lambda e, po=po, dst=dst: e.activation(out=dst, in_=po[:].rearrange("p (a b) -> p a b", b=128), func=AF.Copy),
                                     reads=[pokey], writes=[ackey + f"_{rq}"])
                            if g["tail"] == "win":
                                osb, oskey = osr.next()
                                w = g["w"]
                                if br == "A":
                                    ac, ackey = accs[hh]
                                    wl = (w % 4) * 512
                                    P.op("dve", lambda e, osb=osb, po=po, ac=ac, wl=wl: e.tensor_tensor(out=osb[:], in0=po[:], in1=ac[:, wl:wl + 512], op=ALU.add),
                                         reads=[pokey] + [ackey + f"_{q}" for q in range(4)], writes=[oskey])
                                else:
                                    P.op("act", lambda e, osb=osb, po=po: e.activation(out=osb[:], in_=po[:], func=AF.Copy), reads=[pokey], writes=[oskey])
                                prr, prkey = psR.next()
                                for jj in range(4):
                                    P.op("pe", lambda e, prr=prr, osb=osb, jj=jj: e.transpose(out=prr[:, jj, :], in_=osb[:, jj * 128:(jj + 1) * 128], identity=identf[:]),
                                         reads=[oskey, "identf"], writes=[prkey])
                                rl, rlkey = rlr.next()
                                lcol = 64 if hh == 0 else 0
                                P.op("dve", lambda e, rl=rl, prr=prr, lcol=lcol: e.reciprocal(out=rl[:], in_=prr[:, :, lcol]), reads=[prkey], writes=[rlkey])
                                if hh == 0:
                                    state["og"] = ogr.next()
                                og, ogkey = state["og"]
                                P.op("dve", lambda e, og=og, prr=prr, rl=rl, hh=hh: e.tensor_tensor(
                                    out=og[:, :, hh * 64:(hh + 1) * 64], in0=prr[:, :, hh * 64:(hh + 1) * 64], in1=rl[:].unsqueeze(2).to_broadcast([128, 4, 64]), op=ALU.mult),
                                    reads=[prkey, rlkey], writes=[ogkey + f"_{hh}"])
                                if hh == 1 and dynA:
                                    P.op("sp", lambda e, og=og, w=w: e.dma_start(
                                        out=dr["oaq"][w * 512:(w + 1) * 512, ocol:ocol + 128].rearrange("(b p) c -> p b c", p=128), in_=og[:]),
                                        reads=[ogkey + "_0", ogkey + "_1"], dma=ogkey)
                                elif hh == 1 and dyn:
                                    P.op("sp", lambda e, og=og, w=w: e.dma_start(
                                        out=dr["obq"][w * 512:(w + 1) * 512, ocol - 384:ocol - 384 + 128].rearrange("(b p) c -> p b c", p=128), in_=og[:]),
                                        reads=[ogkey + "_0", ogkey + "_1"], dma=ogkey)
                                elif hh == 1:
                                    P.op("sp", lambda e, og=og, w=w: e.dma_start(
                                        out=od[w * 512:(w + 1) * 512, ocol:ocol + 128].rearrange("(b p) c -> p b c", p=128), in_=og[:]),
                                        reads=[ogkey + "_0", ogkey + "_1"], dma=ogkey)

                        LOOK = 1
                        for i in range(ng + LOOK):
                            if i < ng:
                                emit_front2(groups[i])
                            if i - LOOK >= 0:
                                emit_pv2(groups[i - LOOK])
                                emit_back(groups[i - LOOK], 0)
                                emit_back(groups[i - LOOK], 1)
                P.flush(final=(STOP == "s2"))
            if STOP == "s2":
                return nc

            with ExitStack() as st:
                wo = st.enter_context(nc.sbuf_tensor(U("wo"), [128, 8, D], BF16))
                wst3 = Ring(P, st, "wst3", 2, [128, 8, 256], F32)
                gbr = st.enter_context(nc.sbuf_tensor(U("gbr"), [128, D], F32))
                gpo = st.enter_context(nc.sbuf_tensor(U("gpo"), [128, D], F32))
                invw = st.enter_context(nc.sbuf_tensor(U("invw"), [128, 3], F32))
                o_r = Ring(P, st, "o_r", 4, [128, D], F32)
                z_r = Ring(P, st, "z_r", 4, [128, D // 2], F32)
                x_r = Ring(P, st, "x_r", 5, [128, D], F32)
                g_r = Ring(P, st, "g_r", 6, [128, D], F32)
                pc_r = Ring(P, st, "pc_r", 6, [128, D], F32)
                junk3 = st.enter_context(nc.sbuf_tensor(U("junk3"), [128, D], BF16))
                s3r = Ring(P, st, "s3r", 6, [128, 3], F32)
                s2r = Ring(P, st, "s2r", 6, [128, 2], F32)
                ybr = Ring(P, st, "ybr", 3, [128, D], BF16)
                yTr = Ring(P, st, "yTr", 3, [128, 8, 128], BF16)
                t_r = Ring(P, st, "t_r", 5, [128, D], F32)
                psT3 = Ring(P, st, "psT3", 2, [128, 8, 128], BF16, psum=True)
                psY = Ring(P, st, "psY", 3, [128, D], F32, psum=True)

                for ci in range(4):
                    wbuf, wkey = wst3.next()
                    c0 = ci * 256
                    P.op("sp", lambda e, wbuf=wbuf, c0=c0: e.dma_start(out=wbuf[:], in_=dr["w_out"][l][:, c0:c0 + 256].rearrange("(k p) c -> p k c", p=128)),
                         writes=[wkey], dma=wkey)
                    P.op("pool" if ci % 2 == 0 else "dve", lambda e, wbuf=wbuf, c0=c0: e.tensor_copy(out=wo[:, :, c0:c0 + 256], in_=wbuf[:]), reads=[wkey], writes=[f"wo{ci}"])
                WO = [f"wo{ci}" for ci in range(4)]
                P.op("sp", lambda e: e.dma_start(out=gbr[:], in_=dr["branch_gain"][l].partition_broadcast(128), allow_slow_non_contiguous=True), writes=["gbr"], dma="c5")
                P.op("sp", lambda e: e.dma_start(out=gpo[:], in_=dr["norm_post"][l].partition_broadcast(128), allow_slow_non_contiguous=True), writes=["gpo"], dma="c6")
                P.op("dve", lambda e: e.memset(invw[:, 0:1], 1.0 / 384), writes=["invw"])
                P.op("dve", lambda e: e.memset(invw[:, 1:2], 1.0 / 256), writes=["invw"])
                P.op("dve", lambda e: e.memset(invw[:, 2:3], 1.0 / 384), writes=["invw"])
                BR = ((0, 384), (384, 640), (640, 1024))
                blocks3 = []
                qblocks = []
                for (pn, S, src) in parts:
                    xsrc = dr["x_" + pn] if l == 0 else dr["y1_" + pn]
                    if last and pn == QPN:
                        P.op("sp", lambda e: e.dma_start(out=dr["oq"][:, 0:384], in_=dr["oaq"]), writes=["oq"], dma="cq0")
                        P.op("sp", lambda e, pn=pn: e.dma_start(out=dr["oq"][:, 640:1024], in_=dyn_ap(e, rrow, 640, dr["o_" + pn], [[D, 2048], [1, 384]])), reads=["oq"], writes=["oq"], dma="cq4")
                        P.op("sp", lambda e, pn=pn: e.dma_start(out=dr["szq"], in_=dyn_ap(e, rrow2, 0, dr["sz_" + pn], [[D // 2, 2048], [1, D // 2]])), writes=["szq"], dma="cq1")
                        P.op("sp", lambda e, xsrc=xsrc: e.dma_start(out=dr["xq"], in_=dyn_ap(e, rrow, 0, xsrc, [[D, 2048], [1, D]])), writes=["xq"], dma="cq2")
                        P.op("sp", lambda e: e.dma_start(out=dr["oq"][:, 384:640], in_=dr["obq"]), reads=["oq"], writes=["oq"], dma="cq3")
                        qblocks = [dict(pn=pn, t0=blk * 128, osrc=dr["oq"], zsrc=dr["szq"], xsrc=dr["xq"], ydst=dr["yq_" + pn], keys=["oq", "szq", "xq"]) for blk in range(16)]
                        continue
                    ydst = dr["y_" + pn] if last else dr["y1_" + pn]
                    for blk in range(S // 128):
                        blocks3.append(dict(pn=pn, t0=blk * 128, osrc=dr["o_" + pn], zsrc=dr["sz_" + pn], xsrc=xsrc, ydst=ydst, keys=[]))
                blocks3 = blocks3 + qblocks

                def p_load(c):
                    pn, t0 = c["pn"], c["t0"]
                    ob, okey = o_r.next()
                    zb, zkey = z_r.next()
                    c["o"], c["z"] = (ob, okey), (zb, zkey)
                    osrc, zsrc = c["osrc"], c["zsrc"]
                    P.op("sp", lambda e: e.dma_start(out=ob[:], in_=osrc[t0:t0 + 128, :]), reads=c["keys"][0:1], writes=[okey], dma=okey)
                    P.op("sp", lambda e: e.dma_start(out=zb[:], in_=zsrc[t0:t0 + 128, :]), reads=c["keys"][1:2], writes=[zkey], dma=zkey)

                def p_g(c):
                    ob, okey = c["o"]
                    zb, zkey = c["z"]
                    gb, gkey = g_r.next()
                    c["g"] = (gb, gkey)
                    P.op("pool", lambda e: e.tensor_tensor(out=gb[:], in0=ob[:], in1=zb[:].bitcast(BF16), op=ALU.mult), reads=[okey, zkey], writes=[gkey])

                def p_sq(c):
                    gb, gkey = c["g"]
                    s3, s3key = s3r.next()
                    c["s3"] = (s3, s3key)
                    for bi, (c0, c1) in enumerate(BR):
                        P.op("act", lambda e, bi=bi, c0=c0, c1=c1: e.activation(out=junk3[:, c0:c1], in_=gb[:, c0:c1], func=AF.Square, accum_out=s3[:, bi:bi + 1]),
                             reads=[gkey], writes=[s3key, "junk3"])

                def p_r1(c):
                    s3, s3key = c["s3"]
                    P.op("dve", lambda e: e.tensor_tensor(out=s3[:], in0=s3[:], in1=invw[:], op=ALU.mult), reads=[s3key, "invw"], writes=[s3key])
                    P.op("dve", lambda e: e.tensor_scalar(out=s3[:], in0=s3[:], scalar1=1.0, scalar2=float(EPS), op0=ALU.mult, op1=ALU.add), reads=[s3key], writes=[s3key])

                def p_r2(c):
                    s3, s3key = c["s3"]
                    P.op("pool", lambda e: e.tensor_tensor(out=s3[:], in0=s3[:], in1=nhalf[:, 0:3], op=ALU.pow), reads=[s3key], writes=[s3key])

                def p_y(c):
                    gb, gkey = c["g"]
                    s3, s3key = c["s3"]
                    yb, ykey = ybr.next()
                    c["y"] = (yb, ykey)
                    for bi, (c0, c1) in enumerate(BR):
                        P.op("dve", lambda e, bi=bi, c0=c0, c1=c1: e.scalar_tensor_tensor(
                            out=yb[:, c0:c1], in0=gb[:, c0:c1], scalar=s3[:, bi:bi + 1], in1=gbr[:, c0:c1], op0=ALU.mult, op1=ALU.mult),
                            reads=[gkey, s3key, "gbr"], writes=[ykey])

                def p_T(c):
                    yb, ykey = c["y"]
                    pT, pTkey = psT3.next()
                    c["pT"] = (pT, pTkey)
                    for kc in range(8):
                        P.op("pe", lambda e, kc=kc: e.transpose(out=pT[:, kc, :], in_=yb[:, kc * 128:(kc + 1) * 128], identity=identb[:]),
                             reads=[ykey, "identb"], writes=[pTkey])

                def p_yT(c):
                    pT, pTkey = c["pT"]
                    yT, yTkey = yTr.next()
                    c["yT"] = (yT, yTkey)
                    P.op("act", lambda e: e.activation(out=yT[:], in_=pT[:], func=AF.Copy), reads=[pTkey], writes=[yTkey])

                def p_mm(c):
                    yT, yTkey = c["yT"]
                    py, pykey = psY.next()
                    c["py"] = (py, pykey)
                    for n in range(2):
                        for kc in range(8):
                            P.op("pe", lambda e, n=n, kc=kc: e.matmul(py[:, n * 512:(n + 1) * 512], lhsT=yT[:, kc, :], rhs=wo[:, kc, n * 512:(n + 1) * 512],
                                                                      start=(kc == 0), stop=(kc == 7)),
                                 reads=[yTkey] + WO, writes=[pykey])

                def p_ev(c):
                    py, pykey = c["py"]
                    s2, s2key = s2r.next()
                    c["s2"] = (s2, s2key)
                    pc, pckey = pc_r.next()
                    c["pc"] = (pc, pckey)
                    for n in range(2):
                        P.op("act", lambda e, n=n: e.activation(out=junk3[:, n * 512:(n + 1) * 512], in_=py[:, n * 512:(n + 1) * 512], func=AF.Square, accum_out=s2[:, n:n + 1]),
                             reads=[pykey], writes=[s2key, "junk3"])
                    P.op("act", lambda e: e.activation(out=pc[:], in_=py[:], func=AF.Copy), reads=[pykey], writes=[pckey])

                def p_r3(c):
                    s2, s2key = c["s2"]
                    P.op("dve", lambda e: e.tensor_tensor(out=s2[:, 0:1], in0=s2[:, 0:1], in1=s2[:, 1:2], op=ALU.add), reads=[s2key], writes=[s2key])
                    P.op("dve", lambda e: e.tensor_scalar(out=s2[:, 0:1], in0=s2[:, 0:1], scalar1=1.0 / D, scalar2=float(EPS), op0=ALU.mult, op1=ALU.add), reads=[s2key], writes=[s2key])

                def p_r4(c):
                    s2, s2key = c["s2"]
                    P.op("pool", lambda e: e.tensor_tensor(out=s2[:, 0:1], in0=s2[:, 0:1], in1=nhalf[:, 0:1], op=ALU.pow), reads=[s2key], writes=[s2key])
                    t0, xsrc = c["t0"], c["xsrc"]
                    xb, xkey = x_r.next()
                    c["x"] = (xb, xkey)
                    P.op("sp", lambda e: e.dma_start(out=xb[:], in_=xsrc[t0:t0 + 128, :]), reads=c["keys"][2:3], writes=[xkey], dma=xkey)

                def p_stt(c):
                    pc, pckey = c["pc"]
                    s2, s2key = c["s2"]
                    tb, tkey = t_r.next()
                    c["t"] = (tb, tkey)
                    P.op("dve", lambda e: e.scalar_tensor_tensor(out=tb[:], in0=pc[:], scalar=s2[:, 0:1], in1=gpo[:], op0=ALU.mult, op1=ALU.mult),
                         reads=[pckey, s2key, "gpo"], writes=[tkey])

                def p_add(c):
                    tb, tkey = c["t"]
                    xb, xkey = c["x"]
                    P.op("pool", lambda e: e.tensor_tensor(out=tb[:], in0=tb[:], in1=xb[:], op=ALU.add), reads=[tkey, xkey], writes=[tkey])

                def p_st(c):
                    tb, tkey = c["t"]
                    t0, ydst = c["t0"], c["ydst"]
                    P.op("sp", lambda e: e.dma_start(out=ydst[t0:t0 + 128, :], in_=tb[:]), reads=[tkey], dma=tkey)

                phases = [(p_load, 0), (p_g, 2), (p_sq, 3), (p_r1, 4), (p_r2, 5), (p_y, 6), (p_T, 7), (p_yT, 8), (p_mm, 9), (p_ev, 10),
                          (p_r3, 11), (p_r4, 12), (p_stt, 14), (p_add, 15), (p_st, 17)]
                nb3 = len(blocks3)
                for i in range(nb3 + 18):
                    for fn, dly in phases:
                        if 0 <= i - dly < nb3:
                            fn(blocks3[i - dly])
                P.flush(final=last)
        print(f"[build] instructions: {P.n_ins}", flush=True)
    return nc


_PARTS = [("p", 8192, "xp"), ("s0", 2048, "xs0"), ("s1", 2048, "xs1")]


def kernel(x_prompt, x_sample, norm_pre, w_in, q_norm, k_norm, rel_bias, branch_gain, w_out, norm_post):
    f = lambda a: np.ascontiguousarray(np.asarray(a, dtype=np.float32))
    x_prompt, x_sample = f(x_prompt), f(x_sample)
    ropea, ropeb, ea, ident = _const_tables()
    efraw = _c_bias_tables(f(rel_bias))
    nc = build(_PARTS, 2)
    shared = dict(w_in=f(w_in), w_out=f(w_out), norm_pre=f(norm_pre), norm_post=f(norm_post), branch_gain=f(branch_gain),
                  q_norm=f(q_norm), k_norm=f(k_norm), ropea=ropea, ropeb=ropeb, ea=ea, ident=ident, efraw=efraw)
    in_maps = []
    for c in range(8):
        m = dict(shared)
        q0 = (c % 4) * 2048
        m["qoff"] = np.array([[q0, q0 * D, q0 * (D // 2), q0 * 1536]], dtype=np.int32)
        m["x_p"] = x_prompt[c // 4]
        m["x_s0"] = x_sample[2 * c]
        m["x_s1"] = x_sample[2 * c + 1]
        in_maps.append(m)
    res = run_bass_kernel_spmd(nc, in_maps, core_ids=list(range(8)))
    r = res.results
    y_prompt = np.stack([np.concatenate([np.asarray(r[4 * b + q]["yq_p"], dtype=np.float32) for q in range(4)], axis=0) for b in range(2)], axis=0)
    ys = []
    for c in range(8):
        ys.append(np.asarray(r[c]["y_s0"], dtype=np.float32))
        ys.append(np.asarray(r[c]["y_s1"], dtype=np.float32))
    y_sample = np.stack(ys, axis=0)
    return (y_prompt, y_sample)
```

```python
import numpy as np
from contextlib import ExitStack
import concourse.bass as bass
import concourse.mybir as mybir
from concourse.bass_utils import run_bass_kernel_spmd

F32 = mybir.dt.float32
BF16 = mybir.dt.bfloat16
AF = mybir.ActivationFunctionType
ALU = mybir.AluOpType
AX = mybir.AxisListType

D = 1024
INW = 3840
EPS = 1e-6
NEG = -30000.0
STOP = None
QUARTER = True
LIMIT = None
AQ, AK, AV, AZ, BQ, BK, BV, BZ, CQ, CK, CV, CZ = 0, 384, 768, 1152, 1536, 1792, 1920, 2048, 2304, 2688, 3072, 3456

COMPUTE = ("pe", "act", "dve", "pool")
ISSUERS = ("pe", "act", "dve", "pool", "sp")
ENGOBJ = {"pe": "tensor", "act": "scalar", "dve": "vector", "pool": "gpsimd", "sp": "sync"}


class Op:
    __slots__ = ("eng", "fn", "is_dma", "sem", "ticket", "signal", "waits")

    def __init__(self, eng, fn, is_dma, sem):
        self.eng = eng
        self.fn = fn
        self.is_dma = is_dma
        self.sem = sem
        self.ticket = None
        self.signal = is_dma
        self.waits = []


class Prog:
    def __init__(self, nc, stack, block):
        self.nc = nc
        self.stack = stack
        self.block = block
        self.streams = {e: [] for e in ISSUERS}
        self.esem = {e: self.new_sem("sem_" + e) for e in COMPUTE}
        self.bar = self.new_sem("sem_bar")
        self.bar_count = 0
        self.ecount = {e: 0 for e in COMPUTE}
        self.last_w = {}
        self.readers = {}
        self.dma_sems = {}
        self.dma_count = {}
        self.pending = None
        self.alias = {}
        self.n_ins = 0

    def new_sem(self, name):
        return self.stack.enter_context(self.nc.semaphore(name))

    def _dep(self, op, prod):
        if prod is None or prod is op:
            return
        if (not op.is_dma) and (not prod.is_dma) and prod.eng == op.eng and op.eng == "pe":
            return
        prod.signal = True
        op.waits.append(prod)

    def op(self, eng, fn, reads=(), writes=(), dma=None):
        self.nrec = getattr(self, "nrec", 0) + 1
        if LIMIT is not None and self.nrec > LIMIT:
            return None
        is_dma = dma is not None
        ps_reads = [r for r in reads if r.startswith("ps")]
        if ps_reads:
            reads = [r for r in reads if not r.startswith("ps")]
            writes = list(writes) + ps_reads
        o = Op(eng, fn, is_dma, dma)
        if is_dma:
            if dma not in self.alias:
                self.alias[dma] = f"g{len(self.alias)}"
            dma = self.alias[dma]
            o.sem = dma
            if dma not in self.dma_sems:
                self.dma_sems[dma] = self.new_sem("d_" + dma)
                self.dma_count[dma] = 0
            self.dma_count[dma] += 16
            o.ticket = self.dma_count[dma]
        for r in reads:
            self._dep(o, self.last_w.get(r))
        for w in writes:
            self._dep(o, self.last_w.get(w))
            for rd in self.readers.get(w, ()):
                self._dep(o, rd)
        for r in reads:
            self.readers.setdefault(r, []).append(o)
        for w in writes:
            self.last_w[w] = o
            self.readers[w] = []
        self.streams[eng].append(o)
        return o

    def flush(self, final=False):
        for e in COMPUTE:
            ops = [o for o in self.streams[e] if not o.is_dma]
            if ops:
                ops[-1].signal = True
            for o in ops:
                if o.signal:
                    self.ecount[e] += 1
                    o.ticket = self.ecount[e]
        pending = self.pending
        dma_final = dict(self.dma_count)
        self.bar_count += 1
        bar_val = self.bar_count
        ecount = dict(self.ecount)

        def make(ename):
            ops = self.streams[ename]

            def body(eng):
                waited = {}
                if pending is not None:
                    for key, val in pending.items():
                        if val > 0:
                            sem = self.bar if key == "bar" else self.esem[key]
                            if key != ename:
                                eng.wait_ge(sem, val)
                for o in ops:
                    need = {}
                    for p in o.waits:
                        key = ("d", p.sem) if p.is_dma else ("e", p.eng)
                        if p.ticket > need.get(key, 0):
                            need[key] = p.ticket
                    for key, val in need.items():
                        if waited.get(key, 0) >= val:
                            continue
                        waited[key] = val
                        sem = self.dma_sems[key[1]] if key[0] == "d" else self.esem[key[1]]
                        eng.wait_ge(sem, val)
                    ins = o.fn(eng)
                    self.n_ins += 1
                    if o.is_dma:
                        ins.then_inc(self.dma_sems[o.sem], 16)
                    elif o.signal:
                        ins.then_inc(self.esem[o.eng], 1)
                if ename == "sp":
                    for s, v in dma_final.items():
                        if v > 0:
                            eng.wait_ge(self.dma_sems[s], v)
                    eng.sem_inc(self.bar, 1)

            return body

        for ename in ISSUERS:
            if ename != "sp" and not self.streams[ename] and pending is None:
                continue
            getattr(self.block, ENGOBJ[ename])(make(ename))
        self.pending = dict(ecount)
        self.pending["bar"] = bar_val
        self.streams = {e: [] for e in ISSUERS}
        self.last_w = {}
        self.readers = {}
        self.alias = {}
        if final:
            pend = self.pending

            def fin(eng):
                eng.wait_ge(self.bar, pend["bar"])

            for ename in ("pe", "act", "dve", "pool"):
                getattr(self.block, ENGOBJ[ename])(fin)


_UID = [0]


def U(name):
    _UID[0] += 1
    return f"{name}_u{_UID[0]}"


class Ring:
    def __init__(self, P, st, name, n, shape, dt, psum=False):
        self.n = n
        self.name = name
        self.i = -1
        alloc = P.nc.psum_tensor if psum else P.nc.sbuf_tensor
        self.bufs = [st.enter_context(alloc(U(f"{name}{k}"), list(shape), dt)) for k in range(n)]

    def next(self):
        self.i += 1
        k = self.i % self.n
        return self.bufs[k], f"{self.name}{k}"

    def cur(self):
        k = self.i % self.n
        return self.bufs[k], f"{self.name}{k}"


def _const_tables():
    SMAX = 8192
    t = np.arange(SMAX, dtype=np.float32)
    fa = (500000.0 ** (-np.arange(0, 16, 2, dtype=np.float32) / 16)).astype(np.float32)
    anga = (t[:, None] * fa[None, :]).astype(np.float32).astype(np.float64)
    fb = (10000.0 ** (-np.arange(0, 32, 2, dtype=np.float32) / 32)).astype(np.float32)
    row = (np.arange(SMAX) // 64).astype(np.float32)
    col = (np.arange(SMAX) % 64).astype(np.float32)
    angr = (row[:, None] * fb[None, :]).astype(np.float32).astype(np.float64)
    angc = (col[:, None] * fb[None, :]).astype(np.float32).astype(np.float64)
    angb = np.concatenate([angr, angc], axis=1)

    def tm(a):
        return np.ascontiguousarray(a.reshape(SMAX // 128, 128, -1).transpose(1, 0, 2)).astype(np.float32)

    ropea = np.stack([tm(np.cos(anga)), tm(np.sin(anga))], axis=1)
    ropeb = np.stack([tm(np.cos(angb)), tm(np.sin(angb))], axis=1)
    kk = np.arange(128)[:, None, None]
    dl = (np.arange(5) - 2)[None, :, None]
    ii = np.arange(128)[None, None, :]
    dd = 128 * dl + kk - ii
    ad = np.abs(dd)
    mult = (ad <= 64).astype(np.float32) + ((dd % 4 == 0) & (ad <= 256))
    du = 128 * (np.arange(3) - 1)[None, :, None] + kk - ii
    m16 = (np.abs(du) <= 64).astype(np.float32)
    kk2 = np.arange(128)[:, None]
    ii2 = np.arange(128)[None, :]
    m16q = np.stack([(kk2 >= ii2), (kk2 <= ii2)], axis=1).astype(np.float32)
    ea = np.ascontiguousarray(np.concatenate([mult, m16, m16q], axis=1).astype(np.float32))
    ident = np.eye(128, dtype=np.float32)
    return ropea, ropeb, ea, ident


def _c_bias_tables(rel_bias):
    L = rel_bias.shape[0]
    krl = (np.arange(128) // 64)[:, None, None]
    kc = (np.arange(128) % 64)[:, None, None]
    dlt = (np.arange(7) - 3)[None, :, None]
    rl = (np.arange(128) // 64)[None, None, :]
    qc = (np.arange(128) % 64)[None, None, :]
    dr = 2 * dlt + krl - rl
    ro = dr + 7
    co = np.clip(kc - qc + 15, 0, 30)
    cs = np.clip(qc - 8, 0, 48)
    valid = (kc >= cs) & (kc < cs + 16) & (ro >= 0) & (ro <= 14)
    ro_c = np.clip(ro, 0, 14)
    ro_b, co_b, valid_b = np.broadcast_arrays(ro_c, co, valid)
    out = np.empty((L, 6, 128, 7, 128), dtype=np.float32)
    for l in range(L):
        for h in range(6):
            g = rel_bias[l, h][ro_b, co_b]
            out[l, h] = np.where(valid_b, g, np.float32(NEG))
    return out


def build(parts, n_layers, debug=False):
    nc = bass.Bass("TRN2", target_bir_lowering=False)
    dr = {}

    def din(name, shape, dt=F32):
        dr[name] = nc.dram_tensor(name, list(shape), dt, kind="ExternalInput").ap()
        return dr[name]

    def dout(name, shape, dt=F32):
        dr[name] = nc.dram_tensor(name, list(shape), dt, kind="ExternalOutput").ap()
        return dr[name]

    def dscr(name, shape, dt):
        if debug:
            dr[name] = nc.dram_tensor(name, list(shape), dt, kind="ExternalOutput").ap()
        else:
            dr[name] = nc.dram_tensor(name, list(shape), dt).ap()
        return dr[name]

    QPN = "p" if (QUARTER and any(pn == "p" for (pn, _, _) in parts)) else None
    if QPN is not None:
        dscr("oq", [2048, D], F32)
        dscr("szq", [2048, D // 2], F32)
        dscr("xq", [2048, D], F32)
        dscr("obq", [2048, 256], F32)
    for (pn, S, src) in parts:
        din("x_" + pn, [S, D])
        if pn == QPN:
            dout("yq_" + pn, [2048, D])
        else:
            dout("y_" + pn, [S, D])
        dscr("qt_" + pn, [1024, S], BF16)
        if pn == QPN:
            dscr("ktpad", [896, S + 2048], BF16)
            dscr("vpad", [S + 2048, 8 * 192], BF16)
            dr["kt_" + pn] = dr["ktpad"][:, 1024:1024 + S]
            dr["v_" + pn] = dr["vpad"][1024:1024 + S, :]
            dscr("ktc", [384, 4096], BF16)
            dscr("vc", [4096, 576], BF16)
            dscr("oaq", [2048, 384], F32)
        else:
            dscr("kt_" + pn, [896, S], BF16)
            dscr("v_" + pn, [S, 8 * 192], BF16)
        dscr("sz_" + pn, [S, D // 2], F32)
        dscr("o_" + pn, [S, D], F32)
        if n_layers > 1:
            dscr("y1_" + pn, [S, D], F32)
    din("w_in", [n_layers, D, INW])
    din("w_out", [n_layers, D, D])
    din("norm_pre", [n_layers, D])
    din("norm_post", [n_layers, D])
    din("branch_gain", [n_layers, D])
    din("q_norm", [n_layers, 64])
    din("k_norm", [n_layers, 64])
    din("ropea", [128, 2, 64, 8])
    din("ropeb", [128, 2, 64, 32])
    din("ea", [128, 10, 128])
    din("ident", [128, 128])
    din("efraw", [n_layers, 6, 128, 7, 128])
    if QPN is not None:
        dr["qoff"] = nc.dram_tensor("qoff", [1, 4], mybir.dt.int32, kind="ExternalInput").ap()

    with ExitStack() as top:
        block = top.enter_context(nc.Block())
        P = Prog(nc, top, block)
        identf = top.enter_context(nc.sbuf_tensor("identf", [128, 128], F32))
        identb = top.enter_context(nc.sbuf_tensor("identb", [128, 128], BF16))
        epsc = top.enter_context(nc.sbuf_tensor("epsc", [128, 8], F32))
        nhalf = top.enter_context(nc.sbuf_tensor("nhalf", [128, 8], F32))
        P.op("sp", lambda e: e.dma_start(out=identf[:], in_=dr["ident"]), writes=["identf"], dma="c0")
        P.op("dve", lambda e: e.tensor_copy(out=identb[:], in_=identf[:]), reads=["identf"], writes=["identb"])
        P.op("dve", lambda e: e.memset(epsc[:], EPS), writes=["epsc"])
        P.op("dve", lambda e: e.memset(nhalf[:], -0.5), writes=["nhalf"])
        if QPN is not None:
            qs = top.enter_context(nc.sbuf_tensor("qs", [1, 4], mybir.dt.int32))
            rq0 = top.enter_context(nc.sync.register("rq0"))
            rrow = top.enter_context(nc.sync.register("rrow"))
            rrow2 = top.enter_context(nc.sync.register("rrow2"))
            rv = top.enter_context(nc.sync.register("rv"))
            rtmp = [top.enter_context(nc.sync.register(f"rtmp{i}")) for i in range(4)]
            rti = [0]
            P.op("sp", lambda e: e.dma_start(out=qs[:], in_=dr["qoff"]), writes=["qs"], dma="c1")

            def setregs(e):
                e.reg_load(rq0, qs[0:1, 0:1])
                e.reg_load(rrow2, qs[0:1, 2:3])
                e.reg_load(rv, qs[0:1, 3:4])
                return e.reg_load(rrow, qs[0:1, 1:2])
            P.op("sp", setregs, reads=["qs"], writes=["regs"])

            SQ = [S for (pn_, S, _) in parts if pn_ == QPN][0]
            zt = top.enter_context(nc.sbuf_tensor("zt", [128, 1536], BF16))
            P.op("pool", lambda e: e.memset(zt[:], 0.0), writes=["zt"])
            zi = 0
            for side in (0, 1024 + SQ):
                for k in range(7):
                    P.op("sp", lambda e, k=k, side=side: e.dma_start(out=dr["ktpad"][k * 128:(k + 1) * 128, side:side + 1024], in_=zt[:, 0:1024]), reads=["zt"], dma=f"zp{zi}")
                    zi += 1
                for k in range(8):
                    P.op("sp", lambda e, k=k, side=side: e.dma_start(out=dr["vpad"][side + k * 128:side + (k + 1) * 128, :], in_=zt[:]), reads=["zt"], dma=f"zp{zi}")
                    zi += 1

            def dyn_ap(e, base_reg, const, tensor_ap, pattern):
                t = rtmp[rti[0] % 4]
                rti[0] += 1
                e.reg_add(t, base_reg, int(const))
                return bass.AP(tensor_ap.tensor, t, pattern)
        P.flush(final=(STOP == "pre"))
        if STOP == "pre":
            return nc

        def rstd_ops(v_ap, key, n, scale):
            P.op("dve", lambda e: e.tensor_scalar(out=v_ap, in0=v_ap, scalar1=float(scale), scalar2=float(EPS),
                                                  op0=ALU.mult, op1=ALU.add), reads=[key], writes=[key])
            P.op("pool", lambda e: e.tensor_tensor(out=v_ap, in0=v_ap, in1=nhalf[:, 0:n], op=ALU.pow),
                 reads=[key], writes=[key])

        for l in range(n_layers):
            last = l == n_layers - 1
            with ExitStack() as st:
                wsb = st.enter_context(nc.sbuf_tensor(U("wsb"), [128, 8, INW], BF16))
                wst = Ring(P, st, "wst", 2, [128, 8, 120], F32)
                gpre = st.enter_context(nc.sbuf_tensor(U("gpre"), [128, 8], F32))
                gqk = st.enter_context(nc.sbuf_tensor(U("gqk"), [128, 6, 64], F32))
                ropa = st.enter_context(nc.sbuf_tensor(U("ropa"), [128, 2, 64, 8], F32))
                ropb = st.enter_context(nc.sbuf_tensor(U("ropb"), [128, 2, 64, 32], F32))
                xin = Ring(P, st, "xin", 5, [128, D], F32)
                junk = st.enter_context(nc.sbuf_tensor(U("junk"), [128, D], BF16))
                ssr = Ring(P, st, "ssr", 5, [128, 1], F32)
                xnr = Ring(P, st, "xnr", 2, [128, D], BF16)
                qfr = Ring(P, st, "qfr", 2, [128, 6, 64], F32)
                bfr = Ring(P, st, "bfr", 2, [128, 512], F32)
                hTr = Ring(P, st, "hTr", 2, [128, 8, 512], BF16)
                qts = Ring(P, st, "qts", 2, [128, 8, 512], BF16)
                kts = Ring(P, st, "kts", 2, [128, 7, 512], BF16)
                vsr = Ring(P, st, "vsr", 4, [128, 8, 192], BF16)
                szr = Ring(P, st, "szr", 4, [128, D], BF16)
                qar = Ring(P, st, "qar", 3, [128, 6, 64], BF16)
                kar = Ring(P, st, "kar", 3, [128, 6, 64], BF16)
                qkbr = Ring(P, st, "qkbr", 3, [128, 6, 64], BF16)
                sqb = st.enter_context(nc.sbuf_tensor(U("sqb"), [128, 6, 64], F32))
                xgb = st.enter_context(nc.sbuf_tensor(U("xgb"), [128, 6, 64], F32))
                ss6 = st.enter_context(nc.sbuf_tensor(U("ss6"), [128, 6], F32))
                tA = [st.enter_context(nc.sbuf_tensor(U(f"tA{i}"), [128, 6, 8], F32)) for i in range(4)]
                tB = [st.enter_context(nc.sbuf_tensor(U(f"tB{i}"), [128, 6, 2, 16], F32)) for i in range(4)]
                psT = Ring(P, st, "psT", 2, [128, 8, 128], BF16, psum=True)
                psT2 = Ring(P, st, "psT2", 2, [128, 8, 128], BF16, psum=True)
                psM = Ring(P, st, "psM", 3, [128, 512], F32, psum=True)
                psF = Ring(P, st, "psF", 1, [128, 512], F32, psum=True)

                P.op("sp", lambda e: e.dma_start(out=gpre[:], in_=dr["norm_pre"][l].rearrange("(k p) -> p k", p=128),
                                                 allow_slow_non_contiguous=True), writes=["gpre"], dma="c0")
                P.op("sp", lambda e: e.dma_start(out=gqk[:, 0:4, :], in_=dr["q_norm"][l].partition_broadcast(128).unsqueeze(1).to_broadcast([128, 4, 64]),
                                                 allow_slow_non_contiguous=True), writes=["gqk_q"], dma="c1")
                P.op("sp", lambda e: e.dma_start(out=gqk[:, 4:6, :], in_=dr["k_norm"][l].partition_broadcast(128).unsqueeze(1).to_broadcast([128, 2, 64]),
                                                 allow_slow_non_contiguous=True), writes=["gqk_k"], dma="c2")
                P.op("sp", lambda e: e.dma_start(out=ropa[:], in_=dr["ropea"]), writes=["ropa"], dma="c3")
                P.op("sp", lambda e: e.dma_start(out=ropb[:], in_=dr["ropeb"]), writes=["ropb"], dma="c4")
                for ci in range(32):
                    wbuf, wkey = wst.next()
                    c0 = ci * 120
                    P.op("sp", lambda e, wbuf=wbuf, c0=c0: e.dma_start(
                        out=wbuf[:], in_=dr["w_in"][l][:, c0:c0 + 120].rearrange("(k p) c -> p k c", p=128)),
                        writes=[wkey], dma=wkey)
                    eng = ("pool", "dve", "act")[ci % 3]
                    if eng == "act":
                        P.op("act", lambda e, wbuf=wbuf, c0=c0: e.activation(out=wsb[:, :, c0:c0 + 120], in_=wbuf[:], func=AF.Copy),
                             reads=[wkey], writes=[f"wsb{ci}"])
                    else:
                        P.op(eng, lambda e, wbuf=wbuf, c0=c0: e.tensor_copy(out=wsb[:, :, c0:c0 + 120], in_=wbuf[:]),
                             reads=[wkey], writes=[f"wsb{ci}"])
                WALL = [f"wsb{ci}" for ci in range(32)]

                def wkeys(c0, c1):
                    return [f"wsb{ci}" for ci in range(c0 // 120, (c1 - 1) // 120 + 1)]

                from collections import deque
                blocks1 = []
                for (pn, S, src) in parts:
                    xsrc = dr["x_" + pn] if l == 0 else dr["y1_" + pn]
                    for blk in range(S // 128):
                        blocks1.append(dict(pn=pn, blk=blk, j=blk % 4, tg=blk // 4, xsrc=xsrc))
                pend = deque()

                def run_pend(keep):
                    while len(pend) > keep:
                        pend.popleft()()

                def s1_load(c):
                    xb, xkey = xin.next()
                    c["x"] = (xb, xkey)
                    t0, xsrc = c["blk"] * 128, c["xsrc"]
                    P.op("sp", lambda e: e.dma_start(out=xb[:], in_=xsrc[t0:t0 + 128, :]), writes=[xkey], dma=xkey)

                grp = {}

                def s1_fa(c):
                    xb, xkey = c["x"]
                    ss, sskey = ssr.next()
                    c["ss"] = (ss, sskey)
                    P.op("act", lambda e: e.activation(out=junk[:], in_=xb[:], func=AF.Square, accum_out=ss[:]), reads=[xkey], writes=[sskey, "junk"])
                    rstd_ops(ss[:], sskey, 1, 1.0 / D)

                def s1_front(c):
                    j = c["j"]
                    if j == 0:
                        grp["hT"] = hTr.next()
                        grp["qt"] = qts.next()
                        grp["kt"] = kts.next()
                    c["hT"], c["qt"], c["kt"] = grp["hT"], grp["qt"], grp["kt"]
                    hT, hkey = c["hT"]
                    xb, xkey = c["x"]
                    ss, sskey = c["ss"]
                    xn, xnkey = xnr.next()
                    P.op("act", lambda e: e.activation(out=xn[:], in_=xb[:], func=AF.Copy, scale=ss[:]), reads=[xkey, sskey], writes=[xnkey])
                    pT, pTkey = psT.next()
                    for kc in range(8):
                        P.op("pe", lambda e, kc=kc: e.transpose(out=pT[:, kc, :], in_=xn[:, kc * 128:(kc + 1) * 128], identity=identb[:]),
                             reads=[xnkey, "identb"], writes=[pTkey])
                    P.op("dve", lambda e: e.tensor_tensor(
                        out=hT[:, :, j * 128:(j + 1) * 128], in0=pT[:], in1=gpre[:].unsqueeze(2).to_broadcast([128, 8, 128]), op=ALU.mult),
                        reads=[pTkey, "gpre"], writes=[hkey + f"_{j}"])

                def s1_main(c):
                    j, blk, pn = c["j"], c["blk"], c["pn"]
                    hT, hkey = c["hT"]
                    qt_s, qkey = c["qt"]
                    kt_s, kkey = c["kt"]
                    hk = hkey + f"_{j}"

                    def tok_mm(c0, c1):
                        run_pend(2)
                        pm, pmkey = psM.next()
                        for kc in range(8):
                            P.op("pe", lambda e, kc=kc: e.matmul(pm[:, 0:c1 - c0], lhsT=hT[:, kc, j * 128:(j + 1) * 128], rhs=wsb[:, kc, c0:c1],
                                                                 start=(kc == 0), stop=(kc == 7)),
                                 reads=[hk] + wkeys(c0, c1), writes=[pmkey])
                        return pm, pmkey

                    for which, c0, ring, dst in (("q", AQ, qar, qt_s), ("k", AK, kar, kt_s)):
                        pm, pmkey = tok_mm(c0, c0 + 384)
                        qf, qfkey = qfr.next()
                        P.op("act", lambda e, qf=qf, pm=pm: e.activation(out=qf[:].rearrange("p h d -> p (h d)"), in_=pm[:, 0:384], func=AF.Copy), reads=[pmkey], writes=[qfkey])
                        ob, okey = ring.next()
                        P.op("pool", lambda e, ob=ob, qf=qf: e.tensor_copy(out=ob[:], in_=qf[:]), reads=[qfkey], writes=[okey])
                        cosb = ropa[:, 0, blk, :].unsqueeze(1).to_broadcast([128, 6, 8])
                        sinb = ropa[:, 1, blk, :].unsqueeze(1).to_broadcast([128, 6, 8])
                        x1 = qf[:, :, 0:8]
                        x2 = qf[:, :, 8:16]
                        P.op("dve", lambda e, x1=x1, cosb=cosb: e.tensor_tensor(out=tA[0][:], in0=x1, in1=cosb, op=ALU.mult), reads=[qfkey, "ropa"], writes=["tA0"])
                        P.op("dve", lambda e, x2=x2, sinb=sinb: e.tensor_tensor(out=tA[1][:], in0=x2, in1=sinb, op=ALU.mult), reads=[qfkey, "ropa"], writes=["tA1"])
                        P.op("dve", lambda e, x2=x2, cosb=cosb: e.tensor_tensor(out=tA[2][:], in0=x2, in1=cosb, op=ALU.mult), reads=[qfkey, "ropa"], writes=["tA2"])
                        P.op("dve", lambda e, x1=x1, sinb=sinb: e.tensor_tensor(out=tA[3][:], in0=x1, in1=sinb, op=ALU.mult), reads=[qfkey, "ropa"], writes=["tA3"])
                        P.op("dve", lambda e, ob=ob: e.tensor_tensor(out=ob[:, :, 0:8], in0=tA[0][:], in1=tA[1][:], op=ALU.subtract), reads=["tA0", "tA1", okey], writes=[okey])
                        P.op("dve", lambda e, ob=ob: e.tensor_tensor(out=ob[:, :, 8:16], in0=tA[2][:], in1=tA[3][:], op=ALU.add), reads=["tA2", "tA3", okey], writes=[okey])

                        def fin_a(ob=ob, okey=okey, dst=dst, which=which):
                            pT2, pT2key = psT2.next()
                            obf = ob[:].rearrange("p h d -> p (h d)")
                            for tt in range(3):
                                P.op("pe", lambda e, tt=tt: e.transpose(out=pT2[:, tt, :], in_=obf[:, tt * 128:(tt + 1) * 128], identity=identb[:]),
                                     reads=[okey, "identb"], writes=[pT2key])
                            dkey = (qkey if which == "q" else kkey) + f"_{j}a"
                            P.op("act", lambda e: e.activation(out=dst[:, 0:3, j * 128:(j + 1) * 128], in_=pT2[:, 0:3, :], func=AF.Copy),
                                 reads=[pT2key], writes=[dkey])
                        pend.append(fin_a)
                    vs, vkey = vsr.next()
                    c["vs"] = (vs, vkey)
                    pm, pmkey = tok_mm(AV, AV + 384)
                    P.op("dve", lambda e, pm=pm: e.tensor_copy(out=vs[:, 0:3, :].rearrange("p a (s d) -> p a s d", d=64)[:, :, 0::2, :], in_=pm[:, 0:384].rearrange("p (a s d) -> p a s d", s=2, d=64)),
                         reads=[pmkey], writes=[vkey + "a"])
                    P.op("pool", lambda e: e.memset(vs[:, :, 64:128], 1.0), writes=[vkey + "one"])
                    sz, szkey = szr.next()
                    c["sz"] = (sz, szkey)
                    pm, pmkey = tok_mm(AZ, AZ + 384)
                    P.op("act", lambda e, pm=pm: e.activation(out=sz[:, 0:384], in_=pm[:, 0:384], func=AF.Silu), reads=[pmkey], writes=[szkey + "a"])
                    pm, pmkey = tok_mm(BQ, BQ + 512)
                    bf, bfkey = bfr.next()
                    P.op("act", lambda e, pm=pm, bf=bf: e.activation(out=bf[:], in_=pm[:], func=AF.Copy), reads=[pmkey], writes=[bfkey])
                    bf6 = bf[:, 0:384].rearrange("p (h d) -> p h d", d=64)
                    P.op("pool", lambda e, bf6=bf6: e.tensor_tensor(out=sqb[:], in0=bf6, in1=bf6, op=ALU.mult), reads=[bfkey], writes=["sqb"])
                    P.op("dve", lambda e: e.tensor_reduce(out=ss6[:], in_=sqb[:], axis=AX.X, op=ALU.add), reads=["sqb"], writes=["ss6"])
                    rstd_ops(ss6[:], "ss6", 6, 1.0 / 64)
                    P.op("dve", lambda e, bf6=bf6: e.tensor_tensor(out=xgb[:], in0=bf6, in1=ss6[:].unsqueeze(2).to_broadcast([128, 6, 64]), op=ALU.mult),
                         reads=[bfkey, "ss6"], writes=["xgb"])
                    P.op("pool", lambda e: e.tensor_tensor(out=xgb[:], in0=xgb[:], in1=gqk[:], op=ALU.mult), reads=["xgb", "gqk_q", "gqk_k"], writes=["xgb"])
                    qkb, qkbkey = qkbr.next()
                    xv = xgb[:].rearrange("p h (a b c) -> p h a b c", a=2, b=2)
                    ov = qkb[:].rearrange("p h (a b c) -> p h a b c", a=2, b=2)
                    cb = ropb[:, 0, blk, :].rearrange("p (a c) -> p a c", a=2).unsqueeze(1).to_broadcast([128, 6, 2, 16])
                    sb_ = ropb[:, 1, blk, :].rearrange("p (a c) -> p a c", a=2).unsqueeze(1).to_broadcast([128, 6, 2, 16])
                    x1 = xv[:, :, :, 0, :]
                    x2 = xv[:, :, :, 1, :]
                    P.op("pool", lambda e, x1=x1, cb=cb: e.tensor_tensor(out=tB[0][:], in0=x1, in1=cb, op=ALU.mult), reads=["xgb", "ropb"], writes=["tB0"])
                    P.op("pool", lambda e, x2=x2, sb_=sb_: e.tensor_tensor(out=tB[1][:], in0=x2, in1=sb_, op=ALU.mult), reads=["xgb", "ropb"], writes=["tB1"])
                    P.op("dve", lambda e, x2=x2, cb=cb: e.tensor_tensor(out=tB[2][:], in0=x2, in1=cb, op=ALU.mult), reads=["xgb", "ropb"], writes=["tB2"])
                    P.op("dve", lambda e, x1=x1, sb_=sb_: e.tensor_tensor(out=tB[3][:], in0=x1, in1=sb_, op=ALU.mult), reads=["xgb", "ropb"], writes=["tB3"])
                    P.op("pool", lambda e, ov=ov: e.tensor_tensor(out=ov[:, :, :, 0, :], in0=tB[0][:], in1=tB[1][:], op=ALU.subtract), reads=["tB0", "tB1"], writes=[qkbkey + "x"])
                    P.op("dve", lambda e, ov=ov: e.tensor_tensor(out=ov[:, :, :, 1, :], in0=tB[2][:], in1=tB[3][:], op=ALU.add), reads=["tB2", "tB3"], writes=[qkbkey + "y"])
                    bfv = bf[:, 384:512].rearrange("p (h d) -> p h d", d=64)
                    P.op("pool", lambda e, bfv=bfv: e.tensor_copy(out=vs[:, 3:5, 0:64], in_=bfv), reads=[bfkey], writes=[vkey + "b"])
                    P.op("pool", lambda e, bfv=bfv: e.tensor_copy(out=vs[:, 3:5, 128:192], in_=bfv), reads=[bfkey], writes=[vkey + "b2"])

                    def fin_b(qkb=qkb, qkbkey=qkbkey):
                        pT2, pT2key = psT2.next()
                        qkbf = qkb[:].rearrange("p h d -> p (h d)")
                        for tt in range(3):
                            P.op("pe", lambda e, tt=tt: e.transpose(out=pT2[:, tt, :], in_=qkbf[:, tt * 128:(tt + 1) * 128], identity=identb[:]),
                                 reads=[qkbkey + "x", qkbkey + "y", "identb"], writes=[pT2key])
                        P.op("act", lambda e: e.activation(out=qt_s[:, 3:5, j * 128:(j + 1) * 128], in_=pT2[:, 0:2, :], func=AF.Copy),
                             reads=[pT2key], writes=[qkey + f"_{j}b"])
                        P.op("act", lambda e: e.activation(out=kt_s[:, 3, j * 128:(j + 1) * 128], in_=pT2[:, 2, :], func=AF.Copy),
                             reads=[pT2key], writes=[kkey + f"_{j}b"])
                    pend.append(fin_b)
                    pm, pmkey = tok_mm(BZ, BZ + 256)
                    P.op("act", lambda e, pm=pm: e.activation(out=sz[:, 384:640], in_=pm[:, 0:256], func=AF.Silu), reads=[pmkey], writes=[szkey + "b"])
                    pm, pmkey = tok_mm(CV, CV + 384)
                    P.op("dve", lambda e, pm=pm: e.tensor_copy(out=vs[:, 5:8, :].rearrange("p a (s d) -> p a s d", d=64)[:, :, 0::2, :], in_=pm[:, 0:384].rearrange("p (a s d) -> p a s d", s=2, d=64)),
                         reads=[pmkey], writes=[vkey + "c"])
                    pm, pmkey = tok_mm(CZ, CZ + 384)
                    P.op("act", lambda e, pm=pm: e.activation(out=sz[:, 640:1024], in_=pm[:, 0:384], func=AF.Silu), reads=[pmkey], writes=[szkey + "c"])
                    if j == 3:
                        hks = [hkey + f"_{jj}" for jj in range(4)]
                        for ti in range(6):
                            run_pend(2)
                            c0 = (CQ if ti < 3 else CK) + (ti % 3) * 128
                            pf, pfkey = psF.next()
                            for kc in range(8):
                                P.op("pe", lambda e, pf=pf, kc=kc, c0=c0: e.matmul(pf[:], lhsT=wsb[:, kc, c0:c0 + 128], rhs=hT[:, kc, :], start=(kc == 0), stop=(kc == 7)),
                                     reads=hks + wkeys(c0, c0 + 128), writes=[pfkey])
                            if ti < 3:
                                P.op("dve", lambda e, pf=pf, ti=ti: e.tensor_copy(out=qt_s[:, 5 + ti, :], in_=pf[:]), reads=[pfkey], writes=[qkey + f"_c{ti}"])
                            else:
                                P.op("act", lambda e, pf=pf, ti=ti: e.activation(out=kt_s[:, 4 + ti - 3, :], in_=pf[:], func=AF.Copy), reads=[pfkey], writes=[kkey + f"_c{ti}"])

                def s1_store(c):
                    j, blk, pn = c["j"], c["blk"], c["pn"]
                    t0 = blk * 128
                    vs, vkey = c["vs"]
                    sz, szkey = c["sz"]
                    qt_s, qkey = c["qt"]
                    kt_s, kkey = c["kt"]
                    P.op("sp", lambda e: e.dma_start(out=dr["v_" + pn][t0:t0 + 128, :], in_=vs[:].rearrange("p h d -> p (h d)")),
                         reads=[vkey + "a", vkey + "b", vkey + "b2", vkey + "c", vkey + "one"], dma=vkey)
                    P.op("sp", lambda e: e.dma_start(out=dr["sz_" + pn][t0:t0 + 128, :], in_=sz[:].bitcast(F32)),
                         reads=[szkey + "a", szkey + "b", szkey + "c"], dma=szkey)
                    if j == 3:
                        tt0 = c["tg"] * 512
                        qr = [qkey + f"_{jj}a" for jj in range(4)] + [qkey + f"_{jj}b" for jj in range(4)] + [qkey + f"_c{ti}" for ti in range(3)]
                        kr = [kkey + f"_{jj}a" for jj in range(4)] + [kkey + f"_{jj}b" for jj in range(4)] + [kkey + f"_c{ti}" for ti in range(3, 6)]
                        P.op("sp", lambda e: e.dma_start(out=dr["qt_" + pn][:, tt0:tt0 + 512].rearrange("(k p) t -> p k t", p=128), in_=qt_s[:]), reads=qr, dma=qkey)
                        P.op("sp", lambda e: e.dma_start(out=dr["kt_" + pn][:, tt0:tt0 + 512].rearrange("(k p) t -> p k t", p=128), in_=kt_s[:]), reads=kr, dma=kkey)

                nb1 = len(blocks1)
                for i in range(-3, nb1 + 2):
                    if 0 <= i + 3 < nb1:
                        s1_load(blocks1[i + 3])
                    if 0 <= i + 2 < nb1:
                        s1_fa(blocks1[i + 2])
                    if 0 <= i + 1 < nb1:
                        s1_front(blocks1[i + 1])
                    if 0 <= i < nb1:
                        s1_main(blocks1[i])
                        if blocks1[i]["j"] == 3:
                            run_pend(0)
                    if 0 <= i - 2 < nb1:
                        s1_store(blocks1[i - 2])
                run_pend(0)
                P.flush(final=(STOP == "s1"))
            if STOP == "s1":
                return nc

            with ExitStack() as st:
                SMAXP = max(S for (_, S, _) in parts)
                qpr = Ring(P, st, "qpr", 2, [128, SMAXP], BF16)
                kpr = Ring(P, st, "kpr", 2, [128, SMAXP], BF16)
                vbr = Ring(P, st, "vbr", 2, [128, SMAXP // 128, 192], BF16)
                eab = st.enter_context(nc.sbuf_tensor(U("eab"), [128, 5, 128], BF16))
                m16 = st.enter_context(nc.sbuf_tensor(U("m16"), [128, 5, 128], BF16))
                accs = [(st.enter_context(nc.sbuf_tensor(U(f"acc{hh}"), [128, 2048], F32)), f"acc{hh}") for hh in range(2)]
                v16r = Ring(P, st, "v16r", 1, [128, SMAXP // 128, 192], BF16)
                est = Ring(P, st, "est", 2, [128, 10 * 128], F32)
                efb = st.enter_context(nc.sbuf_tensor(U("efb"), [128, 6, 7, 128], BF16))
                eib = st.enter_context(nc.sbuf_tensor(U("eib"), [128, 6, 5, 128], BF16))
                ptr = Ring(P, st, "ptr", 6, [128, 512], BF16)
                osr = Ring(P, st, "osr", 2, [128, 512], F32)
                ogr = Ring(P, st, "ogr", 2, [128, 4, 128], F32)
                rlr = Ring(P, st, "rlr", 2, [128, 4], F32)
                psS = Ring(P, st, "psS", 4, [128, 512], F32, psum=True)
                psO = Ring(P, st, "psO", 2, [128, 512], F32, psum=True)
                psR = Ring(P, st, "psR", 2, [128, 4, 128], F32, psum=True)

                eb, ekey = est.next()
                P.op("sp", lambda e, eb=eb: e.dma_start(out=eb[:], in_=dr["ea"].rearrange("p a b -> p (a b)")), writes=[ekey], dma=ekey)
                P.op("dve", lambda e, eb=eb: e.tensor_copy(out=eab[:].rearrange("p a b -> p (a b)"), in_=eb[:, 0:5 * 128]), reads=[ekey], writes=["eab"])
                P.op("dve", lambda e, eb=eb: e.tensor_copy(out=m16[:].rearrange("p a b -> p (a b)"), in_=eb[:, 5 * 128:10 * 128]), reads=[ekey], writes=["m16"])
                for h in range(6):
                    eb, ekey = est.next()
                    P.op("sp", lambda e, eb=eb, h=h: e.dma_start(out=eb[:, 0:7 * 128], in_=dr["efraw"][l, h].rearrange("p a b -> p (a b)")), writes=[ekey], dma=ekey)
                    P.op("act", lambda e, eb=eb, h=h: e.activation(out=efb[:, h].rearrange("p a b -> p (a b)"), in_=eb[:, 0:7 * 128], func=AF.Exp),
                         reads=[ekey], writes=[f"efb{h}"])
                    P.op("dve", lambda e, h=h: e.tensor_copy(out=eib[:, h], in_=efb[:, h, 1:6, :]), reads=[f"efb{h}"], writes=[f"eib{h}"])
                    P.op("dve", lambda e, h=h: e.memset(eib[0:64, h, 0, 64:128], 0.0), reads=[f"eib{h}"], writes=[f"eib{h}"])
                    P.op("dve", lambda e, h=h: e.memset(eib[64:128, h, 4, :], 0.0), reads=[f"eib{h}"], writes=[f"eib{h}"])
                    P.op("dve", lambda e, h=h: e.memset(eib[0:64, h, 4, 0:64], 0.0), reads=[f"eib{h}"], writes=[f"eib{h}"])

                jobs = []
                for (pn, S, src) in parts:
                    for gp in range(8):
                        jobs.append(dict(pn=pn, S=S, gp=gp, dyn=(last and pn == QPN and 3 <= gp < 5), dynA=(last and pn == QPN and gp < 3)))
                if last and QPN is not None:
                    P.op("sp", lambda e: e.dma_start(out=dr["ktc"], in_=dyn_ap(e, rq0, 0, dr["ktpad"], [[SQ + 2048, 384], [1, 4096]])), writes=["ktc"], dma="cq5")
                    P.op("sp", lambda e: e.dma_start(out=dr["vc"], in_=dyn_ap(e, rv, 0, dr["vpad"], [[1536, 4096], [1, 576]])), writes=["vc"], dma="cq6")

                def load_job(jb):
                    pn, S, gp = jb["pn"], jb["S"], jb["gp"]
                    NB = S // 128
                    qtd, ktd, vd = dr["qt_" + pn], dr["kt_" + pn], dr["v_" + pn]
                    if gp < 3:
                        pr = gp
                        qrow, krows = pr * 128, [pr * 128, pr * 128 + 64]
                    elif gp < 5:
                        pr = gp - 3
                        qrow, krows = 384 + pr * 128, [384 + pr * 64, 384 + pr * 64]
                    else:
                        pr = gp - 5
                        qrow, krows = 640 + pr * 128, [512 + pr * 128, 512 + pr * 128 + 64]
                    vb, vbkey = vbr.next()
                    dynA = jb.get("dynA")
                    if dynA:
                        nch = 1
                        P.op("sp", lambda e: e.dma_start(out=vb[:, 0:32, :], in_=dr["vc"][:, gp * 192:(gp + 1) * 192].rearrange("(b p) c -> p b c", p=128)),
                             reads=["vc"], writes=[vbkey + "_0"], dma=f"{vbkey}_0")
                    else:
                        nch = 4 if S > 2048 else 1
                        for ch in range(nch):
                            b0 = ch * (NB // nch)
                            b1 = (ch + 1) * (NB // nch)
                            P.op("sp", lambda e, b0=b0, b1=b1: e.dma_start(
                                out=vb[:, b0:b1, :], in_=vd[b0 * 128:b1 * 128, gp * 192:(gp + 1) * 192].rearrange("(b p) c -> p b c", p=128)),
                                writes=[vbkey + f"_{ch}"], dma=f"{vbkey}_{ch}")
                    jb["vb"] = (vb, vbkey, [vbkey + f"_{ch}" for ch in range(nch)])

                    qp, qpkey = qpr.next()
                    kp, kpkey = kpr.next()
                    if jb.get("dyn") or dynA:
                        P.op("sp", lambda e: e.dma_start(out=qp[:, 0:2048], in_=dyn_ap(e, rq0, qrow * S, qtd, [[S, 128], [1, 2048]])), writes=[qpkey], dma=qpkey)
                    else:
                        P.op("sp", lambda e: e.dma_start(out=qp[:, 0:S], in_=qtd[qrow:qrow + 128, :]), writes=[qpkey], dma=qpkey)
                    for hh in range(2):
                        if dynA:
                            P.op("sp", lambda e, hh=hh: e.dma_start(out=kp[hh * 64:(hh + 1) * 64, 0:4096], in_=dr["ktc"][krows[hh]:krows[hh] + 64, :]),
                                 reads=["ktc"], writes=[kpkey + f"_{hh}"], dma=f"{kpkey}_{hh}")
                        else:
                            P.op("sp", lambda e, hh=hh: e.dma_start(out=kp[hh * 64:(hh + 1) * 64, 0:S], in_=ktd[krows[hh]:krows[hh] + 64, :]),
                                 writes=[kpkey + f"_{hh}"], dma=f"{kpkey}_{hh}")
                    jb["qp"] = (qp, qpkey)
                    jb["kp"] = (kp, kpkey)

                load_job(jobs[0])
                for ji, jb in enumerate(jobs):
                    if ji + 1 < len(jobs):
                        load_job(jobs[ji + 1])
                    pn, S, gp = jb["pn"], jb["S"], jb["gp"]
                    NB = S // 128
                    NW = S // 512
                    od = dr["o_" + pn]
                    if True:
                        if gp < 3:
                            br, pr = "A", gp
                            ocol = pr * 128
                        elif gp < 5:
                            br, pr = "B", gp - 3
                            ocol = 384 + pr * 128
                        else:
                            br, pr = "C", gp - 5
                            ocol = 640 + pr * 128
                        vb, vbkey, vkeys = jb["vb"]
                        qp, qpkey = jb["qp"]
                        kp, kpkey = jb["kp"]
                        groups = []

                        def dense_groups(w):
                            for qb in range(4):
                                b = w * 4 + qb
                                if br == "A":
                                    alist = list(range(max(0, b - 2), min(NB, b + 3)))
                                    kind, off = "ea", 2
                                elif b <= 1:
                                    alist, kind, off = list(range(0, 4)), "ef", 3
                                elif b >= NB - 2:
                                    alist, kind, off = list(range(NB - 4, NB)), "ef", 3
                                else:
                                    alist, kind, off = list(range(b - 2, b + 3)), "ei", 2
                                nu = len(alist)
                                gi = 0
                                while gi < nu:
                                    gn = min(4, nu - gi)
                                    if nu - gi == 5:
                                        gn = 3
                                    us = alist[gi:gi + gn]
                                    groups.append(dict(units=[dict(k=(a * 128, 1), q=(b * 128, 1, 128), pc0=ui * 128, v=("nat", a), oc0=qb * 128,
                                                                   st=(gi + ui == 0), sp=(gi + ui == nu - 1)) for ui, a in enumerate(us)],
                                                       e=(kind, us[0] - b + off, gn, False), tail=("win" if (qb == 3 and gi + gn == nu) else None), w=w))
                                    gi += gn

                        dyn = jb.get("dyn")
                        if br == "B":
                            for w in range(4 if dyn else NW):
                                for kb in range(NB):
                                    groups.append(dict(units=[dict(k=(kb * 128, 1), q=(w * 512, 1, 512), pc0=0, v=("nat", kb), oc0=0, st=(kb == 0), sp=(kb == NB - 1))],
                                                       e=None, tail=("win" if kb == NB - 1 else None), w=w))
                        elif br == "C":
                            for w in range(NW):
                                dense_groups(w)
                        elif jb.get("dynA"):
                            for rq in range(4):
                                for ri in range(4):
                                    r = 4 * rq + ri
                                    units = [dict(k=(jj * 2048 + r, 16), q=(r, 16, 128), pc0=jj * 128, v=("v16", r * 2 + jj), oc0=ri * 128, st=(jj == 0), sp=(jj == 1)) for jj in range(2)]
                                    groups.append(dict(units=units, e=("m16", 3, 2, False), tail=("quad" if ri == 3 else None), sw=0, rq=rq))
                            for w in range(4):
                                for qb in range(4):
                                    b = w * 4 + qb
                                    alist = [b + 8 + d_ for d_ in range(-2, 3)]
                                    for (gi, gn) in ((0, 3), (3, 2)):
                                        us = alist[gi:gi + gn]
                                        groups.append(dict(units=[dict(k=(a * 128, 1), q=(b * 128, 1, 128), pc0=ui * 128, v=("nat", a), oc0=qb * 128,
                                                                       st=(gi + ui == 0), sp=(gi + ui == 4)) for ui, a in enumerate(us)],
                                                           e=("ea", gi, gn, False), tail=("win" if (qb == 3 and gi == 3) else None), w=w))
                        else:
                            nj = NB // 16
                            for sw in range(nj):
                                for rq in range(4):
                                    ulist = [u for u in (-1, 0, 1) if 0 <= sw + u < nj]
                                    if len(ulist) == 1:
                                        units = []
                                        for ri in range(4):
                                            r = 4 * rq + ri
                                            units.append(dict(k=(sw * 2048 + r, 16), q=(sw * 2048 + r, 16, 128), pc0=ri * 128, v=("v16", r * nj + sw), oc0=ri * 128, st=True, sp=True))
                                        groups.append(dict(units=units, e=("m16", 1, 4, True), tail="quad", sw=sw, rq=rq))
                                    else:
                                        for ri in range(4):
                                            r = 4 * rq + ri
                                            units = []
                                            for ui, u in enumerate(ulist):
                                                units.append(dict(k=((sw + u) * 2048 + r, 16), q=(sw * 2048 + r, 16, 128), pc0=ui * 128, v=("v16", r * nj + sw + u), oc0=ri * 128,
                                                                  st=(ui == 0), sp=(ui == len(ulist) - 1)))
                                            groups.append(dict(units=units, e=("m16", ulist[0] + 1, len(ulist), False), tail=("quad" if ri == 3 else None), sw=sw, rq=rq))
                                for w in range(sw * 4, sw * 4 + 4):
                                    dense_groups(w)
                        ng = len(groups)
                        pos = [psO.next(), psO.next()]
                        state = {}
                        v16 = None
                        def load_v16(jbx):
                            pnx, Sx, gpx = jbx["pn"], jbx["S"], jbx["gp"]
                            v16b, v16key = v16r.next()
                            if jbx.get("dynA"):
                                nj_ = 2
                                vsrc = dr["vc"][:, gpx * 192:(gpx + 1) * 192].rearrange("(jj i r) c -> r i jj c", i=128, r=16)
                                v16reads = ["vc"]
                            else:
                                nj_ = (Sx // 128) // 16
                                vsrc = dr["v_" + pnx][:, gpx * 192:(gpx + 1) * 192].rearrange("(jj i r) c -> r i jj c", i=128, r=16)
                                v16reads = []
                            for r in range(16):
                                P.op("sp", lambda e, r=r, v16b=v16b, vsrc=vsrc, nj_=nj_: e.dma_start(out=v16b[:, r * nj_:(r + 1) * nj_, :], in_=vsrc[r]),
                                     reads=v16reads, writes=[v16key + f"_{r}"], dma=f"{v16key}_{r}")
                            jbx["v16"] = (v16b, [v16key + f"_{r}" for r in range(16)])

                        if br == "A":
                            if "v16" not in jb:
                                load_v16(jb)
                            v16 = jb["v16"]
                        nxt = jobs[ji + 1] if ji + 1 < len(jobs) else None
                        pre_at = None
                        if br == "A" and nxt is not None and nxt["gp"] < 3:
                            lastpat = max(gi_ for gi_, g_ in enumerate(groups) if g_["units"][0]["v"][0] == "v16")
                            pre_at = lastpat + 2

                        def sl(start, stride, n):
                            return slice(start, start + (n - 1) * stride + 1, stride) if stride != 1 else slice(start, start + n)

                        def emit_front2(g, qp=qp, kp=kp, qpkey=qpkey, kpkey=kpkey, pr=pr):
                            pss = [psS.next(), psS.next()]
                            ncols = 0
                            for u in g["units"]:
                                ks, kst = u["k"]
                                qs, qst, n = u["q"]
                                pc0 = u["pc0"]
                                for hh in range(2):
                                    ps, pskey = pss[hh]
                                    P.op("pe", lambda e, ps=ps, ks=ks, kst=kst, qs=qs, qst=qst, n=n, pc0=pc0, hh=hh: e.matmul(
                                        ps[:, pc0:pc0 + n], lhsT=kp[hh * 64:(hh + 1) * 64, sl(ks, kst, 128)], rhs=qp[hh * 64:(hh + 1) * 64, sl(qs, qst, n)], start=True, stop=True),
                                        reads=[qpkey, kpkey + f"_{hh}"], writes=[pskey])
                                ncols = max(ncols, pc0 + n)
                            for hh in range(2):
                                ps, pskey = pss[hh]
                                pt, ptkey = ptr.next()
                                P.op("act", lambda e, ps=ps, pt=pt, ncols=ncols: e.activation(out=pt[:, 0:ncols], in_=ps[:, 0:ncols], func=AF.Exp, scale=0.125),
                                     reads=[pskey], writes=[ptkey])
                                if g["e"] is not None:
                                    kind, i0_, gn, bc = g["e"]
                                    hglob = 2 * pr + hh
                                    if kind == "ea":
                                        e_ap, ekeys = eab[:, i0_:i0_ + gn, :], ["eab"]
                                    elif kind == "m16":
                                        if bc:
                                            e_ap, ekeys = m16[:, i0_:i0_ + 1, :].to_broadcast([128, gn, 128]), ["m16"]
                                        else:
                                            e_ap, ekeys = m16[:, i0_:i0_ + gn, :], ["m16"]
                                    elif kind == "ef":
                                        e_ap, ekeys = efb[:, hglob, i0_:i0_ + gn, :], [f"efb{hglob}"]
                                    else:
                                        e_ap, ekeys = eib[:, hglob, i0_:i0_ + gn, :], [f"eib{hglob}"]
                                    P.op("dve", lambda e, pt=pt, e_ap=e_ap, gn=gn: e.tensor_tensor(
                                        out=pt[:, 0:gn * 128].rearrange("p (a b) -> p a b", b=128), in0=pt[:, 0:gn * 128].rearrange("p (a b) -> p a b", b=128), in1=e_ap, op=ALU.mult),
                                        reads=[ptkey] + ekeys, writes=[ptkey])
                                g["pt%d" % hh] = (pt, ptkey)

                        def emit_pv2(g, vb=vb, vkeys=vkeys):
                            for u in g["units"]:
                                vkind, vblk = u["v"]
                                pc0, oc0, st_, sp_ = u["pc0"], u["oc0"], u["st"], u["sp"]
                                n = u["q"][2]
                                if vkind == "nat":
                                    vt, vks = vb, vkeys
                                else:
                                    vt, vks = v16[0], v16[1]
                                for hh in range(2):
                                    pt, ptkey = g["pt%d" % hh]
                                    po, pokey = pos[hh]
                                    P.op("pe", lambda e, po=po, vt=vt, vblk=vblk, pc0=pc0, n=n, oc0=oc0, st_=st_, sp_=sp_, pt=pt, hh=hh: e.matmul(
                                        po[:, oc0:oc0 + n], lhsT=vt[:, vblk, hh * 64:hh * 64 + 128], rhs=pt[:, pc0:pc0 + n], start=st_, stop=sp_),
                                        reads=[ptkey] + vks, writes=[pokey])

                        def emit_back(g, hh, ocol=ocol, od=od, br=br, dyn=dyn, dynA=jb.get("dynA")):
                            po, pokey = pos[hh]
                            if g["tail"] == "quad":
                                rq = g["rq"]
                                ac, ackey = accs[hh]
                                dst = ac[:].rearrange("p (l r) -> p r l", r=16)[:, 4 * rq:4 * rq + 4, :]
                                P.op("act", lambda e, po=po, dst=dst: e.activation(out=dst, in_=po[:].rearrange("p (a b) -> p a b", b=128), func=AF.Copy),
                                     reads=[pokey], writes=[ackey + f"_{rq}"])
                            if g["tail"] == "win":
                                osb, oskey = osr.next()
                                w = g["w"]
                                if br == "A":
                                    ac, ackey = accs[hh]
                                    wl = (w % 4) * 512
                                    P.op("dve", lambda e, osb=osb, po=po, ac=ac, wl=wl: e.tensor_tensor(out=osb[:], in0=po[:], in1=ac[:, wl:wl + 512], op=ALU.add),
                                         reads=[pokey] + [ackey + f"_{q}" for q in range(4)], writes=[oskey])
                                else:
                                    P.op("act", lambda e, osb=osb, po=po: e.activation(out=osb[:], in_=po[:], func=AF.Copy), reads=[pokey], writes=[oskey])
                                prr, prkey = psR.next()
                                for jj in range(4):
                                    P.op("pe", lambda e, prr=prr, osb=osb, jj=jj: e.transpose(out=prr[:, jj, :], in_=osb[:, jj * 128:(jj + 1) * 128], identity=identf[:]),
                                         reads=[oskey, "identf"], writes=[prkey])
                                rl, rlkey = rlr.next()
                                lcol = 64 if hh == 0 else 0
                                P.op("dve", lambda e, rl=rl, prr=prr, lcol=lcol: e.reciprocal(out=rl[:], in_=prr[:, :, lcol]), reads=[prkey], writes=[rlkey])
                                if hh == 0:
                                    state["og"] = ogr.next()
                                og, ogkey = state["og"]
                                P.op("dve", lambda e, og=og, prr=prr, rl=rl, hh=hh: e.tensor_tensor(
                                    out=og[:, :, hh * 64:(hh + 1) * 64], in0=prr[:, :, hh * 64:(hh + 1) * 64], in1=rl[:].unsqueeze(2).to_broadcast([128, 4, 64]), op=ALU.mult),
                                    reads=[prkey, rlkey], writes=[ogkey + f"_{hh}"])
                                if hh == 1 and dynA:
                                    P.op("sp", lambda e, og=og, w=w: e.dma_start(
                                        out=dr["oaq"][w * 512:(w + 1) * 512, ocol:ocol + 128].rearrange("(b p) c -> p b c", p=128), in_=og[:]),
                                        reads=[ogkey + "_0", ogkey + "_1"], dma=ogkey)
                                elif hh == 1 and dyn:
                                    P.op("sp", lambda e, og=og, w=w: e.dma_start(
                                        out=dr["obq"][w * 512:(w + 1) * 512, ocol - 384:ocol - 384 + 128].rearrange("(b p) c -> p b c", p=128), in_=og[:]),
                                        reads=[ogkey + "_0", ogkey + "_1"], dma=ogkey)
                                elif hh == 1:
                                    P.op("sp", lambda e, og=og, w=w: e.dma_start(
                                        out=od[w * 512:(w + 1) * 512, ocol:ocol + 128].rearrange("(b p) c -> p b c", p=128), in_=og[:]),
                                        reads=[ogkey + "_0", ogkey + "_1"], dma=ogkey)

                        LOOK = 1
                        for i in range(ng + LOOK):
                            if pre_at is not None and i == pre_at:
                                load_v16(nxt)
                            if i < ng:
                                emit_front2(groups[i])
                            if i - LOOK >= 0:
                                emit_pv2(groups[i - LOOK])
                                emit_back(groups[i - LOOK], 0)
                                emit_back(groups[i - LOOK], 1)
                P.flush(final=(STOP == "s2"))
            if STOP == "s2":
                return nc

            with ExitStack() as st:
                wo = st.enter_context(nc.sbuf_tensor(U("wo"), [128, 8, D], BF16))
                wst3 = Ring(P, st, "wst3", 2, [128, 8, 256], F32)
                gbr = st.enter_context(nc.sbuf_tensor(U("gbr"), [128, D], F32))
                gpo = st.enter_context(nc.sbuf_tensor(U("gpo"), [128, D], F32))
                invw = st.enter_context(nc.sbuf_tensor(U("invw"), [128, 3], F32))
                o_r = Ring(P, st, "o_r", 4, [128, D], F32)
                z_r = Ring(P, st, "z_r", 4, [128, D // 2], F32)
                x_r = Ring(P, st, "x_r", 5, [128, D], F32)
                g_r = Ring(P, st, "g_r", 6, [128, D], F32)
                pc_r = Ring(P, st, "pc_r", 6, [128, D], F32)
                junk3 = st.enter_context(nc.sbuf_tensor(U("junk3"), [128, D], BF16))
                s3r = Ring(P, st, "s3r", 6, [128, 3], F32)
                s2r = Ring(P, st, "s2r", 6, [128, 2], F32)
                ybr = Ring(P, st, "ybr", 3, [128, D], BF16)
                yTr = Ring(P, st, "yTr", 3, [128, 8, 128], BF16)
                t_r = Ring(P, st, "t_r", 5, [128, D], F32)
                psT3 = Ring(P, st, "psT3", 2, [128, 8, 128], BF16, psum=True)
                psY = Ring(P, st, "psY", 3, [128, D], F32, psum=True)

                for ci in range(4):
                    wbuf, wkey = wst3.next()
                    c0 = ci * 256
                    P.op("sp", lambda e, wbuf=wbuf, c0=c0: e.dma_start(out=wbuf[:], in_=dr["w_out"][l][:, c0:c0 + 256].rearrange("(k p) c -> p k c", p=128)),
                         writes=[wkey], dma=wkey)
                    P.op("pool" if ci % 2 == 0 else "dve", lambda e, wbuf=wbuf, c0=c0: e.tensor_copy(out=wo[:, :, c0:c0 + 256], in_=wbuf[:]), reads=[wkey], writes=[f"wo{ci}"])
                WO = [f"wo{ci}" for ci in range(4)]
                P.op("sp", lambda e: e.dma_start(out=gbr[:], in_=dr["branch_gain"][l].partition_broadcast(128), allow_slow_non_contiguous=True), writes=["gbr"], dma="c5")
                P.op("sp", lambda e: e.dma_start(out=gpo[:], in_=dr["norm_post"][l].partition_broadcast(128), allow_slow_non_contiguous=True), writes=["gpo"], dma="c6")
                P.op("dve", lambda e: e.memset(invw[:, 0:1], 1.0 / 384), writes=["invw"])
                P.op("dve", lambda e: e.memset(invw[:, 1:2], 1.0 / 256), writes=["invw"])
                P.op("dve", lambda e: e.memset(invw[:, 2:3], 1.0 / 384), writes=["invw"])
                BR = ((0, 384), (384, 640), (640, 1024))
                blocks3 = []
                qblocks = []
                for (pn, S, src) in parts:
                    xsrc = dr["x_" + pn] if l == 0 else dr["y1_" + pn]
                    if last and pn == QPN:
                        P.op("sp", lambda e: e.dma_start(out=dr["oq"][:, 0:384], in_=dr["oaq"]), writes=["oq"], dma="cq0")
                        P.op("sp", lambda e, pn=pn: e.dma_start(out=dr["oq"][:, 640:1024], in_=dyn_ap(e, rrow, 640, dr["o_" + pn], [[D, 2048], [1, 384]])), reads=["oq"], writes=["oq"], dma="cq4")
                        P.op("sp", lambda e, pn=pn: e.dma_start(out=dr["szq"], in_=dyn_ap(e, rrow2, 0, dr["sz_" + pn], [[D // 2, 2048], [1, D // 2]])), writes=["szq"], dma="cq1")
                        P.op("sp", lambda e, xsrc=xsrc: e.dma_start(out=dr["xq"], in_=dyn_ap(e, rrow, 0, xsrc, [[D, 2048], [1, D]])), writes=["xq"], dma="cq2")
                        P.op("sp", lambda e: e.dma_start(out=dr["oq"][:, 384:640], in_=dr["obq"]), reads=["oq"], writes=["oq"], dma="cq3")
                        qblocks = [dict(pn=pn, t0=blk * 128, osrc=dr["oq"], zsrc=dr["szq"], xsrc=dr["xq"], ydst=dr["yq_" + pn], keys=["oq", "szq", "xq"]) for blk in range(16)]
                        continue
                    ydst = dr["y_" + pn] if last else dr["y1_" + pn]
                    for blk in range(S // 128):
                        blocks3.append(dict(pn=pn, t0=blk * 128, osrc=dr["o_" + pn], zsrc=dr["sz_" + pn], xsrc=xsrc, ydst=ydst, keys=[]))
                blocks3 = blocks3 + qblocks

                def p_load(c):
                    pn, t0 = c["pn"], c["t0"]
                    ob, okey = o_r.next()
                    zb, zkey = z_r.next()
                    c["o"], c["z"] = (ob, okey), (zb, zkey)
                    osrc, zsrc = c["osrc"], c["zsrc"]
                    P.op("sp", lambda e: e.dma_start(out=ob[:], in_=osrc[t0:t0 + 128, :]), reads=c["keys"][0:1], writes=[okey], dma=okey)
                    P.op("sp", lambda e: e.dma_start(out=zb[:], in_=zsrc[t0:t0 + 128, :]), reads=c["keys"][1:2], writes=[zkey], dma=zkey)

                def p_g(c):
                    ob, okey = c["o"]
                    zb, zkey = c["z"]
                    gb, gkey = g_r.next()
                    c["g"] = (gb, gkey)
                    P.op("pool", lambda e: e.tensor_tensor(out=gb[:], in0=ob[:], in1=zb[:].bitcast(BF16), op=ALU.mult), reads=[okey, zkey], writes=[gkey])

                def p_sq(c):
                    gb, gkey = c["g"]
                    s3, s3key = s3r.next()
                    c["s3"] = (s3, s3key)
                    for bi, (c0, c1) in enumerate(BR):
                        P.op("act", lambda e, bi=bi, c0=c0, c1=c1: e.activation(out=junk3[:, c0:c1], in_=gb[:, c0:c1], func=AF.Square, accum_out=s3[:, bi:bi + 1]),
                             reads=[gkey], writes=[s3key, "junk3"])

                def p_r1(c):
                    s3, s3key = c["s3"]
                    P.op("dve", lambda e: e.tensor_tensor(out=s3[:], in0=s3[:], in1=invw[:], op=ALU.mult), reads=[s3key, "invw"], writes=[s3key])
                    P.op("dve", lambda e: e.tensor_scalar(out=s3[:], in0=s3[:], scalar1=1.0, scalar2=float(EPS), op0=ALU.mult, op1=ALU.add), reads=[s3key], writes=[s3key])

                def p_r2(c):
                    s3, s3key = c["s3"]
                    P.op("pool", lambda e: e.tensor_tensor(out=s3[:], in0=s3[:], in1=nhalf[:, 0:3], op=ALU.pow), reads=[s3key], writes=[s3key])

                def p_y(c):
                    gb, gkey = c["g"]
                    s3, s3key = c["s3"]
                    yb, ykey = ybr.next()
                    c["y"] = (yb, ykey)
                    for bi, (c0, c1) in enumerate(BR):
                        P.op("dve", lambda e, bi=bi, c0=c0, c1=c1: e.scalar_tensor_tensor(
                            out=yb[:, c0:c1], in0=gb[:, c0:c1], scalar=s3[:, bi:bi + 1], in1=gbr[:, c0:c1], op0=ALU.mult, op1=ALU.mult),
                            reads=[gkey, s3key, "gbr"], writes=[ykey])

                def p_T(c):
                    yb, ykey = c["y"]
                    pT, pTkey = psT3.next()
                    c["pT"] = (pT, pTkey)
                    for kc in range(8):
                        P.op("pe", lambda e, kc=kc: e.transpose(out=pT[:, kc, :], in_=yb[:, kc * 128:(kc + 1) * 128], identity=identb[:]),
                             reads=[ykey, "identb"], writes=[pTkey])

                def p_yT(c):
                    pT, pTkey = c["pT"]
                    yT, yTkey = yTr.next()
                    c["yT"] = (yT, yTkey)
                    P.op("act", lambda e: e.activation(out=yT[:], in_=pT[:], func=AF.Copy), reads=[pTkey], writes=[yTkey])

                def p_mm(c):
                    yT, yTkey = c["yT"]
                    py, pykey = psY.next()
                    c["py"] = (py, pykey)
                    for n in range(2):
                        for kc in range(8):
                            P.op("pe", lambda e, n=n, kc=kc: e.matmul(py[:, n * 512:(n + 1) * 512], lhsT=yT[:, kc, :], rhs=wo[:, kc, n * 512:(n + 1) * 512],
                                                                      start=(kc == 0), stop=(kc == 7)),
                                 reads=[yTkey] + WO, writes=[pykey])

                def p_ev(c):
                    py, pykey = c["py"]
                    s2, s2key = s2r.next()
                    c["s2"] = (s2, s2key)
                    pc, pckey = pc_r.next()
                    c["pc"] = (pc, pckey)
                    for n in range(2):
                        P.op("act", lambda e, n=n: e.activation(out=junk3[:, n * 512:(n + 1) * 512], in_=py[:, n * 512:(n + 1) * 512], func=AF.Square, accum_out=s2[:, n:n + 1]),
                             reads=[pykey], writes=[s2key, "junk3"])
                    P.op("act", lambda e: e.activation(out=pc[:], in_=py[:], func=AF.Copy), reads=[pykey], writes=[pckey])

                def p_r3(c):
                    s2, s2key = c["s2"]
                    P.op("dve", lambda e: e.tensor_tensor(out=s2[:, 0:1], in0=s2[:, 0:1], in1=s2[:, 1:2], op=ALU.add), reads=[s2key], writes=[s2key])
                    P.op("dve", lambda e: e.tensor_scalar(out=s2[:, 0:1], in0=s2[:, 0:1], scalar1=1.0 / D, scalar2=float(EPS), op0=ALU.mult, op1=ALU.add), reads=[s2key], writes=[s2key])

                def p_r4(c):
                    s2, s2key = c["s2"]
                    P.op("pool", lambda e: e.tensor_tensor(out=s2[:, 0:1], in0=s2[:, 0:1], in1=nhalf[:, 0:1], op=ALU.pow), reads=[s2key], writes=[s2key])
                    t0, xsrc = c["t0"], c["xsrc"]
                    xb, xkey = x_r.next()
                    c["x"] = (xb, xkey)
                    P.op("sp", lambda e: e.dma_start(out=xb[:], in_=xsrc[t0:t0 + 128, :]), reads=c["keys"][2:3], writes=[xkey], dma=xkey)

                def p_stt(c):
                    pc, pckey = c["pc"]
                    s2, s2key = c["s2"]
                    tb, tkey = t_r.next()
                    c["t"] = (tb, tkey)
                    P.op("dve", lambda e: e.scalar_tensor_tensor(out=tb[:], in0=pc[:], scalar=s2[:, 0:1], in1=gpo[:], op0=ALU.mult, op1=ALU.mult),
                         reads=[pckey, s2key, "gpo"], writes=[tkey])

                def p_add(c):
                    tb, tkey = c["t"]
                    xb, xkey = c["x"]
                    P.op("pool", lambda e: e.tensor_tensor(out=tb[:], in0=tb[:], in1=xb[:], op=ALU.add), reads=[tkey, xkey], writes=[tkey])

                def p_st(c):
                    tb, tkey = c["t"]
                    t0, ydst = c["t0"], c["ydst"]
                    P.op("sp", lambda e: e.dma_start(out=ydst[t0:t0 + 128, :], in_=tb[:]), reads=[tkey], dma=tkey)

                phases = [(p_load, 0), (p_g, 2), (p_sq, 3), (p_r1, 4), (p_r2, 5), (p_y, 6), (p_T, 7), (p_yT, 8), (p_mm, 9), (p_ev, 10),
                          (p_r3, 11), (p_r4, 12), (p_stt, 14), (p_add, 15), (p_st, 17)]
                nb3 = len(blocks3)
                for i in range(nb3 + 18):
                    for fn, dly in phases:
                        if 0 <= i - dly < nb3:
                            fn(blocks3[i - dly])
                P.flush(final=last)
        print(f"[build] instructions: {P.n_ins}", flush=True)
    return nc


_PARTS = [("p", 8192, "xp"), ("s0", 2048, "xs0"), ("s1", 2048, "xs1")]


def kernel(x_prompt, x_sample, norm_pre, w_in, q_norm, k_norm, rel_bias, branch_gain, w_out, norm_post):
    f = lambda a: np.ascontiguousarray(np.asarray(a, dtype=np.float32))
    x_prompt, x_sample = f(x_prompt), f(x_sample)
    ropea, ropeb, ea, ident = _const_tables()
    efraw = _c_bias_tables(f(rel_bias))
    nc = build(_PARTS, 2)
    shared = dict(w_in=f(w_in), w_out=f(w_out), norm_pre=f(norm_pre), norm_post=f(norm_post), branch_gain=f(branch_gain),
                  q_norm=f(q_norm), k_norm=f(k_norm), ropea=ropea, ropeb=ropeb, ea=ea, ident=ident, efraw=efraw)
    in_maps = []
    for c in range(8):
        m = dict(shared)
        q0 = (c % 4) * 2048
        m["qoff"] = np.array([[q0, q0 * D, q0 * (D // 2), q0 * 1536]], dtype=np.int32)
        m["x_p"] = x_prompt[c // 4]
        m["x_s0"] = x_sample[2 * c]
        m["x_s1"] = x_sample[2 * c + 1]
        in_maps.append(m)
    res = run_bass_kernel_spmd(nc, in_maps, core_ids=list(range(8)))
    r = res.results
    y_prompt = np.stack([np.concatenate([np.asarray(r[4 * b + q]["yq_p"], dtype=np.float32) for q in range(4)], axis=0) for b in range(2)], axis=0)
    ys = []
    for c in range(8):
        ys.append(np.asarray(r[c]["y_s0"], dtype=np.float32))
        ys.append(np.asarray(r[c]["y_s1"], dtype=np.float32))
    y_sample = np.stack(ys, axis=0)
    return (y_prompt, y_sample)
```

```python
import numpy as np
from contextlib import ExitStack
import concourse.bass as bass
import concourse.mybir as mybir
from concourse.bass_utils import run_bass_kernel_spmd

F32 = mybir.dt.float32
BF16 = mybir.dt.bfloat16
AF = mybir.ActivationFunctionType
ALU = mybir.AluOpType
AX = mybir.AxisListType

D = 1024
INW = 3840
EPS = 1e-6
NEG = -30000.0
STOP = None
QUARTER = True
LIMIT = None
AQ, AK, AV, AZ, BQ, BK, BV, BZ, CQ, CK, CV, CZ = 0, 384, 768, 1152, 1536, 1792, 1920, 2048, 2304, 2688, 3072, 3456

COMPUTE = ("pe", "act", "dve", "pool")
ISSUERS = ("pe", "act", "dve", "pool", "sp")
ENGOBJ = {"pe": "tensor", "act": "scalar", "dve": "vector", "pool": "gpsimd", "sp": "sync"}


class Op:
    __slots__ = ("eng", "fn", "is_dma", "sem", "ticket", "signal", "waits")

    def __init__(self, eng, fn, is_dma, sem):
        self.eng = eng
        self.fn = fn
        self.is_dma = is_dma
        self.sem = sem
        self.ticket = None
        self.signal = is_dma
        self.waits = []


class Prog:
    def __init__(self, nc, stack, block):
        self.nc = nc
        self.stack = stack
        self.block = block
        self.streams = {e: [] for e in ISSUERS}
        self.esem = {e: self.new_sem("sem_" + e) for e in COMPUTE}
        self.bar = self.new_sem("sem_bar")
        self.bar_count = 0
        self.ecount = {e: 0 for e in COMPUTE}
        self.last_w = {}
        self.readers = {}
        self.dma_sems = {}
        self.dma_count = {}
        self.pending = None
        self.alias = {}
        self.n_ins = 0

    def new_sem(self, name):
        return self.stack.enter_context(self.nc.semaphore(name))

    def _dep(self, op, prod):
        if prod is None or prod is op:
            return
        if (not op.is_dma) and (not prod.is_dma) and prod.eng == op.eng and op.eng == "pe":
            return
        prod.signal = True
        op.waits.append(prod)

    def op(self, eng, fn, reads=(), writes=(), dma=None):
        self.nrec = getattr(self, "nrec", 0) + 1
        if LIMIT is not None and self.nrec > LIMIT:
            return None
        is_dma = dma is not None
        ps_reads = [r for r in reads if r.startswith("ps")]
        if ps_reads:
            reads = [r for r in reads if not r.startswith("ps")]
            writes = list(writes) + ps_reads
        o = Op(eng, fn, is_dma, dma)
        if is_dma:
            if dma not in self.alias:
                self.alias[dma] = f"g{len(self.alias)}"
            dma = self.alias[dma]
            o.sem = dma
            if dma not in self.dma_sems:
                self.dma_sems[dma] = self.new_sem("d_" + dma)
                self.dma_count[dma] = 0
            self.dma_count[dma] += 16
            o.ticket = self.dma_count[dma]
        for r in reads:
            self._dep(o, self.last_w.get(r))
        for w in writes:
            self._dep(o, self.last_w.get(w))
            for rd in self.readers.get(w, ()):
                self._dep(o, rd)
        for r in reads:
            self.readers.setdefault(r, []).append(o)
        for w in writes:
            self.last_w[w] = o
            self.readers[w] = []
        self.streams[eng].append(o)
        return o

    def flush(self, final=False):
        for e in COMPUTE:
            ops = [o for o in self.streams[e] if not o.is_dma]
            if ops:
                ops[-1].signal = True
            for o in ops:
                if o.signal:
                    self.ecount[e] += 1
                    o.ticket = self.ecount[e]
        pending = self.pending
        dma_final = dict(self.dma_count)
        self.bar_count += 1
        bar_val = self.bar_count
        ecount = dict(self.ecount)

        def make(ename):
            ops = self.streams[ename]

            def body(eng):
                waited = {}
                if pending is not None:
                    for key, val in pending.items():
                        if val > 0:
                            sem = self.bar if key == "bar" else self.esem[key]
                            if key != ename:
                                eng.wait_ge(sem, val)
                for o in ops:
                    need = {}
                    for p in o.waits:
                        key = ("d", p.sem) if p.is_dma else ("e", p.eng)
                        if p.ticket > need.get(key, 0):
                            need[key] = p.ticket
                    for key, val in need.items():
                        if waited.get(key, 0) >= val:
                            continue
                        waited[key] = val
                        sem = self.dma_sems[key[1]] if key[0] == "d" else self.esem[key[1]]
                        eng.wait_ge(sem, val)
                    ins = o.fn(eng)
                    self.n_ins += 1
                    if o.is_dma:
                        ins.then_inc(self.dma_sems[o.sem], 16)
                    elif o.signal:
                        ins.then_inc(self.esem[o.eng], 1)
                if ename == "sp":
                    for s, v in dma_final.items():
                        if v > 0:
                            eng.wait_ge(self.dma_sems[s], v)
                    eng.sem_inc(self.bar, 1)

            return body

        for ename in ISSUERS:
            if ename != "sp" and not self.streams[ename] and pending is None:
                continue
            getattr(self.block, ENGOBJ[ename])(make(ename))
        self.pending = dict(ecount)
        self.pending["bar"] = bar_val
        self.streams = {e: [] for e in ISSUERS}
        self.last_w = {}
        self.readers = {}
        self.alias = {}
        if final:
            pend = self.pending

            def fin(eng):
                eng.wait_ge(self.bar, pend["bar"])

            for ename in ("pe", "act", "dve", "pool"):
                getattr(self.block, ENGOBJ[ename])(fin)


_UID = [0]


def U(name):
    _UID[0] += 1
    return f"{name}_u{_UID[0]}"


class Ring:
    def __init__(self, P, st, name, n, shape, dt, psum=False):
        self.n = n
        self.name = name
        self.i = -1
        alloc = P.nc.psum_tensor if psum else P.nc.sbuf_tensor
        self.bufs = [st.enter_context(alloc(U(f"{name}{k}"), list(shape), dt)) for k in range(n)]

    def next(self):
        self.i += 1
        k = self.i % self.n
        return self.bufs[k], f"{self.name}{k}"

    def cur(self):
        k = self.i % self.n
        return self.bufs[k], f"{self.name}{k}"


def _const_tables():
    SMAX = 8192
    t = np.arange(SMAX, dtype=np.float32)
    fa = (500000.0 ** (-np.arange(0, 16, 2, dtype=np.float32) / 16)).astype(np.float32)
    anga = (t[:, None] * fa[None, :]).astype(np.float32).astype(np.float64)
    fb = (10000.0 ** (-np.arange(0, 32, 2, dtype=np.float32) / 32)).astype(np.float32)
    row = (np.arange(SMAX) // 64).astype(np.float32)
    col = (np.arange(SMAX) % 64).astype(np.float32)
    angr = (row[:, None] * fb[None, :]).astype(np.float32).astype(np.float64)
    angc = (col[:, None] * fb[None, :]).astype(np.float32).astype(np.float64)
    angb = np.concatenate([angr, angc], axis=1)

    def tm(a):
        return np.ascontiguousarray(a.reshape(SMAX // 128, 128, -1).transpose(1, 0, 2)).astype(np.float32)

    ropea = np.stack([tm(np.cos(anga)), tm(np.sin(anga))], axis=1)
    ropeb = np.stack([tm(np.cos(angb)), tm(np.sin(angb))], axis=1)
    kk = np.arange(128)[:, None, None]
    dl = (np.arange(5) - 2)[None, :, None]
    ii = np.arange(128)[None, None, :]
    dd = 128 * dl + kk - ii
    ad = np.abs(dd)
    mult = (ad <= 64).astype(np.float32) + ((dd % 4 == 0) & (ad <= 256))
    du = 128 * (np.arange(3) - 1)[None, :, None] + kk - ii
    m16 = (np.abs(du) <= 64).astype(np.float32)
    kk2 = np.arange(128)[:, None]
    ii2 = np.arange(128)[None, :]
    m16q = np.stack([(kk2 >= ii2), (kk2 <= ii2)], axis=1).astype(np.float32)
    ea = np.ascontiguousarray(np.concatenate([mult, m16, m16q], axis=1).astype(np.float32))
    ident = np.eye(128, dtype=np.float32)
    return ropea, ropeb, ea, ident


def _c_bias_tables(rel_bias):
    L = rel_bias.shape[0]
    krl = (np.arange(128) // 64)[:, None, None]
    kc = (np.arange(128) % 64)[:, None, None]
    dlt = (np.arange(7) - 3)[None, :, None]
    rl = (np.arange(128) // 64)[None, None, :]
    qc = (np.arange(128) % 64)[None, None, :]
    dr = 2 * dlt + krl - rl
    ro = dr + 7
    co = np.clip(kc - qc + 15, 0, 30)
    cs = np.clip(qc - 8, 0, 48)
    valid = (kc >= cs) & (kc < cs + 16) & (ro >= 0) & (ro <= 14)
    ro_c = np.clip(ro, 0, 14)
    ro_b, co_b, valid_b = np.broadcast_arrays(ro_c, co, valid)
    out = np.empty((L, 6, 128, 7, 128), dtype=np.float32)
    for l in range(L):
        for h in range(6):
            g = rel_bias[l, h][ro_b, co_b]
            out[l, h] = np.where(valid_b, g, np.float32(NEG))
    return out


def _c_edge_tables(efraw_l, qd):
    negt = np.full((128, 128), np.float32(NEG), dtype=np.float32)

    def interior_tile(h, dl):
        if dl < -2 or dl > 2:
            return negt
        t = efraw_l[h][:, dl + 3, :].copy()
        if dl == -2:
            t[0:64, 64:128] = NEG
        if dl == 2:
            t[64:128, :] = NEG
            t[0:64, 0:64] = NEG
        return t

    def full_tile(h, dl):
        if dl < -3 or dl > 3:
            return negt
        return efraw_l[h][:, dl + 3, :]

    out = np.empty((6, 128, 4, 6, 128), dtype=np.float32)
    for h in range(6):
        for side in range(2):
            for bi in range(2):
                for sl_ in range(6):
                    if side == 0:
                        b, a_own = bi, sl_ - 2
                        edge, ok = (qd == 0), (0 <= a_own <= 3)
                    else:
                        b, a_own = 14 + bi, 12 + sl_
                        edge, ok = (qd == 3), (12 <= a_own <= 15)
                    if edge:
                        t = full_tile(h, a_own - b) if ok else negt
                    else:
                        t = interior_tile(h, a_own - b)
                    out[h, :, side * 2 + bi, sl_, :] = t
    return out


def build(parts, n_layers, debug=False):
    nc = bass.Bass("TRN2", target_bir_lowering=False)
    dr = {}

    def din(name, shape, dt=F32):
        dr[name] = nc.dram_tensor(name, list(shape), dt, kind="ExternalInput").ap()
        return dr[name]

    def dout(name, shape, dt=F32):
        dr[name] = nc.dram_tensor(name, list(shape), dt, kind="ExternalOutput").ap()
        return dr[name]

    def dscr(name, shape, dt):
        if debug:
            dr[name] = nc.dram_tensor(name, list(shape), dt, kind="ExternalOutput").ap()
        else:
            dr[name] = nc.dram_tensor(name, list(shape), dt).ap()
        return dr[name]

    QPN = "p" if (QUARTER and any(pn == "p" for (pn, _, _) in parts)) else None
    if QPN is not None:
        dscr("oq", [2048, D], F32)
        dscr("szq", [2048, D // 2], F32)
        dscr("xq", [2048, D], F32)
        dscr("obq", [2048, 256], F32)
    for (pn, S, src) in parts:
        din("x_" + pn, [S, D])
        if pn == QPN:
            dout("yq_" + pn, [2048, D])
        else:
            dout("y_" + pn, [S, D])
        dscr("qt_" + pn, [1024, S], BF16)
        if pn == QPN:
            dscr("ktpad", [896, S + 2048], BF16)
            dscr("vpad", [S + 2048, 8 * 192], BF16)
            dr["kt_" + pn] = dr["ktpad"][:, 1024:1024 + S]
            dr["v_" + pn] = dr["vpad"][1024:1024 + S, :]
            dscr("ktc", [768, 4096], BF16)
            dscr("vc", [4096, 1152], BF16)
            dscr("oaq", [2048, 384], F32)
            dscr("ocq", [2048, 384], F32)
        else:
            dscr("kt_" + pn, [896, S], BF16)
            dscr("v_" + pn, [S, 8 * 192], BF16)
        dscr("sz_" + pn, [S, D // 2], F32)
        dscr("o_" + pn, [S, D], F32)
        if n_layers > 1:
            dscr("y1_" + pn, [S, D], F32)
    din("w_in", [n_layers, D, INW])
    din("w_out", [n_layers, D, D])
    din("norm_pre", [n_layers, D])
    din("norm_post", [n_layers, D])
    din("branch_gain", [n_layers, D])
    din("q_norm", [n_layers, 64])
    din("k_norm", [n_layers, 64])
    din("ropea", [128, 2, 64, 8])
    din("ropeb", [128, 2, 64, 32])
    din("ea", [128, 10, 128])
    din("ident", [128, 128])
    din("efraw", [n_layers, 6, 128, 7, 128])
    if QPN is not None:
        dr["qoff"] = nc.dram_tensor("qoff", [1, 4], mybir.dt.int32, kind="ExternalInput").ap()
        din("cedge", [6, 128, 24 * 128])

    with ExitStack() as top:
        block = top.enter_context(nc.Block())
        P = Prog(nc, top, block)
        identf = top.enter_context(nc.sbuf_tensor("identf", [128, 128], F32))
        identb = top.enter_context(nc.sbuf_tensor("identb", [128, 128], BF16))
        epsc = top.enter_context(nc.sbuf_tensor("epsc", [128, 8], F32))
        nhalf = top.enter_context(nc.sbuf_tensor("nhalf", [128, 8], F32))
        P.op("sp", lambda e: e.dma_start(out=identf[:], in_=dr["ident"]), writes=["identf"], dma="c0")
        P.op("dve", lambda e: e.tensor_copy(out=identb[:], in_=identf[:]), reads=["identf"], writes=["identb"])
        P.op("dve", lambda e: e.memset(epsc[:], EPS), writes=["epsc"])
        P.op("dve", lambda e: e.memset(nhalf[:], -0.5), writes=["nhalf"])
        if QPN is not None:
            qs = top.enter_context(nc.sbuf_tensor("qs", [1, 4], mybir.dt.int32))
            rq0 = top.enter_context(nc.sync.register("rq0"))
            rrow = top.enter_context(nc.sync.register("rrow"))
            rrow2 = top.enter_context(nc.sync.register("rrow2"))
            rv = top.enter_context(nc.sync.register("rv"))
            rtmp = [top.enter_context(nc.sync.register(f"rtmp{i}")) for i in range(4)]
            rti = [0]
            P.op("sp", lambda e: e.dma_start(out=qs[:], in_=dr["qoff"]), writes=["qs"], dma="c1")

            def setregs(e):
                e.reg_load(rq0, qs[0:1, 0:1])
                e.reg_load(rrow2, qs[0:1, 2:3])
                e.reg_load(rv, qs[0:1, 3:4])
                return e.reg_load(rrow, qs[0:1, 1:2])
            P.op("sp", setregs, reads=["qs"], writes=["regs"])

            SQ = [S for (pn_, S, _) in parts if pn_ == QPN][0]
            ztstack = ExitStack()
            zt = ztstack.enter_context(nc.sbuf_tensor("zt", [128, 1536], BF16))
            P.op("pool", lambda e: e.memset(zt[:], 0.0), writes=["zt"])
            zi = 0
            for side in (0, 1024 + SQ):
                for k in range(7):
                    P.op("sp", lambda e, k=k, side=side: e.dma_start(out=dr["ktpad"][k * 128:(k + 1) * 128, side:side + 1024], in_=zt[:, 0:1024]), reads=["zt"], dma=f"zp{zi}")
                    zi += 1
                for k in range(8):
                    P.op("sp", lambda e, k=k, side=side: e.dma_start(out=dr["vpad"][side + k * 128:side + (k + 1) * 128, :], in_=zt[:]), reads=["zt"], dma=f"zp{zi}")
                    zi += 1

            def dyn_ap(e, base_reg, const, tensor_ap, pattern):
                t = rtmp[rti[0] % 4]
                rti[0] += 1
                e.reg_add(t, base_reg, int(const))
                return bass.AP(tensor_ap.tensor, t, pattern)
        P.flush(final=(STOP == "pre"))
        if QPN is not None:
            ztstack.close()
        if STOP == "pre":
            return nc

        def rstd_ops(v_ap, key, n, scale):
            P.op("dve", lambda e: e.tensor_scalar(out=v_ap, in0=v_ap, scalar1=float(scale), scalar2=float(EPS),
                                                  op0=ALU.mult, op1=ALU.add), reads=[key], writes=[key])
            P.op("pool", lambda e: e.tensor_tensor(out=v_ap, in0=v_ap, in1=nhalf[:, 0:n], op=ALU.pow),
                 reads=[key], writes=[key])

        for l in range(n_layers):
            last = l == n_layers - 1
            with ExitStack() as st:
                wsb = st.enter_context(nc.sbuf_tensor(U("wsb"), [128, 8, INW], BF16))
                wst = Ring(P, st, "wst", 2, [128, 8, 120], F32)
                gpre = st.enter_context(nc.sbuf_tensor(U("gpre"), [128, 8], F32))
                gqk = st.enter_context(nc.sbuf_tensor(U("gqk"), [128, 6, 64], F32))
                ropa = st.enter_context(nc.sbuf_tensor(U("ropa"), [128, 2, 64, 8], F32))
                ropb = st.enter_context(nc.sbuf_tensor(U("ropb"), [128, 2, 64, 32], F32))
                xin = Ring(P, st, "xin", 5, [128, D], F32)
                junk = st.enter_context(nc.sbuf_tensor(U("junk"), [128, D], BF16))
                ssr = Ring(P, st, "ssr", 5, [128, 1], F32)
                xnr = Ring(P, st, "xnr", 2, [128, D], BF16)
                qfr = Ring(P, st, "qfr", 2, [128, 6, 64], F32)
                bfr = Ring(P, st, "bfr", 2, [128, 512], F32)
                hTr = Ring(P, st, "hTr", 2, [128, 8, 512], BF16)
                qts = Ring(P, st, "qts", 2, [128, 8, 512], BF16)
                kts = Ring(P, st, "kts", 2, [128, 7, 512], BF16)
                vsr = Ring(P, st, "vsr", 4, [128, 8, 192], BF16)
                szr = Ring(P, st, "szr", 4, [128, D], BF16)
                qar = Ring(P, st, "qar", 3, [128, 6, 64], BF16)
                kar = Ring(P, st, "kar", 3, [128, 6, 64], BF16)
                qkbr = Ring(P, st, "qkbr", 3, [128, 6, 64], BF16)
                sqb = st.enter_context(nc.sbuf_tensor(U("sqb"), [128, 6, 64], F32))
                xgb = st.enter_context(nc.sbuf_tensor(U("xgb"), [128, 6, 64], F32))
                ss6 = st.enter_context(nc.sbuf_tensor(U("ss6"), [128, 6], F32))
                tA = [st.enter_context(nc.sbuf_tensor(U(f"tA{i}"), [128, 6, 8], F32)) for i in range(4)]
                tB = [st.enter_context(nc.sbuf_tensor(U(f"tB{i}"), [128, 6, 2, 16], F32)) for i in range(4)]
                psT = Ring(P, st, "psT", 2, [128, 8, 128], BF16, psum=True)
                psT2 = Ring(P, st, "psT2", 2, [128, 8, 128], BF16, psum=True)
                psM = Ring(P, st, "psM", 3, [128, 512], F32, psum=True)
                psF = Ring(P, st, "psF", 1, [128, 512], F32, psum=True)

                P.op("sp", lambda e: e.dma_start(out=gpre[:], in_=dr["norm_pre"][l].rearrange("(k p) -> p k", p=128),
                                                 allow_slow_non_contiguous=True), writes=["gpre"], dma="c0")
                P.op("sp", lambda e: e.dma_start(out=gqk[:, 0:4, :], in_=dr["q_norm"][l].partition_broadcast(128).unsqueeze(1).to_broadcast([128, 4, 64]),
                                                 allow_slow_non_contiguous=True), writes=["gqk_q"], dma="c1")
                P.op("sp", lambda e: e.dma_start(out=gqk[:, 4:6, :], in_=dr["k_norm"][l].partition_broadcast(128).unsqueeze(1).to_broadcast([128, 2, 64]),
                                                 allow_slow_non_contiguous=True), writes=["gqk_k"], dma="c2")
                P.op("sp", lambda e: e.dma_start(out=ropa[:], in_=dr["ropea"]), writes=["ropa"], dma="c3")
                P.op("sp", lambda e: e.dma_start(out=ropb[:], in_=dr["ropeb"]), writes=["ropb"], dma="c4")
                for ci in range(32):
                    wbuf, wkey = wst.next()
                    c0 = ci * 120
                    P.op("sp", lambda e, wbuf=wbuf, c0=c0: e.dma_start(
                        out=wbuf[:], in_=dr["w_in"][l][:, c0:c0 + 120].rearrange("(k p) c -> p k c", p=128)),
                        writes=[wkey], dma=wkey)
                    eng = ("pool", "dve", "act")[ci % 3]
                    if eng == "act":
                        P.op("act", lambda e, wbuf=wbuf, c0=c0: e.activation(out=wsb[:, :, c0:c0 + 120], in_=wbuf[:], func=AF.Copy),
                             reads=[wkey], writes=[f"wsb{ci}"])
                    else:
                        P.op(eng, lambda e, wbuf=wbuf, c0=c0: e.tensor_copy(out=wsb[:, :, c0:c0 + 120], in_=wbuf[:]),
                             reads=[wkey], writes=[f"wsb{ci}"])
                WALL = [f"wsb{ci}" for ci in range(32)]

                def wkeys(c0, c1):
                    return [f"wsb{ci}" for ci in range(c0 // 120, (c1 - 1) // 120 + 1)]

                from collections import deque
                blocks1 = []
                for (pn, S, src) in parts:
                    xsrc = dr["x_" + pn] if l == 0 else dr["y1_" + pn]
                    for blk in range(S // 128):
                        blocks1.append(dict(pn=pn, blk=blk, j=blk % 4, tg=blk // 4, xsrc=xsrc))
                pend = deque()

                def run_pend(keep):
                    while len(pend) > keep:
                        pend.popleft()()

                def s1_load(c):
                    xb, xkey = xin.next()
                    c["x"] = (xb, xkey)
                    t0, xsrc = c["blk"] * 128, c["xsrc"]
                    P.op("sp", lambda e: e.dma_start(out=xb[:], in_=xsrc[t0:t0 + 128, :]), writes=[xkey], dma=xkey)

                grp = {}

                def s1_fa(c):
                    xb, xkey = c["x"]
                    ss, sskey = ssr.next()
                    c["ss"] = (ss, sskey)
                    P.op("act", lambda e: e.activation(out=junk[:], in_=xb[:], func=AF.Square, accum_out=ss[:]), reads=[xkey], writes=[sskey, "junk"])
                    rstd_ops(ss[:], sskey, 1, 1.0 / D)

                def s1_front(c):
                    j = c["j"]
                    if j == 0:
                        grp["hT"] = hTr.next()
                        grp["qt"] = qts.next()
                        grp["kt"] = kts.next()
                    c["hT"], c["qt"], c["kt"] = grp["hT"], grp["qt"], grp["kt"]
                    hT, hkey = c["hT"]
                    xb, xkey = c["x"]
                    ss, sskey = c["ss"]
                    xn, xnkey = xnr.next()
                    P.op("act", lambda e: e.activation(out=xn[:], in_=xb[:], func=AF.Copy, scale=ss[:]), reads=[xkey, sskey], writes=[xnkey])
                    pT, pTkey = psT.next()
                    for kc in range(8):
                        P.op("pe", lambda e, kc=kc: e.transpose(out=pT[:, kc, :], in_=xn[:, kc * 128:(kc + 1) * 128], identity=identb[:]),
                             reads=[xnkey, "identb"], writes=[pTkey])
                    P.op("dve", lambda e: e.tensor_tensor(
                        out=hT[:, :, j * 128:(j + 1) * 128], in0=pT[:], in1=gpre[:].unsqueeze(2).to_broadcast([128, 8, 128]), op=ALU.mult),
                        reads=[pTkey, "gpre"], writes=[hkey + f"_{j}"])

                def s1_main(c):
                    j, blk, pn = c["j"], c["blk"], c["pn"]
                    hT, hkey = c["hT"]
                    qt_s, qkey = c["qt"]
                    kt_s, kkey = c["kt"]
                    hk = hkey + f"_{j}"

                    def tok_mm(c0, c1):
                        run_pend(2)
                        pm, pmkey = psM.next()
                        for kc in range(8):
                            P.op("pe", lambda e, kc=kc: e.matmul(pm[:, 0:c1 - c0], lhsT=hT[:, kc, j * 128:(j + 1) * 128], rhs=wsb[:, kc, c0:c1],
                                                                 start=(kc == 0), stop=(kc == 7)),
                                 reads=[hk] + wkeys(c0, c1), writes=[pmkey])
                        return pm, pmkey

                    for which, c0, ring, dst in (("q", AQ, qar, qt_s), ("k", AK, kar, kt_s)):
                        pm, pmkey = tok_mm(c0, c0 + 384)
                        qf, qfkey = qfr.next()
                        P.op("act", lambda e, qf=qf, pm=pm: e.activation(out=qf[:].rearrange("p h d -> p (h d)"), in_=pm[:, 0:384], func=AF.Copy), reads=[pmkey], writes=[qfkey])
                        ob, okey = ring.next()
                        P.op("pool", lambda e, ob=ob, qf=qf: e.tensor_copy(out=ob[:], in_=qf[:]), reads=[qfkey], writes=[okey])
                        cosb = ropa[:, 0, blk, :].unsqueeze(1).to_broadcast([128, 6, 8])
                        sinb = ropa[:, 1, blk, :].unsqueeze(1).to_broadcast([128, 6, 8])
                        x1 = qf[:, :, 0:8]
                        x2 = qf[:, :, 8:16]
                        P.op("dve", lambda e, x1=x1, cosb=cosb: e.tensor_tensor(out=tA[0][:], in0=x1, in1=cosb, op=ALU.mult), reads=[qfkey, "ropa"], writes=["tA0"])
                        P.op("dve", lambda e, x2=x2, sinb=sinb: e.tensor_tensor(out=tA[1][:], in0=x2, in1=sinb, op=ALU.mult), reads=[qfkey, "ropa"], writes=["tA1"])
                        P.op("dve", lambda e, x2=x2, cosb=cosb: e.tensor_tensor(out=tA[2][:], in0=x2, in1=cosb, op=ALU.mult), reads=[qfkey, "ropa"], writes=["tA2"])
                        P.op("dve", lambda e, x1=x1, sinb=sinb: e.tensor_tensor(out=tA[3][:], in0=x1, in1=sinb, op=ALU.mult), reads=[qfkey, "ropa"], writes=["tA3"])
                        P.op("dve", lambda e, ob=ob: e.tensor_tensor(out=ob[:, :, 0:8], in0=tA[0][:], in1=tA[1][:], op=ALU.subtract), reads=["tA0", "tA1", okey], writes=[okey])
                        P.op("dve", lambda e, ob=ob: e.tensor_tensor(out=ob[:, :, 8:16], in0=tA[2][:], in1=tA[3][:], op=ALU.add), reads=["tA2", "tA3", okey], writes=[okey])

                        def fin_a(ob=ob, okey=okey, dst=dst, which=which):
                            pT2, pT2key = psT2.next()
                            obf = ob[:].rearrange("p h d -> p (h d)")
                            for tt in range(3):
                                P.op("pe", lambda e, tt=tt: e.transpose(out=pT2[:, tt, :], in_=obf[:, tt * 128:(tt + 1) * 128], identity=identb[:]),
                                     reads=[okey, "identb"], writes=[pT2key])
                            dkey = (qkey if which == "q" else kkey) + f"_{j}a"
                            P.op("act", lambda e: e.activation(out=dst[:, 0:3, j * 128:(j + 1) * 128], in_=pT2[:, 0:3, :], func=AF.Copy),
                                 reads=[pT2key], writes=[dkey])
                        pend.append(fin_a)
                    vs, vkey = vsr.next()
                    c["vs"] = (vs, vkey)
                    pm, pmkey = tok_mm(AV, AV + 384)
                    P.op("dve", lambda e, pm=pm: e.tensor_copy(out=vs[:, 0:3, :].rearrange("p a (s d) -> p a s d", d=64)[:, :, 0::2, :], in_=pm[:, 0:384].rearrange("p (a s d) -> p a s d", s=2, d=64)),
                         reads=[pmkey], writes=[vkey + "a"])
                    P.op("pool", lambda e: e.memset(vs[:, :, 64:128], 1.0), writes=[vkey + "one"])
                    sz, szkey = szr.next()
                    c["sz"] = (sz, szkey)
                    pm, pmkey = tok_mm(AZ, AZ + 384)
                    P.op("act", lambda e, pm=pm: e.activation(out=sz[:, 0:384], in_=pm[:, 0:384], func=AF.Silu), reads=[pmkey], writes=[szkey + "a"])
                    pm, pmkey = tok_mm(BQ, BQ + 512)
                    bf, bfkey = bfr.next()
                    P.op("act", lambda e, pm=pm, bf=bf: e.activation(out=bf[:], in_=pm[:], func=AF.Copy), reads=[pmkey], writes=[bfkey])
                    bf6 = bf[:, 0:384].rearrange("p (h d) -> p h d", d=64)
                    P.op("pool", lambda e, bf6=bf6: e.tensor_tensor(out=sqb[:], in0=bf6, in1=bf6, op=ALU.mult), reads=[bfkey], writes=["sqb"])
                    P.op("dve", lambda e: e.tensor_reduce(out=ss6[:], in_=sqb[:], axis=AX.X, op=ALU.add), reads=["sqb"], writes=["ss6"])
                    rstd_ops(ss6[:], "ss6", 6, 1.0 / 64)
                    P.op("dve", lambda e, bf6=bf6: e.tensor_tensor(out=xgb[:], in0=bf6, in1=ss6[:].unsqueeze(2).to_broadcast([128, 6, 64]), op=ALU.mult),
                         reads=[bfkey, "ss6"], writes=["xgb"])
                    P.op("pool", lambda e: e.tensor_tensor(out=xgb[:], in0=xgb[:], in1=gqk[:], op=ALU.mult), reads=["xgb", "gqk_q", "gqk_k"], writes=["xgb"])
                    qkb, qkbkey = qkbr.next()
                    xv = xgb[:].rearrange("p h (a b c) -> p h a b c", a=2, b=2)
                    ov = qkb[:].rearrange("p h (a b c) -> p h a b c", a=2, b=2)
                    cb = ropb[:, 0, blk, :].rearrange("p (a c) -> p a c", a=2).unsqueeze(1).to_broadcast([128, 6, 2, 16])
                    sb_ = ropb[:, 1, blk, :].rearrange("p (a c) -> p a c", a=2).unsqueeze(1).to_broadcast([128, 6, 2, 16])
                    x1 = xv[:, :, :, 0, :]
                    x2 = xv[:, :, :, 1, :]
                    P.op("pool", lambda e, x1=x1, cb=cb: e.tensor_tensor(out=tB[0][:], in0=x1, in1=cb, op=ALU.mult), reads=["xgb", "ropb"], writes=["tB0"])
                    P.op("pool", lambda e, x2=x2, sb_=sb_: e.tensor_tensor(out=tB[1][:], in0=x2, in1=sb_, op=ALU.mult), reads=["xgb", "ropb"], writes=["tB1"])
                    P.op("dve", lambda e, x2=x2, cb=cb: e.tensor_tensor(out=tB[2][:], in0=x2, in1=cb, op=ALU.mult), reads=["xgb", "ropb"], writes=["tB2"])
                    P.op("dve", lambda e, x1=x1, sb_=sb_: e.tensor_tensor(out=tB[3][:], in0=x1, in1=sb_, op=ALU.mult), reads=["xgb", "ropb"], writes=["tB3"])
                    P.op("pool", lambda e, ov=ov: e.tensor_tensor(out=ov[:, :, :, 0, :], in0=tB[0][:], in1=tB[1][:], op=ALU.subtract), reads=["tB0", "tB1"], writes=[qkbkey + "x"])
                    P.op("dve", lambda e, ov=ov: e.tensor_tensor(out=ov[:, :, :, 1, :], in0=tB[2][:], in1=tB[3][:], op=ALU.add), reads=["tB2", "tB3"], writes=[qkbkey + "y"])
                    bfv = bf[:, 384:512].rearrange("p (h d) -> p h d", d=64)
                    P.op("pool", lambda e, bfv=bfv: e.tensor_copy(out=vs[:, 3:5, 0:64], in_=bfv), reads=[bfkey], writes=[vkey + "b"])
                    P.op("pool", lambda e, bfv=bfv: e.tensor_copy(out=vs[:, 3:5, 128:192], in_=bfv), reads=[bfkey], writes=[vkey + "b2"])

                    def fin_b(qkb=qkb, qkbkey=qkbkey):
                        pT2, pT2key = psT2.next()
                        qkbf = qkb[:].rearrange("p h d -> p (h d)")
                        for tt in range(3):
                            P.op("pe", lambda e, tt=tt: e.transpose(out=pT2[:, tt, :], in_=qkbf[:, tt * 128:(tt + 1) * 128], identity=identb[:]),
                                 reads=[qkbkey + "x", qkbkey + "y", "identb"], writes=[pT2key])
                        P.op("act", lambda e: e.activation(out=qt_s[:, 3:5, j * 128:(j + 1) * 128], in_=pT2[:, 0:2, :], func=AF.Copy),
                             reads=[pT2key], writes=[qkey + f"_{j}b"])
                        P.op("act", lambda e: e.activation(out=kt_s[:, 3, j * 128:(j + 1) * 128], in_=pT2[:, 2, :], func=AF.Copy),
                             reads=[pT2key], writes=[kkey + f"_{j}b"])
                    pend.append(fin_b)
                    pm, pmkey = tok_mm(BZ, BZ + 256)
                    P.op("act", lambda e, pm=pm: e.activation(out=sz[:, 384:640], in_=pm[:, 0:256], func=AF.Silu), reads=[pmkey], writes=[szkey + "b"])
                    pm, pmkey = tok_mm(CV, CV + 384)
                    P.op("dve", lambda e, pm=pm: e.tensor_copy(out=vs[:, 5:8, :].rearrange("p a (s d) -> p a s d", d=64)[:, :, 0::2, :], in_=pm[:, 0:384].rearrange("p (a s d) -> p a s d", s=2, d=64)),
                         reads=[pmkey], writes=[vkey + "c"])
                    pm, pmkey = tok_mm(CZ, CZ + 384)
                    P.op("act", lambda e, pm=pm: e.activation(out=sz[:, 640:1024], in_=pm[:, 0:384], func=AF.Silu), reads=[pmkey], writes=[szkey + "c"])
                    if j == 3:
                        hks = [hkey + f"_{jj}" for jj in range(4)]
                        for ti in range(6):
                            run_pend(2)
                            c0 = (CQ if ti < 3 else CK) + (ti % 3) * 128
                            pf, pfkey = psF.next()
                            for kc in range(8):
                                P.op("pe", lambda e, pf=pf, kc=kc, c0=c0: e.matmul(pf[:], lhsT=wsb[:, kc, c0:c0 + 128], rhs=hT[:, kc, :], start=(kc == 0), stop=(kc == 7)),
                                     reads=hks + wkeys(c0, c0 + 128), writes=[pfkey])
                            if ti < 3:
                                P.op("dve", lambda e, pf=pf, ti=ti: e.tensor_copy(out=qt_s[:, 5 + ti, :], in_=pf[:]), reads=[pfkey], writes=[qkey + f"_c{ti}"])
                            else:
                                P.op("act", lambda e, pf=pf, ti=ti: e.activation(out=kt_s[:, 4 + ti - 3, :], in_=pf[:], func=AF.Copy), reads=[pfkey], writes=[kkey + f"_c{ti}"])

                def s1_store(c):
                    j, blk, pn = c["j"], c["blk"], c["pn"]
                    t0 = blk * 128
                    vs, vkey = c["vs"]
                    sz, szkey = c["sz"]
                    qt_s, qkey = c["qt"]
                    kt_s, kkey = c["kt"]
                    P.op("sp", lambda e: e.dma_start(out=dr["v_" + pn][t0:t0 + 128, :], in_=vs[:].rearrange("p h d -> p (h d)")),
                         reads=[vkey + "a", vkey + "b", vkey + "b2", vkey + "c", vkey + "one"], dma=vkey)
                    P.op("sp", lambda e: e.dma_start(out=dr["sz_" + pn][t0:t0 + 128, :], in_=sz[:].bitcast(F32)),
                         reads=[szkey + "a", szkey + "b", szkey + "c"], dma=szkey)
                    if j == 3:
                        tt0 = c["tg"] * 512
                        qr = [qkey + f"_{jj}a" for jj in range(4)] + [qkey + f"_{jj}b" for jj in range(4)] + [qkey + f"_c{ti}" for ti in range(3)]
                        kr = [kkey + f"_{jj}a" for jj in range(4)] + [kkey + f"_{jj}b" for jj in range(4)] + [kkey + f"_c{ti}" for ti in range(3, 6)]
                        P.op("sp", lambda e: e.dma_start(out=dr["qt_" + pn][:, tt0:tt0 + 512].rearrange("(k p) t -> p k t", p=128), in_=qt_s[:]), reads=qr, dma=qkey)
                        P.op("sp", lambda e: e.dma_start(out=dr["kt_" + pn][:, tt0:tt0 + 512].rearrange("(k p) t -> p k t", p=128), in_=kt_s[:]), reads=kr, dma=kkey)

                nb1 = len(blocks1)
                for i in range(-3, nb1 + 2):
                    if 0 <= i + 3 < nb1:
                        s1_load(blocks1[i + 3])
                    if 0 <= i + 2 < nb1:
                        s1_fa(blocks1[i + 2])
                    if 0 <= i + 1 < nb1:
                        s1_front(blocks1[i + 1])
                    if 0 <= i < nb1:
                        s1_main(blocks1[i])
                        if blocks1[i]["j"] == 3:
                            run_pend(0)
                    if 0 <= i - 2 < nb1:
                        s1_store(blocks1[i - 2])
                run_pend(0)
                P.flush(final=(STOP == "s1"))
            if STOP == "s1":
                return nc

            with ExitStack() as st:
                SMAXP = max(S for (_, S, _) in parts)
                qpr = Ring(P, st, "qpr", 2, [128, SMAXP], BF16)
                kpr = Ring(P, st, "kpr", 2, [128, SMAXP], BF16)
                vbr = Ring(P, st, "vbr", 2, [128, SMAXP // 128, 192], BF16)
                eab = st.enter_context(nc.sbuf_tensor(U("eab"), [128, 5, 128], BF16))
                m16 = st.enter_context(nc.sbuf_tensor(U("m16"), [128, 5, 128], BF16))
                accs = [(st.enter_context(nc.sbuf_tensor(U(f"acc{hh}"), [128, 2048], F32)), f"acc{hh}") for hh in range(2)]
                v16r = Ring(P, st, "v16r", 1, [128, SMAXP // 128, 192], BF16)
                ecr = Ring(P, st, "ecr", 1, [128, 2, 4, 6, 128], BF16)
                est = Ring(P, st, "est", 2, [128, 1024], F32)
                efb = st.enter_context(nc.sbuf_tensor(U("efb"), [128, 6, 7, 128], BF16))
                eib = st.enter_context(nc.sbuf_tensor(U("eib"), [128, 6, 5, 128], BF16))
                ptr = Ring(P, st, "ptr", 4, [128, 512], BF16)
                osr = Ring(P, st, "osr", 2, [128, 512], F32)
                ogr = Ring(P, st, "ogr", 2, [128, 4, 128], F32)
                rlr = Ring(P, st, "rlr", 2, [128, 4], F32)
                psS = Ring(P, st, "psS", 4, [128, 512], F32, psum=True)
                psO = Ring(P, st, "psO", 2, [128, 512], F32, psum=True)
                psR = Ring(P, st, "psR", 2, [128, 4, 128], F32, psum=True)

                eb, ekey = est.next()
                P.op("sp", lambda e, eb=eb: e.dma_start(out=eb[:, 0:640], in_=dr["ea"][:, 0:5, :].rearrange("p a b -> p (a b)")), writes=[ekey], dma=ekey)
                P.op("dve", lambda e, eb=eb: e.tensor_copy(out=eab[:].rearrange("p a b -> p (a b)"), in_=eb[:, 0:5 * 128]), reads=[ekey], writes=["eab"])
                eb, ekey = est.next()
                P.op("sp", lambda e, eb=eb: e.dma_start(out=eb[:, 0:640], in_=dr["ea"][:, 5:10, :].rearrange("p a b -> p (a b)")), writes=[ekey], dma=ekey)
                P.op("dve", lambda e, eb=eb: e.tensor_copy(out=m16[:].rearrange("p a b -> p (a b)"), in_=eb[:, 0:5 * 128]), reads=[ekey], writes=["m16"])
                for h in range(6):
                    eb, ekey = est.next()
                    P.op("sp", lambda e, eb=eb, h=h: e.dma_start(out=eb[:, 0:7 * 128], in_=dr["efraw"][l, h].rearrange("p a b -> p (a b)")), writes=[ekey], dma=ekey)
                    P.op("act", lambda e, eb=eb, h=h: e.activation(out=efb[:, h].rearrange("p a b -> p (a b)"), in_=eb[:, 0:7 * 128], func=AF.Exp),
                         reads=[ekey], writes=[f"efb{h}"])
                    P.op("dve", lambda e, h=h: e.tensor_copy(out=eib[:, h], in_=efb[:, h, 1:6, :]), reads=[f"efb{h}"], writes=[f"eib{h}"])
                    P.op("dve", lambda e, h=h: e.memset(eib[0:64, h, 0, 64:128], 0.0), reads=[f"eib{h}"], writes=[f"eib{h}"])
                    P.op("dve", lambda e, h=h: e.memset(eib[64:128, h, 4, :], 0.0), reads=[f"eib{h}"], writes=[f"eib{h}"])
                    P.op("dve", lambda e, h=h: e.memset(eib[0:64, h, 4, 0:64], 0.0), reads=[f"eib{h}"], writes=[f"eib{h}"])

                jobs = []
                for (pn, S, src) in parts:
                    for gp in range(8):
                        jobs.append(dict(pn=pn, S=S, gp=gp, dyn=(last and pn == QPN and 3 <= gp < 5), dynA=(last and pn == QPN and gp < 3), dynC=(last and pn == QPN and gp >= 5)))
                if last and QPN is not None:
                    P.op("sp", lambda e: e.dma_start(out=dr["ktc"][0:384, :], in_=dyn_ap(e, rq0, 0, dr["ktpad"], [[SQ + 2048, 384], [1, 4096]])), writes=["ktc"], dma="cq5")
                    P.op("sp", lambda e: e.dma_start(out=dr["vc"][:, 0:576], in_=dyn_ap(e, rv, 0, dr["vpad"], [[1536, 4096], [1, 576]])), writes=["vc"], dma="cq6")
                    P.op("sp", lambda e: e.dma_start(out=dr["ktc"][384:768, :], in_=dyn_ap(e, rq0, 512 * (SQ + 2048), dr["ktpad"], [[SQ + 2048, 384], [1, 4096]])), writes=["ktc2"], dma="cq7")
                    P.op("sp", lambda e: e.dma_start(out=dr["vc"][:, 576:1152], in_=dyn_ap(e, rv, 960, dr["vpad"], [[1536, 4096], [1, 576]])), writes=["vc2"], dma="cq8")

                def load_job(jb):
                    pn, S, gp = jb["pn"], jb["S"], jb["gp"]
                    NB = S // 128
                    qtd, ktd, vd = dr["qt_" + pn], dr["kt_" + pn], dr["v_" + pn]
                    if gp < 3:
                        pr = gp
                        qrow, krows = pr * 128, [pr * 128, pr * 128 + 64]
                    elif gp < 5:
                        pr = gp - 3
                        qrow, krows = 384 + pr * 128, [384 + pr * 64, 384 + pr * 64]
                    else:
                        pr = gp - 5
                        qrow, krows = 640 + pr * 128, [512 + pr * 128, 512 + pr * 128 + 64]
                    vb, vbkey = vbr.next()
                    dynA = jb.get("dynA")
                    dynC = jb.get("dynC")
                    if dynA or dynC:
                        nch = 1
                        vcol = gp * 192 if dynA else 576 + pr * 192
                        P.op("sp", lambda e: e.dma_start(out=vb[:, 0:32, :], in_=dr["vc"][:, vcol:vcol + 192].rearrange("(b p) c -> p b c", p=128)),
                             reads=["vc", "vc2"], writes=[vbkey + "_0"], dma=f"{vbkey}_0")
                    else:
                        nch = 4 if S > 2048 else 1
                        for ch in range(nch):
                            b0 = ch * (NB // nch)
                            b1 = (ch + 1) * (NB // nch)
                            P.op("sp", lambda e, b0=b0, b1=b1: e.dma_start(
                                out=vb[:, b0:b1, :], in_=vd[b0 * 128:b1 * 128, gp * 192:(gp + 1) * 192].rearrange("(b p) c -> p b c", p=128)),
                                writes=[vbkey + f"_{ch}"], dma=f"{vbkey}_{ch}")
                    jb["vb"] = (vb, vbkey, [vbkey + f"_{ch}" for ch in range(nch)])

                    qp, qpkey = qpr.next()
                    kp, kpkey = kpr.next()
                    if jb.get("dyn") or dynA or dynC:
                        P.op("sp", lambda e: e.dma_start(out=qp[:, 0:2048], in_=dyn_ap(e, rq0, qrow * S, qtd, [[S, 128], [1, 2048]])), writes=[qpkey], dma=qpkey)
                    else:
                        P.op("sp", lambda e: e.dma_start(out=qp[:, 0:S], in_=qtd[qrow:qrow + 128, :]), writes=[qpkey], dma=qpkey)
                    for hh in range(2):
                        if dynA or dynC:
                            kr0 = krows[hh] if dynA else krows[hh] - 512 + 384
                            P.op("sp", lambda e, hh=hh, kr0=kr0: e.dma_start(out=kp[hh * 64:(hh + 1) * 64, 0:4096], in_=dr["ktc"][kr0:kr0 + 64, :]),
                                 reads=["ktc", "ktc2"], writes=[kpkey + f"_{hh}"], dma=f"{kpkey}_{hh}")
                        else:
                            P.op("sp", lambda e, hh=hh: e.dma_start(out=kp[hh * 64:(hh + 1) * 64, 0:S], in_=ktd[krows[hh]:krows[hh] + 64, :]),
                                 writes=[kpkey + f"_{hh}"], dma=f"{kpkey}_{hh}")
                    jb["qp"] = (qp, qpkey)
                    jb["kp"] = (kp, kpkey)

                def load_ec(jb):
                    pr = jb["gp"] - 5
                    ecp, eckey = ecr.next()
                    for hh in range(2):
                        for ch in range(3):
                            eb, ekey = est.next()
                            P.op("sp", lambda e, eb=eb, hh=hh, ch=ch: e.dma_start(out=eb[:, 0:1024], in_=dr["cedge"][2 * pr + hh][:, ch * 1024:(ch + 1) * 1024]), writes=[ekey], dma=ekey)
                            P.op("act", lambda e, eb=eb, hh=hh, ch=ch: e.activation(out=ecp[:, hh].rearrange("p a b c -> p (a b c)")[:, ch * 1024:(ch + 1) * 1024], in_=eb[:, 0:1024], func=AF.Exp),
                                 reads=[ekey], writes=[eckey + f"_{hh}_{ch}"])
                    jb["ec"] = (ecp, [eckey + f"_{hh}_{ch}" for hh in range(2) for ch in range(3)])

                load_job(jobs[0])
                for ji, jb in enumerate(jobs):
                    if ji + 1 < len(jobs):
                        load_job(jobs[ji + 1])
                    pn, S, gp = jb["pn"], jb["S"], jb["gp"]
                    NB = S // 128
                    NW = S // 512
                    od = dr["o_" + pn]
                    if jb.get("dynC"):
                        load_ec(jb)
                    if True:
                        if gp < 3:
                            br, pr = "A", gp
                            ocol = pr * 128
                        elif gp < 5:
                            br, pr = "B", gp - 3
                            ocol = 384 + pr * 128
                        else:
                            br, pr = "C", gp - 5
                            ocol = 640 + pr * 128
                        vb, vbkey, vkeys = jb["vb"]
                        qp, qpkey = jb["qp"]
                        kp, kpkey = jb["kp"]
                        groups = []

                        def dense_groups(w):
                            for qb in range(4):
                                b = w * 4 + qb
                                if br == "A":
                                    alist = list(range(max(0, b - 2), min(NB, b + 3)))
                                    kind, off = "ea", 2
                                elif b <= 1:
                                    alist, kind, off = list(range(0, 4)), "ef", 3
                                elif b >= NB - 2:
                                    alist, kind, off = list(range(NB - 4, NB)), "ef", 3
                                else:
                                    alist, kind, off = list(range(b - 2, b + 3)), "ei", 2
                                nu = len(alist)
                                gi = 0
                                while gi < nu:
                                    gn = min(4, nu - gi)
                                    if nu - gi == 5:
                                        gn = 3
                                    us = alist[gi:gi + gn]
                                    groups.append(dict(units=[dict(k=(a * 128, 1), q=(b * 128, 1, 128), pc0=ui * 128, v=("nat", a), oc0=qb * 128,
                                                                   st=(gi + ui == 0), sp=(gi + ui == nu - 1)) for ui, a in enumerate(us)],
                                                       e=(kind, us[0] - b + off, gn, False), tail=("win" if (qb == 3 and gi + gn == nu) else None), w=w))
                                    gi += gn

                        dyn = jb.get("dyn")
                        if br == "B":
                            for w in range(4 if dyn else NW):
                                for kb in range(NB):
                                    groups.append(dict(units=[dict(k=(kb * 128, 1), q=(w * 512, 1, 512), pc0=0, v=("nat", kb), oc0=0, st=(kb == 0), sp=(kb == NB - 1))],
                                                       e=None, tail=("win" if kb == NB - 1 else None), w=w))
                        elif jb.get("dynC"):
                            for w in range(4):
                                for qb in range(4):
                                    b = w * 4 + qb
                                    if b <= 1 or b >= 14:
                                        sb = b if b <= 1 else 2 + (b - 14)
                                        a0 = 6 if b <= 1 else 20
                                        for gi in (0, 3):
                                            groups.append(dict(units=[dict(k=((a0 + gi + ui) * 128, 1), q=(b * 128, 1, 128), pc0=ui * 128, v=("nat", a0 + gi + ui), oc0=qb * 128,
                                                                           st=(gi + ui == 0), sp=(gi + ui == 5)) for ui in range(3)],
                                                               e=("ec", (sb, gi), 3, False), tail=("win" if (qb == 3 and gi == 3) else None), w=w))
                                    else:
                                        alist = [b + 8 + d_ for d_ in range(-2, 3)]
                                        for (gi, gn) in ((0, 3), (3, 2)):
                                            us = alist[gi:gi + gn]
                                            groups.append(dict(units=[dict(k=(a * 128, 1), q=(b * 128, 1, 128), pc0=ui * 128, v=("nat", a), oc0=qb * 128,
                                                                           st=(gi + ui == 0), sp=(gi + ui == 4)) for ui, a in enumerate(us)],
                                                               e=("ei", gi, gn, False), tail=("win" if (qb == 3 and gi == 3) else None), w=w))
                        elif br == "C":
                            for w in range(NW):
                                dense_groups(w)
                        elif jb.get("dynA"):
                            for rq in range(4):
                                for ri in range(4):
                                    r = 4 * rq + ri
                                    units = [dict(k=(jj * 2048 + r, 16), q=(r, 16, 128), pc0=jj * 128, v=("v16", r * 2 + jj), oc0=ri * 128, st=(jj == 0), sp=(jj == 1)) for jj in range(2)]
                                    groups.append(dict(units=units, e=("m16", 3, 2, False), tail=("quad" if ri == 3 else None), sw=0, rq=rq))
                            for w in range(4):
                                for qb in range(4):
                                    b = w * 4 + qb
                                    alist = [b + 8 + d_ for d_ in range(-2, 3)]
                                    for (gi, gn) in ((0, 3), (3, 2)):
                                        us = alist[gi:gi + gn]
                                        groups.append(dict(units=[dict(k=(a * 128, 1), q=(b * 128, 1, 128), pc0=ui * 128, v=("nat", a), oc0=qb * 128,
                                                                       st=(gi + ui == 0), sp=(gi + ui == 4)) for ui, a in enumerate(us)],
                                                           e=("ea", gi, gn, False), tail=("win" if (qb == 3 and gi == 3) else None), w=w))
                        else:
                            nj = NB // 16
                            for sw in range(nj):
                                for rq in range(4):
                                    ulist = [u for u in (-1, 0, 1) if 0 <= sw + u < nj]
                                    if len(ulist) == 1:
                                        units = []
                                        for ri in range(4):
                                            r = 4 * rq + ri
                                            units.append(dict(k=(sw * 2048 + r, 16), q=(sw * 2048 + r, 16, 128), pc0=ri * 128, v=("v16", r * nj + sw), oc0=ri * 128, st=True, sp=True))
                                        groups.append(dict(units=units, e=("m16", 1, 4, True), tail="quad", sw=sw, rq=rq))
                                    else:
                                        for ri in range(4):
                                            r = 4 * rq + ri
                                            units = []
                                            for ui, u in enumerate(ulist):
                                                units.append(dict(k=((sw + u) * 2048 + r, 16), q=(sw * 2048 + r, 16, 128), pc0=ui * 128, v=("v16", r * nj + sw + u), oc0=ri * 128,
                                                                  st=(ui == 0), sp=(ui == len(ulist) - 1)))
                                            groups.append(dict(units=units, e=("m16", ulist[0] + 1, len(ulist), False), tail=("quad" if ri == 3 else None), sw=sw, rq=rq))
                                for w in range(sw * 4, sw * 4 + 4):
                                    dense_groups(w)
                        ng = len(groups)
                        pos = [psO.next(), psO.next()]
                        state = {}
                        v16 = None
                        def load_v16(jbx):
                            pnx, Sx, gpx = jbx["pn"], jbx["S"], jbx["gp"]
                            v16b, v16key = v16r.next()
                            if jbx.get("dynA"):
                                nj_ = 2
                                vsrc = dr["vc"][:, gpx * 192:(gpx + 1) * 192].rearrange("(jj i r) c -> r i jj c", i=128, r=16)
                                v16reads = ["vc"]
                            else:
                                nj_ = (Sx // 128) // 16
                                vsrc = dr["v_" + pnx][:, gpx * 192:(gpx + 1) * 192].rearrange("(jj i r) c -> r i jj c", i=128, r=16)
                                v16reads = []
                            for r in range(16):
                                P.op("sp", lambda e, r=r, v16b=v16b, vsrc=vsrc, nj_=nj_: e.dma_start(out=v16b[:, r * nj_:(r + 1) * nj_, :], in_=vsrc[r]),
                                     reads=v16reads, writes=[v16key + f"_{r}"], dma=f"{v16key}_{r}")
                            jbx["v16"] = (v16b, [v16key + f"_{r}" for r in range(16)])

                        if br == "A":
                            if "v16" not in jb:
                                load_v16(jb)
                            v16 = jb["v16"]
                        nxt = jobs[ji + 1] if ji + 1 < len(jobs) else None
                        pre_at = None
                        if br == "A" and nxt is not None and nxt["gp"] < 3:
                            lastpat = max(gi_ for gi_, g_ in enumerate(groups) if g_["units"][0]["v"][0] == "v16")
                            pre_at = lastpat + 2

                        def sl(start, stride, n):
                            return slice(start, start + (n - 1) * stride + 1, stride) if stride != 1 else slice(start, start + n)

                        def emit_front2(g, qp=qp, kp=kp, qpkey=qpkey, kpkey=kpkey, pr=pr, jb=jb):
                            pss = [psS.next(), psS.next()]
                            ncols = 0
                            for u in g["units"]:
                                ks, kst = u["k"]
                                qs, qst, n = u["q"]
                                pc0 = u["pc0"]
                                for hh in range(2):
                                    ps, pskey = pss[hh]
                                    P.op("pe", lambda e, ps=ps, ks=ks, kst=kst, qs=qs, qst=qst, n=n, pc0=pc0, hh=hh: e.matmul(
                                        ps[:, pc0:pc0 + n], lhsT=kp[hh * 64:(hh + 1) * 64, sl(ks, kst, 128)], rhs=qp[hh * 64:(hh + 1) * 64, sl(qs, qst, n)], start=True, stop=True),
                                        reads=[qpkey, kpkey + f"_{hh}"], writes=[pskey])
                                ncols = max(ncols, pc0 + n)
                            for hh in range(2):
                                ps, pskey = pss[hh]
                                pt, ptkey = ptr.next()
                                P.op("act", lambda e, ps=ps, pt=pt, ncols=ncols: e.activation(out=pt[:, 0:ncols], in_=ps[:, 0:ncols], func=AF.Exp, scale=0.125),
                                     reads=[pskey], writes=[ptkey])
                                if g["e"] is not None:
                                    kind, i0_, gn, bc = g["e"]
                                    hglob = 2 * pr + hh
                                    if kind == "ea":
                                        e_ap, ekeys = eab[:, i0_:i0_ + gn, :], ["eab"]
                                    elif kind == "m16":
                                        if bc:
                                            e_ap, ekeys = m16[:, i0_:i0_ + 1, :].to_broadcast([128, gn, 128]), ["m16"]
                                        else:
                                            e_ap, ekeys = m16[:, i0_:i0_ + gn, :], ["m16"]
                                    elif kind == "ec":
                                        ecp_, eckeys_ = jb["ec"]
                                        e_ap, ekeys = ecp_[:, hh, i0_[0], i0_[1]:i0_[1] + gn, :], eckeys_
                                    elif kind == "ef":
                                        e_ap, ekeys = efb[:, hglob, i0_:i0_ + gn, :], [f"efb{hglob}"]
                                    else:
                                        e_ap, ekeys = eib[:, hglob, i0_:i0_ + gn, :], [f"eib{hglob}"]
                                    P.op("dve", lambda e, pt=pt, e_ap=e_ap, gn=gn: e.tensor_tensor(
                                        out=pt[:, 0:gn * 128].rearrange("p (a b) -> p a b", b=128), in0=pt[:, 0:gn * 128].rearrange("p (a b) -> p a b", b=128), in1=e_ap, op=ALU.mult),
                                        reads=[ptkey] + ekeys, writes=[ptkey])
                                g["pt%d" % hh] = (pt, ptkey)

                        def emit_pv2(g, vb=vb, vkeys=vkeys):
                            for u in g["units"]:
                                vkind, vblk = u["v"]
                                pc0, oc0, st_, sp_ = u["pc0"], u["oc0"], u["st"], u["sp"]
                                n = u["q"][2]
                                if vkind == "nat":
                                    vt, vks = vb, vkeys
                                else:
                                    vt, vks = v16[0], v16[1]
                                for hh in range(2):
                                    pt, ptkey = g["pt%d" % hh]
                                    po, pokey = pos[hh]
                                    P.op("pe", lambda e, po=po, vt=vt, vblk=vblk, pc0=pc0, n=n, oc0=oc0, st_=st_, sp_=sp_, pt=pt, hh=hh: e.matmul(
                                        po[:, oc0:oc0 + n], lhsT=vt[:, vblk, hh * 64:hh * 64 + 128], rhs=pt[:, pc0:pc0 + n], start=st_, stop=sp_),
                                        reads=[ptkey] + vks, writes=[pokey])

                        def emit_back(g, hh, ocol=ocol, od=od, br=br, dyn=dyn, dynA=jb.get("dynA"), dynC=jb.get("dynC")):
                            po, pokey = pos[hh]
                            if g["tail"] == "quad":
                                rq = g["rq"]
                                ac, ackey = accs[hh]
                                dst = ac[:].rearrange("p (l r) -> p r l", r=16)[:, 4 * rq:4 * rq + 4, :]
                                P.op("act", lambda e, po=po, dst=dst: e.activation(out=dst, in_=po[:].rearrange("p (a b) -> p a b", b=128), func=AF.Copy),
                                     reads=[pokey], writes=[ackey + f"_{rq}"])
                            if g["tail"] == "win":
                                osb, oskey = osr.next()
                                w = g["w"]
                                if br == "A":
                                    ac, ackey = accs[hh]
                                    wl = (w % 4) * 512
                                    P.op("dve", lambda e, osb=osb, po=po, ac=ac, wl=wl: e.tensor_tensor(out=osb[:], in0=po[:], in1=ac[:, wl:wl + 512], op=ALU.add),
                                         reads=[pokey] + [ackey + f"_{q}" for q in range(4)], writes=[oskey])
                                else:
                                    P.op("act", lambda e, osb=osb, po=po: e.activation(out=osb[:], in_=po[:], func=AF.Copy), reads=[pokey], writes=[oskey])
                                prr, prkey = psR.next()
                                for jj in range(4):
                                    P.op("pe", lambda e, prr=prr, osb=osb, jj=jj: e.transpose(out=prr[:, jj, :], in_=osb[:, jj * 128:(jj + 1) * 128], identity=identf[:]),
                                         reads=[oskey, "identf"], writes=[prkey])
                                rl, rlkey = rlr.next()
                                lcol = 64 if hh == 0 else 0
                                P.op("dve", lambda e, rl=rl, prr=prr, lcol=lcol: e.reciprocal(out=rl[:], in_=prr[:, :, lcol]), reads=[prkey], writes=[rlkey])
                                if hh == 0:
                                    state["og"] = ogr.next()
                                og, ogkey = state["og"]
                                P.op("dve", lambda e, og=og, prr=prr, rl=rl, hh=hh: e.tensor_tensor(
                                    out=og[:, :, hh * 64:(hh + 1) * 64], in0=prr[:, :, hh * 64:(hh + 1) * 64], in1=rl[:].unsqueeze(2).to_broadcast([128, 4, 64]), op=ALU.mult),
                                    reads=[prkey, rlkey], writes=[ogkey + f"_{hh}"])
                                if hh == 1 and dynC:
                                    P.op("sp", lambda e, og=og, w=w: e.dma_start(
                                        out=dr["ocq"][w * 512:(w + 1) * 512, ocol - 640:ocol - 640 + 128].rearrange("(b p) c -> p b c", p=128), in_=og[:]),
                                        reads=[ogkey + "_0", ogkey + "_1"], dma=ogkey)
                                elif hh == 1 and dynA:
                                    P.op("sp", lambda e, og=og, w=w: e.dma_start(
                                        out=dr["oaq"][w * 512:(w + 1) * 512, ocol:ocol + 128].rearrange("(b p) c -> p b c", p=128), in_=og[:]),
                                        reads=[ogkey + "_0", ogkey + "_1"], dma=ogkey)
                                elif hh == 1 and dyn:
                                    P.op("sp", lambda e, og=og, w=w: e.dma_start(
                                        out=dr["obq"][w * 512:(w + 1) * 512, ocol - 384:ocol - 384 + 128].rearrange("(b p) c -> p b c", p=128), in_=og[:]),
                                        reads=[ogkey + "_0", ogkey + "_1"], dma=ogkey)
                                elif hh == 1:
                                    P.op("sp", lambda e, og=og, w=w: e.dma_start(
                                        out=od[w * 512:(w + 1) * 512, ocol:ocol + 128].rearrange("(b p) c -> p b c", p=128), in_=og[:]),
                                        reads=[ogkey + "_0", ogkey + "_1"], dma=ogkey)

                        LOOK = 1
                        for i in range(ng + LOOK):
                            if pre_at is not None and i == pre_at:
                                load_v16(nxt)
                            if i < ng:
                                emit_front2(groups[i])
                            if i - LOOK >= 0:
                                emit_pv2(groups[i - LOOK])
                                emit_back(groups[i - LOOK], 0)
                                emit_back(groups[i - LOOK], 1)
                P.flush(final=(STOP == "s2"))
            if STOP == "s2":
                return nc

            with ExitStack() as st:
                wo = st.enter_context(nc.sbuf_tensor(U("wo"), [128, 8, D], BF16))
                wst3 = Ring(P, st, "wst3", 2, [128, 8, 256], F32)
                gbr = st.enter_context(nc.sbuf_tensor(U("gbr"), [128, D], F32))
                gpo = st.enter_context(nc.sbuf_tensor(U("gpo"), [128, D], F32))
                invw = st.enter_context(nc.sbuf_tensor(U("invw"), [128, 3], F32))
                o_r = Ring(P, st, "o_r", 4, [128, D], F32)
                z_r = Ring(P, st, "z_r", 4, [128, D // 2], F32)
                x_r = Ring(P, st, "x_r", 5, [128, D], F32)
                g_r = Ring(P, st, "g_r", 6, [128, D], F32)
                pc_r = Ring(P, st, "pc_r", 6, [128, D], F32)
                junk3 = st.enter_context(nc.sbuf_tensor(U("junk3"), [128, D], BF16))
                s3r = Ring(P, st, "s3r", 6, [128, 3], F32)
                s2r = Ring(P, st, "s2r", 6, [128, 2], F32)
                ybr = Ring(P, st, "ybr", 3, [128, D], BF16)
                yTr = Ring(P, st, "yTr", 3, [128, 8, 128], BF16)
                t_r = Ring(P, st, "t_r", 5, [128, D], F32)
                psT3 = Ring(P, st, "psT3", 2, [128, 8, 128], BF16, psum=True)
                psY = Ring(P, st, "psY", 3, [128, D], F32, psum=True)

                for ci in range(4):
                    wbuf, wkey = wst3.next()
                    c0 = ci * 256
                    P.op("sp", lambda e, wbuf=wbuf, c0=c0: e.dma_start(out=wbuf[:], in_=dr["w_out"][l][:, c0:c0 + 256].rearrange("(k p) c -> p k c", p=128)),
                         writes=[wkey], dma=wkey)
                    P.op("pool" if ci % 2 == 0 else "dve", lambda e, wbuf=wbuf, c0=c0: e.tensor_copy(out=wo[:, :, c0:c0 + 256], in_=wbuf[:]), reads=[wkey], writes=[f"wo{ci}"])
                WO = [f"wo{ci}" for ci in range(4)]
                P.op("sp", lambda e: e.dma_start(out=gbr[:], in_=dr["branch_gain"][l].partition_broadcast(128), allow_slow_non_contiguous=True), writes=["gbr"], dma="c5")
                P.op("sp", lambda e: e.dma_start(out=gpo[:], in_=dr["norm_post"][l].partition_broadcast(128), allow_slow_non_contiguous=True), writes=["gpo"], dma="c6")
                P.op("dve", lambda e: e.memset(invw[:, 0:1], 1.0 / 384), writes=["invw"])
                P.op("dve", lambda e: e.memset(invw[:, 1:2], 1.0 / 256), writes=["invw"])
                P.op("dve", lambda e: e.memset(invw[:, 2:3], 1.0 / 384), writes=["invw"])
                BR = ((0, 384), (384, 640), (640, 1024))
                blocks3 = []
                qblocks = []
                for (pn, S, src) in parts:
                    xsrc = dr["x_" + pn] if l == 0 else dr["y1_" + pn]
                    if last and pn == QPN:
                        P.op("sp", lambda e: e.dma_start(out=dr["oq"][:, 0:384], in_=dr["oaq"]), writes=["oq"], dma="cq0")
                        P.op("sp", lambda e: e.dma_start(out=dr["oq"][:, 640:1024], in_=dr["ocq"]), reads=["oq"], writes=["oq"], dma="cq4")
                        P.op("sp", lambda e, pn=pn: e.dma_start(out=dr["szq"], in_=dyn_ap(e, rrow2, 0, dr["sz_" + pn], [[D // 2, 2048], [1, D // 2]])), writes=["szq"], dma="cq1")
                        P.op("sp", lambda e, xsrc=xsrc: e.dma_start(out=dr["xq"], in_=dyn_ap(e, rrow, 0, xsrc, [[D, 2048], [1, D]])), writes=["xq"], dma="cq2")
                        P.op("sp", lambda e: e.dma_start(out=dr["oq"][:, 384:640], in_=dr["obq"]), reads=["oq"], writes=["oq"], dma="cq3")
                        qblocks = [dict(pn=pn, t0=blk * 128, osrc=dr["oq"], zsrc=dr["szq"], xsrc=dr["xq"], ydst=dr["yq_" + pn], keys=["oq", "szq", "xq"]) for blk in range(16)]
                        continue
                    ydst = dr["y_" + pn] if last else dr["y1_" + pn]
                    for blk in range(S // 128):
                        blocks3.append(dict(pn=pn, t0=blk * 128, osrc=dr["o_" + pn], zsrc=dr["sz_" + pn], xsrc=xsrc, ydst=ydst, keys=[]))
                blocks3 = blocks3 + qblocks

                def p_load(c):
                    pn, t0 = c["pn"], c["t0"]
                    ob, okey = o_r.next()
                    zb, zkey = z_r.next()
                    c["o"], c["z"] = (ob, okey), (zb, zkey)
                    osrc, zsrc = c["osrc"], c["zsrc"]
                    P.op("sp", lambda e: e.dma_start(out=ob[:], in_=osrc[t0:t0 + 128, :]), reads=c["keys"][0:1], writes=[okey], dma=okey)
                    P.op("sp", lambda e: e.dma_start(out=zb[:], in_=zsrc[t0:t0 + 128, :]), reads=c["keys"][1:2], writes=[zkey], dma=zkey)

                def p_g(c):
                    ob, okey = c["o"]
                    zb, zkey = c["z"]
                    gb, gkey = g_r.next()
                    c["g"] = (gb, gkey)
                    P.op("pool", lambda e: e.tensor_tensor(out=gb[:], in0=ob[:], in1=zb[:].bitcast(BF16), op=ALU.mult), reads=[okey, zkey], writes=[gkey])

                def p_sq(c):
                    gb, gkey = c["g"]
                    s3, s3key = s3r.next()
                    c["s3"] = (s3, s3key)
                    for bi, (c0, c1) in enumerate(BR):
                        P.op("act", lambda e, bi=bi, c0=c0, c1=c1: e.activation(out=junk3[:, c0:c1], in_=gb[:, c0:c1], func=AF.Square, accum_out=s3[:, bi:bi + 1]),
                             reads=[gkey], writes=[s3key, "junk3"])

                def p_r1(c):
                    s3, s3key = c["s3"]
                    P.op("dve", lambda e: e.tensor_tensor(out=s3[:], in0=s3[:], in1=invw[:], op=ALU.mult), reads=[s3key, "invw"], writes=[s3key])
                    P.op("dve", lambda e: e.tensor_scalar(out=s3[:], in0=s3[:], scalar1=1.0, scalar2=float(EPS), op0=ALU.mult, op1=ALU.add), reads=[s3key], writes=[s3key])

                def p_r2(c):
                    s3, s3key = c["s3"]
                    P.op("pool", lambda e: e.tensor_tensor(out=s3[:], in0=s3[:], in1=nhalf[:, 0:3], op=ALU.pow), reads=[s3key], writes=[s3key])

                def p_y(c):
                    gb, gkey = c["g"]
                    s3, s3key = c["s3"]
                    yb, ykey = ybr.next()
                    c["y"] = (yb, ykey)
                    for bi, (c0, c1) in enumerate(BR):
                        P.op("dve", lambda e, bi=bi, c0=c0, c1=c1: e.scalar_tensor_tensor(
                            out=yb[:, c0:c1], in0=gb[:, c0:c1], scalar=s3[:, bi:bi + 1], in1=gbr[:, c0:c1], op0=ALU.mult, op1=ALU.mult),
                            reads=[gkey, s3key, "gbr"], writes=[ykey])

                def p_T(c):
                    yb, ykey = c["y"]
                    pT, pTkey = psT3.next()
                    c["pT"] = (pT, pTkey)
                    for kc in range(8):
                        P.op("pe", lambda e, kc=kc: e.transpose(out=pT[:, kc, :], in_=yb[:, kc * 128:(kc + 1) * 128], identity=identb[:]),
                             reads=[ykey, "identb"], writes=[pTkey])

                def p_yT(c):
                    pT, pTkey = c["pT"]
                    yT, yTkey = yTr.next()
                    c["yT"] = (yT, yTkey)
                    P.op("act", lambda e: e.activation(out=yT[:], in_=pT[:], func=AF.Copy), reads=[pTkey], writes=[yTkey])

                def p_mm(c):
                    yT, yTkey = c["yT"]
                    py, pykey = psY.next()
                    c["py"] = (py, pykey)
                    for n in range(2):
                        for kc in range(8):
                            P.op("pe", lambda e, n=n, kc=kc: e.matmul(py[:, n * 512:(n + 1) * 512], lhsT=yT[:, kc, :], rhs=wo[:, kc, n * 512:(n + 1) * 512],
                                                                      start=(kc == 0), stop=(kc == 7)),
                                 reads=[yTkey] + WO, writes=[pykey])

                def p_ev(c):
                    py, pykey = c["py"]
                    s2, s2key = s2r.next()
                    c["s2"] = (s2, s2key)
                    pc, pckey = pc_r.next()
                    c["pc"] = (pc, pckey)
                    for n in range(2):
                        P.op("act", lambda e, n=n: e.activation(out=junk3[:, n * 512:(n + 1) * 512], in_=py[:, n * 512:(n + 1) * 512], func=AF.Square, accum_out=s2[:, n:n + 1]),
                             reads=[pykey], writes=[s2key, "junk3"])
                    P.op("act", lambda e: e.activation(out=pc[:], in_=py[:], func=AF.Copy), reads=[pykey], writes=[pckey])

                def p_r3(c):
                    s2, s2key = c["s2"]
                    P.op("dve", lambda e: e.tensor_tensor(out=s2[:, 0:1], in0=s2[:, 0:1], in1=s2[:, 1:2], op=ALU.add), reads=[s2key], writes=[s2key])
                    P.op("dve", lambda e: e.tensor_scalar(out=s2[:, 0:1], in0=s2[:, 0:1], scalar1=1.0 / D, scalar2=float(EPS), op0=ALU.mult, op1=ALU.add), reads=[s2key], writes=[s2key])

                def p_r4(c):
                    s2, s2key = c["s2"]
                    P.op("pool", lambda e: e.tensor_tensor(out=s2[:, 0:1], in0=s2[:, 0:1], in1=nhalf[:, 0:1], op=ALU.pow), reads=[s2key], writes=[s2key])
                    t0, xsrc = c["t0"], c["xsrc"]
                    xb, xkey = x_r.next()
                    c["x"] = (xb, xkey)
                    P.op("sp", lambda e: e.dma_start(out=xb[:], in_=xsrc[t0:t0 + 128, :]), reads=c["keys"][2:3], writes=[xkey], dma=xkey)

                def p_stt(c):
                    pc, pckey = c["pc"]
                    s2, s2key = c["s2"]
                    tb, tkey = t_r.next()
                    c["t"] = (tb, tkey)
                    P.op("dve", lambda e: e.scalar_tensor_tensor(out=tb[:], in0=pc[:], scalar=s2[:, 0:1], in1=gpo[:], op0=ALU.mult, op1=ALU.mult),
                         reads=[pckey, s2key, "gpo"], writes=[tkey])

                def p_add(c):
                    tb, tkey = c["t"]
                    xb, xkey = c["x"]
                    P.op("pool", lambda e: e.tensor_tensor(out=tb[:], in0=tb[:], in1=xb[:], op=ALU.add), reads=[tkey, xkey], writes=[tkey])

                def p_st(c):
                    tb, tkey = c["t"]
                    t0, ydst = c["t0"], c["ydst"]
                    P.op("sp", lambda e: e.dma_start(out=ydst[t0:t0 + 128, :], in_=tb[:]), reads=[tkey], dma=tkey)

                phases = [(p_load, 0), (p_g, 2), (p_sq, 3), (p_r1, 4), (p_r2, 5), (p_y, 6), (p_T, 7), (p_yT, 8), (p_mm, 9), (p_ev, 10),
                          (p_r3, 11), (p_r4, 12), (p_stt, 14), (p_add, 15), (p_st, 17)]
                nb3 = len(blocks3)
                for i in range(nb3 + 18):
                    for fn, dly in phases:
                        if 0 <= i - dly < nb3:
                            fn(blocks3[i - dly])
                P.flush(final=last)
        print(f"[build] instructions: {P.n_ins}", flush=True)
    return nc


_PARTS = [("p", 8192, "xp"), ("s0", 2048, "xs0"), ("s1", 2048, "xs1")]


def kernel(x_prompt, x_sample, norm_pre, w_in, q_norm, k_norm, rel_bias, branch_gain, w_out, norm_post):
    f = lambda a: np.ascontiguousarray(np.asarray(a, dtype=np.float32))
    x_prompt, x_sample = f(x_prompt), f(x_sample)
    ropea, ropeb, ea, ident = _const_tables()
    efraw = _c_bias_tables(f(rel_bias))
    nc = build(_PARTS, 2)
    shared = dict(w_in=f(w_in), w_out=f(w_out), norm_pre=f(norm_pre), norm_post=f(norm_post), branch_gain=f(branch_gain),
                  q_norm=f(q_norm), k_norm=f(k_norm), ropea=ropea, ropeb=ropeb, ea=ea, ident=ident, efraw=efraw)
    in_maps = []
    for c in range(8):
        m = dict(shared)
        q0 = (c % 4) * 2048
        m["qoff"] = np.array([[q0, q0 * D, q0 * (D // 2), q0 * 1536]], dtype=np.int32)
        m["cedge"] = np.ascontiguousarray(_c_edge_tables(efraw[-1], c % 4).reshape(6, 128, 24 * 128))
        m["x_p"] = x_prompt[c // 4]
        m["x_s0"] = x_sample[2 * c]
        m["x_s1"] = x_sample[2 * c + 1]
        in_maps.append(m)
    res = run_bass_kernel_spmd(nc, in_maps, core_ids=list(range(8)))
    r = res.results
    y_prompt = np.stack([np.concatenate([np.asarray(r[4 * b + q]["yq_p"], dtype=np.float32) for q in range(4)], axis=0) for b in range(2)], axis=0)
    ys = []
    for c in range(8):
        ys.append(np.asarray(r[c]["y_s0"], dtype=np.float32))
        ys.append(np.asarray(r[c]["y_s1"], dtype=np.float32))
    y_sample = np.stack(ys, axis=0)
    return (y_prompt, y_sample)
```

```python
import numpy as np
from contextlib import ExitStack
import concourse.bass as bass
import concourse.mybir as mybir
from concourse.bass_utils import run_bass_kernel_spmd

F32 = mybir.dt.float32
BF16 = mybir.dt.bfloat16
AF = mybir.ActivationFunctionType
ALU = mybir.AluOpType
AX = mybir.AxisListType

D = 1024
INW = 3840
EPS = 1e-6
NEG = -30000.0
STOP = None
QUARTER = True
LIMIT = None
AQ, AK, AV, AZ, BQ, BK, BV, BZ, CQ, CK, CV, CZ = 0, 384, 768, 1152, 1536, 1792, 1920, 2048, 2304, 2688, 3072, 3456

COMPUTE = ("pe", "act", "dve", "pool")
ISSUERS = ("pe", "act", "dve", "pool", "sp")
ENGOBJ = {"pe": "tensor", "act": "scalar", "dve": "vector", "pool": "gpsimd", "sp": "sync"}


class Op:
    __slots__ = ("eng", "fn", "is_dma", "sem", "ticket", "signal", "waits")

    def __init__(self, eng, fn, is_dma, sem):
        self.eng = eng
        self.fn = fn
        self.is_dma = is_dma
        self.sem = sem
        self.ticket = None
        self.signal = is_dma
        self.waits = []


class Prog:
    def __init__(self, nc, stack, block):
        self.nc = nc
        self.stack = stack
        self.block = block
        self.streams = {e: [] for e in ISSUERS}
        self.esem = {e: self.new_sem("sem_" + e) for e in COMPUTE}
        self.bar = self.new_sem("sem_bar")
        self.bar_count = 0
        self.ecount = {e: 0 for e in COMPUTE}
        self.last_w = {}
        self.readers = {}
        self.dma_sems = {}
        self.dma_count = {}
        self.pending = None
        self.alias = {}
        self.n_ins = 0

    def new_sem(self, name):
        return self.stack.enter_context(self.nc.semaphore(name))

    def _dep(self, op, prod):
        if prod is None or prod is op:
            return
        if (not op.is_dma) and (not prod.is_dma) and prod.eng == op.eng and op.eng == "pe":
            return
        prod.signal = True
        op.waits.append(prod)

    def op(self, eng, fn, reads=(), writes=(), dma=None):
        self.nrec = getattr(self, "nrec", 0) + 1
        if LIMIT is not None and self.nrec > LIMIT:
            return None
        is_dma = dma is not None
        ps_reads = [r for r in reads if r.startswith("ps")]
        if ps_reads:
            reads = [r for r in reads if not r.startswith("ps")]
            writes = list(writes) + ps_reads
        o = Op(eng, fn, is_dma, dma)
        if is_dma:
            if dma not in self.alias:
                self.alias[dma] = f"g{len(self.alias)}"
            dma = self.alias[dma]
            o.sem = dma
            if dma not in self.dma_sems:
                self.dma_sems[dma] = self.new_sem("d_" + dma)
                self.dma_count[dma] = 0
            self.dma_count[dma] += 16
            o.ticket = self.dma_count[dma]
        for r in reads:
            self._dep(o, self.last_w.get(r))
        for w in writes:
            self._dep(o, self.last_w.get(w))
            for rd in self.readers.get(w, ()):
                self._dep(o, rd)
        for r in reads:
            self.readers.setdefault(r, []).append(o)
        for w in writes:
            self.last_w[w] = o
            self.readers[w] = []
        self.streams[eng].append(o)
        return o

    def flush(self, final=False):
        for e in COMPUTE:
            ops = [o for o in self.streams[e] if not o.is_dma]
            if ops:
                ops[-1].signal = True
            for o in ops:
                if o.signal:
                    self.ecount[e] += 1
                    o.ticket = self.ecount[e]
        pending = self.pending
        dma_final = dict(self.dma_count)
        self.bar_count += 1
        bar_val = self.bar_count
        ecount = dict(self.ecount)

        def make(ename):
            ops = self.streams[ename]

            def body(eng):
                waited = {}
                if pending is not None:
                    for key, val in pending.items():
                        if val > 0:
                            sem = self.bar if key == "bar" else self.esem[key]
                            if key != ename:
                                eng.wait_ge(sem, val)
                for o in ops:
                    need = {}
                    for p in o.waits:
                        key = ("d", p.sem) if p.is_dma else ("e", p.eng)
                        if p.ticket > need.get(key, 0):
                            need[key] = p.ticket
                    for key, val in need.items():
                        if waited.get(key, 0) >= val:
                            continue
                        waited[key] = val
                        sem = self.dma_sems[key[1]] if key[0] == "d" else self.esem[key[1]]
                        eng.wait_ge(sem, val)
                    ins = o.fn(eng)
                    self.n_ins += 1
                    if o.is_dma:
                        ins.then_inc(self.dma_sems[o.sem], 16)
                    elif o.signal:
                        ins.then_inc(self.esem[o.eng], 1)
                if ename == "sp":
                    for s, v in dma_final.items():
                        if v > 0:
                            eng.wait_ge(self.dma_sems[s], v)
                    eng.sem_inc(self.bar, 1)

            return body

        for ename in ISSUERS:
            if ename != "sp" and not self.streams[ename] and pending is None:
                continue
            getattr(self.block, ENGOBJ[ename])(make(ename))
        self.pending = dict(ecount)
        self.pending["bar"] = bar_val
        self.streams = {e: [] for e in ISSUERS}
        self.last_w = {}
        self.readers = {}
        self.alias = {}
        if final:
            pend = self.pending

            def fin(eng):
                eng.wait_ge(self.bar, pend["bar"])

            for ename in ("pe", "act", "dve", "pool"):
                getattr(self.block, ENGOBJ[ename])(fin)


_UID = [0]


def U(name):
    _UID[0] += 1
    return f"{name}_u{_UID[0]}"


class Ring:
    def __init__(self, P, st, name, n, shape, dt, psum=False):
        self.n = n
        self.name = name
        self.i = -1
        alloc = P.nc.psum_tensor if psum else P.nc.sbuf_tensor
        self.bufs = [st.enter_context(alloc(U(f"{name}{k}"), list(shape), dt)) for k in range(n)]

    def next(self):
        self.i += 1
        k = self.i % self.n
        return self.bufs[k], f"{self.name}{k}"

    def cur(self):
        k = self.i % self.n
        return self.bufs[k], f"{self.name}{k}"


def _const_tables():
    SMAX = 8192
    t = np.arange(SMAX, dtype=np.float32)
    fa = (500000.0 ** (-np.arange(0, 16, 2, dtype=np.float32) / 16)).astype(np.float32)
    anga = (t[:, None] * fa[None, :]).astype(np.float32).astype(np.float64)
    fb = (10000.0 ** (-np.arange(0, 32, 2, dtype=np.float32) / 32)).astype(np.float32)
    row = (np.arange(SMAX) // 64).astype(np.float32)
    col = (np.arange(SMAX) % 64).astype(np.float32)
    angr = (row[:, None] * fb[None, :]).astype(np.float32).astype(np.float64)
    angc = (col[:, None] * fb[None, :]).astype(np.float32).astype(np.float64)
    angb = np.concatenate([angr, angc], axis=1)

    def tm(a):
        return np.ascontiguousarray(a.reshape(SMAX // 128, 128, -1).transpose(1, 0, 2)).astype(np.float32)

    ropea = np.stack([tm(np.cos(anga)), tm(np.sin(anga))], axis=1)
    ropeb = np.stack([tm(np.cos(angb)), tm(np.sin(angb))], axis=1)
    kk = np.arange(128)[:, None, None]
    dl = (np.arange(5) - 2)[None, :, None]
    ii = np.arange(128)[None, None, :]
    dd = 128 * dl + kk - ii
    ad = np.abs(dd)
    mult = (ad <= 64).astype(np.float32) + ((dd % 4 == 0) & (ad <= 256))
    du = 128 * (np.arange(3) - 1)[None, :, None] + kk - ii
    m16 = (np.abs(du) <= 64).astype(np.float32)
    kk2 = np.arange(128)[:, None]
    ii2 = np.arange(128)[None, :]
    m16q = np.stack([(kk2 >= ii2), (kk2 <= ii2)], axis=1).astype(np.float32)
    ea = np.ascontiguousarray(np.concatenate([mult, m16, m16q], axis=1).astype(np.float32))
    ident = np.eye(128, dtype=np.float32)
    return ropea, ropeb, ea, ident


def _c_bias_tables(rel_bias):
    L = rel_bias.shape[0]
    krl = (np.arange(128) // 64)[:, None, None]
    kc = (np.arange(128) % 64)[:, None, None]
    dlt = (np.arange(7) - 3)[None, :, None]
    rl = (np.arange(128) // 64)[None, None, :]
    qc = (np.arange(128) % 64)[None, None, :]
    dr = 2 * dlt + krl - rl
    ro = dr + 7
    co = np.clip(kc - qc + 15, 0, 30)
    cs = np.clip(qc - 8, 0, 48)
    valid = (kc >= cs) & (kc < cs + 16) & (ro >= 0) & (ro <= 14)
    ro_c = np.clip(ro, 0, 14)
    ro_b, co_b, valid_b = np.broadcast_arrays(ro_c, co, valid)
    out = np.empty((L, 6, 128, 7, 128), dtype=np.float32)
    for l in range(L):
        for h in range(6):
            g = rel_bias[l, h][ro_b, co_b]
            out[l, h] = np.where(valid_b, g, np.float32(NEG))
    return out


def _c_edge_tables(efraw_l, qd):
    negt = np.full((128, 128), np.float32(NEG), dtype=np.float32)

    def interior_tile(h, dl):
        if dl < -2 or dl > 2:
            return negt
        t = efraw_l[h][:, dl + 3, :].copy()
        if dl == -2:
            t[0:64, 64:128] = NEG
        if dl == 2:
            t[64:128, :] = NEG
            t[0:64, 0:64] = NEG
        return t

    def full_tile(h, dl):
        if dl < -3 or dl > 3:
            return negt
        return efraw_l[h][:, dl + 3, :]

    out = np.empty((6, 128, 4, 6, 128), dtype=np.float32)
    for h in range(6):
        for side in range(2):
            for bi in range(2):
                for sl_ in range(6):
                    if side == 0:
                        b, a_own = bi, sl_ - 2
                        edge, ok = (qd == 0), (0 <= a_own <= 3)
                    else:
                        b, a_own = 14 + bi, 12 + sl_
                        edge, ok = (qd == 3), (12 <= a_own <= 15)
                    if edge:
                        t = full_tile(h, a_own - b) if ok else negt
                    else:
                        t = interior_tile(h, a_own - b)
                    out[h, :, side * 2 + bi, sl_, :] = t
    return out


def build(parts, n_layers, debug=False):
    nc = bass.Bass("TRN2", target_bir_lowering=False)
    dr = {}

    def din(name, shape, dt=F32):
        dr[name] = nc.dram_tensor(name, list(shape), dt, kind="ExternalInput").ap()
        return dr[name]

    def dout(name, shape, dt=F32):
        dr[name] = nc.dram_tensor(name, list(shape), dt, kind="ExternalOutput").ap()
        return dr[name]

    def dscr(name, shape, dt):
        if debug:
            dr[name] = nc.dram_tensor(name, list(shape), dt, kind="ExternalOutput").ap()
        else:
            dr[name] = nc.dram_tensor(name, list(shape), dt).ap()
        return dr[name]

    QPN = "p" if (QUARTER and any(pn == "p" for (pn, _, _) in parts)) else None
    if QPN is not None:
        dscr("oq", [2048, D], F32)
        dscr("szq", [2048, D // 2], F32)
        dscr("xq", [2048, D], F32)
        dscr("obq", [2048, 256], F32)
    for (pn, S, src) in parts:
        din("x_" + pn, [S, D])
        if pn == QPN:
            dout("yq_" + pn, [2048, D])
        else:
            dout("y_" + pn, [S, D])
        dscr("qt_" + pn, [1024, S], BF16)
        if pn == QPN:
            dscr("ktpad", [896, S + 2048], BF16)
            dscr("vpad", [S + 2048, 8 * 192], BF16)
            dr["kt_" + pn] = dr["ktpad"][:, 1024:1024 + S]
            dr["v_" + pn] = dr["vpad"][1024:1024 + S, :]
            dscr("ktc", [768, 4096], BF16)
            dscr("vc", [4096, 1152], BF16)
            dscr("oaq", [2048, 384], F32)
            dscr("ocq", [2048, 384], F32)
        else:
            dscr("kt_" + pn, [896, S], BF16)
            dscr("v_" + pn, [S, 8 * 192], BF16)
        dscr("sz_" + pn, [S, D // 2], F32)
        dscr("o_" + pn, [S, D], F32)
        if n_layers > 1:
            dscr("y1_" + pn, [S, D], F32)
    din("w_in", [n_layers, D, INW])
    din("w_out", [n_layers, D, D])
    din("norm_pre", [n_layers, D])
    din("norm_post", [n_layers, D])
    din("branch_gain", [n_layers, D])
    din("q_norm", [n_layers, 64])
    din("k_norm", [n_layers, 64])
    din("ropea", [128, 2, 64, 8])
    din("ropeb", [128, 2, 64, 32])
    din("ea", [128, 10, 128])
    din("ident", [128, 128])
    din("efraw", [n_layers, 6, 128, 7, 128])
    if QPN is not None:
        dr["qoff"] = nc.dram_tensor("qoff", [1, 4], mybir.dt.int32, kind="ExternalInput").ap()
        din("cedge", [6, 128, 24 * 128])

    with ExitStack() as top:
        block = top.enter_context(nc.Block())
        P = Prog(nc, top, block)
        identf = top.enter_context(nc.sbuf_tensor("identf", [128, 128], F32))
        identb = top.enter_context(nc.sbuf_tensor("identb", [128, 128], BF16))
        epsc = top.enter_context(nc.sbuf_tensor("epsc", [128, 8], F32))
        nhalf = top.enter_context(nc.sbuf_tensor("nhalf", [128, 8], F32))
        P.op("sp", lambda e: e.dma_start(out=identf[:], in_=dr["ident"]), writes=["identf"], dma="c0")
        P.op("dve", lambda e: e.tensor_copy(out=identb[:], in_=identf[:]), reads=["identf"], writes=["identb"])
        P.op("dve", lambda e: e.memset(epsc[:], EPS), writes=["epsc"])
        P.op("dve", lambda e: e.memset(nhalf[:], -0.5), writes=["nhalf"])
        if QPN is not None:
            qs = top.enter_context(nc.sbuf_tensor("qs", [1, 4], mybir.dt.int32))
            rq0 = top.enter_context(nc.sync.register("rq0"))
            rrow = top.enter_context(nc.sync.register("rrow"))
            rrow2 = top.enter_context(nc.sync.register("rrow2"))
            rv = top.enter_context(nc.sync.register("rv"))
            rtmp = [top.enter_context(nc.sync.register(f"rtmp{i}")) for i in range(4)]
            rti = [0]
            P.op("sp", lambda e: e.dma_start(out=qs[:], in_=dr["qoff"]), writes=["qs"], dma="c1")

            def setregs(e):
                e.reg_load(rq0, qs[0:1, 0:1])
                e.reg_load(rrow2, qs[0:1, 2:3])
                e.reg_load(rv, qs[0:1, 3:4])
                return e.reg_load(rrow, qs[0:1, 1:2])
            P.op("sp", setregs, reads=["qs"], writes=["regs"])

            SQ = [S for (pn_, S, _) in parts if pn_ == QPN][0]
            ztstack = ExitStack()
            zt = ztstack.enter_context(nc.sbuf_tensor("zt", [128, 1536], BF16))
            P.op("pool", lambda e: e.memset(zt[:], 0.0), writes=["zt"])
            zi = 0
            for side in (0, 1024 + SQ):
                for k in range(7):
                    P.op("sp", lambda e, k=k, side=side: e.dma_start(out=dr["ktpad"][k * 128:(k + 1) * 128, side:side + 1024], in_=zt[:, 0:1024]), reads=["zt"], dma=f"zp{zi}")
                    zi += 1
                for k in range(8):
                    P.op("sp", lambda e, k=k, side=side: e.dma_start(out=dr["vpad"][side + k * 128:side + (k + 1) * 128, :], in_=zt[:]), reads=["zt"], dma=f"zp{zi}")
                    zi += 1

            def dyn_ap(e, base_reg, const, tensor_ap, pattern):
                t = rtmp[rti[0] % 4]
                rti[0] += 1
                e.reg_add(t, base_reg, int(const))
                return bass.AP(tensor_ap.tensor, t, pattern)
        P.flush(final=(STOP == "pre"))
        if QPN is not None:
            ztstack.close()
        if STOP == "pre":
            return nc

        def rstd_ops(v_ap, key, n, scale):
            P.op("dve", lambda e: e.tensor_scalar(out=v_ap, in0=v_ap, scalar1=float(scale), scalar2=float(EPS),
                                                  op0=ALU.mult, op1=ALU.add), reads=[key], writes=[key])
            P.op("pool", lambda e: e.tensor_tensor(out=v_ap, in0=v_ap, in1=nhalf[:, 0:n], op=ALU.pow),
                 reads=[key], writes=[key])

        for l in range(n_layers):
            last = l == n_layers - 1
            with ExitStack() as st:
                wsb = st.enter_context(nc.sbuf_tensor(U("wsb"), [128, 8, INW], BF16))
                wst = Ring(P, st, "wst", 2, [128, 8, 120], F32)
                gpre = st.enter_context(nc.sbuf_tensor(U("gpre"), [128, 8], F32))
                gqk = st.enter_context(nc.sbuf_tensor(U("gqk"), [128, 6, 64], F32))
                ropa = st.enter_context(nc.sbuf_tensor(U("ropa"), [128, 2, 64, 8], F32))
                ropb = st.enter_context(nc.sbuf_tensor(U("ropb"), [128, 2, 64, 32], F32))
                xin = Ring(P, st, "xin", 5, [128, D], F32)
                junk = st.enter_context(nc.sbuf_tensor(U("junk"), [128, D], BF16))
                ssr = Ring(P, st, "ssr", 5, [128, 1], F32)
                xnr = Ring(P, st, "xnr", 2, [128, D], BF16)
                qfr = Ring(P, st, "qfr", 2, [128, 6, 64], F32)
                bfr = Ring(P, st, "bfr", 2, [128, 512], F32)
                hTr = Ring(P, st, "hTr", 2, [128, 8, 512], BF16)
                qts = Ring(P, st, "qts", 2, [128, 8, 512], BF16)
                kts = Ring(P, st, "kts", 2, [128, 7, 512], BF16)
                vsr = Ring(P, st, "vsr", 4, [128, 8, 192], BF16)
                szr = Ring(P, st, "szr", 4, [128, D], BF16)
                qar = Ring(P, st, "qar", 3, [128, 6, 64], BF16)
                kar = Ring(P, st, "kar", 3, [128, 6, 64], BF16)
                qkbr = Ring(P, st, "qkbr", 3, [128, 6, 64], BF16)
                sqb = st.enter_context(nc.sbuf_tensor(U("sqb"), [128, 6, 64], F32))
                xgb = st.enter_context(nc.sbuf_tensor(U("xgb"), [128, 6, 64], F32))
                ss6 = st.enter_context(nc.sbuf_tensor(U("ss6"), [128, 6], F32))
                tA = [st.enter_context(nc.sbuf_tensor(U(f"tA{i}"), [128, 6, 8], F32)) for i in range(4)]
                tB = [st.enter_context(nc.sbuf_tensor(U(f"tB{i}"), [128, 6, 2, 16], F32)) for i in range(4)]
                psT = Ring(P, st, "psT", 2, [128, 8, 128], BF16, psum=True)
                psT2 = Ring(P, st, "psT2", 2, [128, 8, 128], BF16, psum=True)
                psM = Ring(P, st, "psM", 4, [128, 512], F32, psum=True)
                psF = psM

                P.op("sp", lambda e: e.dma_start(out=gpre[:], in_=dr["norm_pre"][l].rearrange("(k p) -> p k", p=128),
                                                 allow_slow_non_contiguous=True), writes=["gpre"], dma="c0")
                P.op("sp", lambda e: e.dma_start(out=gqk[:, 0:4, :], in_=dr["q_norm"][l].partition_broadcast(128).unsqueeze(1).to_broadcast([128, 4, 64]),
                                                 allow_slow_non_contiguous=True), writes=["gqk_q"], dma="c1")
                P.op("sp", lambda e: e.dma_start(out=gqk[:, 4:6, :], in_=dr["k_norm"][l].partition_broadcast(128).unsqueeze(1).to_broadcast([128, 2, 64]),
                                                 allow_slow_non_contiguous=True), writes=["gqk_k"], dma="c2")
                P.op("sp", lambda e: e.dma_start(out=ropa[:], in_=dr["ropea"]), writes=["ropa"], dma="c3")
                P.op("sp", lambda e: e.dma_start(out=ropb[:], in_=dr["ropeb"]), writes=["ropb"], dma="c4")
                for ci in range(32):
                    wbuf, wkey = wst.next()
                    c0 = ci * 120
                    P.op("sp", lambda e, wbuf=wbuf, c0=c0: e.dma_start(
                        out=wbuf[:], in_=dr["w_in"][l][:, c0:c0 + 120].rearrange("(k p) c -> p k c", p=128)),
                        writes=[wkey], dma=wkey)
                    eng = ("pool", "dve", "act")[ci % 3]
                    if eng == "act":
                        P.op("act", lambda e, wbuf=wbuf, c0=c0: e.activation(out=wsb[:, :, c0:c0 + 120], in_=wbuf[:], func=AF.Copy),
                             reads=[wkey], writes=[f"wsb{ci}"])
                    else:
                        P.op(eng, lambda e, wbuf=wbuf, c0=c0: e.tensor_copy(out=wsb[:, :, c0:c0 + 120], in_=wbuf[:]),
                             reads=[wkey], writes=[f"wsb{ci}"])
                WALL = [f"wsb{ci}" for ci in range(32)]

                def wkeys(c0, c1):
                    return [f"wsb{ci}" for ci in range(c0 // 120, (c1 - 1) // 120 + 1)]

                from collections import deque
                blocks1 = []
                for (pn, S, src) in parts:
                    xsrc = dr["x_" + pn] if l == 0 else dr["y1_" + pn]
                    for blk in range(S // 128):
                        blocks1.append(dict(pn=pn, blk=blk, j=blk % 4, tg=blk // 4, xsrc=xsrc))
                pend = deque()

                def run_pend(keep):
                    while len(pend) > keep:
                        pend.popleft()()

                def s1_load(c):
                    xb, xkey = xin.next()
                    c["x"] = (xb, xkey)
                    t0, xsrc = c["blk"] * 128, c["xsrc"]
                    P.op("sp", lambda e: e.dma_start(out=xb[:], in_=xsrc[t0:t0 + 128, :]), writes=[xkey], dma=xkey)

                grp = {}

                def s1_fa(c):
                    xb, xkey = c["x"]
                    ss, sskey = ssr.next()
                    c["ss"] = (ss, sskey)
                    P.op("act", lambda e: e.activation(out=junk[:], in_=xb[:], func=AF.Square, accum_out=ss[:]), reads=[xkey], writes=[sskey, "junk"])
                    rstd_ops(ss[:], sskey, 1, 1.0 / D)

                def s1_front(c):
                    j = c["j"]
                    if j == 0:
                        grp["hT"] = hTr.next()
                        grp["qt"] = qts.next()
                        grp["kt"] = kts.next()
                    c["hT"], c["qt"], c["kt"] = grp["hT"], grp["qt"], grp["kt"]
                    hT, hkey = c["hT"]
                    xb, xkey = c["x"]
                    ss, sskey = c["ss"]
                    xn, xnkey = xnr.next()
                    P.op("act", lambda e: e.activation(out=xn[:], in_=xb[:], func=AF.Copy, scale=ss[:]), reads=[xkey, sskey], writes=[xnkey])
                    pT, pTkey = psT.next()
                    for kc in range(8):
                        P.op("pe", lambda e, kc=kc: e.transpose(out=pT[:, kc, :], in_=xn[:, kc * 128:(kc + 1) * 128], identity=identb[:]),
                             reads=[xnkey, "identb"], writes=[pTkey])
                    P.op("dve", lambda e: e.tensor_tensor(
                        out=hT[:, :, j * 128:(j + 1) * 128], in0=pT[:], in1=gpre[:].unsqueeze(2).to_broadcast([128, 8, 128]), op=ALU.mult),
                        reads=[pTkey, "gpre"], writes=[hkey + f"_{j}"])

                def s1_main(c):
                    j, blk, pn = c["j"], c["blk"], c["pn"]
                    hT, hkey = c["hT"]
                    qt_s, qkey = c["qt"]
                    kt_s, kkey = c["kt"]
                    hk = hkey + f"_{j}"

                    def tok_mm(c0, c1):
                        run_pend(2)
                        pm, pmkey = psM.next()
                        for kc in range(8):
                            P.op("pe", lambda e, kc=kc: e.matmul(pm[:, 0:c1 - c0], lhsT=hT[:, kc, j * 128:(j + 1) * 128], rhs=wsb[:, kc, c0:c1],
                                                                 start=(kc == 0), stop=(kc == 7)),
                                 reads=[hk] + wkeys(c0, c1), writes=[pmkey])
                        return pm, pmkey

                    for which, c0, ring, dst in (("q", AQ, qar, qt_s), ("k", AK, kar, kt_s)):
                        pm, pmkey = tok_mm(c0, c0 + 384)
                        qf, qfkey = qfr.next()
                        P.op("act", lambda e, qf=qf, pm=pm: e.activation(out=qf[:].rearrange("p h d -> p (h d)"), in_=pm[:, 0:384], func=AF.Copy), reads=[pmkey], writes=[qfkey])
                        ob, okey = ring.next()
                        P.op("pool", lambda e, ob=ob, qf=qf: e.tensor_copy(out=ob[:], in_=qf[:]), reads=[qfkey], writes=[okey])
                        cosb = ropa[:, 0, blk, :].unsqueeze(1).to_broadcast([128, 6, 8])
                        sinb = ropa[:, 1, blk, :].unsqueeze(1).to_broadcast([128, 6, 8])
                        x1 = qf[:, :, 0:8]
                        x2 = qf[:, :, 8:16]
                        P.op("dve", lambda e, x1=x1, cosb=cosb: e.tensor_tensor(out=tA[0][:], in0=x1, in1=cosb, op=ALU.mult), reads=[qfkey, "ropa"], writes=["tA0"])
                        P.op("dve", lambda e, x2=x2, sinb=sinb: e.tensor_tensor(out=tA[1][:], in0=x2, in1=sinb, op=ALU.mult), reads=[qfkey, "ropa"], writes=["tA1"])
                        P.op("dve", lambda e, x2=x2, cosb=cosb: e.tensor_tensor(out=tA[2][:], in0=x2, in1=cosb, op=ALU.mult), reads=[qfkey, "ropa"], writes=["tA2"])
                        P.op("dve", lambda e, x1=x1, sinb=sinb: e.tensor_tensor(out=tA[3][:], in0=x1, in1=sinb, op=ALU.mult), reads=[qfkey, "ropa"], writes=["tA3"])
                        P.op("dve", lambda e, ob=ob: e.tensor_tensor(out=ob[:, :, 0:8], in0=tA[0][:], in1=tA[1][:], op=ALU.subtract), reads=["tA0", "tA1", okey], writes=[okey])
                        P.op("dve", lambda e, ob=ob: e.tensor_tensor(out=ob[:, :, 8:16], in0=tA[2][:], in1=tA[3][:], op=ALU.add), reads=["tA2", "tA3", okey], writes=[okey])

                        def fin_a(ob=ob, okey=okey, dst=dst, which=which):
                            pT2, pT2key = psT2.next()
                            obf = ob[:].rearrange("p h d -> p (h d)")
                            for tt in range(3):
                                P.op("pe", lambda e, tt=tt: e.transpose(out=pT2[:, tt, :], in_=obf[:, tt * 128:(tt + 1) * 128], identity=identb[:]),
                                     reads=[okey, "identb"], writes=[pT2key])
                            dkey = (qkey if which == "q" else kkey) + f"_{j}a"
                            P.op("act", lambda e: e.activation(out=dst[:, 0:3, j * 128:(j + 1) * 128], in_=pT2[:, 0:3, :], func=AF.Copy),
                                 reads=[pT2key], writes=[dkey])
                        pend.append(fin_a)
                    vs, vkey = vsr.next()
                    c["vs"] = (vs, vkey)
                    pm, pmkey = tok_mm(AV, AV + 384)
                    P.op("dve", lambda e, pm=pm: e.tensor_copy(out=vs[:, 0:3, :].rearrange("p a (s d) -> p a s d", d=64)[:, :, 0::2, :], in_=pm[:, 0:384].rearrange("p (a s d) -> p a s d", s=2, d=64)),
                         reads=[pmkey], writes=[vkey + "a"])
                    P.op("pool", lambda e: e.memset(vs[:, :, 64:128], 1.0), writes=[vkey + "one"])
                    sz, szkey = szr.next()
                    c["sz"] = (sz, szkey)
                    pm, pmkey = tok_mm(AZ, AZ + 384)
                    P.op("act", lambda e, pm=pm: e.activation(out=sz[:, 0:384], in_=pm[:, 0:384], func=AF.Silu), reads=[pmkey], writes=[szkey + "a"])
                    pm, pmkey = tok_mm(BQ, BQ + 512)
                    bf, bfkey = bfr.next()
                    P.op("act", lambda e, pm=pm, bf=bf: e.activation(out=bf[:], in_=pm[:], func=AF.Copy), reads=[pmkey], writes=[bfkey])
                    bf6 = bf[:, 0:384].rearrange("p (h d) -> p h d", d=64)
                    P.op("pool", lambda e, bf6=bf6: e.tensor_tensor(out=sqb[:], in0=bf6, in1=bf6, op=ALU.mult), reads=[bfkey], writes=["sqb"])
                    P.op("dve", lambda e: e.tensor_reduce(out=ss6[:], in_=sqb[:], axis=AX.X, op=ALU.add), reads=["sqb"], writes=["ss6"])
                    rstd_ops(ss6[:], "ss6", 6, 1.0 / 64)
                    P.op("dve", lambda e, bf6=bf6: e.tensor_tensor(out=xgb[:], in0=bf6, in1=ss6[:].unsqueeze(2).to_broadcast([128, 6, 64]), op=ALU.mult),
                         reads=[bfkey, "ss6"], writes=["xgb"])
                    P.op("pool", lambda e: e.tensor_tensor(out=xgb[:], in0=xgb[:], in1=gqk[:], op=ALU.mult), reads=["xgb", "gqk_q", "gqk_k"], writes=["xgb"])
                    qkb, qkbkey = qkbr.next()
                    xv = xgb[:].rearrange("p h (a b c) -> p h a b c", a=2, b=2)
                    ov = qkb[:].rearrange("p h (a b c) -> p h a b c", a=2, b=2)
                    cb = ropb[:, 0, blk, :].rearrange("p (a c) -> p a c", a=2).unsqueeze(1).to_broadcast([128, 6, 2, 16])
                    sb_ = ropb[:, 1, blk, :].rearrange("p (a c) -> p a c", a=2).unsqueeze(1).to_broadcast([128, 6, 2, 16])
                    x1 = xv[:, :, :, 0, :]
                    x2 = xv[:, :, :, 1, :]
                    P.op("pool", lambda e, x1=x1, cb=cb: e.tensor_tensor(out=tB[0][:], in0=x1, in1=cb, op=ALU.mult), reads=["xgb", "ropb"], writes=["tB0"])
                    P.op("pool", lambda e, x2=x2, sb_=sb_: e.tensor_tensor(out=tB[1][:], in0=x2, in1=sb_, op=ALU.mult), reads=["xgb", "ropb"], writes=["tB1"])
                    P.op("dve", lambda e, x2=x2, cb=cb: e.tensor_tensor(out=tB[2][:], in0=x2, in1=cb, op=ALU.mult), reads=["xgb", "ropb"], writes=["tB2"])
                    P.op("dve", lambda e, x1=x1, sb_=sb_: e.tensor_tensor(out=tB[3][:], in0=x1, in1=sb_, op=ALU.mult), reads=["xgb", "ropb"], writes=["tB3"])
                    P.op("pool", lambda e, ov=ov: e.tensor_tensor(out=ov[:, :, :, 0, :], in0=tB[0][:], in1=tB[1][:], op=ALU.subtract), reads=["tB0", "tB1"], writes=[qkbkey + "x"])
                    P.op("dve", lambda e, ov=ov: e.tensor_tensor(out=ov[:, :, :, 1, :], in0=tB[2][:], in1=tB[3][:], op=ALU.add), reads=["tB2", "tB3"], writes=[qkbkey + "y"])
                    bfv = bf[:, 384:512].rearrange("p (h d) -> p h d", d=64)
                    P.op("pool", lambda e, bfv=bfv: e.tensor_copy(out=vs[:, 3:5, 0:64], in_=bfv), reads=[bfkey], writes=[vkey + "b"])
                    P.op("pool", lambda e, bfv=bfv: e.tensor_copy(out=vs[:, 3:5, 128:192], in_=bfv), reads=[bfkey], writes=[vkey + "b2"])

                    def fin_b(qkb=qkb, qkbkey=qkbkey):
                        pT2, pT2key = psT2.next()
                        qkbf = qkb[:].rearrange("p h d -> p (h d)")
                        for tt in range(3):
                            P.op("pe", lambda e, tt=tt: e.transpose(out=pT2[:, tt, :], in_=qkbf[:, tt * 128:(tt + 1) * 128], identity=identb[:]),
                                 reads=[qkbkey + "x", qkbkey + "y", "identb"], writes=[pT2key])
                        P.op("act", lambda e: e.activation(out=qt_s[:, 3:5, j * 128:(j + 1) * 128], in_=pT2[:, 0:2, :], func=AF.Copy),
                             reads=[pT2key], writes=[qkey + f"_{j}b"])
                        P.op("act", lambda e: e.activation(out=kt_s[:, 3, j * 128:(j + 1) * 128], in_=pT2[:, 2, :], func=AF.Copy),
                             reads=[pT2key], writes=[kkey + f"_{j}b"])
                    pend.append(fin_b)
                    pm, pmkey = tok_mm(BZ, BZ + 256)
                    P.op("act", lambda e, pm=pm: e.activation(out=sz[:, 384:640], in_=pm[:, 0:256], func=AF.Silu), reads=[pmkey], writes=[szkey + "b"])
                    pm, pmkey = tok_mm(CV, CV + 384)
                    P.op("dve", lambda e, pm=pm: e.tensor_copy(out=vs[:, 5:8, :].rearrange("p a (s d) -> p a s d", d=64)[:, :, 0::2, :], in_=pm[:, 0:384].rearrange("p (a s d) -> p a s d", s=2, d=64)),
                         reads=[pmkey], writes=[vkey + "c"])
                    pm, pmkey = tok_mm(CZ, CZ + 384)
                    P.op("act", lambda e, pm=pm: e.activation(out=sz[:, 640:1024], in_=pm[:, 0:384], func=AF.Silu), reads=[pmkey], writes=[szkey + "c"])
                    if j == 3:
                        hks = [hkey + f"_{jj}" for jj in range(4)]
                        for ti in range(6):
                            run_pend(2)
                            c0 = (CQ if ti < 3 else CK) + (ti % 3) * 128
                            pf, pfkey = psF.next()
                            for kc in range(8):
                                P.op("pe", lambda e, pf=pf, kc=kc, c0=c0: e.matmul(pf[:], lhsT=wsb[:, kc, c0:c0 + 128], rhs=hT[:, kc, :], start=(kc == 0), stop=(kc == 7)),
                                     reads=hks + wkeys(c0, c0 + 128), writes=[pfkey])
                            if ti < 3:
                                P.op("dve", lambda e, pf=pf, ti=ti: e.tensor_copy(out=qt_s[:, 5 + ti, :], in_=pf[:]), reads=[pfkey], writes=[qkey + f"_c{ti}"])
                            else:
                                P.op("act", lambda e, pf=pf, ti=ti: e.activation(out=kt_s[:, 4 + ti - 3, :], in_=pf[:], func=AF.Copy), reads=[pfkey], writes=[kkey + f"_c{ti}"])

                def s1_store(c):
                    j, blk, pn = c["j"], c["blk"], c["pn"]
                    t0 = blk * 128
                    vs, vkey = c["vs"]
                    sz, szkey = c["sz"]
                    qt_s, qkey = c["qt"]
                    kt_s, kkey = c["kt"]
                    P.op("sp", lambda e: e.dma_start(out=dr["v_" + pn][t0:t0 + 128, :], in_=vs[:].rearrange("p h d -> p (h d)")),
                         reads=[vkey + "a", vkey + "b", vkey + "b2", vkey + "c", vkey + "one"], dma=vkey)
                    P.op("sp", lambda e: e.dma_start(out=dr["sz_" + pn][t0:t0 + 128, :], in_=sz[:].bitcast(F32)),
                         reads=[szkey + "a", szkey + "b", szkey + "c"], dma=szkey)
                    if j == 3:
                        tt0 = c["tg"] * 512
                        qr = [qkey + f"_{jj}a" for jj in range(4)] + [qkey + f"_{jj}b" for jj in range(4)] + [qkey + f"_c{ti}" for ti in range(3)]
                        kr = [kkey + f"_{jj}a" for jj in range(4)] + [kkey + f"_{jj}b" for jj in range(4)] + [kkey + f"_c{ti}" for ti in range(3, 6)]
                        P.op("sp", lambda e: e.dma_start(out=dr["qt_" + pn][:, tt0:tt0 + 512].rearrange("(k p) t -> p k t", p=128), in_=qt_s[:]), reads=qr, dma=qkey)
                        P.op("sp", lambda e: e.dma_start(out=dr["kt_" + pn][:, tt0:tt0 + 512].rearrange("(k p) t -> p k t", p=128), in_=kt_s[:]), reads=kr, dma=kkey)

                nb1 = len(blocks1)
                for i in range(-3, nb1 + 2):
                    if 0 <= i + 3 < nb1:
                        s1_load(blocks1[i + 3])
                    if 0 <= i + 2 < nb1:
                        s1_fa(blocks1[i + 2])
                    if 0 <= i + 1 < nb1:
                        s1_front(blocks1[i + 1])
                    if 0 <= i < nb1:
                        s1_main(blocks1[i])
                        if blocks1[i]["j"] == 3:
                            run_pend(0)
                    if 0 <= i - 2 < nb1:
                        s1_store(blocks1[i - 2])
                run_pend(0)
                P.flush(final=(STOP == "s1"))
            if STOP == "s1":
                return nc

            with ExitStack() as st:
                SMAXP = max(S for (_, S, _) in parts)
                qpr = Ring(P, st, "qpr", 2, [128, SMAXP], BF16)
                kpr = Ring(P, st, "kpr", 2, [128, SMAXP], BF16)
                vbr = Ring(P, st, "vbr", 2, [128, SMAXP // 128, 192], BF16)
                eab = st.enter_context(nc.sbuf_tensor(U("eab"), [128, 5, 128], BF16))
                m16 = st.enter_context(nc.sbuf_tensor(U("m16"), [128, 5, 128], BF16))
                accs = [(st.enter_context(nc.sbuf_tensor(U(f"acc{hh}"), [128, 2048], F32)), f"acc{hh}") for hh in range(2)]
                v16r = Ring(P, st, "v16r", 1, [128, SMAXP // 128, 192], BF16)
                ecr = Ring(P, st, "ecr", 1, [128, 2, 4, 6, 128], BF16)
                est = Ring(P, st, "est", 2, [128, 1024], F32)
                efb = st.enter_context(nc.sbuf_tensor(U("efb"), [128, 6, 7, 128], BF16))
                eib = st.enter_context(nc.sbuf_tensor(U("eib"), [128, 6, 5, 128], BF16))
                ptr = Ring(P, st, "ptr", 4, [128, 512], BF16)
                osr = Ring(P, st, "osr", 2, [128, 512], F32)
                ogr = Ring(P, st, "ogr", 2, [128, 4, 128], F32)
                rlr = Ring(P, st, "rlr", 2, [128, 4], F32)
                psS = Ring(P, st, "psS", 4, [128, 512], F32, psum=True)
                psO = Ring(P, st, "psO", 2, [128, 512], F32, psum=True)
                psR = Ring(P, st, "psR", 2, [128, 4, 128], F32, psum=True)

                eb, ekey = est.next()
                P.op("sp", lambda e, eb=eb: e.dma_start(out=eb[:, 0:640], in_=dr["ea"][:, 0:5, :].rearrange("p a b -> p (a b)")), writes=[ekey], dma=ekey)
                P.op("dve", lambda e, eb=eb: e.tensor_copy(out=eab[:].rearrange("p a b -> p (a b)"), in_=eb[:, 0:5 * 128]), reads=[ekey], writes=["eab"])
                eb, ekey = est.next()
                P.op("sp", lambda e, eb=eb: e.dma_start(out=eb[:, 0:640], in_=dr["ea"][:, 5:10, :].rearrange("p a b -> p (a b)")), writes=[ekey], dma=ekey)
                P.op("dve", lambda e, eb=eb: e.tensor_copy(out=m16[:].rearrange("p a b -> p (a b)"), in_=eb[:, 0:5 * 128]), reads=[ekey], writes=["m16"])
                for h in range(6):
                    eb, ekey = est.next()
                    P.op("sp", lambda e, eb=eb, h=h: e.dma_start(out=eb[:, 0:7 * 128], in_=dr["efraw"][l, h].rearrange("p a b -> p (a b)")), writes=[ekey], dma=ekey)
                    P.op("act", lambda e, eb=eb, h=h: e.activation(out=efb[:, h].rearrange("p a b -> p (a b)"), in_=eb[:, 0:7 * 128], func=AF.Exp),
                         reads=[ekey], writes=[f"efb{h}"])
                    P.op("dve", lambda e, h=h: e.tensor_copy(out=eib[:, h], in_=efb[:, h, 1:6, :]), reads=[f"efb{h}"], writes=[f"eib{h}"])
                    P.op("dve", lambda e, h=h: e.memset(eib[0:64, h, 0, 64:128], 0.0), reads=[f"eib{h}"], writes=[f"eib{h}"])
                    P.op("dve", lambda e, h=h: e.memset(eib[64:128, h, 4, :], 0.0), reads=[f"eib{h}"], writes=[f"eib{h}"])
                    P.op("dve", lambda e, h=h: e.memset(eib[0:64, h, 4, 0:64], 0.0), reads=[f"eib{h}"], writes=[f"eib{h}"])

                jobs = []
                for (pn, S, src) in parts:
                    for gp in range(8):
                        jobs.append(dict(pn=pn, S=S, gp=gp, dyn=(last and pn == QPN and 3 <= gp < 5), dynA=(last and pn == QPN and gp < 3), dynC=(last and pn == QPN and gp >= 5)))
                if last and QPN is not None:
                    P.op("sp", lambda e: e.dma_start(out=dr["ktc"][0:384, :], in_=dyn_ap(e, rq0, 0, dr["ktpad"], [[SQ + 2048, 384], [1, 4096]])), writes=["ktc"], dma="cq5")
                    P.op("sp", lambda e: e.dma_start(out=dr["vc"][:, 0:576], in_=dyn_ap(e, rv, 0, dr["vpad"], [[1536, 4096], [1, 576]])), writes=["vc"], dma="cq6")
                    P.op("sp", lambda e: e.dma_start(out=dr["ktc"][384:768, :], in_=dyn_ap(e, rq0, 512 * (SQ + 2048), dr["ktpad"], [[SQ + 2048, 384], [1, 4096]])), writes=["ktc2"], dma="cq7")
                    P.op("sp", lambda e: e.dma_start(out=dr["vc"][:, 576:1152], in_=dyn_ap(e, rv, 960, dr["vpad"], [[1536, 4096], [1, 576]])), writes=["vc2"], dma="cq8")

                def load_job(jb):
                    pn, S, gp = jb["pn"], jb["S"], jb["gp"]
                    NB = S // 128
                    qtd, ktd, vd = dr["qt_" + pn], dr["kt_" + pn], dr["v_" + pn]
                    if gp < 3:
                        pr = gp
                        qrow, krows = pr * 128, [pr * 128, pr * 128 + 64]
                    elif gp < 5:
                        pr = gp - 3
                        qrow, krows = 384 + pr * 128, [384 + pr * 64, 384 + pr * 64]
                    else:
                        pr = gp - 5
                        qrow, krows = 640 + pr * 128, [512 + pr * 128, 512 + pr * 128 + 64]
                    vb, vbkey = vbr.next()
                    dynA = jb.get("dynA")
                    dynC = jb.get("dynC")
                    if dynA or dynC:
                        nch = 1
                        vcol = gp * 192 if dynA else 576 + pr * 192
                        P.op("sp", lambda e: e.dma_start(out=vb[:, 0:32, :], in_=dr["vc"][:, vcol:vcol + 192].rearrange("(b p) c -> p b c", p=128)),
                             reads=["vc", "vc2"], writes=[vbkey + "_0"], dma=f"{vbkey}_0")
                    else:
                        nch = 4 if S > 2048 else 1
                        for ch in range(nch):
                            b0 = ch * (NB // nch)
                            b1 = (ch + 1) * (NB // nch)
                            P.op("sp", lambda e, b0=b0, b1=b1: e.dma_start(
                                out=vb[:, b0:b1, :], in_=vd[b0 * 128:b1 * 128, gp * 192:(gp + 1) * 192].rearrange("(b p) c -> p b c", p=128)),
                                writes=[vbkey + f"_{ch}"], dma=f"{vbkey}_{ch}")
                    jb["vb"] = (vb, vbkey, [vbkey + f"_{ch}" for ch in range(nch)])

                    qp, qpkey = qpr.next()
                    kp, kpkey = kpr.next()
                    if jb.get("dyn") or dynA or dynC:
                        P.op("sp", lambda e: e.dma_start(out=qp[:, 0:2048], in_=dyn_ap(e, rq0, qrow * S, qtd, [[S, 128], [1, 2048]])), writes=[qpkey], dma=qpkey)
                    else:
                        P.op("sp", lambda e: e.dma_start(out=qp[:, 0:S], in_=qtd[qrow:qrow + 128, :]), writes=[qpkey], dma=qpkey)
                    for hh in range(2):
                        if dynA or dynC:
                            kr0 = krows[hh] if dynA else krows[hh] - 512 + 384
                            P.op("sp", lambda e, hh=hh, kr0=kr0: e.dma_start(out=kp[hh * 64:(hh + 1) * 64, 0:4096], in_=dr["ktc"][kr0:kr0 + 64, :]),
                                 reads=["ktc", "ktc2"], writes=[kpkey + f"_{hh}"], dma=f"{kpkey}_{hh}")
                        else:
                            P.op("sp", lambda e, hh=hh: e.dma_start(out=kp[hh * 64:(hh + 1) * 64, 0:S], in_=ktd[krows[hh]:krows[hh] + 64, :]),
                                 writes=[kpkey + f"_{hh}"], dma=f"{kpkey}_{hh}")
                    jb["qp"] = (qp, qpkey)
                    jb["kp"] = (kp, kpkey)

                def load_ec(jb):
                    pr = jb["gp"] - 5
                    ecp, eckey = ecr.next()
                    for hh in range(2):
                        for ch in range(3):
                            eb, ekey = est.next()
                            P.op("sp", lambda e, eb=eb, hh=hh, ch=ch: e.dma_start(out=eb[:, 0:1024], in_=dr["cedge"][2 * pr + hh][:, ch * 1024:(ch + 1) * 1024]), writes=[ekey], dma=ekey)
                            P.op("act", lambda e, eb=eb, hh=hh, ch=ch: e.activation(out=ecp[:, hh].rearrange("p a b c -> p (a b c)")[:, ch * 1024:(ch + 1) * 1024], in_=eb[:, 0:1024], func=AF.Exp),
                                 reads=[ekey], writes=[eckey + f"_{hh}_{ch}"])
                    jb["ec"] = (ecp, [eckey + f"_{hh}_{ch}" for hh in range(2) for ch in range(3)])

                load_job(jobs[0])
                for ji, jb in enumerate(jobs):
                    if ji + 1 < len(jobs):
                        load_job(jobs[ji + 1])
                    pn, S, gp = jb["pn"], jb["S"], jb["gp"]
                    NB = S // 128
                    NW = S // 512
                    od = dr["o_" + pn]
                    if jb.get("dynC"):
                        load_ec(jb)
                    if True:
                        if gp < 3:
                            br, pr = "A", gp
                            ocol = pr * 128
                        elif gp < 5:
                            br, pr = "B", gp - 3
                            ocol = 384 + pr * 128
                        else:
                            br, pr = "C", gp - 5
                            ocol = 640 + pr * 128
                        vb, vbkey, vkeys = jb["vb"]
                        qp, qpkey = jb["qp"]
                        kp, kpkey = jb["kp"]
                        groups = []

                        def dense_groups(w):
                            for qb in range(4):
                                b = w * 4 + qb
                                if br == "A":
                                    alist = list(range(max(0, b - 2), min(NB, b + 3)))
                                    kind, off = "ea", 2
                                elif b <= 1:
                                    alist, kind, off = list(range(0, 4)), "ef", 3
                                elif b >= NB - 2:
                                    alist, kind, off = list(range(NB - 4, NB)), "ef", 3
                                else:
                                    alist, kind, off = list(range(b - 2, b + 3)), "ei", 2
                                nu = len(alist)
                                gi = 0
                                while gi < nu:
                                    gn = min(4, nu - gi)
                                    if nu - gi == 5:
                                        gn = 3
                                    us = alist[gi:gi + gn]
                                    groups.append(dict(units=[dict(k=(a * 128, 1), q=(b * 128, 1, 128), pc0=ui * 128, v=("nat", a), oc0=qb * 128,
                                                                   st=(gi + ui == 0), sp=(gi + ui == nu - 1)) for ui, a in enumerate(us)],
                                                       e=(kind, us[0] - b + off, gn, False), tail=("win" if (qb == 3 and gi + gn == nu) else None), w=w))
                                    gi += gn

                        dyn = jb.get("dyn")
                        if br == "B":
                            for w in range(4 if dyn else NW):
                                for kb in range(NB):
                                    groups.append(dict(units=[dict(k=(kb * 128, 1), q=(w * 512, 1, 512), pc0=0, v=("nat", kb), oc0=0, st=(kb == 0), sp=(kb == NB - 1))],
                                                       e=None, tail=("win" if kb == NB - 1 else None), w=w))
                        elif jb.get("dynC"):
                            for w in range(4):
                                for qb in range(4):
                                    b = w * 4 + qb
                                    if b <= 1 or b >= 14:
                                        sb = b if b <= 1 else 2 + (b - 14)
                                        a0 = 6 if b <= 1 else 20
                                        for gi in (0, 3):
                                            groups.append(dict(units=[dict(k=((a0 + gi + ui) * 128, 1), q=(b * 128, 1, 128), pc0=ui * 128, v=("nat", a0 + gi + ui), oc0=qb * 128,
                                                                           st=(gi + ui == 0), sp=(gi + ui == 5)) for ui in range(3)],
                                                               e=("ec", (sb, gi), 3, False), tail=("win" if (qb == 3 and gi == 3) else None), w=w))
                                    else:
                                        alist = [b + 8 + d_ for d_ in range(-2, 3)]
                                        for (gi, gn) in ((0, 3), (3, 2)):
                                            us = alist[gi:gi + gn]
                                            groups.append(dict(units=[dict(k=(a * 128, 1), q=(b * 128, 1, 128), pc0=ui * 128, v=("nat", a), oc0=qb * 128,
                                                                           st=(gi + ui == 0), sp=(gi + ui == 4)) for ui, a in enumerate(us)],
                                                               e=("ei", gi, gn, False), tail=("win" if (qb == 3 and gi == 3) else None), w=w))
                        elif br == "C":
                            for w in range(NW):
                                dense_groups(w)
                        elif jb.get("dynA"):
                            for rq in range(4):
                                for ri in range(4):
                                    r = 4 * rq + ri
                                    units = [dict(k=(jj * 2048 + r, 16), q=(r, 16, 128), pc0=jj * 128, v=("v16", r * 2 + jj), oc0=ri * 128, st=(jj == 0), sp=(jj == 1)) for jj in range(2)]
                                    groups.append(dict(units=units, e=("m16", 3, 2, False), tail=("quad" if ri == 3 else None), sw=0, rq=rq))
                            for w in range(4):
                                for qb in range(4):
                                    b = w * 4 + qb
                                    alist = [b + 8 + d_ for d_ in range(-2, 3)]
                                    for (gi, gn) in ((0, 3), (3, 2)):
                                        us = alist[gi:gi + gn]
                                        groups.append(dict(units=[dict(k=(a * 128, 1), q=(b * 128, 1, 128), pc0=ui * 128, v=("nat", a), oc0=qb * 128,
                                                                       st=(gi + ui == 0), sp=(gi + ui == 4)) for ui, a in enumerate(us)],
                                                           e=("ea", gi, gn, False), tail=("win" if (qb == 3 and gi == 3) else None), w=w))
                        else:
                            nj = NB // 16
                            for sw in range(nj):
                                for rq in range(4):
                                    ulist = [u for u in (-1, 0, 1) if 0 <= sw + u < nj]
                                    if len(ulist) == 1:
                                        units = []
                                        for ri in range(4):
                                            r = 4 * rq + ri
                                            units.append(dict(k=(sw * 2048 + r, 16), q=(sw * 2048 + r, 16, 128), pc0=ri * 128, v=("v16", r * nj + sw), oc0=ri * 128, st=True, sp=True))
                                        groups.append(dict(units=units, e=("m16", 1, 4, True), tail="quad", sw=sw, rq=rq))
                                    else:
                                        for ri in range(4):
                                            r = 4 * rq + ri
                                            units = []
                                            for ui, u in enumerate(ulist):
                                                units.append(dict(k=((sw + u) * 2048 + r, 16), q=(sw * 2048 + r, 16, 128), pc0=ui * 128, v=("v16", r * nj + sw + u), oc0=ri * 128,
                                                                  st=(ui == 0), sp=(ui == len(ulist) - 1)))
                                            groups.append(dict(units=units, e=("m16", ulist[0] + 1, len(ulist), False), tail=("quad" if ri == 3 else None), sw=sw, rq=rq))
                                for w in range(sw * 4, sw * 4 + 4):
                                    dense_groups(w)
                        ng = len(groups)
                        pos = [psO.next(), psO.next()]
                        state = {}
                        v16 = None
                        def load_v16(jbx):
                            pnx, Sx, gpx = jbx["pn"], jbx["S"], jbx["gp"]
                            v16b, v16key = v16r.next()
                            if jbx.get("dynA"):
                                nj_ = 2
                                vsrc = dr["vc"][:, gpx * 192:(gpx + 1) * 192].rearrange("(jj i r) c -> r i jj c", i=128, r=16)
                                v16reads = ["vc"]
                            else:
                                nj_ = (Sx // 128) // 16
                                vsrc = dr["v_" + pnx][:, gpx * 192:(gpx + 1) * 192].rearrange("(jj i r) c -> r i jj c", i=128, r=16)
                                v16reads = []
                            for r in range(16):
                                P.op("sp", lambda e, r=r, v16b=v16b, vsrc=vsrc, nj_=nj_: e.dma_start(out=v16b[:, r * nj_:(r + 1) * nj_, :], in_=vsrc[r]),
                                     reads=v16reads, writes=[v16key + f"_{r}"], dma=f"{v16key}_{r}")
                            jbx["v16"] = (v16b, [v16key + f"_{r}" for r in range(16)])

                        if br == "A":
                            if "v16" not in jb:
                                load_v16(jb)
                            v16 = jb["v16"]
                        nxt = jobs[ji + 1] if ji + 1 < len(jobs) else None
                        pre_at = None
                        if br == "A" and nxt is not None and nxt["gp"] < 3:
                            lastpat = max(gi_ for gi_, g_ in enumerate(groups) if g_["units"][0]["v"][0] == "v16")
                            pre_at = lastpat + 2

                        def sl(start, stride, n):
                            return slice(start, start + (n - 1) * stride + 1, stride) if stride != 1 else slice(start, start + n)

                        def emit_front2(g, qp=qp, kp=kp, qpkey=qpkey, kpkey=kpkey, pr=pr, jb=jb):
                            pss = [psS.next(), psS.next()]
                            ncols = 0
                            for u in g["units"]:
                                ks, kst = u["k"]
                                qs, qst, n = u["q"]
                                pc0 = u["pc0"]
                                for hh in range(2):
                                    ps, pskey = pss[hh]
                                    P.op("pe", lambda e, ps=ps, ks=ks, kst=kst, qs=qs, qst=qst, n=n, pc0=pc0, hh=hh: e.matmul(
                                        ps[:, pc0:pc0 + n], lhsT=kp[hh * 64:(hh + 1) * 64, sl(ks, kst, 128)], rhs=qp[hh * 64:(hh + 1) * 64, sl(qs, qst, n)], start=True, stop=True),
                                        reads=[qpkey, kpkey + f"_{hh}"], writes=[pskey])
                                ncols = max(ncols, pc0 + n)
                            for hh in range(2):
                                ps, pskey = pss[hh]
                                pt, ptkey = ptr.next()
                                P.op("act", lambda e, ps=ps, pt=pt, ncols=ncols: e.activation(out=pt[:, 0:ncols], in_=ps[:, 0:ncols], func=AF.Exp, scale=0.125),
                                     reads=[pskey], writes=[ptkey])
                                if g["e"] is not None:
                                    kind, i0_, gn, bc = g["e"]
                                    hglob = 2 * pr + hh
                                    if kind == "ea":
                                        e_ap, ekeys = eab[:, i0_:i0_ + gn, :], ["eab"]
                                    elif kind == "m16":
                                        if bc:
                                            e_ap, ekeys = m16[:, i0_:i0_ + 1, :].to_broadcast([128, gn, 128]), ["m16"]
                                        else:
                                            e_ap, ekeys = m16[:, i0_:i0_ + gn, :], ["m16"]
                                    elif kind == "ec":
                                        ecp_, eckeys_ = jb["ec"]
                                        e_ap, ekeys = ecp_[:, hh, i0_[0], i0_[1]:i0_[1] + gn, :], eckeys_
                                    elif kind == "ef":
                                        e_ap, ekeys = efb[:, hglob, i0_:i0_ + gn, :], [f"efb{hglob}"]
                                    else:
                                        e_ap, ekeys = eib[:, hglob, i0_:i0_ + gn, :], [f"eib{hglob}"]
                                    P.op("dve", lambda e, pt=pt, e_ap=e_ap, gn=gn: e.tensor_tensor(
                                        out=pt[:, 0:gn * 128].rearrange("p (a b) -> p a b", b=128), in0=pt[:, 0:gn * 128].rearrange("p (a b) -> p a b", b=128), in1=e_ap, op=ALU.mult),
                                        reads=[ptkey] + ekeys, writes=[ptkey])
                                g["pt%d" % hh] = (pt, ptkey)

                        def emit_pv2(g, vb=vb, vkeys=vkeys):
                            for u in g["units"]:
                                vkind, vblk = u["v"]
                                pc0, oc0, st_, sp_ = u["pc0"], u["oc0"], u["st"], u["sp"]
                                n = u["q"][2]
                                if vkind == "nat":
                                    vt, vks = vb, vkeys
                                else:
                                    vt, vks = v16[0], v16[1]
                                for hh in range(2):
                                    pt, ptkey = g["pt%d" % hh]
                                    po, pokey = pos[hh]
                                    P.op("pe", lambda e, po=po, vt=vt, vblk=vblk, pc0=pc0, n=n, oc0=oc0, st_=st_, sp_=sp_, pt=pt, hh=hh: e.matmul(
                                        po[:, oc0:oc0 + n], lhsT=vt[:, vblk, hh * 64:hh * 64 + 128], rhs=pt[:, pc0:pc0 + n], start=st_, stop=sp_),
                                        reads=[ptkey] + vks, writes=[pokey])

                        def emit_back(g, hh, ocol=ocol, od=od, br=br, dyn=dyn, dynA=jb.get("dynA"), dynC=jb.get("dynC")):
                            po, pokey = pos[hh]
                            if g["tail"] == "quad":
                                rq = g["rq"]
                                ac, ackey = accs[hh]
                                dst = ac[:].rearrange("p (l r) -> p r l", r=16)[:, 4 * rq:4 * rq + 4, :]
                                P.op("act", lambda e, po=po, dst=dst: e.activation(out=dst, in_=po[:].rearrange("p (a b) -> p a b", b=128), func=AF.Copy),
                                     reads=[pokey], writes=[ackey + f"_{rq}"])
                            if g["tail"] == "win":
                                osb, oskey = osr.next()
                                w = g["w"]
                                if br == "A":
                                    ac, ackey = accs[hh]
                                    wl = (w % 4) * 512
                                    P.op("dve", lambda e, osb=osb, po=po, ac=ac, wl=wl: e.tensor_tensor(out=osb[:], in0=po[:], in1=ac[:, wl:wl + 512], op=ALU.add),
                                         reads=[pokey] + [ackey + f"_{q}" for q in range(4)], writes=[oskey])
                                else:
                                    P.op("act", lambda e, osb=osb, po=po: e.activation(out=osb[:], in_=po[:], func=AF.Copy), reads=[pokey], writes=[oskey])
                                prr, prkey = psR.next()
                                for jj in range(4):
                                    P.op("pe", lambda e, prr=prr, osb=osb, jj=jj: e.transpose(out=prr[:, jj, :], in_=osb[:, jj * 128:(jj + 1) * 128], identity=identf[:]),
                                         reads=[oskey, "identf"], writes=[prkey])
                                rl, rlkey = rlr.next()
                                lcol = 64 if hh == 0 else 0
                                P.op("dve", lambda e, rl=rl, prr=prr, lcol=lcol: e.reciprocal(out=rl[:], in_=prr[:, :, lcol]), reads=[prkey], writes=[rlkey])
                                if hh == 0:
                                    state["og"] = ogr.next()
                                og, ogkey = state["og"]
                                P.op("dve", lambda e, og=og, prr=prr, rl=rl, hh=hh: e.tensor_tensor(
                                    out=og[:, :, hh * 64:(hh + 1) * 64], in0=prr[:, :, hh * 64:(hh + 1) * 64], in1=rl[:].unsqueeze(2).to_broadcast([128, 4, 64]), op=ALU.mult),
                                    reads=[prkey, rlkey], writes=[ogkey + f"_{hh}"])
                                if hh == 1 and dynC:
                                    P.op("sp", lambda e, og=og, w=w: e.dma_start(
                                        out=dr["ocq"][w * 512:(w + 1) * 512, ocol - 640:ocol - 640 + 128].rearrange("(b p) c -> p b c", p=128), in_=og[:]),
                                        reads=[ogkey + "_0", ogkey + "_1"], dma=ogkey)
                                elif hh == 1 and dynA:
                                    P.op("sp", lambda e, og=og, w=w: e.dma_start(
                                        out=dr["oaq"][w * 512:(w + 1) * 512, ocol:ocol + 128].rearrange("(b p) c -> p b c", p=128), in_=og[:]),
                                        reads=[ogkey + "_0", ogkey + "_1"], dma=ogkey)
                                elif hh == 1 and dyn:
                                    P.op("sp", lambda e, og=og, w=w: e.dma_start(
                                        out=dr["obq"][w * 512:(w + 1) * 512, ocol - 384:ocol - 384 + 128].rearrange("(b p) c -> p b c", p=128), in_=og[:]),
                                        reads=[ogkey + "_0", ogkey + "_1"], dma=ogkey)
                                elif hh == 1:
                                    P.op("sp", lambda e, og=og, w=w: e.dma_start(
                                        out=od[w * 512:(w + 1) * 512, ocol:ocol + 128].rearrange("(b p) c -> p b c", p=128), in_=og[:]),
                                        reads=[ogkey + "_0", ogkey + "_1"], dma=ogkey)

                        LOOK = 1
                        for i in range(ng + LOOK):
                            if pre_at is not None and i == pre_at:
                                load_v16(nxt)
                            if i < ng:
                                emit_front2(groups[i])
                            if i - LOOK >= 0:
                                emit_pv2(groups[i - LOOK])
                                emit_back(groups[i - LOOK], 0)
                                emit_back(groups[i - LOOK], 1)
                P.flush(final=(STOP == "s2"))
            if STOP == "s2":
                return nc

            with ExitStack() as st:
                wo = st.enter_context(nc.sbuf_tensor(U("wo"), [128, 8, D], BF16))
                wst3 = Ring(P, st, "wst3", 2, [128, 8, 256], F32)
                gbr = st.enter_context(nc.sbuf_tensor(U("gbr"), [128, D], F32))
                gpo = st.enter_context(nc.sbuf_tensor(U("gpo"), [128, D], F32))
                invw = st.enter_context(nc.sbuf_tensor(U("invw"), [128, 3], F32))
                o_r = Ring(P, st, "o_r", 4, [128, D], F32)
                z_r = Ring(P, st, "z_r", 4, [128, D // 2], F32)
                x_r = Ring(P, st, "x_r", 5, [128, D], F32)
                g_r = Ring(P, st, "g_r", 6, [128, D], F32)
                pc_r = Ring(P, st, "pc_r", 6, [128, D], F32)
                junk3 = st.enter_context(nc.sbuf_tensor(U("junk3"), [128, D], BF16))
                s3r = Ring(P, st, "s3r", 6, [128, 3], F32)
                s2r = Ring(P, st, "s2r", 6, [128, 2], F32)
                ybr = Ring(P, st, "ybr", 3, [128, D], BF16)
                yTr = Ring(P, st, "yTr", 3, [128, 8, 128], BF16)
                t_r = Ring(P, st, "t_r", 5, [128, D], F32)
                psT3 = Ring(P, st, "psT3", 2, [128, 8, 128], BF16, psum=True)
                psY = Ring(P, st, "psY", 3, [128, D], F32, psum=True)

                for ci in range(4):
                    wbuf, wkey = wst3.next()
                    c0 = ci * 256
                    P.op("sp", lambda e, wbuf=wbuf, c0=c0: e.dma_start(out=wbuf[:], in_=dr["w_out"][l][:, c0:c0 + 256].rearrange("(k p) c -> p k c", p=128)),
                         writes=[wkey], dma=wkey)
                    P.op("pool" if ci % 2 == 0 else "dve", lambda e, wbuf=wbuf, c0=c0: e.tensor_copy(out=wo[:, :, c0:c0 + 256], in_=wbuf[:]), reads=[wkey], writes=[f"wo{ci}"])
                WO = [f"wo{ci}" for ci in range(4)]
                P.op("sp", lambda e: e.dma_start(out=gbr[:], in_=dr["branch_gain"][l].partition_broadcast(128), allow_slow_non_contiguous=True), writes=["gbr"], dma="c5")
                P.op("sp", lambda e: e.dma_start(out=gpo[:], in_=dr["norm_post"][l].partition_broadcast(128), allow_slow_non_contiguous=True), writes=["gpo"], dma="c6")
                P.op("dve", lambda e: e.memset(invw[:, 0:1], 1.0 / 384), writes=["invw"])
                P.op("dve", lambda e: e.memset(invw[:, 1:2], 1.0 / 256), writes=["invw"])
                P.op("dve", lambda e: e.memset(invw[:, 2:3], 1.0 / 384), writes=["invw"])
                BR = ((0, 384), (384, 640), (640, 1024))
                blocks3 = []
                qblocks = []
                for (pn, S, src) in parts:
                    xsrc = dr["x_" + pn] if l == 0 else dr["y1_" + pn]
                    if last and pn == QPN:
                        P.op("sp", lambda e: e.dma_start(out=dr["oq"][:, 0:384], in_=dr["oaq"]), writes=["oq"], dma="cq0")
                        P.op("sp", lambda e: e.dma_start(out=dr["oq"][:, 640:1024], in_=dr["ocq"]), reads=["oq"], writes=["oq"], dma="cq4")
                        P.op("sp", lambda e, pn=pn: e.dma_start(out=dr["szq"], in_=dyn_ap(e, rrow2, 0, dr["sz_" + pn], [[D // 2, 2048], [1, D // 2]])), writes=["szq"], dma="cq1")
                        P.op("sp", lambda e, xsrc=xsrc: e.dma_start(out=dr["xq"], in_=dyn_ap(e, rrow, 0, xsrc, [[D, 2048], [1, D]])), writes=["xq"], dma="cq2")
                        P.op("sp", lambda e: e.dma_start(out=dr["oq"][:, 384:640], in_=dr["obq"]), reads=["oq"], writes=["oq"], dma="cq3")
                        qblocks = [dict(pn=pn, t0=blk * 128, osrc=dr["oq"], zsrc=dr["szq"], xsrc=dr["xq"], ydst=dr["yq_" + pn], keys=["oq", "szq", "xq"]) for blk in range(16)]
                        continue
                    ydst = dr["y_" + pn] if last else dr["y1_" + pn]
                    for blk in range(S // 128):
                        blocks3.append(dict(pn=pn, t0=blk * 128, osrc=dr["o_" + pn], zsrc=dr["sz_" + pn], xsrc=xsrc, ydst=ydst, keys=[]))
                blocks3 = blocks3 + qblocks

                def p_load(c):
                    pn, t0 = c["pn"], c["t0"]
                    ob, okey = o_r.next()
                    zb, zkey = z_r.next()
                    c["o"], c["z"] = (ob, okey), (zb, zkey)
                    osrc, zsrc = c["osrc"], c["zsrc"]
                    P.op("sp", lambda e: e.dma_start(out=ob[:], in_=osrc[t0:t0 + 128, :]), reads=c["keys"][0:1], writes=[okey], dma=okey)
                    P.op("sp", lambda e: e.dma_start(out=zb[:], in_=zsrc[t0:t0 + 128, :]), reads=c["keys"][1:2], writes=[zkey], dma=zkey)

                def p_g(c):
                    ob, okey = c["o"]
                    zb, zkey = c["z"]
                    gb, gkey = g_r.next()
                    c["g"] = (gb, gkey)
                    P.op("pool", lambda e: e.tensor_tensor(out=gb[:], in0=ob[:], in1=zb[:].bitcast(BF16), op=ALU.mult), reads=[okey, zkey], writes=[gkey])

                def p_sq(c):
                    gb, gkey = c["g"]
                    s3, s3key = s3r.next()
                    c["s3"] = (s3, s3key)
                    for bi, (c0, c1) in enumerate(BR):
                        P.op("act", lambda e, bi=bi, c0=c0, c1=c1: e.activation(out=junk3[:, c0:c1], in_=gb[:, c0:c1], func=AF.Square, accum_out=s3[:, bi:bi + 1]),
                             reads=[gkey], writes=[s3key, "junk3"])

                def p_r1(c):
                    s3, s3key = c["s3"]
                    P.op("dve", lambda e: e.tensor_tensor(out=s3[:], in0=s3[:], in1=invw[:], op=ALU.mult), reads=[s3key, "invw"], writes=[s3key])
                    P.op("dve", lambda e: e.tensor_scalar(out=s3[:], in0=s3[:], scalar1=1.0, scalar2=float(EPS), op0=ALU.mult, op1=ALU.add), reads=[s3key], writes=[s3key])

                def p_r2(c):
                    s3, s3key = c["s3"]
                    P.op("pool", lambda e: e.tensor_tensor(out=s3[:], in0=s3[:], in1=nhalf[:, 0:3], op=ALU.pow), reads=[s3key], writes=[s3key])

                def p_y(c):
                    gb, gkey = c["g"]
                    s3, s3key = c["s3"]
                    yb, ykey = ybr.next()
                    c["y"] = (yb, ykey)
                    for bi, (c0, c1) in enumerate(BR):
                        P.op("dve", lambda e, bi=bi, c0=c0, c1=c1: e.scalar_tensor_tensor(
                            out=yb[:, c0:c1], in0=gb[:, c0:c1], scalar=s3[:, bi:bi + 1], in1=gbr[:, c0:c1], op0=ALU.mult, op1=ALU.mult),
                            reads=[gkey, s3key, "gbr"], writes=[ykey])

                def p_T(c):
                    yb, ykey = c["y"]
                    pT, pTkey = psT3.next()
                    c["pT"] = (pT, pTkey)
                    for kc in range(8):
                        P.op("pe", lambda e, kc=kc: e.transpose(out=pT[:, kc, :], in_=yb[:, kc * 128:(kc + 1) * 128], identity=identb[:]),
                             reads=[ykey, "identb"], writes=[pTkey])

                def p_yT(c):
                    pT, pTkey = c["pT"]
                    yT, yTkey = yTr.next()
                    c["yT"] = (yT, yTkey)
                    P.op("act", lambda e: e.activation(out=yT[:], in_=pT[:], func=AF.Copy), reads=[pTkey], writes=[yTkey])

                def p_mm(c):
                    yT, yTkey = c["yT"]
                    py, pykey = psY.next()
                    c["py"] = (py, pykey)
                    for n in range(2):
                        for kc in range(8):
                            P.op("pe", lambda e, n=n, kc=kc: e.matmul(py[:, n * 512:(n + 1) * 512], lhsT=yT[:, kc, :], rhs=wo[:, kc, n * 512:(n + 1) * 512],
                                                                      start=(kc == 0), stop=(kc == 7)),
                                 reads=[yTkey] + WO, writes=[pykey])

                def p_ev(c):
                    py, pykey = c["py"]
                    s2, s2key = s2r.next()
                    c["s2"] = (s2, s2key)
                    pc, pckey = pc_r.next()
                    c["pc"] = (pc, pckey)
                    for n in range(2):
                        P.op("act", lambda e, n=n: e.activation(out=junk3[:, n * 512:(n + 1) * 512], in_=py[:, n * 512:(n + 1) * 512], func=AF.Square, accum_out=s2[:, n:n + 1]),
                             reads=[pykey], writes=[s2key, "junk3"])
                    P.op("act", lambda e: e.activation(out=pc[:], in_=py[:], func=AF.Copy), reads=[pykey], writes=[pckey])

                def p_r3(c):
                    s2, s2key = c["s2"]
                    P.op("dve", lambda e: e.tensor_tensor(out=s2[:, 0:1], in0=s2[:, 0:1], in1=s2[:, 1:2], op=ALU.add), reads=[s2key], writes=[s2key])
                    P.op("dve", lambda e: e.tensor_scalar(out=s2[:, 0:1], in0=s2[:, 0:1], scalar1=1.0 / D, scalar2=float(EPS), op0=ALU.mult, op1=ALU.add), reads=[s2key], writes=[s2key])

                def p_r4(c):
                    s2, s2key = c["s2"]
                    P.op("pool", lambda e: e.tensor_tensor(out=s2[:, 0:1], in0=s2[:, 0:1], in1=nhalf[:, 0:1], op=ALU.pow), reads=[s2key], writes=[s2key])
                    t0, xsrc = c["t0"], c["xsrc"]
                    xb, xkey = x_r.next()
                    c["x"] = (xb, xkey)
                    P.op("sp", lambda e: e.dma_start(out=xb[:], in_=xsrc[t0:t0 + 128, :]), reads=c["keys"][2:3], writes=[xkey], dma=xkey)

                def p_stt(c):
                    pc, pckey = c["pc"]
                    s2, s2key = c["s2"]
                    tb, tkey = t_r.next()
                    c["t"] = (tb, tkey)
                    P.op("dve", lambda e: e.scalar_tensor_tensor(out=tb[:], in0=pc[:], scalar=s2[:, 0:1], in1=gpo[:], op0=ALU.mult, op1=ALU.mult),
                         reads=[pckey, s2key, "gpo"], writes=[tkey])

                def p_add(c):
                    tb, tkey = c["t"]
                    xb, xkey = c["x"]
                    P.op("pool", lambda e: e.tensor_tensor(out=tb[:], in0=tb[:], in1=xb[:], op=ALU.add), reads=[tkey, xkey], writes=[tkey])

                def p_st(c):
                    tb, tkey = c["t"]
                    t0, ydst = c["t0"], c["ydst"]
                    P.op("sp", lambda e: e.dma_start(out=ydst[t0:t0 + 128, :], in_=tb[:]), reads=[tkey], dma=tkey)

                phases = [(p_load, 0), (p_g, 2), (p_sq, 3), (p_r1, 4), (p_r2, 5), (p_y, 6), (p_T, 7), (p_yT, 8), (p_mm, 9), (p_ev, 10),
                          (p_r3, 11), (p_r4, 12), (p_stt, 14), (p_add, 15), (p_st, 17)]
                nb3 = len(blocks3)
                for i in range(nb3 + 18):
                    for fn, dly in phases:
                        if 0 <= i - dly < nb3:
                            fn(blocks3[i - dly])
                P.flush(final=last)
        print(f"[build] instructions: {P.n_ins}", flush=True)
    return nc


_PARTS = [("p", 8192, "xp"), ("s0", 2048, "xs0"), ("s1", 2048, "xs1")]


def kernel(x_prompt, x_sample, norm_pre, w_in, q_norm, k_norm, rel_bias, branch_gain, w_out, norm_post):
    f = lambda a: np.ascontiguousarray(np.asarray(a, dtype=np.float32))
    x_prompt, x_sample = f(x_prompt), f(x_sample)
    ropea, ropeb, ea, ident = _const_tables()
    efraw = _c_bias_tables(f(rel_bias))
    nc = build(_PARTS, 2)
    shared = dict(w_in=f(w_in), w_out=f(w_out), norm_pre=f(norm_pre), norm_post=f(norm_post), branch_gain=f(branch_gain),
                  q_norm=f(q_norm), k_norm=f(k_norm), ropea=ropea, ropeb=ropeb, ea=ea, ident=ident, efraw=efraw)
    in_maps = []
    for c in range(8):
        m = dict(shared)
        q0 = (c % 4) * 2048
        m["qoff"] = np.array([[q0, q0 * D, q0 * (D // 2), q0 * 1536]], dtype=np.int32)
        m["cedge"] = np.ascontiguousarray(_c_edge_tables(efraw[-1], c % 4).reshape(6, 128, 24 * 128))
        m["x_p"] = x_prompt[c // 4]
        m["x_s0"] = x_sample[2 * c]
        m["x_s1"] = x_sample[2 * c + 1]
        in_maps.append(m)
    res = run_bass_kernel_spmd(nc, in_maps, core_ids=list(range(8)))
    r = res.results
    y_prompt = np.stack([np.concatenate([np.asarray(r[4 * b + q]["yq_p"], dtype=np.float32) for q in range(4)], axis=0) for b in range(2)], axis=0)
    ys = []
    for c in range(8):
        ys.append(np.asarray(r[c]["y_s0"], dtype=np.float32))
        ys.append(np.asarray(r[c]["y_s1"], dtype=np.float32))
    y_sample = np.stack(ys, axis=0)
    return (y_prompt, y_sample)
```

```python
import numpy as np
from contextlib import ExitStack
import concourse.bass as bass
import concourse.mybir as mybir
from concourse.bass_utils import run_bass_kernel_spmd

F32 = mybir.dt.float32
BF16 = mybir.dt.bfloat16
AF = mybir.ActivationFunctionType
ALU = mybir.AluOpType
AX = mybir.AxisListType

D = 1024
INW = 3840
EPS = 1e-6
NEG = -30000.0
STOP = None
QUARTER = True
LIMIT = None
AQ, AK, AV, AZ, BQ, BK, BV, BZ, CQ, CK, CV, CZ = 0, 384, 768, 1152, 1536, 1792, 1920, 2048, 2304, 2688, 3072, 3456

COMPUTE = ("pe", "act", "dve", "pool")
ISSUERS = ("pe", "act", "dve", "pool", "sp")
ENGOBJ = {"pe": "tensor", "act": "scalar", "dve": "vector", "pool": "gpsimd", "sp": "sync"}


class Op:
    __slots__ = ("eng", "fn", "is_dma", "sem", "ticket", "signal", "waits")

    def __init__(self, eng, fn, is_dma, sem):
        self.eng = eng
        self.fn = fn
        self.is_dma = is_dma
        self.sem = sem
        self.ticket = None
        self.signal = is_dma
        self.waits = []


class Prog:
    def __init__(self, nc, stack, block):
        self.nc = nc
        self.stack = stack
        self.block = block
        self.streams = {e: [] for e in ISSUERS}
        self.esem = {e: self.new_sem("sem_" + e) for e in COMPUTE}
        self.bar = self.new_sem("sem_bar")
        self.bar_count = 0
        self.ecount = {e: 0 for e in COMPUTE}
        self.last_w = {}
        self.readers = {}
        self.dma_sems = {}
        self.dma_count = {}
        self.pending = None
        self.alias = {}
        self.n_ins = 0

    def new_sem(self, name):
        return self.stack.enter_context(self.nc.semaphore(name))

    def _dep(self, op, prod):
        if prod is None or prod is op:
            return
        if (not op.is_dma) and (not prod.is_dma) and prod.eng == op.eng and op.eng == "pe":
            return
        prod.signal = True
        op.waits.append(prod)

    def op(self, eng, fn, reads=(), writes=(), dma=None):
        self.nrec = getattr(self, "nrec", 0) + 1
        if LIMIT is not None and self.nrec > LIMIT:
            return None
        is_dma = dma is not None
        ps_reads = [r for r in reads if r.startswith("ps")]
        if ps_reads:
            reads = [r for r in reads if not r.startswith("ps")]
            writes = list(writes) + ps_reads
        o = Op(eng, fn, is_dma, dma)
        if is_dma:
            if dma not in self.alias:
                self.alias[dma] = f"g{len(self.alias)}"
            dma = self.alias[dma]
            o.sem = dma
            if dma not in self.dma_sems:
                self.dma_sems[dma] = self.new_sem("d_" + dma)
                self.dma_count[dma] = 0
            self.dma_count[dma] += 16
            o.ticket = self.dma_count[dma]
        for r in reads:
            self._dep(o, self.last_w.get(r))
        for w in writes:
            self._dep(o, self.last_w.get(w))
            for rd in self.readers.get(w, ()):
                self._dep(o, rd)
        for r in reads:
            self.readers.setdefault(r, []).append(o)
        for w in writes:
            self.last_w[w] = o
            self.readers[w] = []
        self.streams[eng].append(o)
        return o

    def flush(self, final=False):
        for e in COMPUTE:
            ops = [o for o in self.streams[e] if not o.is_dma]
            if ops:
                ops[-1].signal = True
            for o in ops:
                if o.signal:
                    self.ecount[e] += 1
                    o.ticket = self.ecount[e]
        pending = self.pending
        dma_final = dict(self.dma_count)
        self.bar_count += 1
        bar_val = self.bar_count
        ecount = dict(self.ecount)

        def make(ename):
            ops = self.streams[ename]

            def body(eng):
                waited = {}
                if pending is not None:
                    for key, val in pending.items():
                        if val > 0:
                            sem = self.bar if key == "bar" else self.esem[key]
                            if key != ename:
                                eng.wait_ge(sem, val)
                for o in ops:
                    need = {}
                    for p in o.waits:
                        key = ("d", p.sem) if p.is_dma else ("e", p.eng)
                        if p.ticket > need.get(key, 0):
                            need[key] = p.ticket
                    for key, val in need.items():
                        if waited.get(key, 0) >= val:
                            continue
                        waited[key] = val
                        sem = self.dma_sems[key[1]] if key[0] == "d" else self.esem[key[1]]
                        eng.wait_ge(sem, val)
                    ins = o.fn(eng)
                    self.n_ins += 1
                    if o.is_dma:
                        ins.then_inc(self.dma_sems[o.sem], 16)
                    elif o.signal:
                        ins.then_inc(self.esem[o.eng], 1)
                if ename == "sp":
                    for s, v in dma_final.items():
                        if v > 0:
                            eng.wait_ge(self.dma_sems[s], v)
                    eng.sem_inc(self.bar, 1)

            return body

        for ename in ISSUERS:
            if ename != "sp" and not self.streams[ename] and pending is None:
                continue
            getattr(self.block, ENGOBJ[ename])(make(ename))
        self.pending = dict(ecount)
        self.pending["bar"] = bar_val
        self.streams = {e: [] for e in ISSUERS}
        self.last_w = {}
        self.readers = {}
        self.alias = {}
        if final:
            pend = self.pending

            def fin(eng):
                eng.wait_ge(self.bar, pend["bar"])

            for ename in ("pe", "act", "dve", "pool"):
                getattr(self.block, ENGOBJ[ename])(fin)


_UID = [0]


def U(name):
    _UID[0] += 1
    return f"{name}_u{_UID[0]}"


class Ring:
    def __init__(self, P, st, name, n, shape, dt, psum=False):
        self.n = n
        self.name = name
        self.i = -1
        alloc = P.nc.psum_tensor if psum else P.nc.sbuf_tensor
        self.bufs = [st.enter_context(alloc(U(f"{name}{k}"), list(shape), dt)) for k in range(n)]

    def next(self):
        self.i += 1
        k = self.i % self.n
        return self.bufs[k], f"{self.name}{k}"

    def cur(self):
        k = self.i % self.n
        return self.bufs[k], f"{self.name}{k}"


def _const_tables():
    SMAX = 8192
    t = np.arange(SMAX, dtype=np.float32)
    fa = (500000.0 ** (-np.arange(0, 16, 2, dtype=np.float32) / 16)).astype(np.float32)
    anga = (t[:, None] * fa[None, :]).astype(np.float32).astype(np.float64)
    fb = (10000.0 ** (-np.arange(0, 32, 2, dtype=np.float32) / 32)).astype(np.float32)
    row = (np.arange(SMAX) // 64).astype(np.float32)
    col = (np.arange(SMAX) % 64).astype(np.float32)
    angr = (row[:, None] * fb[None, :]).astype(np.float32).astype(np.float64)
    angc = (col[:, None] * fb[None, :]).astype(np.float32).astype(np.float64)
    angb = np.concatenate([angr, angc], axis=1)

    def tm(a):
        return np.ascontiguousarray(a.reshape(SMAX // 128, 128, -1).transpose(1, 0, 2)).astype(np.float32)

    ropea = np.stack([tm(np.cos(anga)), tm(np.sin(anga))], axis=1)
    ropeb = np.stack([tm(np.cos(angb)), tm(np.sin(angb))], axis=1)
    kk = np.arange(128)[:, None, None]
    dl = (np.arange(5) - 2)[None, :, None]
    ii = np.arange(128)[None, None, :]
    dd = 128 * dl + kk - ii
    ad = np.abs(dd)
    mult = (ad <= 64).astype(np.float32) + ((dd % 4 == 0) & (ad <= 256))
    du = 128 * (np.arange(3) - 1)[None, :, None] + kk - ii
    m16 = (np.abs(du) <= 64).astype(np.float32)
    kk2 = np.arange(128)[:, None]
    ii2 = np.arange(128)[None, :]
    m16q = np.stack([(kk2 >= ii2), (kk2 <= ii2)], axis=1).astype(np.float32)
    ea = np.ascontiguousarray(np.concatenate([mult, m16, m16q], axis=1).astype(np.float32))
    ident = np.eye(128, dtype=np.float32)
    return ropea, ropeb, ea, ident


def _c_bias_tables(rel_bias):
    L = rel_bias.shape[0]
    krl = (np.arange(128) // 64)[:, None, None]
    kc = (np.arange(128) % 64)[:, None, None]
    dlt = (np.arange(7) - 3)[None, :, None]
    rl = (np.arange(128) // 64)[None, None, :]
    qc = (np.arange(128) % 64)[None, None, :]
    dr = 2 * dlt + krl - rl
    ro = dr + 7
    co = np.clip(kc - qc + 15, 0, 30)
    cs = np.clip(qc - 8, 0, 48)
    valid = (kc >= cs) & (kc < cs + 16) & (ro >= 0) & (ro <= 14)
    ro_c = np.clip(ro, 0, 14)
    ro_b, co_b, valid_b = np.broadcast_arrays(ro_c, co, valid)
    out = np.empty((L, 6, 128, 7, 128), dtype=np.float32)
    for l in range(L):
        for h in range(6):
            g = rel_bias[l, h][ro_b, co_b]
            out[l, h] = np.where(valid_b, g, np.float32(NEG))
    return out


def _c_edge_tables(efraw_l, qd):
    negt = np.full((128, 128), np.float32(NEG), dtype=np.float32)

    def interior_tile(h, dl):
        if dl < -2 or dl > 2:
            return negt
        t = efraw_l[h][:, dl + 3, :].copy()
        if dl == -2:
            t[0:64, 64:128] = NEG
        if dl == 2:
            t[64:128, :] = NEG
            t[0:64, 0:64] = NEG
        return t

    def full_tile(h, dl):
        if dl < -3 or dl > 3:
            return negt
        return efraw_l[h][:, dl + 3, :]

    out = np.empty((6, 128, 4, 6, 128), dtype=np.float32)
    for h in range(6):
        for side in range(2):
            for bi in range(2):
                for sl_ in range(6):
                    if side == 0:
                        b, a_own = bi, sl_ - 2
                        edge, ok = (qd == 0), (0 <= a_own <= 3)
                    else:
                        b, a_own = 14 + bi, 12 + sl_
                        edge, ok = (qd == 3), (12 <= a_own <= 15)
                    if edge:
                        t = full_tile(h, a_own - b) if ok else negt
                    else:
                        t = interior_tile(h, a_own - b)
                    out[h, :, side * 2 + bi, sl_, :] = t
    return out


def build(parts, n_layers, debug=False):
    nc = bass.Bass("TRN2", target_bir_lowering=False)
    dr = {}

    def din(name, shape, dt=F32):
        dr[name] = nc.dram_tensor(name, list(shape), dt, kind="ExternalInput").ap()
        return dr[name]

    def dout(name, shape, dt=F32):
        dr[name] = nc.dram_tensor(name, list(shape), dt, kind="ExternalOutput").ap()
        return dr[name]

    def dscr(name, shape, dt):
        if debug:
            dr[name] = nc.dram_tensor(name, list(shape), dt, kind="ExternalOutput").ap()
        else:
            dr[name] = nc.dram_tensor(name, list(shape), dt).ap()
        return dr[name]

    QPN = "p" if (QUARTER and any(pn == "p" for (pn, _, _) in parts)) else None
    if QPN is not None:
        dscr("oq", [2048, D], F32)
        dscr("szq", [2048, D // 2], F32)
        dscr("xq", [2048, D], F32)
        dscr("obq", [2048, 256], F32)
    for (pn, S, src) in parts:
        din("x_" + pn, [S, D])
        if pn == QPN:
            dout("yq_" + pn, [2048, D])
        else:
            dout("y_" + pn, [S, D])
        dscr("qt_" + pn, [1024, S], BF16)
        if pn == QPN:
            dscr("ktpad", [896, S + 2048], BF16)
            dscr("vpad", [S + 2048, 8 * 192], BF16)
            dr["kt_" + pn] = dr["ktpad"][:, 1024:1024 + S]
            dr["v_" + pn] = dr["vpad"][1024:1024 + S, :]
            dscr("ktc", [768, 4096], BF16)
            dscr("vc", [4096, 1152], BF16)
            dscr("oaq", [2048, 384], F32)
            dscr("ocq", [2048, 384], F32)
        else:
            dscr("kt_" + pn, [896, S], BF16)
            dscr("v_" + pn, [S, 8 * 192], BF16)
        dscr("sz_" + pn, [S, D // 2], F32)
        dscr("o_" + pn, [S, D], F32)
        if n_layers > 1:
            dscr("y1_" + pn, [S, D], F32)
    din("w_in", [n_layers, D, INW])
    din("w_out", [n_layers, D, D])
    din("norm_pre", [n_layers, D])
    din("norm_post", [n_layers, D])
    din("branch_gain", [n_layers, D])
    din("q_norm", [n_layers, 64])
    din("k_norm", [n_layers, 64])
    din("ropea", [128, 2, 64, 8])
    din("ropeb", [128, 2, 64, 32])
    din("ea", [128, 10, 128])
    din("ident", [128, 128])
    din("efraw", [n_layers, 6, 128, 7, 128])
    if QPN is not None:
        dr["qoff"] = nc.dram_tensor("qoff", [1, 4], mybir.dt.int32, kind="ExternalInput").ap()
        din("cedge", [6, 128, 24 * 128])

    with ExitStack() as top:
        block = top.enter_context(nc.Block())
        P = Prog(nc, top, block)
        identf = top.enter_context(nc.sbuf_tensor("identf", [128, 128], F32))
        identb = top.enter_context(nc.sbuf_tensor("identb", [128, 128], BF16))
        epsc = top.enter_context(nc.sbuf_tensor("epsc", [128, 8], F32))
        nhalf = top.enter_context(nc.sbuf_tensor("nhalf", [128, 8], F32))
        P.op("sp", lambda e: e.dma_start(out=identf[:], in_=dr["ident"]), writes=["identf"], dma="c0")
        P.op("dve", lambda e: e.tensor_copy(out=identb[:], in_=identf[:]), reads=["identf"], writes=["identb"])
        P.op("dve", lambda e: e.memset(epsc[:], EPS), writes=["epsc"])
        P.op("dve", lambda e: e.memset(nhalf[:], -0.5), writes=["nhalf"])
        if QPN is not None:
            qs = top.enter_context(nc.sbuf_tensor("qs", [1, 4], mybir.dt.int32))
            rq0 = top.enter_context(nc.sync.register("rq0"))
            rrow = top.enter_context(nc.sync.register("rrow"))
            rrow2 = top.enter_context(nc.sync.register("rrow2"))
            rv = top.enter_context(nc.sync.register("rv"))
            rtmp = [top.enter_context(nc.sync.register(f"rtmp{i}")) for i in range(4)]
            rti = [0]
            P.op("sp", lambda e: e.dma_start(out=qs[:], in_=dr["qoff"]), writes=["qs"], dma="c1")

            def setregs(e):
                e.reg_load(rq0, qs[0:1, 0:1])
                e.reg_load(rrow2, qs[0:1, 2:3])
                e.reg_load(rv, qs[0:1, 3:4])
                return e.reg_load(rrow, qs[0:1, 1:2])
            P.op("sp", setregs, reads=["qs"], writes=["regs"])

            SQ = [S for (pn_, S, _) in parts if pn_ == QPN][0]
            ztstack = ExitStack()
            zt = ztstack.enter_context(nc.sbuf_tensor("zt", [128, 1536], BF16))
            P.op("pool", lambda e: e.memset(zt[:], 0.0), writes=["zt"])
            zi = 0
            for side in (0, 1024 + SQ):
                for k in range(7):
                    P.op("sp", lambda e, k=k, side=side: e.dma_start(out=dr["ktpad"][k * 128:(k + 1) * 128, side:side + 1024], in_=zt[:, 0:1024]), reads=["zt"], dma=f"zp{zi}")
                    zi += 1
                for k in range(8):
                    P.op("sp", lambda e, k=k, side=side: e.dma_start(out=dr["vpad"][side + k * 128:side + (k + 1) * 128, :], in_=zt[:]), reads=["zt"], dma=f"zp{zi}")
                    zi += 1

            def dyn_ap(e, base_reg, const, tensor_ap, pattern):
                t = rtmp[rti[0] % 4]
                rti[0] += 1
                e.reg_add(t, base_reg, int(const))
                return bass.AP(tensor_ap.tensor, t, pattern)
        P.flush(final=(STOP == "pre"))
        if QPN is not None:
            ztstack.close()
        if STOP == "pre":
            return nc

        def rstd_ops(v_ap, key, n, scale):
            P.op("dve", lambda e: e.tensor_scalar(out=v_ap, in0=v_ap, scalar1=float(scale), scalar2=float(EPS),
                                                  op0=ALU.mult, op1=ALU.add), reads=[key], writes=[key])
            P.op("pool", lambda e: e.tensor_tensor(out=v_ap, in0=v_ap, in1=nhalf[:, 0:n], op=ALU.pow),
                 reads=[key], writes=[key])

        for l in range(n_layers):
            last = l == n_layers - 1
            with ExitStack() as st:
                wsb = st.enter_context(nc.sbuf_tensor(U("wsb"), [128, 8, INW], BF16))
                wst = Ring(P, st, "wst", 2, [128, 8, 120], F32)
                gpre = st.enter_context(nc.sbuf_tensor(U("gpre"), [128, 8], F32))
                gqk = st.enter_context(nc.sbuf_tensor(U("gqk"), [128, 6, 64], F32))
                ropa = st.enter_context(nc.sbuf_tensor(U("ropa"), [128, 2, 64, 8], F32))
                ropb = st.enter_context(nc.sbuf_tensor(U("ropb"), [128, 2, 64, 32], F32))
                xin = Ring(P, st, "xin", 5, [128, D], F32)
                junk = st.enter_context(nc.sbuf_tensor(U("junk"), [128, D], BF16))
                ssr = Ring(P, st, "ssr", 5, [128, 1], F32)
                xnr = Ring(P, st, "xnr", 2, [128, D], BF16)
                qfr = Ring(P, st, "qfr", 2, [128, 6, 64], F32)
                bfr = Ring(P, st, "bfr", 2, [128, 512], F32)
                hTr = Ring(P, st, "hTr", 2, [128, 8, 512], BF16)
                qts = Ring(P, st, "qts", 2, [128, 8, 512], BF16)
                kts = Ring(P, st, "kts", 2, [128, 7, 512], BF16)
                vsr = Ring(P, st, "vsr", 4, [128, 8, 192], BF16)
                szr = Ring(P, st, "szr", 4, [128, D], BF16)
                qar = Ring(P, st, "qar", 3, [128, 6, 64], BF16)
                kar = Ring(P, st, "kar", 3, [128, 6, 64], BF16)
                qkbr = Ring(P, st, "qkbr", 3, [128, 6, 64], BF16)
                sqb = st.enter_context(nc.sbuf_tensor(U("sqb"), [128, 6, 64], F32))
                xgb = st.enter_context(nc.sbuf_tensor(U("xgb"), [128, 6, 64], F32))
                ss6 = st.enter_context(nc.sbuf_tensor(U("ss6"), [128, 6], F32))
                tA = [st.enter_context(nc.sbuf_tensor(U(f"tA{i}"), [128, 6, 8], F32)) for i in range(4)]
                tB = [st.enter_context(nc.sbuf_tensor(U(f"tB{i}"), [128, 6, 2, 16], F32)) for i in range(4)]
                psT = Ring(P, st, "psT", 2, [128, 8, 128], BF16, psum=True)
                psT2 = Ring(P, st, "psT2", 2, [128, 8, 128], BF16, psum=True)
                psM = Ring(P, st, "psM", 4, [128, 512], F32, psum=True)
                psF = psM

                P.op("sp", lambda e: e.dma_start(out=gpre[:], in_=dr["norm_pre"][l].rearrange("(k p) -> p k", p=128),
                                                 allow_slow_non_contiguous=True), writes=["gpre"], dma="c0")
                P.op("sp", lambda e: e.dma_start(out=gqk[:, 0:4, :], in_=dr["q_norm"][l].partition_broadcast(128).unsqueeze(1).to_broadcast([128, 4, 64]),
                                                 allow_slow_non_contiguous=True), writes=["gqk_q"], dma="c1")
                P.op("sp", lambda e: e.dma_start(out=gqk[:, 4:6, :], in_=dr["k_norm"][l].partition_broadcast(128).unsqueeze(1).to_broadcast([128, 2, 64]),
                                                 allow_slow_non_contiguous=True), writes=["gqk_k"], dma="c2")
                P.op("sp", lambda e: e.dma_start(out=ropa[:], in_=dr["ropea"]), writes=["ropa"], dma="c3")
                P.op("sp", lambda e: e.dma_start(out=ropb[:], in_=dr["ropeb"]), writes=["ropb"], dma="c4")
                for ci in range(32):
                    wbuf, wkey = wst.next()
                    c0 = ci * 120
                    P.op("sp", lambda e, wbuf=wbuf, c0=c0: e.dma_start(
                        out=wbuf[:], in_=dr["w_in"][l][:, c0:c0 + 120].rearrange("(k p) c -> p k c", p=128)),
                        writes=[wkey], dma=wkey)
                    eng = ("pool", "dve", "act")[ci % 3]
                    if eng == "act":
                        P.op("act", lambda e, wbuf=wbuf, c0=c0: e.activation(out=wsb[:, :, c0:c0 + 120], in_=wbuf[:], func=AF.Copy),
                             reads=[wkey], writes=[f"wsb{ci}"])
                    else:
                        P.op(eng, lambda e, wbuf=wbuf, c0=c0: e.tensor_copy(out=wsb[:, :, c0:c0 + 120], in_=wbuf[:]),
                             reads=[wkey], writes=[f"wsb{ci}"])
                WALL = [f"wsb{ci}" for ci in range(32)]

                def wkeys(c0, c1):
                    return [f"wsb{ci}" for ci in range(c0 // 120, (c1 - 1) // 120 + 1)]

                from collections import deque
                blocks1 = []
                for (pn, S, src) in parts:
                    xsrc = dr["x_" + pn] if l == 0 else dr["y1_" + pn]
                    for blk in range(S // 128):
                        blocks1.append(dict(pn=pn, blk=blk, j=blk % 4, tg=blk // 4, xsrc=xsrc))
                pend = deque()

                def run_pend(keep):
                    while len(pend) > keep:
                        pend.popleft()()

                def s1_load(c):
                    xb, xkey = xin.next()
                    c["x"] = (xb, xkey)
                    t0, xsrc = c["blk"] * 128, c["xsrc"]
                    P.op("sp", lambda e: e.dma_start(out=xb[:], in_=xsrc[t0:t0 + 128, :]), writes=[xkey], dma=xkey)

                grp = {}

                def s1_fa(c):
                    xb, xkey = c["x"]
                    ss, sskey = ssr.next()
                    c["ss"] = (ss, sskey)
                    P.op("act", lambda e: e.activation(out=junk[:], in_=xb[:], func=AF.Square, accum_out=ss[:]), reads=[xkey], writes=[sskey, "junk"])
                    rstd_ops(ss[:], sskey, 1, 1.0 / D)

                def s1_front(c):
                    j = c["j"]
                    if j == 0:
                        grp["hT"] = hTr.next()
                        grp["qt"] = qts.next()
                        grp["kt"] = kts.next()
                    c["hT"], c["qt"], c["kt"] = grp["hT"], grp["qt"], grp["kt"]
                    hT, hkey = c["hT"]
                    xb, xkey = c["x"]
                    ss, sskey = c["ss"]
                    xn, xnkey = xnr.next()
                    P.op("act", lambda e: e.activation(out=xn[:], in_=xb[:], func=AF.Copy, scale=ss[:]), reads=[xkey, sskey], writes=[xnkey])
                    pT, pTkey = psT.next()
                    for kc in range(8):
                        P.op("pe", lambda e, kc=kc: e.transpose(out=pT[:, kc, :], in_=xn[:, kc * 128:(kc + 1) * 128], identity=identb[:]),
                             reads=[xnkey, "identb"], writes=[pTkey])
                    P.op("dve", lambda e: e.tensor_tensor(
                        out=hT[:, :, j * 128:(j + 1) * 128], in0=pT[:], in1=gpre[:].unsqueeze(2).to_broadcast([128, 8, 128]), op=ALU.mult),
                        reads=[pTkey, "gpre"], writes=[hkey + f"_{j}"])

                def s1_main(c):
                    j, blk, pn = c["j"], c["blk"], c["pn"]
                    hT, hkey = c["hT"]
                    qt_s, qkey = c["qt"]
                    kt_s, kkey = c["kt"]
                    hk = hkey + f"_{j}"

                    def tok_mm(c0, c1):
                        run_pend(2)
                        pm, pmkey = psM.next()
                        for kc in range(8):
                            P.op("pe", lambda e, kc=kc: e.matmul(pm[:, 0:c1 - c0], lhsT=hT[:, kc, j * 128:(j + 1) * 128], rhs=wsb[:, kc, c0:c1],
                                                                 start=(kc == 0), stop=(kc == 7)),
                                 reads=[hk] + wkeys(c0, c1), writes=[pmkey])
                        return pm, pmkey

                    for which, c0, ring, dst in (("q", AQ, qar, qt_s), ("k", AK, kar, kt_s)):
                        pm, pmkey = tok_mm(c0, c0 + 384)
                        qf, qfkey = qfr.next()
                        P.op("act", lambda e, qf=qf, pm=pm: e.activation(out=qf[:].rearrange("p h d -> p (h d)"), in_=pm[:, 0:384], func=AF.Copy), reads=[pmkey], writes=[qfkey])
                        ob, okey = ring.next()
                        P.op("pool", lambda e, ob=ob, qf=qf: e.tensor_copy(out=ob[:], in_=qf[:]), reads=[qfkey], writes=[okey])
                        cosb = ropa[:, 0, blk, :].unsqueeze(1).to_broadcast([128, 6, 8])
                        sinb = ropa[:, 1, blk, :].unsqueeze(1).to_broadcast([128, 6, 8])
                        x1 = qf[:, :, 0:8]
                        x2 = qf[:, :, 8:16]
                        P.op("dve", lambda e, x1=x1, cosb=cosb: e.tensor_tensor(out=tA[0][:], in0=x1, in1=cosb, op=ALU.mult), reads=[qfkey, "ropa"], writes=["tA0"])
                        P.op("dve", lambda e, x2=x2, sinb=sinb: e.tensor_tensor(out=tA[1][:], in0=x2, in1=sinb, op=ALU.mult), reads=[qfkey, "ropa"], writes=["tA1"])
                        P.op("dve", lambda e, x2=x2, cosb=cosb: e.tensor_tensor(out=tA[2][:], in0=x2, in1=cosb, op=ALU.mult), reads=[qfkey, "ropa"], writes=["tA2"])
                        P.op("dve", lambda e, x1=x1, sinb=sinb: e.tensor_tensor(out=tA[3][:], in0=x1, in1=sinb, op=ALU.mult), reads=[qfkey, "ropa"], writes=["tA3"])
                        P.op("dve", lambda e, ob=ob: e.tensor_tensor(out=ob[:, :, 0:8], in0=tA[0][:], in1=tA[1][:], op=ALU.subtract), reads=["tA0", "tA1", okey], writes=[okey])
                        P.op("dve", lambda e, ob=ob: e.tensor_tensor(out=ob[:, :, 8:16], in0=tA[2][:], in1=tA[3][:], op=ALU.add), reads=["tA2", "tA3", okey], writes=[okey])

                        def fin_a(ob=ob, okey=okey, dst=dst, which=which):
                            pT2, pT2key = psT2.next()
                            obf = ob[:].rearrange("p h d -> p (h d)")
                            for tt in range(3):
                                P.op("pe", lambda e, tt=tt: e.transpose(out=pT2[:, tt, :], in_=obf[:, tt * 128:(tt + 1) * 128], identity=identb[:]),
                                     reads=[okey, "identb"], writes=[pT2key])
                            dkey = (qkey if which == "q" else kkey) + f"_{j}a"
                            P.op("act", lambda e: e.activation(out=dst[:, 0:3, j * 128:(j + 1) * 128], in_=pT2[:, 0:3, :], func=AF.Copy),
                                 reads=[pT2key], writes=[dkey])
                        pend.append(fin_a)
                    vs, vkey = vsr.next()
                    c["vs"] = (vs, vkey)
                    pm, pmkey = tok_mm(AV, AV + 384)
                    P.op("dve", lambda e, pm=pm: e.tensor_copy(out=vs[:, 0:3, :].rearrange("p a (s d) -> p a s d", d=64)[:, :, 0::2, :], in_=pm[:, 0:384].rearrange("p (a s d) -> p a s d", s=2, d=64)),
                         reads=[pmkey], writes=[vkey + "a"])
                    P.op("pool", lambda e: e.memset(vs[:, :, 64:128], 1.0), writes=[vkey + "one"])
                    sz, szkey = szr.next()
                    c["sz"] = (sz, szkey)
                    pm, pmkey = tok_mm(AZ, AZ + 384)
                    P.op("act", lambda e, pm=pm: e.activation(out=sz[:, 0:384], in_=pm[:, 0:384], func=AF.Silu), reads=[pmkey], writes=[szkey + "a"])
                    pm, pmkey = tok_mm(BQ, BQ + 512)
                    bf, bfkey = bfr.next()
                    P.op("act", lambda e, pm=pm, bf=bf: e.activation(out=bf[:], in_=pm[:], func=AF.Copy), reads=[pmkey], writes=[bfkey])
                    bf6 = bf[:, 0:384].rearrange("p (h d) -> p h d", d=64)
                    P.op("pool", lambda e, bf6=bf6: e.tensor_tensor(out=sqb[:], in0=bf6, in1=bf6, op=ALU.mult), reads=[bfkey], writes=["sqb"])
                    P.op("dve", lambda e: e.tensor_reduce(out=ss6[:], in_=sqb[:], axis=AX.X, op=ALU.add), reads=["sqb"], writes=["ss6"])
                    rstd_ops(ss6[:], "ss6", 6, 1.0 / 64)
                    P.op("dve", lambda e, bf6=bf6: e.tensor_tensor(out=xgb[:], in0=bf6, in1=ss6[:].unsqueeze(2).to_broadcast([128, 6, 64]), op=ALU.mult),
                         reads=[bfkey, "ss6"], writes=["xgb"])
                    P.op("pool", lambda e: e.tensor_tensor(out=xgb[:], in0=xgb[:], in1=gqk[:], op=ALU.mult), reads=["xgb", "gqk_q", "gqk_k"], writes=["xgb"])
                    qkb, qkbkey = qkbr.next()
                    xv = xgb[:].rearrange("p h (a b c) -> p h a b c", a=2, b=2)
                    ov = qkb[:].rearrange("p h (a b c) -> p h a b c", a=2, b=2)
                    cb = ropb[:, 0, blk, :].rearrange("p (a c) -> p a c", a=2).unsqueeze(1).to_broadcast([128, 6, 2, 16])
                    sb_ = ropb[:, 1, blk, :].rearrange("p (a c) -> p a c", a=2).unsqueeze(1).to_broadcast([128, 6, 2, 16])
                    x1 = xv[:, :, :, 0, :]
                    x2 = xv[:, :, :, 1, :]
                    P.op("pool", lambda e, x1=x1, cb=cb: e.tensor_tensor(out=tB[0][:], in0=x1, in1=cb, op=ALU.mult), reads=["xgb", "ropb"], writes=["tB0"])
                    P.op("pool", lambda e, x2=x2, sb_=sb_: e.tensor_tensor(out=tB[1][:], in0=x2, in1=sb_, op=ALU.mult), reads=["xgb", "ropb"], writes=["tB1"])
                    P.op("dve", lambda e, x2=x2, cb=cb: e.tensor_tensor(out=tB[2][:], in0=x2, in1=cb, op=ALU.mult), reads=["xgb", "ropb"], writes=["tB2"])
                    P.op("dve", lambda e, x1=x1, sb_=sb_: e.tensor_tensor(out=tB[3][:], in0=x1, in1=sb_, op=ALU.mult), reads=["xgb", "ropb"], writes=["tB3"])
                    P.op("pool", lambda e, ov=ov: e.tensor_tensor(out=ov[:, :, :, 0, :], in0=tB[0][:], in1=tB[1][:], op=ALU.subtract), reads=["tB0", "tB1"], writes=[qkbkey + "x"])
                    P.op("dve", lambda e, ov=ov: e.tensor_tensor(out=ov[:, :, :, 1, :], in0=tB[2][:], in1=tB[3][:], op=ALU.add), reads=["tB2", "tB3"], writes=[qkbkey + "y"])
                    bfv = bf[:, 384:512].rearrange("p (h d) -> p h d", d=64)
                    P.op("pool", lambda e, bfv=bfv: e.tensor_copy(out=vs[:, 3:5, 0:64], in_=bfv), reads=[bfkey], writes=[vkey + "b"])
                    P.op("pool", lambda e, bfv=bfv: e.tensor_copy(out=vs[:, 3:5, 128:192], in_=bfv), reads=[bfkey], writes=[vkey + "b2"])

                    def fin_b(qkb=qkb, qkbkey=qkbkey):
                        pT2, pT2key = psT2.next()
                        qkbf = qkb[:].rearrange("p h d -> p (h d)")
                        for tt in range(3):
                            P.op("pe", lambda e, tt=tt: e.transpose(out=pT2[:, tt, :], in_=qkbf[:, tt * 128:(tt + 1) * 128], identity=identb[:]),
                                 reads=[qkbkey + "x", qkbkey + "y", "identb"], writes=[pT2key])
                        P.op("act", lambda e: e.activation(out=qt_s[:, 3:5, j * 128:(j + 1) * 128], in_=pT2[:, 0:2, :], func=AF.Copy),
                             reads=[pT2key], writes=[qkey + f"_{j}b"])
                        P.op("act", lambda e: e.activation(out=kt_s[:, 3, j * 128:(j + 1) * 128], in_=pT2[:, 2, :], func=AF.Copy),
                             reads=[pT2key], writes=[kkey + f"_{j}b"])
                    pend.append(fin_b)
                    pm, pmkey = tok_mm(BZ, BZ + 256)
                    P.op("act", lambda e, pm=pm: e.activation(out=sz[:, 384:640], in_=pm[:, 0:256], func=AF.Silu), reads=[pmkey], writes=[szkey + "b"])
                    pm, pmkey = tok_mm(CV, CV + 384)
                    P.op("dve", lambda e, pm=pm: e.tensor_copy(out=vs[:, 5:8, :].rearrange("p a (s d) -> p a s d", d=64)[:, :, 0::2, :], in_=pm[:, 0:384].rearrange("p (a s d) -> p a s d", s=2, d=64)),
                         reads=[pmkey], writes=[vkey + "c"])
                    pm, pmkey = tok_mm(CZ, CZ + 384)
                    P.op("act", lambda e, pm=pm: e.activation(out=sz[:, 640:1024], in_=pm[:, 0:384], func=AF.Silu), reads=[pmkey], writes=[szkey + "c"])
                    if j == 3:
                        hks = [hkey + f"_{jj}" for jj in range(4)]
                        for ti in range(6):
                            run_pend(2)
                            c0 = (CQ if ti < 3 else CK) + (ti % 3) * 128
                            pf, pfkey = psF.next()
                            for kc in range(8):
                                P.op("pe", lambda e, pf=pf, kc=kc, c0=c0: e.matmul(pf[:], lhsT=wsb[:, kc, c0:c0 + 128], rhs=hT[:, kc, :], start=(kc == 0), stop=(kc == 7)),
                                     reads=hks + wkeys(c0, c0 + 128), writes=[pfkey])
                            if ti < 3:
                                P.op("dve", lambda e, pf=pf, ti=ti: e.tensor_copy(out=qt_s[:, 5 + ti, :], in_=pf[:]), reads=[pfkey], writes=[qkey + f"_c{ti}"])
                            else:
                                P.op("act", lambda e, pf=pf, ti=ti: e.activation(out=kt_s[:, 4 + ti - 3, :], in_=pf[:], func=AF.Copy), reads=[pfkey], writes=[kkey + f"_c{ti}"])

                def s1_store(c):
                    j, blk, pn = c["j"], c["blk"], c["pn"]
                    t0 = blk * 128
                    vs, vkey = c["vs"]
                    sz, szkey = c["sz"]
                    qt_s, qkey = c["qt"]
                    kt_s, kkey = c["kt"]
                    P.op("sp", lambda e: e.dma_start(out=dr["v_" + pn][t0:t0 + 128, :], in_=vs[:].rearrange("p h d -> p (h d)")),
                         reads=[vkey + "a", vkey + "b", vkey + "b2", vkey + "c", vkey + "one"], dma=vkey)
                    P.op("sp", lambda e: e.dma_start(out=dr["sz_" + pn][t0:t0 + 128, :], in_=sz[:].bitcast(F32)),
                         reads=[szkey + "a", szkey + "b", szkey + "c"], dma=szkey)
                    if j == 3:
                        tt0 = c["tg"] * 512
                        qr = [qkey + f"_{jj}a" for jj in range(4)] + [qkey + f"_{jj}b" for jj in range(4)] + [qkey + f"_c{ti}" for ti in range(3)]
                        kr = [kkey + f"_{jj}a" for jj in range(4)] + [kkey + f"_{jj}b" for jj in range(4)] + [kkey + f"_c{ti}" for ti in range(3, 6)]
                        P.op("sp", lambda e: e.dma_start(out=dr["qt_" + pn][:, tt0:tt0 + 512].rearrange("(k p) t -> p k t", p=128), in_=qt_s[:]), reads=qr, dma=qkey)
                        P.op("sp", lambda e: e.dma_start(out=dr["kt_" + pn][:, tt0:tt0 + 512].rearrange("(k p) t -> p k t", p=128), in_=kt_s[:]), reads=kr, dma=kkey)

                nb1 = len(blocks1)
                for i in range(-3, nb1 + 2):
                    if 0 <= i + 3 < nb1:
                        s1_load(blocks1[i + 3])
                    if 0 <= i + 2 < nb1:
                        s1_fa(blocks1[i + 2])
                    if 0 <= i + 1 < nb1:
                        s1_front(blocks1[i + 1])
                    if 0 <= i < nb1:
                        s1_main(blocks1[i])
                        if blocks1[i]["j"] == 3:
                            run_pend(0)
                    if 0 <= i - 2 < nb1:
                        s1_store(blocks1[i - 2])
                run_pend(0)
                P.flush(final=(STOP == "s1"))
            if STOP == "s1":
                return nc

            with ExitStack() as st:
                SMAXP = max(S for (_, S, _) in parts)
                qpr = Ring(P, st, "qpr", 2, [128, SMAXP], BF16)
                kpr = Ring(P, st, "kpr", 2, [128, SMAXP], BF16)
                vbr = Ring(P, st, "vbr", 2, [128, SMAXP // 128, 192], BF16)
                eab = st.enter_context(nc.sbuf_tensor(U("eab"), [128, 5, 128], BF16))
                m16 = st.enter_context(nc.sbuf_tensor(U("m16"), [128, 5, 128], BF16))
                accs = [(st.enter_context(nc.sbuf_tensor(U(f"acc{hh}"), [128, 2048], F32)), f"acc{hh}") for hh in range(2)]
                v16r = Ring(P, st, "v16r", 1, [128, SMAXP // 128, 192], BF16)
                ecr = Ring(P, st, "ecr", 1, [128, 2, 4, 6, 128], BF16)
                est = Ring(P, st, "est", 2, [128, 1024], F32)
                efb = st.enter_context(nc.sbuf_tensor(U("efb"), [128, 6, 7, 128], BF16))
                eib = st.enter_context(nc.sbuf_tensor(U("eib"), [128, 6, 5, 128], BF16))
                ptr = Ring(P, st, "ptr", 6, [128, 512], BF16)
                osr = Ring(P, st, "osr", 2, [128, 512], F32)
                ogr = Ring(P, st, "ogr", 2, [128, 4, 128], F32)
                rlr = Ring(P, st, "rlr", 2, [128, 4], F32)
                psS = Ring(P, st, "psS", 6, [128, 512], F32, psum=True)
                psO = Ring(P, st, "psO", 2, [128, 512], F32, psum=True)
                psR = None

                eb, ekey = est.next()
                P.op("sp", lambda e, eb=eb: e.dma_start(out=eb[:, 0:640], in_=dr["ea"][:, 0:5, :].rearrange("p a b -> p (a b)")), writes=[ekey], dma=ekey)
                P.op("dve", lambda e, eb=eb: e.tensor_copy(out=eab[:].rearrange("p a b -> p (a b)"), in_=eb[:, 0:5 * 128]), reads=[ekey], writes=["eab"])
                eb, ekey = est.next()
                P.op("sp", lambda e, eb=eb: e.dma_start(out=eb[:, 0:640], in_=dr["ea"][:, 5:10, :].rearrange("p a b -> p (a b)")), writes=[ekey], dma=ekey)
                P.op("dve", lambda e, eb=eb: e.tensor_copy(out=m16[:].rearrange("p a b -> p (a b)"), in_=eb[:, 0:5 * 128]), reads=[ekey], writes=["m16"])
                for h in range(6):
                    eb, ekey = est.next()
                    P.op("sp", lambda e, eb=eb, h=h: e.dma_start(out=eb[:, 0:7 * 128], in_=dr["efraw"][l, h].rearrange("p a b -> p (a b)")), writes=[ekey], dma=ekey)
                    P.op("act", lambda e, eb=eb, h=h: e.activation(out=efb[:, h].rearrange("p a b -> p (a b)"), in_=eb[:, 0:7 * 128], func=AF.Exp),
                         reads=[ekey], writes=[f"efb{h}"])
                    P.op("dve", lambda e, h=h: e.tensor_copy(out=eib[:, h], in_=efb[:, h, 1:6, :]), reads=[f"efb{h}"], writes=[f"eib{h}"])
                    P.op("dve", lambda e, h=h: e.memset(eib[0:64, h, 0, 64:128], 0.0), reads=[f"eib{h}"], writes=[f"eib{h}"])
                    P.op("dve", lambda e, h=h: e.memset(eib[64:128, h, 4, :], 0.0), reads=[f"eib{h}"], writes=[f"eib{h}"])
                    P.op("dve", lambda e, h=h: e.memset(eib[0:64, h, 4, 0:64], 0.0), reads=[f"eib{h}"], writes=[f"eib{h}"])

                jobs = []
                for (pn, S, src) in parts:
                    for gp in range(8):
                        jobs.append(dict(pn=pn, S=S, gp=gp, dyn=(last and pn == QPN and 3 <= gp < 5), dynA=(last and pn == QPN and gp < 3), dynC=(last and pn == QPN and gp >= 5)))
                if last and QPN is not None:
                    P.op("sp", lambda e: e.dma_start(out=dr["ktc"][0:384, :], in_=dyn_ap(e, rq0, 0, dr["ktpad"], [[SQ + 2048, 384], [1, 4096]])), writes=["ktc"], dma="cq5")
                    P.op("sp", lambda e: e.dma_start(out=dr["vc"][:, 0:576], in_=dyn_ap(e, rv, 0, dr["vpad"], [[1536, 4096], [1, 576]])), writes=["vc"], dma="cq6")
                    P.op("sp", lambda e: e.dma_start(out=dr["ktc"][384:768, :], in_=dyn_ap(e, rq0, 512 * (SQ + 2048), dr["ktpad"], [[SQ + 2048, 384], [1, 4096]])), writes=["ktc2"], dma="cq7")
                    P.op("sp", lambda e: e.dma_start(out=dr["vc"][:, 576:1152], in_=dyn_ap(e, rv, 960, dr["vpad"], [[1536, 4096], [1, 576]])), writes=["vc2"], dma="cq8")

                def load_job(jb):
                    pn, S, gp = jb["pn"], jb["S"], jb["gp"]
                    NB = S // 128
                    qtd, ktd, vd = dr["qt_" + pn], dr["kt_" + pn], dr["v_" + pn]
                    if gp < 3:
                        pr = gp
                        qrow, krows = pr * 128, [pr * 128, pr * 128 + 64]
                    elif gp < 5:
                        pr = gp - 3
                        qrow, krows = 384 + pr * 128, [384 + pr * 64, 384 + pr * 64]
                    else:
                        pr = gp - 5
                        qrow, krows = 640 + pr * 128, [512 + pr * 128, 512 + pr * 128 + 64]
                    vb, vbkey = vbr.next()
                    dynA = jb.get("dynA")
                    dynC = jb.get("dynC")
                    if dynA or dynC:
                        nch = 1
                        vcol = gp * 192 if dynA else 576 + pr * 192
                        P.op("sp", lambda e: e.dma_start(out=vb[:, 0:32, :], in_=dr["vc"][:, vcol:vcol + 192].rearrange("(b p) c -> p b c", p=128)),
                             reads=["vc", "vc2"], writes=[vbkey + "_0"], dma=f"{vbkey}_0")
                    else:
                        nch = 4 if S > 2048 else 1
                        for ch in range(nch):
                            b0 = ch * (NB // nch)
                            b1 = (ch + 1) * (NB // nch)
                            P.op("sp", lambda e, b0=b0, b1=b1: e.dma_start(
                                out=vb[:, b0:b1, :], in_=vd[b0 * 128:b1 * 128, gp * 192:(gp + 1) * 192].rearrange("(b p) c -> p b c", p=128)),
                                writes=[vbkey + f"_{ch}"], dma=f"{vbkey}_{ch}")
                    jb["vb"] = (vb, vbkey, [vbkey + f"_{ch}" for ch in range(nch)])

                    qp, qpkey = qpr.next()
                    kp, kpkey = kpr.next()
                    if jb.get("dyn") or dynA or dynC:
                        P.op("sp", lambda e: e.dma_start(out=qp[:, 0:2048], in_=dyn_ap(e, rq0, qrow * S, qtd, [[S, 128], [1, 2048]])), writes=[qpkey], dma=qpkey)
                    else:
                        P.op("sp", lambda e: e.dma_start(out=qp[:, 0:S], in_=qtd[qrow:qrow + 128, :]), writes=[qpkey], dma=qpkey)
                    for hh in range(2):
                        if dynA or dynC:
                            kr0 = krows[hh] if dynA else krows[hh] - 512 + 384
                            P.op("sp", lambda e, hh=hh, kr0=kr0: e.dma_start(out=kp[hh * 64:(hh + 1) * 64, 0:4096], in_=dr["ktc"][kr0:kr0 + 64, :]),
                                 reads=["ktc", "ktc2"], writes=[kpkey + f"_{hh}"], dma=f"{kpkey}_{hh}")
                        else:
                            P.op("sp", lambda e, hh=hh: e.dma_start(out=kp[hh * 64:(hh + 1) * 64, 0:S], in_=ktd[krows[hh]:krows[hh] + 64, :]),
                                 writes=[kpkey + f"_{hh}"], dma=f"{kpkey}_{hh}")
                    jb["qp"] = (qp, qpkey)
                    jb["kp"] = (kp, kpkey)

                def load_ec(jb):
                    pr = jb["gp"] - 5
                    ecp, eckey = ecr.next()
                    for hh in range(2):
                        for ch in range(3):
                            eb, ekey = est.next()
                            P.op("sp", lambda e, eb=eb, hh=hh, ch=ch: e.dma_start(out=eb[:, 0:1024], in_=dr["cedge"][2 * pr + hh][:, ch * 1024:(ch + 1) * 1024]), writes=[ekey], dma=ekey)
                            P.op("act", lambda e, eb=eb, hh=hh, ch=ch: e.activation(out=ecp[:, hh].rearrange("p a b c -> p (a b c)")[:, ch * 1024:(ch + 1) * 1024], in_=eb[:, 0:1024], func=AF.Exp),
                                 reads=[ekey], writes=[eckey + f"_{hh}_{ch}"])
                    jb["ec"] = (ecp, [eckey + f"_{hh}_{ch}" for hh in range(2) for ch in range(3)])

                load_job(jobs[0])
                for ji, jb in enumerate(jobs):
                    if ji + 1 < len(jobs):
                        load_job(jobs[ji + 1])
                    pn, S, gp = jb["pn"], jb["S"], jb["gp"]
                    NB = S // 128
                    NW = S // 512
                    od = dr["o_" + pn]
                    if jb.get("dynC"):
                        load_ec(jb)
                    if True:
                        if gp < 3:
                            br, pr = "A", gp
                            ocol = pr * 128
                        elif gp < 5:
                            br, pr = "B", gp - 3
                            ocol = 384 + pr * 128
                        else:
                            br, pr = "C", gp - 5
                            ocol = 640 + pr * 128
                        vb, vbkey, vkeys = jb["vb"]
                        qp, qpkey = jb["qp"]
                        kp, kpkey = jb["kp"]
                        groups = []

                        def dense_groups(w):
                            for qb in range(4):
                                b = w * 4 + qb
                                if br == "A":
                                    alist = list(range(max(0, b - 2), min(NB, b + 3)))
                                    kind, off = "ea", 2
                                elif b <= 1:
                                    alist, kind, off = list(range(0, 4)), "ef", 3
                                elif b >= NB - 2:
                                    alist, kind, off = list(range(NB - 4, NB)), "ef", 3
                                else:
                                    alist, kind, off = list(range(b - 2, b + 3)), "ei", 2
                                nu = len(alist)
                                gi = 0
                                while gi < nu:
                                    gn = min(4, nu - gi)
                                    if nu - gi == 5:
                                        gn = 3
                                    us = alist[gi:gi + gn]
                                    groups.append(dict(units=[dict(k=(a * 128, 1), q=(b * 128, 1, 128), pc0=ui * 128, v=("nat", a), oc0=qb * 128,
                                                                   st=(gi + ui == 0), sp=(gi + ui == nu - 1)) for ui, a in enumerate(us)],
                                                       e=(kind, us[0] - b + off, gn, False), tail=("win" if (qb == 3 and gi + gn == nu) else None), w=w))
                                    gi += gn

                        dyn = jb.get("dyn")
                        if br == "B":
                            for w in range(4 if dyn else NW):
                                for kb in range(NB):
                                    groups.append(dict(units=[dict(k=(kb * 128, 1), q=(w * 512, 1, 512), pc0=0, v=("nat", kb), oc0=0, st=(kb == 0), sp=(kb == NB - 1))],
                                                       e=None, tail=("win" if kb == NB - 1 else None), w=w))
                        elif jb.get("dynC"):
                            for w in range(4):
                                for qb in range(4):
                                    b = w * 4 + qb
                                    if b <= 1 or b >= 14:
                                        sb = b if b <= 1 else 2 + (b - 14)
                                        a0 = 6 if b <= 1 else 20
                                        for gi in (0, 3):
                                            groups.append(dict(units=[dict(k=((a0 + gi + ui) * 128, 1), q=(b * 128, 1, 128), pc0=ui * 128, v=("nat", a0 + gi + ui), oc0=qb * 128,
                                                                           st=(gi + ui == 0), sp=(gi + ui == 5)) for ui in range(3)],
                                                               e=("ec", (sb, gi), 3, False), tail=("win" if (qb == 3 and gi == 3) else None), w=w))
                                    else:
                                        alist = [b + 8 + d_ for d_ in range(-2, 3)]
                                        for (gi, gn) in ((0, 3), (3, 2)):
                                            us = alist[gi:gi + gn]
                                            groups.append(dict(units=[dict(k=(a * 128, 1), q=(b * 128, 1, 128), pc0=ui * 128, v=("nat", a), oc0=qb * 128,
                                                                           st=(gi + ui == 0), sp=(gi + ui == 4)) for ui, a in enumerate(us)],
                                                               e=("ei", gi, gn, False), tail=("win" if (qb == 3 and gi == 3) else None), w=w))
                        elif br == "C":
                            for w in range(NW):
                                dense_groups(w)
                        elif jb.get("dynA"):
                            for rq in range(4):
                                for ri in range(4):
                                    r = 4 * rq + ri
                                    units = [dict(k=(jj * 2048 + r, 16), q=(r, 16, 128), pc0=jj * 128, v=("v16", r * 2 + jj), oc0=ri * 128, st=(jj == 0), sp=(jj == 1)) for jj in range(2)]
                                    groups.append(dict(units=units, e=("m16", 3, 2, False), tail=("quad" if ri == 3 else None), sw=0, rq=rq))
                            for w in range(4):
                                for qb in range(4):
                                    b = w * 4 + qb
                                    alist = [b + 8 + d_ for d_ in range(-2, 3)]
                                    for (gi, gn) in ((0, 3), (3, 2)):
                                        us = alist[gi:gi + gn]
                                        groups.append(dict(units=[dict(k=(a * 128, 1), q=(b * 128, 1, 128), pc0=ui * 128, v=("nat", a), oc0=qb * 128,
                                                                       st=(gi + ui == 0), sp=(gi + ui == 4)) for ui, a in enumerate(us)],
                                                           e=("ea", gi, gn, False), tail=("win" if (qb == 3 and gi == 3) else None), w=w))
                        else:
                            nj = NB // 16
                            for sw in range(nj):
                                for rq in range(4):
                                    ulist = [u for u in (-1, 0, 1) if 0 <= sw + u < nj]
                                    if len(ulist) == 1:
                                        units = []
                                        for ri in range(4):
                                            r = 4 * rq + ri
                                            units.append(dict(k=(sw * 2048 + r, 16), q=(sw * 2048 + r, 16, 128), pc0=ri * 128, v=("v16", r * nj + sw), oc0=ri * 128, st=True, sp=True))
                                        groups.append(dict(units=units, e=("m16", 1, 4, True), tail="quad", sw=sw, rq=rq))
                                    else:
                                        for ri in range(4):
                                            r = 4 * rq + ri
                                            units = []
                                            for ui, u in enumerate(ulist):
                                                units.append(dict(k=((sw + u) * 2048 + r, 16), q=(sw * 2048 + r, 16, 128), pc0=ui * 128, v=("v16", r * nj + sw + u), oc0=ri * 128,
                                                                  st=(ui == 0), sp=(ui == len(ulist) - 1)))
                                            groups.append(dict(units=units, e=("m16", ulist[0] + 1, len(ulist), False), tail=("quad" if ri == 3 else None), sw=sw, rq=rq))
                                for w in range(sw * 4, sw * 4 + 4):
                                    dense_groups(w)
                        ng = len(groups)
                        pos = [psO.next(), psO.next()]
                        state = {}
                        v16 = None
                        def load_v16(jbx):
                            pnx, Sx, gpx = jbx["pn"], jbx["S"], jbx["gp"]
                            v16b, v16key = v16r.next()
                            if jbx.get("dynA"):
                                nj_ = 2
                                vsrc = dr["vc"][:, gpx * 192:(gpx + 1) * 192].rearrange("(jj i r) c -> r i jj c", i=128, r=16)
                                v16reads = ["vc"]
                            else:
                                nj_ = (Sx // 128) // 16
                                vsrc = dr["v_" + pnx][:, gpx * 192:(gpx + 1) * 192].rearrange("(jj i r) c -> r i jj c", i=128, r=16)
                                v16reads = []
                            for r in range(16):
                                P.op("sp", lambda e, r=r, v16b=v16b, vsrc=vsrc, nj_=nj_: e.dma_start(out=v16b[:, r * nj_:(r + 1) * nj_, :], in_=vsrc[r]),
                                     reads=v16reads, writes=[v16key + f"_{r}"], dma=f"{v16key}_{r}")
                            jbx["v16"] = (v16b, [v16key + f"_{r}" for r in range(16)])

                        if br == "A":
                            if "v16" not in jb:
                                load_v16(jb)
                            v16 = jb["v16"]
                        nxt = jobs[ji + 1] if ji + 1 < len(jobs) else None
                        pre_at = None
                        if br == "A" and nxt is not None and nxt["gp"] < 3:
                            lastpat = max(gi_ for gi_, g_ in enumerate(groups) if g_["units"][0]["v"][0] == "v16")
                            pre_at = lastpat + 3

                        def sl(start, stride, n):
                            return slice(start, start + (n - 1) * stride + 1, stride) if stride != 1 else slice(start, start + n)

                        def emit_front2(g, qp=qp, kp=kp, qpkey=qpkey, kpkey=kpkey, pr=pr, jb=jb):
                            pss = [psS.next(), psS.next()]
                            ncols = 0
                            for u in g["units"]:
                                ks, kst = u["k"]
                                qs, qst, n = u["q"]
                                pc0 = u["pc0"]
                                for hh in range(2):
                                    ps, pskey = pss[hh]
                                    P.op("pe", lambda e, ps=ps, ks=ks, kst=kst, qs=qs, qst=qst, n=n, pc0=pc0, hh=hh: e.matmul(
                                        ps[:, pc0:pc0 + n], lhsT=kp[hh * 64:(hh + 1) * 64, sl(ks, kst, 128)], rhs=qp[hh * 64:(hh + 1) * 64, sl(qs, qst, n)], start=True, stop=True),
                                        reads=[qpkey, kpkey + f"_{hh}"], writes=[pskey])
                                ncols = max(ncols, pc0 + n)
                            for hh in range(2):
                                ps, pskey = pss[hh]
                                pt, ptkey = ptr.next()
                                P.op("act", lambda e, ps=ps, pt=pt, ncols=ncols: e.activation(out=pt[:, 0:ncols], in_=ps[:, 0:ncols], func=AF.Exp, scale=0.125),
                                     reads=[pskey], writes=[ptkey])
                                if g["e"] is not None:
                                    kind, i0_, gn, bc = g["e"]
                                    hglob = 2 * pr + hh
                                    if kind == "ea":
                                        e_ap, ekeys = eab[:, i0_:i0_ + gn, :], ["eab"]
                                    elif kind == "m16":
                                        if bc:
                                            e_ap, ekeys = m16[:, i0_:i0_ + 1, :].to_broadcast([128, gn, 128]), ["m16"]
                                        else:
                                            e_ap, ekeys = m16[:, i0_:i0_ + gn, :], ["m16"]
                                    elif kind == "ec":
                                        ecp_, eckeys_ = jb["ec"]
                                        e_ap, ekeys = ecp_[:, hh, i0_[0], i0_[1]:i0_[1] + gn, :], eckeys_
                                    elif kind == "ef":
                                        e_ap, ekeys = efb[:, hglob, i0_:i0_ + gn, :], [f"efb{hglob}"]
                                    else:
                                        e_ap, ekeys = eib[:, hglob, i0_:i0_ + gn, :], [f"eib{hglob}"]
                                    P.op("dve", lambda e, pt=pt, e_ap=e_ap, gn=gn: e.tensor_tensor(
                                        out=pt[:, 0:gn * 128].rearrange("p (a b) -> p a b", b=128), in0=pt[:, 0:gn * 128].rearrange("p (a b) -> p a b", b=128), in1=e_ap, op=ALU.mult),
                                        reads=[ptkey] + ekeys, writes=[ptkey])
                                g["pt%d" % hh] = (pt, ptkey)

                        def emit_pv2(g, vb=vb, vkeys=vkeys):
                            for u in g["units"]:
                                vkind, vblk = u["v"]
                                pc0, oc0, st_, sp_ = u["pc0"], u["oc0"], u["st"], u["sp"]
                                n = u["q"][2]
                                if vkind == "nat":
                                    vt, vks = vb, vkeys
                                else:
                                    vt, vks = v16[0], v16[1]
                                for hh in range(2):
                                    pt, ptkey = g["pt%d" % hh]
                                    po, pokey = pos[hh]
                                    P.op("pe", lambda e, po=po, vt=vt, vblk=vblk, pc0=pc0, n=n, oc0=oc0, st_=st_, sp_=sp_, pt=pt, hh=hh: e.matmul(
                                        po[:, oc0:oc0 + n], lhsT=vt[:, vblk, hh * 64:hh * 64 + 128], rhs=pt[:, pc0:pc0 + n], start=st_, stop=sp_),
                                        reads=[ptkey] + vks, writes=[pokey])

                        def emit_back(g, hh, ocol=ocol, od=od, br=br, dyn=dyn, dynA=jb.get("dynA"), dynC=jb.get("dynC")):
                            po, pokey = pos[hh]
                            if g["tail"] == "quad":
                                rq = g["rq"]
                                ac, ackey = accs[hh]
                                dst = ac[:].rearrange("p (l r) -> p r l", r=16)[:, 4 * rq:4 * rq + 4, :]
                                P.op("act", lambda e, po=po, dst=dst: e.activation(out=dst, in_=po[:].rearrange("p (a b) -> p a b", b=128), func=AF.Copy),
                                     reads=[pokey], writes=[ackey + f"_{rq}"])
                            if g["tail"] == "win":
                                osb, oskey = osr.next()
                                w = g["w"]
                                if br == "A":
                                    ac, ackey = accs[hh]
                                    wl = (w % 4) * 512
                                    P.op("dve", lambda e, osb=osb, po=po, ac=ac, wl=wl: e.tensor_tensor(out=osb[:], in0=po[:], in1=ac[:, wl:wl + 512], op=ALU.add),
                                         reads=[pokey] + [ackey + f"_{q}" for q in range(4)], writes=[oskey])
                                else:
                                    P.op("act", lambda e, osb=osb, po=po: e.activation(out=osb[:], in_=po[:], func=AF.Copy), reads=[pokey], writes=[oskey])
                                prr_, prkey = psS.next()
                                prr = prr_[:].rearrange("p (a b) -> p a b", b=128)
                                for jj in range(4):
                                    P.op("pe", lambda e, prr=prr, osb=osb, jj=jj: e.transpose(out=prr[:, jj, :], in_=osb[:, jj * 128:(jj + 1) * 128], identity=identf[:]),
                                         reads=[oskey, "identf"], writes=[prkey])
                                rl, rlkey = rlr.next()
                                lcol = 64 if hh == 0 else 0
                                P.op("dve", lambda e, rl=rl, prr=prr, lcol=lcol: e.reciprocal(out=rl[:], in_=prr[:, :, lcol]), reads=[prkey], writes=[rlkey])
                                if hh == 0:
                                    state["og"] = ogr.next()
                                og, ogkey = state["og"]
                                P.op("dve", lambda e, og=og, prr=prr, rl=rl, hh=hh: e.tensor_tensor(
                                    out=og[:, :, hh * 64:(hh + 1) * 64], in0=prr[:, :, hh * 64:(hh + 1) * 64], in1=rl[:].unsqueeze(2).to_broadcast([128, 4, 64]), op=ALU.mult),
                                    reads=[prkey, rlkey], writes=[ogkey + f"_{hh}"])
                                if hh == 1 and dynC:
                                    P.op("sp", lambda e, og=og, w=w: e.dma_start(
                                        out=dr["ocq"][w * 512:(w + 1) * 512, ocol - 640:ocol - 640 + 128].rearrange("(b p) c -> p b c", p=128), in_=og[:]),
                                        reads=[ogkey + "_0", ogkey + "_1"], dma=ogkey)
                                elif hh == 1 and dynA:
                                    P.op("sp", lambda e, og=og, w=w: e.dma_start(
                                        out=dr["oaq"][w * 512:(w + 1) * 512, ocol:ocol + 128].rearrange("(b p) c -> p b c", p=128), in_=og[:]),
                                        reads=[ogkey + "_0", ogkey + "_1"], dma=ogkey)
                                elif hh == 1 and dyn:
                                    P.op("sp", lambda e, og=og, w=w: e.dma_start(
                                        out=dr["obq"][w * 512:(w + 1) * 512, ocol - 384:ocol - 384 + 128].rearrange("(b p) c -> p b c", p=128), in_=og[:]),
                                        reads=[ogkey + "_0", ogkey + "_1"], dma=ogkey)
                                elif hh == 1:
                                    P.op("sp", lambda e, og=og, w=w: e.dma_start(
                                        out=od[w * 512:(w + 1) * 512, ocol:ocol + 128].rearrange("(b p) c -> p b c", p=128), in_=og[:]),
                                        reads=[ogkey + "_0", ogkey + "_1"], dma=ogkey)

                        LOOK = 2
                        for i in range(ng + LOOK):
                            if pre_at is not None and i == pre_at:
                                load_v16(nxt)
                            if i < ng:
                                emit_front2(groups[i])
                            if i - LOOK >= 0:
                                emit_pv2(groups[i - LOOK])
                                emit_back(groups[i - LOOK], 0)
                                emit_back(groups[i - LOOK], 1)
                P.flush(final=(STOP == "s2"))
            if STOP == "s2":
                return nc

            with ExitStack() as st:
                wo = st.enter_context(nc.sbuf_tensor(U("wo"), [128, 8, D], BF16))
                wst3 = Ring(P, st, "wst3", 2, [128, 8, 256], F32)
                gbr = st.enter_context(nc.sbuf_tensor(U("gbr"), [128, D], F32))
                gpo = st.enter_context(nc.sbuf_tensor(U("gpo"), [128, D], F32))
                invw = st.enter_context(nc.sbuf_tensor(U("invw"), [128, 3], F32))
                o_r = Ring(P, st, "o_r", 4, [128, D], F32)
                z_r = Ring(P, st, "z_r", 4, [128, D // 2], F32)
                x_r = Ring(P, st, "x_r", 5, [128, D], F32)
                g_r = Ring(P, st, "g_r", 6, [128, D], F32)
                pc_r = Ring(P, st, "pc_r", 6, [128, D], F32)
                junk3 = st.enter_context(nc.sbuf_tensor(U("junk3"), [128, D], BF16))
                s3r = Ring(P, st, "s3r", 6, [128, 3], F32)
                s2r = Ring(P, st, "s2r", 6, [128, 2], F32)
                ybr = Ring(P, st, "ybr", 3, [128, D], BF16)
                yTr = Ring(P, st, "yTr", 3, [128, 8, 128], BF16)
                t_r = Ring(P, st, "t_r", 5, [128, D], F32)
                psT3 = Ring(P, st, "psT3", 2, [128, 8, 128], BF16, psum=True)
                psY = Ring(P, st, "psY", 3, [128, D], F32, psum=True)

                for ci in range(4):
                    wbuf, wkey = wst3.next()
                    c0 = ci * 256
                    P.op("sp", lambda e, wbuf=wbuf, c0=c0: e.dma_start(out=wbuf[:], in_=dr["w_out"][l][:, c0:c0 + 256].rearrange("(k p) c -> p k c", p=128)),
                         writes=[wkey], dma=wkey)
                    P.op("pool" if ci % 2 == 0 else "dve", lambda e, wbuf=wbuf, c0=c0: e.tensor_copy(out=wo[:, :, c0:c0 + 256], in_=wbuf[:]), reads=[wkey], writes=[f"wo{ci}"])
                WO = [f"wo{ci}" for ci in range(4)]
                P.op("sp", lambda e: e.dma_start(out=gbr[:], in_=dr["branch_gain"][l].partition_broadcast(128), allow_slow_non_contiguous=True), writes=["gbr"], dma="c5")
                P.op("sp", lambda e: e.dma_start(out=gpo[:], in_=dr["norm_post"][l].partition_broadcast(128), allow_slow_non_contiguous=True), writes=["gpo"], dma="c6")
                P.op("dve", lambda e: e.memset(invw[:, 0:1], 1.0 / 384), writes=["invw"])
                P.op("dve", lambda e: e.memset(invw[:, 1:2], 1.0 / 256), writes=["invw"])
                P.op("dve", lambda e: e.memset(invw[:, 2:3], 1.0 / 384), writes=["invw"])
                BR = ((0, 384), (384, 640), (640, 1024))
                blocks3 = []
                qblocks = []
                for (pn, S, src) in parts:
                    xsrc = dr["x_" + pn] if l == 0 else dr["y1_" + pn]
                    if last and pn == QPN:
                        P.op("sp", lambda e: e.dma_start(out=dr["oq"][:, 0:384], in_=dr["oaq"]), writes=["oq"], dma="cq0")
                        P.op("sp", lambda e: e.dma_start(out=dr["oq"][:, 640:1024], in_=dr["ocq"]), reads=["oq"], writes=["oq"], dma="cq4")
                        P.op("sp", lambda e, pn=pn: e.dma_start(out=dr["szq"], in_=dyn_ap(e, rrow2, 0, dr["sz_" + pn], [[D // 2, 2048], [1, D // 2]])), writes=["szq"], dma="cq1")
                        P.op("sp", lambda e, xsrc=xsrc: e.dma_start(out=dr["xq"], in_=dyn_ap(e, rrow, 0, xsrc, [[D, 2048], [1, D]])), writes=["xq"], dma="cq2")
                        P.op("sp", lambda e: e.dma_start(out=dr["oq"][:, 384:640], in_=dr["obq"]), reads=["oq"], writes=["oq"], dma="cq3")
                        qblocks = [dict(pn=pn, t0=blk * 128, osrc=dr["oq"], zsrc=dr["szq"], xsrc=dr["xq"], ydst=dr["yq_" + pn], keys=["oq", "szq", "xq"]) for blk in range(16)]
                        continue
                    ydst = dr["y_" + pn] if last else dr["y1_" + pn]
                    for blk in range(S // 128):
                        blocks3.append(dict(pn=pn, t0=blk * 128, osrc=dr["o_" + pn], zsrc=dr["sz_" + pn], xsrc=xsrc, ydst=ydst, keys=[]))
                blocks3 = blocks3 + qblocks

                def p_load(c):
                    pn, t0 = c["pn"], c["t0"]
                    ob, okey = o_r.next()
                    zb, zkey = z_r.next()
                    c["o"], c["z"] = (ob, okey), (zb, zkey)
                    osrc, zsrc = c["osrc"], c["zsrc"]
                    P.op("sp", lambda e: e.dma_start(out=ob[:], in_=osrc[t0:t0 + 128, :]), reads=c["keys"][0:1], writes=[okey], dma=okey)
                    P.op("sp", lambda e: e.dma_start(out=zb[:], in_=zsrc[t0:t0 + 128, :]), reads=c["keys"][1:2], writes=[zkey], dma=zkey)

                def p_g(c):
                    ob, okey = c["o"]
                    zb, zkey = c["z"]
                    gb, gkey = g_r.next()
                    c["g"] = (gb, gkey)
                    P.op("pool", lambda e: e.tensor_tensor(out=gb[:], in0=ob[:], in1=zb[:].bitcast(BF16), op=ALU.mult), reads=[okey, zkey], writes=[gkey])

                def p_sq(c):
                    gb, gkey = c["g"]
                    s3, s3key = s3r.next()
                    c["s3"] = (s3, s3key)
                    for bi, (c0, c1) in enumerate(BR):
                        P.op("act", lambda e, bi=bi, c0=c0, c1=c1: e.activation(out=junk3[:, c0:c1], in_=gb[:, c0:c1], func=AF.Square, accum_out=s3[:, bi:bi + 1]),
                             reads=[gkey], writes=[s3key, "junk3"])

                def p_r1(c):
                    s3, s3key = c["s3"]
                    P.op("dve", lambda e: e.tensor_tensor(out=s3[:], in0=s3[:], in1=invw[:], op=ALU.mult), reads=[s3key, "invw"], writes=[s3key])
                    P.op("dve", lambda e: e.tensor_scalar(out=s3[:], in0=s3[:], scalar1=1.0, scalar2=float(EPS), op0=ALU.mult, op1=ALU.add), reads=[s3key], writes=[s3key])

                def p_r2(c):
                    s3, s3key = c["s3"]
                    P.op("pool", lambda e: e.tensor_tensor(out=s3[:], in0=s3[:], in1=nhalf[:, 0:3], op=ALU.pow), reads=[s3key], writes=[s3key])

                def p_y(c):
                    gb, gkey = c["g"]
                    s3, s3key = c["s3"]
                    yb, ykey = ybr.next()
                    c["y"] = (yb, ykey)
                    for bi, (c0, c1) in enumerate(BR):
                        P.op("dve", lambda e, bi=bi, c0=c0, c1=c1: e.scalar_tensor_tensor(
                            out=yb[:, c0:c1], in0=gb[:, c0:c1], scalar=s3[:, bi:bi + 1], in1=gbr[:, c0:c1], op0=ALU.mult, op1=ALU.mult),
                            reads=[gkey, s3key, "gbr"], writes=[ykey])

                def p_T(c):
                    yb, ykey = c["y"]
                    pT, pTkey = psT3.next()
                    c["pT"] = (pT, pTkey)
                    for kc in range(8):
                        P.op("pe", lambda e, kc=kc: e.transpose(out=pT[:, kc, :], in_=yb[:, kc * 128:(kc + 1) * 128], identity=identb[:]),
                             reads=[ykey, "identb"], writes=[pTkey])

                def p_yT(c):
                    pT, pTkey = c["pT"]
                    yT, yTkey = yTr.next()
                    c["yT"] = (yT, yTkey)
                    P.op("act", lambda e: e.activation(out=yT[:], in_=pT[:], func=AF.Copy), reads=[pTkey], writes=[yTkey])

                def p_mm(c):
                    yT, yTkey = c["yT"]
                    py, pykey = psY.next()
                    c["py"] = (py, pykey)
                    for n in range(2):
                        for kc in range(8):
                            P.op("pe", lambda e, n=n, kc=kc: e.matmul(py[:, n * 512:(n + 1) * 512], lhsT=yT[:, kc, :], rhs=wo[:, kc, n * 512:(n + 1) * 512],
                                                                      start=(kc == 0), stop=(kc == 7)),
                                 reads=[yTkey] + WO, writes=[pykey])

                def p_ev(c):
                    py, pykey = c["py"]
                    s2, s2key = s2r.next()
                    c["s2"] = (s2, s2key)
                    pc, pckey = pc_r.next()
                    c["pc"] = (pc, pckey)
                    for n in range(2):
                        P.op("act", lambda e, n=n: e.activation(out=junk3[:, n * 512:(n + 1) * 512], in_=py[:, n * 512:(n + 1) * 512], func=AF.Square, accum_out=s2[:, n:n + 1]),
                             reads=[pykey], writes=[s2key, "junk3"])
                    P.op("act", lambda e: e.activation(out=pc[:], in_=py[:], func=AF.Copy), reads=[pykey], writes=[pckey])

                def p_r3(c):
                    s2, s2key = c["s2"]
                    P.op("dve", lambda e: e.tensor_tensor(out=s2[:, 0:1], in0=s2[:, 0:1], in1=s2[:, 1:2], op=ALU.add), reads=[s2key], writes=[s2key])
                    P.op("dve", lambda e: e.tensor_scalar(out=s2[:, 0:1], in0=s2[:, 0:1], scalar1=1.0 / D, scalar2=float(EPS), op0=ALU.mult, op1=ALU.add), reads=[s2key], writes=[s2key])

                def p_r4(c):
                    s2, s2key = c["s2"]
                    P.op("pool", lambda e: e.tensor_tensor(out=s2[:, 0:1], in0=s2[:, 0:1], in1=nhalf[:, 0:1], op=ALU.pow), reads=[s2key], writes=[s2key])
                    t0, xsrc = c["t0"], c["xsrc"]
                    xb, xkey = x_r.next()
                    c["x"] = (xb, xkey)
                    P.op("sp", lambda e: e.dma_start(out=xb[:], in_=xsrc[t0:t0 + 128, :]), reads=c["keys"][2:3], writes=[xkey], dma=xkey)

                def p_stt(c):
                    pc, pckey = c["pc"]
                    s2, s2key = c["s2"]
                    tb, tkey = t_r.next()
                    c["t"] = (tb, tkey)
                    P.op("dve", lambda e: e.scalar_tensor_tensor(out=tb[:], in0=pc[:], scalar=s2[:, 0:1], in1=gpo[:], op0=ALU.mult, op1=ALU.mult),
                         reads=[pckey, s2key, "gpo"], writes=[tkey])

                def p_add(c):
                    tb, tkey = c["t"]
                    xb, xkey = c["x"]
                    P.op("pool", lambda e: e.tensor_tensor(out=tb[:], in0=tb[:], in1=xb[:], op=ALU.add), reads=[tkey, xkey], writes=[tkey])

                def p_st(c):
                    tb, tkey = c["t"]
                    t0, ydst = c["t0"], c["ydst"]
                    P.op("sp", lambda e: e.dma_start(out=ydst[t0:t0 + 128, :], in_=tb[:]), reads=[tkey], dma=tkey)

                phases = [(p_load, 0), (p_g, 2), (p_sq, 3), (p_r1, 4), (p_r2, 5), (p_y, 6), (p_T, 7), (p_yT, 8), (p_mm, 9), (p_ev, 10),
                          (p_r3, 11), (p_r4, 12), (p_stt, 14), (p_add, 15), (p_st, 17)]
                nb3 = len(blocks3)
                for i in range(nb3 + 18):
                    for fn, dly in phases:
                        if 0 <= i - dly < nb3:
                            fn(blocks3[i - dly])
                P.flush(final=last)
        print(f"[build] instructions: {P.n_ins}", flush=True)
    return nc


_PARTS = [("p", 8192, "xp"), ("s0", 2048, "xs0"), ("s1", 2048, "xs1")]


def kernel(x_prompt, x_sample, norm_pre, w_in, q_norm, k_norm, rel_bias, branch_gain, w_out, norm_post):
    f = lambda a: np.ascontiguousarray(np.asarray(a, dtype=np.float32))
    x_prompt, x_sample = f(x_prompt), f(x_sample)
    ropea, ropeb, ea, ident = _const_tables()
    efraw = _c_bias_tables(f(rel_bias))
    nc = build(_PARTS, 2)
    shared = dict(w_in=f(w_in), w_out=f(w_out), norm_pre=f(norm_pre), norm_post=f(norm_post), branch_gain=f(branch_gain),
                  q_norm=f(q_norm), k_norm=f(k_norm), ropea=ropea, ropeb=ropeb, ea=ea, ident=ident, efraw=efraw)
    in_maps = []
    for c in range(8):
        m = dict(shared)
        q0 = (c % 4) * 2048
        m["qoff"] = np.array([[q0, q0 * D, q0 * (D // 2), q0 * 1536]], dtype=np.int32)
        m["cedge"] = np.ascontiguousarray(_c_edge_tables(efraw[-1], c % 4).reshape(6, 128, 24 * 128))
        m["x_p"] = x_prompt[c // 4]
        m["x_s0"] = x_sample[2 * c]
        m["x_s1"] = x_sample[2 * c + 1]
        in_maps.append(m)
    res = run_bass_kernel_spmd(nc, in_maps, core_ids=list(range(8)))
    r = res.results
    y_prompt = np.stack([np.concatenate([np.asarray(r[4 * b + q]["yq_p"], dtype=np.float32) for q in range(4)], axis=0) for b in range(2)], axis=0)
    ys = []
    for c in range(8):
        ys.append(np.asarray(r[c]["y_s0"], dtype=np.float32))
        ys.append(np.asarray(r[c]["y_s1"], dtype=np.float32))
    y_sample = np.stack(ys, axis=0)
    return (y_prompt, y_sample)
```

```python
import numpy as np
from contextlib import ExitStack
import concourse.bass as bass
import concourse.mybir as mybir
from concourse.bass_utils import run_bass_kernel_spmd

F32 = mybir.dt.float32
BF16 = mybir.dt.bfloat16
AF = mybir.ActivationFunctionType
ALU = mybir.AluOpType
AX = mybir.AxisListType

D = 1024
INW = 3840
EPS = 1e-6
NEG = -30000.0
STOP = None
QUARTER = True
LIMIT = None
AQ, AK, AV, AZ, BQ, BK, BV, BZ, CQ, CK, CV, CZ = 0, 384, 768, 1152, 1536, 1792, 1920, 2048, 2304, 2688, 3072, 3456

COMPUTE = ("pe", "act", "dve", "pool")
ISSUERS = ("pe", "act", "dve", "pool", "sp")
ENGOBJ = {"pe": "tensor", "act": "scalar", "dve": "vector", "pool": "gpsimd", "sp": "sync"}


class Op:
    __slots__ = ("eng", "fn", "is_dma", "sem", "ticket", "signal", "waits")

    def __init__(self, eng, fn, is_dma, sem):
        self.eng = eng
        self.fn = fn
        self.is_dma = is_dma
        self.sem = sem
        self.ticket = None
        self.signal = is_dma
        self.waits = []


class Prog:
    def __init__(self, nc, stack, block):
        self.nc = nc
        self.stack = stack
        self.block = block
        self.streams = {e: [] for e in ISSUERS}
        self.esem = {e: self.new_sem("sem_" + e) for e in COMPUTE}
        self.bar = self.new_sem("sem_bar")
        self.bar_count = 0
        self.ecount = {e: 0 for e in COMPUTE}
        self.last_w = {}
        self.readers = {}
        self.dma_sems = {}
        self.dma_count = {}
        self.pending = None
        self.alias = {}
        self.n_ins = 0

    def new_sem(self, name):
        return self.stack.enter_context(self.nc.semaphore(name))

    def _dep(self, op, prod):
        if prod is None or prod is op:
            return
        if (not op.is_dma) and (not prod.is_dma) and prod.eng == op.eng and op.eng == "pe":
            return
        prod.signal = True
        op.waits.append(prod)

    def op(self, eng, fn, reads=(), writes=(), dma=None):
        self.nrec = getattr(self, "nrec", 0) + 1
        if LIMIT is not None and self.nrec > LIMIT:
            return None
        is_dma = dma is not None
        ps_reads = [r for r in reads if r.startswith("ps")]
        if ps_reads:
            reads = [r for r in reads if not r.startswith("ps")]
            writes = list(writes) + ps_reads
        o = Op(eng, fn, is_dma, dma)
        if is_dma:
            if dma not in self.alias:
                pref = "w" if eng == "pool" else "g"
                self.alias[dma] = f"{pref}{sum(1 for v in self.alias.values() if v.startswith(pref))}"
            dma = self.alias[dma]
            o.sem = dma
            if dma not in self.dma_sems:
                self.dma_sems[dma] = self.new_sem("d_" + dma)
                self.dma_count[dma] = 0
            self.dma_count[dma] += 16
            o.ticket = self.dma_count[dma]
        for r in reads:
            self._dep(o, self.last_w.get(r))
        for w in writes:
            self._dep(o, self.last_w.get(w))
            for rd in self.readers.get(w, ()):
                self._dep(o, rd)
        for r in reads:
            self.readers.setdefault(r, []).append(o)
        for w in writes:
            self.last_w[w] = o
            self.readers[w] = []
        self.streams[eng].append(o)
        return o

    def flush(self, final=False):
        for e in COMPUTE:
            ops = [o for o in self.streams[e] if not o.is_dma]
            if ops:
                ops[-1].signal = True
            for o in ops:
                if o.signal:
                    self.ecount[e] += 1
                    o.ticket = self.ecount[e]
        pending = self.pending
        dma_final = dict(self.dma_count)
        self.bar_count += 1
        bar_val = self.bar_count
        ecount = dict(self.ecount)

        def make(ename):
            ops = self.streams[ename]

            def body(eng):
                waited = {}
                if pending is not None:
                    for key, val in pending.items():
                        if val > 0:
                            sem = self.bar if key == "bar" else self.esem[key]
                            if key != ename:
                                eng.wait_ge(sem, val)
                for o in ops:
                    need = {}
                    for p in o.waits:
                        key = ("d", p.sem) if p.is_dma else ("e", p.eng)
                        if p.ticket > need.get(key, 0):
                            need[key] = p.ticket
                    for key, val in need.items():
                        if waited.get(key, 0) >= val:
                            continue
                        waited[key] = val
                        sem = self.dma_sems[key[1]] if key[0] == "d" else self.esem[key[1]]
                        eng.wait_ge(sem, val)
                    ins = o.fn(eng)
                    self.n_ins += 1
                    if o.is_dma:
                        ins.then_inc(self.dma_sems[o.sem], 16)
                    elif o.signal:
                        ins.then_inc(self.esem[o.eng], 1)
                if ename == "sp":
                    for s, v in dma_final.items():
                        if v > 0:
                            eng.wait_ge(self.dma_sems[s], v)
                    eng.sem_inc(self.bar, 1)

            return body

        for ename in ISSUERS:
            if ename != "sp" and not self.streams[ename] and pending is None:
                continue
            getattr(self.block, ENGOBJ[ename])(make(ename))
        self.pending = dict(ecount)
        self.pending["bar"] = bar_val
        self.streams = {e: [] for e in ISSUERS}
        self.last_w = {}
        self.readers = {}
        self.alias = {}
        if final:
            pend = self.pending

            def fin(eng):
                eng.wait_ge(self.bar, pend["bar"])

            for ename in ("pe", "act", "dve", "pool"):
                getattr(self.block, ENGOBJ[ename])(fin)


_UID = [0]


def U(name):
    _UID[0] += 1
    return f"{name}_u{_UID[0]}"


class Ring:
    def __init__(self, P, st, name, n, shape, dt, psum=False):
        self.n = n
        self.name = name
        self.i = -1
        alloc = P.nc.psum_tensor if psum else P.nc.sbuf_tensor
        self.bufs = [st.enter_context(alloc(U(f"{name}{k}"), list(shape), dt)) for k in range(n)]

    def next(self):
        self.i += 1
        k = self.i % self.n
        return self.bufs[k], f"{self.name}{k}"

    def cur(self):
        k = self.i % self.n
        return self.bufs[k], f"{self.name}{k}"


def _const_tables():
    SMAX = 8192
    t = np.arange(SMAX, dtype=np.float32)
    fa = (500000.0 ** (-np.arange(0, 16, 2, dtype=np.float32) / 16)).astype(np.float32)
    anga = (t[:, None] * fa[None, :]).astype(np.float32).astype(np.float64)
    fb = (10000.0 ** (-np.arange(0, 32, 2, dtype=np.float32) / 32)).astype(np.float32)
    row = (np.arange(SMAX) // 64).astype(np.float32)
    col = (np.arange(SMAX) % 64).astype(np.float32)
    angr = (row[:, None] * fb[None, :]).astype(np.float32).astype(np.float64)
    angc = (col[:, None] * fb[None, :]).astype(np.float32).astype(np.float64)
    angb = np.concatenate([angr, angc], axis=1)

    def tm(a):
        return np.ascontiguousarray(a.reshape(SMAX // 128, 128, -1).transpose(1, 0, 2)).astype(np.float32)

    ropea = np.stack([tm(np.cos(anga)), tm(np.sin(anga))], axis=1)
    ropeb = np.stack([tm(np.cos(angb)), tm(np.sin(angb))], axis=1)
    kk = np.arange(128)[:, None, None]
    dl = (np.arange(5) - 2)[None, :, None]
    ii = np.arange(128)[None, None, :]
    dd = 128 * dl + kk - ii
    ad = np.abs(dd)
    mult = (ad <= 64).astype(np.float32) + ((dd % 4 == 0) & (ad <= 256))
    du = 128 * (np.arange(3) - 1)[None, :, None] + kk - ii
    m16 = (np.abs(du) <= 64).astype(np.float32)
    kk2 = np.arange(128)[:, None]
    ii2 = np.arange(128)[None, :]
    m16q = np.stack([(kk2 >= ii2), (kk2 <= ii2)], axis=1).astype(np.float32)
    ea = np.ascontiguousarray(np.concatenate([mult, m16, m16q], axis=1).astype(np.float32))
    ident = np.eye(128, dtype=np.float32)
    return ropea, ropeb, ea, ident


def _c_bias_tables(rel_bias):
    L = rel_bias.shape[0]
    krl = (np.arange(128) // 64)[:, None, None]
    kc = (np.arange(128) % 64)[:, None, None]
    dlt = (np.arange(7) - 3)[None, :, None]
    rl = (np.arange(128) // 64)[None, None, :]
    qc = (np.arange(128) % 64)[None, None, :]
    dr = 2 * dlt + krl - rl
    ro = dr + 7
    co = np.clip(kc - qc + 15, 0, 30)
    cs = np.clip(qc - 8, 0, 48)
    valid = (kc >= cs) & (kc < cs + 16) & (ro >= 0) & (ro <= 14)
    ro_c = np.clip(ro, 0, 14)
    ro_b, co_b, valid_b = np.broadcast_arrays(ro_c, co, valid)
    out = np.empty((L, 6, 128, 7, 128), dtype=np.float32)
    for l in range(L):
        for h in range(6):
            g = rel_bias[l, h][ro_b, co_b]
            out[l, h] = np.where(valid_b, g, np.float32(NEG))
    return out


def _c_edge_tables(efraw_l, qd):
    negt = np.full((128, 128), np.float32(NEG), dtype=np.float32)

    def interior_tile(h, dl):
        if dl < -2 or dl > 2:
            return negt
        t = efraw_l[h][:, dl + 3, :].copy()
        if dl == -2:
            t[0:64, 64:128] = NEG
        if dl == 2:
            t[64:128, :] = NEG
            t[0:64, 0:64] = NEG
        return t

    def full_tile(h, dl):
        if dl < -3 or dl > 3:
            return negt
        return efraw_l[h][:, dl + 3, :]

    out = np.empty((6, 128, 4, 6, 128), dtype=np.float32)
    for h in range(6):
        for side in range(2):
            for bi in range(2):
                for sl_ in range(6):
                    if side == 0:
                        b, a_own = bi, sl_ - 2
                        edge, ok = (qd == 0), (0 <= a_own <= 3)
                    else:
                        b, a_own = 14 + bi, 12 + sl_
                        edge, ok = (qd == 3), (12 <= a_own <= 15)
                    if edge:
                        t = full_tile(h, a_own - b) if ok else negt
                    else:
                        t = interior_tile(h, a_own - b)
                    out[h, :, side * 2 + bi, sl_, :] = t
    return out


def build(parts, n_layers, debug=False):
    nc = bass.Bass("TRN2", target_bir_lowering=False)
    dr = {}

    def din(name, shape, dt=F32):
        dr[name] = nc.dram_tensor(name, list(shape), dt, kind="ExternalInput").ap()
        return dr[name]

    def dout(name, shape, dt=F32):
        dr[name] = nc.dram_tensor(name, list(shape), dt, kind="ExternalOutput").ap()
        return dr[name]

    def dscr(name, shape, dt):
        if debug:
            dr[name] = nc.dram_tensor(name, list(shape), dt, kind="ExternalOutput").ap()
        else:
            dr[name] = nc.dram_tensor(name, list(shape), dt).ap()
        return dr[name]

    QPN = "p" if (QUARTER and any(pn == "p" for (pn, _, _) in parts)) else None
    if QPN is not None:
        dscr("oq", [2048, D], F32)
        dscr("szq", [2048, D // 2], F32)
        dscr("xq", [2048, D], F32)
        dscr("obq", [2048, 256], F32)
    for (pn, S, src) in parts:
        din("x_" + pn, [S, D])
        if pn == QPN:
            dout("yq_" + pn, [2048, D])
        else:
            dout("y_" + pn, [S, D])
        dscr("qt_" + pn, [1024, S], BF16)
        if pn == QPN:
            dscr("ktpad", [896, S + 2048], BF16)
            dscr("vpad", [S + 2048, 8 * 192], BF16)
            dr["kt_" + pn] = dr["ktpad"][:, 1024:1024 + S]
            dr["v_" + pn] = dr["vpad"][1024:1024 + S, :]
            dscr("ktc", [768, 4096], BF16)
            dscr("vc", [4096, 1152], BF16)
            dscr("oaq", [2048, 384], F32)
            dscr("ocq", [2048, 384], F32)
        else:
            dscr("kt_" + pn, [896, S], BF16)
            dscr("v_" + pn, [S, 8 * 192], BF16)
        dscr("sz_" + pn, [S, D // 2], F32)
        dscr("o_" + pn, [S, D], F32)
        if n_layers > 1:
            dscr("y1_" + pn, [S, D], F32)
    din("w_in", [n_layers, D, INW])
    din("w_out", [n_layers, D, D])
    din("norm_pre", [n_layers, D])
    din("norm_post", [n_layers, D])
    din("branch_gain", [n_layers, D])
    din("q_norm", [n_layers, 64])
    din("k_norm", [n_layers, 64])
    din("ropea", [128, 2, 64, 8])
    din("ropeb", [128, 2, 64, 32])
    din("ea", [128, 10, 128])
    din("ident", [128, 128])
    din("efraw", [n_layers, 6, 128, 7, 128])
    if QPN is not None:
        dr["qoff"] = nc.dram_tensor("qoff", [1, 4], mybir.dt.int32, kind="ExternalInput").ap()
        din("cedge", [6, 128, 24 * 128])

    with ExitStack() as top:
        block = top.enter_context(nc.Block())
        P = Prog(nc, top, block)
        identf = top.enter_context(nc.sbuf_tensor("identf", [128, 128], F32))
        identb = top.enter_context(nc.sbuf_tensor("identb", [128, 128], BF16))
        epsc = top.enter_context(nc.sbuf_tensor("epsc", [128, 8], F32))
        nhalf = top.enter_context(nc.sbuf_tensor("nhalf", [128, 8], F32))
        P.op("sp", lambda e: e.dma_start(out=identf[:], in_=dr["ident"]), writes=["identf"], dma="c0")
        P.op("dve", lambda e: e.tensor_copy(out=identb[:], in_=identf[:]), reads=["identf"], writes=["identb"])
        P.op("dve", lambda e: e.memset(epsc[:], EPS), writes=["epsc"])
        P.op("dve", lambda e: e.memset(nhalf[:], -0.5), writes=["nhalf"])
        if QPN is not None:
            qs = top.enter_context(nc.sbuf_tensor("qs", [1, 4], mybir.dt.int32))
            rq0 = top.enter_context(nc.sync.register("rq0"))
            rrow = top.enter_context(nc.sync.register("rrow"))
            rrow2 = top.enter_context(nc.sync.register("rrow2"))
            rv = top.enter_context(nc.sync.register("rv"))
            rtmp = [top.enter_context(nc.sync.register(f"rtmp{i}")) for i in range(4)]
            rti = [0]
            P.op("sp", lambda e: e.dma_start(out=qs[:], in_=dr["qoff"]), writes=["qs"], dma="c1")

            def setregs(e):
                e.reg_load(rq0, qs[0:1, 0:1])
                e.reg_load(rrow2, qs[0:1, 2:3])
                e.reg_load(rv, qs[0:1, 3:4])
                return e.reg_load(rrow, qs[0:1, 1:2])
            P.op("sp", setregs, reads=["qs"], writes=["regs"])

            SQ = [S for (pn_, S, _) in parts if pn_ == QPN][0]
            ztstack = ExitStack()
            zt = ztstack.enter_context(nc.sbuf_tensor("zt", [128, 1536], BF16))
            P.op("pool", lambda e: e.memset(zt[:], 0.0), writes=["zt"])
            zi = 0
            for side in (0, 1024 + SQ):
                for k in range(7):
                    P.op("sp", lambda e, k=k, side=side: e.dma_start(out=dr["ktpad"][k * 128:(k + 1) * 128, side:side + 1024], in_=zt[:, 0:1024]), reads=["zt"], dma=f"zp{zi}")
                    zi += 1
                for k in range(8):
                    P.op("sp", lambda e, k=k, side=side: e.dma_start(out=dr["vpad"][side + k * 128:side + (k + 1) * 128, :], in_=zt[:]), reads=["zt"], dma=f"zp{zi}")
                    zi += 1

            def dyn_ap(e, base_reg, const, tensor_ap, pattern):
                t = rtmp[rti[0] % 4]
                rti[0] += 1
                e.reg_add(t, base_reg, int(const))
                return bass.AP(tensor_ap.tensor, t, pattern)
        P.flush(final=(STOP == "pre"))
        if QPN is not None:
            ztstack.close()
        if STOP == "pre":
            return nc

        def rstd_ops(v_ap, key, n, scale):
            P.op("dve", lambda e: e.tensor_scalar(out=v_ap, in0=v_ap, scalar1=float(scale), scalar2=float(EPS),
                                                  op0=ALU.mult, op1=ALU.add), reads=[key], writes=[key])
            P.op("pool", lambda e: e.tensor_tensor(out=v_ap, in0=v_ap, in1=nhalf[:, 0:n], op=ALU.pow),
                 reads=[key], writes=[key])

        for l in range(n_layers):
            last = l == n_layers - 1
            with ExitStack() as st:
                wsb = st.enter_context(nc.sbuf_tensor(U("wsb"), [128, 8, INW], BF16))
                gpre = st.enter_context(nc.sbuf_tensor(U("gpre"), [128, 8], F32))
                gqk = st.enter_context(nc.sbuf_tensor(U("gqk"), [128, 6, 64], F32))
                ropa = st.enter_context(nc.sbuf_tensor(U("ropa"), [128, 2, 64, 8], F32))
                ropb = st.enter_context(nc.sbuf_tensor(U("ropb"), [128, 2, 64, 32], F32))
                xin = Ring(P, st, "xin", 5, [128, D], F32)
                junk = st.enter_context(nc.sbuf_tensor(U("junk"), [128, D], BF16))
                ssr = Ring(P, st, "ssr", 5, [128, 1], F32)
                xnr = Ring(P, st, "xnr", 2, [128, D], BF16)
                qfr = Ring(P, st, "qfr", 2, [128, 6, 64], F32)
                bfr = Ring(P, st, "bfr", 2, [128, 512], F32)
                hTr = Ring(P, st, "hTr", 2, [128, 8, 512], BF16)
                qts = Ring(P, st, "qts", 2, [128, 8, 512], BF16)
                kts = Ring(P, st, "kts", 2, [128, 7, 512], BF16)
                vsr = Ring(P, st, "vsr", 4, [128, 8, 192], BF16)
                szr = Ring(P, st, "szr", 4, [128, D], BF16)
                qar = Ring(P, st, "qar", 3, [128, 6, 64], BF16)
                kar = Ring(P, st, "kar", 3, [128, 6, 64], BF16)
                qkbr = Ring(P, st, "qkbr", 3, [128, 6, 64], BF16)
                sqb = st.enter_context(nc.sbuf_tensor(U("sqb"), [128, 6, 64], F32))
                xgb = st.enter_context(nc.sbuf_tensor(U("xgb"), [128, 6, 64], F32))
                ss6 = st.enter_context(nc.sbuf_tensor(U("ss6"), [128, 6], F32))
                tA = [st.enter_context(nc.sbuf_tensor(U(f"tA{i}"), [128, 6, 8], F32)) for i in range(4)]
                tB = [st.enter_context(nc.sbuf_tensor(U(f"tB{i}"), [128, 6, 2, 16], F32)) for i in range(4)]
                psT = Ring(P, st, "psT", 2, [128, 8, 128], BF16, psum=True)
                psT2 = Ring(P, st, "psT2", 2, [128, 8, 128], BF16, psum=True)
                psM = Ring(P, st, "psM", 4, [128, 512], F32, psum=True)
                psF = psM

                P.op("sp", lambda e: e.dma_start(out=gpre[:], in_=dr["norm_pre"][l].rearrange("(k p) -> p k", p=128),
                                                 allow_slow_non_contiguous=True), writes=["gpre"], dma="c0")
                P.op("sp", lambda e: e.dma_start(out=gqk[:, 0:4, :], in_=dr["q_norm"][l].partition_broadcast(128).unsqueeze(1).to_broadcast([128, 4, 64]),
                                                 allow_slow_non_contiguous=True), writes=["gqk_q"], dma="c1")
                P.op("sp", lambda e: e.dma_start(out=gqk[:, 4:6, :], in_=dr["k_norm"][l].partition_broadcast(128).unsqueeze(1).to_broadcast([128, 2, 64]),
                                                 allow_slow_non_contiguous=True), writes=["gqk_k"], dma="c2")
                P.op("sp", lambda e: e.dma_start(out=ropa[:], in_=dr["ropea"]), writes=["ropa"], dma="c3")
                P.op("sp", lambda e: e.dma_start(out=ropb[:], in_=dr["ropeb"]), writes=["ropb"], dma="c4")
                for ci in range(8):
                    c0 = ci * 480
                    P.op("pool", lambda e, c0=c0: e.dma_start(
                        out=wsb[:, :, c0:c0 + 480], in_=dr["w_in"][l][:, c0:c0 + 480].rearrange("(k p) c -> p k c", p=128)),
                        writes=[f"wsb{ci}"], dma=f"wld{ci}")
                WALL = [f"wsb{ci}" for ci in range(8)]

                def wkeys(c0, c1):
                    return [f"wsb{ci}" for ci in range(c0 // 480, (c1 - 1) // 480 + 1)]

                from collections import deque
                blocks1 = []
                for (pn, S, src) in parts:
                    xsrc = dr["x_" + pn] if l == 0 else dr["y1_" + pn]
                    for blk in range(S // 128):
                        blocks1.append(dict(pn=pn, blk=blk, j=blk % 4, tg=blk // 4, xsrc=xsrc))
                pend = deque()

                def run_pend(keep):
                    while len(pend) > keep:
                        pend.popleft()()

                def s1_load(c):
                    xb, xkey = xin.next()
                    c["x"] = (xb, xkey)
                    t0, xsrc = c["blk"] * 128, c["xsrc"]
                    P.op("sp", lambda e: e.dma_start(out=xb[:], in_=xsrc[t0:t0 + 128, :]), writes=[xkey], dma=xkey)

                grp = {}

                def s1_fa(c):
                    xb, xkey = c["x"]
                    ss, sskey = ssr.next()
                    c["ss"] = (ss, sskey)
                    P.op("act", lambda e: e.activation(out=junk[:], in_=xb[:], func=AF.Square, accum_out=ss[:]), reads=[xkey], writes=[sskey, "junk"])
                    rstd_ops(ss[:], sskey, 1, 1.0 / D)

                def s1_front(c):
                    j = c["j"]
                    if j == 0:
                        grp["hT"] = hTr.next()
                        grp["qt"] = qts.next()
                        grp["kt"] = kts.next()
                    c["hT"], c["qt"], c["kt"] = grp["hT"], grp["qt"], grp["kt"]
                    hT, hkey = c["hT"]
                    xb, xkey = c["x"]
                    ss, sskey = c["ss"]
                    xn, xnkey = xnr.next()
                    P.op("act", lambda e: e.activation(out=xn[:], in_=xb[:], func=AF.Copy, scale=ss[:]), reads=[xkey, sskey], writes=[xnkey])
                    pT, pTkey = psT.next()
                    for kc in range(8):
                        P.op("pe", lambda e, kc=kc: e.transpose(out=pT[:, kc, :], in_=xn[:, kc * 128:(kc + 1) * 128], identity=identb[:]),
                             reads=[xnkey, "identb"], writes=[pTkey])
                    P.op("dve", lambda e: e.tensor_tensor(
                        out=hT[:, :, j * 128:(j + 1) * 128], in0=pT[:], in1=gpre[:].unsqueeze(2).to_broadcast([128, 8, 128]), op=ALU.mult),
                        reads=[pTkey, "gpre"], writes=[hkey + f"_{j}"])

                def s1_main(c):
                    j, blk, pn = c["j"], c["blk"], c["pn"]
                    hT, hkey = c["hT"]
                    qt_s, qkey = c["qt"]
                    kt_s, kkey = c["kt"]
                    hk = hkey + f"_{j}"

                    def tok_mm(c0, c1):
                        run_pend(2)
                        pm, pmkey = psM.next()
                        for kc in range(8):
                            P.op("pe", lambda e, kc=kc: e.matmul(pm[:, 0:c1 - c0], lhsT=hT[:, kc, j * 128:(j + 1) * 128], rhs=wsb[:, kc, c0:c1],
                                                                 start=(kc == 0), stop=(kc == 7)),
                                 reads=[hk] + wkeys(c0, c1), writes=[pmkey])
                        return pm, pmkey

                    for which, c0, ring, dst in (("q", AQ, qar, qt_s), ("k", AK, kar, kt_s)):
                        pm, pmkey = tok_mm(c0, c0 + 384)
                        qf, qfkey = qfr.next()
                        P.op("act", lambda e, qf=qf, pm=pm: e.activation(out=qf[:].rearrange("p h d -> p (h d)"), in_=pm[:, 0:384], func=AF.Copy), reads=[pmkey], writes=[qfkey])
                        ob, okey = ring.next()
                        P.op("pool", lambda e, ob=ob, qf=qf: e.tensor_copy(out=ob[:], in_=qf[:]), reads=[qfkey], writes=[okey])
                        cosb = ropa[:, 0, blk, :].unsqueeze(1).to_broadcast([128, 6, 8])
                        sinb = ropa[:, 1, blk, :].unsqueeze(1).to_broadcast([128, 6, 8])
                        x1 = qf[:, :, 0:8]
                        x2 = qf[:, :, 8:16]
                        P.op("dve", lambda e, x1=x1, cosb=cosb: e.tensor_tensor(out=tA[0][:], in0=x1, in1=cosb, op=ALU.mult), reads=[qfkey, "ropa"], writes=["tA0"])
                        P.op("dve", lambda e, x2=x2, sinb=sinb: e.tensor_tensor(out=tA[1][:], in0=x2, in1=sinb, op=ALU.mult), reads=[qfkey, "ropa"], writes=["tA1"])
                        P.op("dve", lambda e, x2=x2, cosb=cosb: e.tensor_tensor(out=tA[2][:], in0=x2, in1=cosb, op=ALU.mult), reads=[qfkey, "ropa"], writes=["tA2"])
                        P.op("dve", lambda e, x1=x1, sinb=sinb: e.tensor_tensor(out=tA[3][:], in0=x1, in1=sinb, op=ALU.mult), reads=[qfkey, "ropa"], writes=["tA3"])
                        P.op("dve", lambda e, ob=ob: e.tensor_tensor(out=ob[:, :, 0:8], in0=tA[0][:], in1=tA[1][:], op=ALU.subtract), reads=["tA0", "tA1", okey], writes=[okey])
                        P.op("dve", lambda e, ob=ob: e.tensor_tensor(out=ob[:, :, 8:16], in0=tA[2][:], in1=tA[3][:], op=ALU.add), reads=["tA2", "tA3", okey], writes=[okey])

                        def fin_a(ob=ob, okey=okey, dst=dst, which=which):
                            pT2, pT2key = psT2.next()
                            obf = ob[:].rearrange("p h d -> p (h d)")
                            for tt in range(3):
                                P.op("pe", lambda e, tt=tt: e.transpose(out=pT2[:, tt, :], in_=obf[:, tt * 128:(tt + 1) * 128], identity=identb[:]),
                                     reads=[okey, "identb"], writes=[pT2key])
                            dkey = (qkey if which == "q" else kkey) + f"_{j}a"
                            P.op("act", lambda e: e.activation(out=dst[:, 0:3, j * 128:(j + 1) * 128], in_=pT2[:, 0:3, :], func=AF.Copy),
                                 reads=[pT2key], writes=[dkey])
                        pend.append(fin_a)
                    vs, vkey = vsr.next()
                    c["vs"] = (vs, vkey)
                    pm, pmkey = tok_mm(AV, AV + 384)
                    P.op("dve", lambda e, pm=pm: e.tensor_copy(out=vs[:, 0:3, :].rearrange("p a (s d) -> p a s d", d=64)[:, :, 0::2, :], in_=pm[:, 0:384].rearrange("p (a s d) -> p a s d", s=2, d=64)),
                         reads=[pmkey], writes=[vkey + "a"])
                    P.op("pool", lambda e: e.memset(vs[:, :, 64:128], 1.0), writes=[vkey + "one"])
                    sz, szkey = szr.next()
                    c["sz"] = (sz, szkey)
                    pm, pmkey = tok_mm(AZ, AZ + 384)
                    P.op("act", lambda e, pm=pm: e.activation(out=sz[:, 0:384], in_=pm[:, 0:384], func=AF.Silu), reads=[pmkey], writes=[szkey + "a"])
                    pm, pmkey = tok_mm(BQ, BQ + 512)
                    bf, bfkey = bfr.next()
                    P.op("act", lambda e, pm=pm, bf=bf: e.activation(out=bf[:], in_=pm[:], func=AF.Copy), reads=[pmkey], writes=[bfkey])
                    bf6 = bf[:, 0:384].rearrange("p (h d) -> p h d", d=64)
                    P.op("pool", lambda e, bf6=bf6: e.tensor_tensor(out=sqb[:], in0=bf6, in1=bf6, op=ALU.mult), reads=[bfkey], writes=["sqb"])
                    P.op("dve", lambda e: e.tensor_reduce(out=ss6[:], in_=sqb[:], axis=AX.X, op=ALU.add), reads=["sqb"], writes=["ss6"])
                    rstd_ops(ss6[:], "ss6", 6, 1.0 / 64)
                    P.op("dve", lambda e, bf6=bf6: e.tensor_tensor(out=xgb[:], in0=bf6, in1=ss6[:].unsqueeze(2).to_broadcast([128, 6, 64]), op=ALU.mult),
                         reads=[bfkey, "ss6"], writes=["xgb"])
                    P.op("pool", lambda e: e.tensor_tensor(out=xgb[:], in0=xgb[:], in1=gqk[:], op=ALU.mult), reads=["xgb", "gqk_q", "gqk_k"], writes=["xgb"])
                    qkb, qkbkey = qkbr.next()
                    xv = xgb[:].rearrange("p h (a b c) -> p h a b c", a=2, b=2)
                    ov = qkb[:].rearrange("p h (a b c) -> p h a b c", a=2, b=2)
                    cb = ropb[:, 0, blk, :].rearrange("p (a c) -> p a c", a=2).unsqueeze(1).to_broadcast([128, 6, 2, 16])
                    sb_ = ropb[:, 1, blk, :].rearrange("p (a c) -> p a c", a=2).unsqueeze(1).to_broadcast([128, 6, 2, 16])
                    x1 = xv[:, :, :, 0, :]
                    x2 = xv[:, :, :, 1, :]
                    P.op("pool", lambda e, x1=x1, cb=cb: e.tensor_tensor(out=tB[0][:], in0=x1, in1=cb, op=ALU.mult), reads=["xgb", "ropb"], writes=["tB0"])
                    P.op("pool", lambda e, x2=x2, sb_=sb_: e.tensor_tensor(out=tB[1][:], in0=x2, in1=sb_, op=ALU.mult), reads=["xgb", "ropb"], writes=["tB1"])
                    P.op("dve", lambda e, x2=x2, cb=cb: e.tensor_tensor(out=tB[2][:], in0=x2, in1=cb, op=ALU.mult), reads=["xgb", "ropb"], writes=["tB2"])
                    P.op("dve", lambda e, x1=x1, sb_=sb_: e.tensor_tensor(out=tB[3][:], in0=x1, in1=sb_, op=ALU.mult), reads=["xgb", "ropb"], writes=["tB3"])
                    P.op("pool", lambda e, ov=ov: e.tensor_tensor(out=ov[:, :, :, 0, :], in0=tB[0][:], in1=tB[1][:], op=ALU.subtract), reads=["tB0", "tB1"], writes=[qkbkey + "x"])
                    P.op("dve", lambda e, ov=ov: e.tensor_tensor(out=ov[:, :, :, 1, :], in0=tB[2][:], in1=tB[3][:], op=ALU.add), reads=["tB2", "tB3"], writes=[qkbkey + "y"])
                    bfv = bf[:, 384:512].rearrange("p (h d) -> p h d", d=64)
                    P.op("pool", lambda e, bfv=bfv: e.tensor_copy(out=vs[:, 3:5, 0:64], in_=bfv), reads=[bfkey], writes=[vkey + "b"])
                    P.op("pool", lambda e, bfv=bfv: e.tensor_copy(out=vs[:, 3:5, 128:192], in_=bfv), reads=[bfkey], writes=[vkey + "b2"])

                    def fin_b(qkb=qkb, qkbkey=qkbkey):
                        pT2, pT2key = psT2.next()
                        qkbf = qkb[:].rearrange("p h d -> p (h d)")
                        for tt in range(3):
                            P.op("pe", lambda e, tt=tt: e.transpose(out=pT2[:, tt, :], in_=qkbf[:, tt * 128:(tt + 1) * 128], identity=identb[:]),
                                 reads=[qkbkey + "x", qkbkey + "y", "identb"], writes=[pT2key])
                        P.op("act", lambda e: e.activation(out=qt_s[:, 3:5, j * 128:(j + 1) * 128], in_=pT2[:, 0:2, :], func=AF.Copy),
                             reads=[pT2key], writes=[qkey + f"_{j}b"])
                        P.op("act", lambda e: e.activation(out=kt_s[:, 3, j * 128:(j + 1) * 128], in_=pT2[:, 2, :], func=AF.Copy),
                             reads=[pT2key], writes=[kkey + f"_{j}b"])
                    pend.append(fin_b)
                    pm, pmkey = tok_mm(BZ, BZ + 256)
                    P.op("act", lambda e, pm=pm: e.activation(out=sz[:, 384:640], in_=pm[:, 0:256], func=AF.Silu), reads=[pmkey], writes=[szkey + "b"])
                    pm, pmkey = tok_mm(CV, CV + 384)
                    P.op("dve", lambda e, pm=pm: e.tensor_copy(out=vs[:, 5:8, :].rearrange("p a (s d) -> p a s d", d=64)[:, :, 0::2, :], in_=pm[:, 0:384].rearrange("p (a s d) -> p a s d", s=2, d=64)),
                         reads=[pmkey], writes=[vkey + "c"])
                    pm, pmkey = tok_mm(CZ, CZ + 384)
                    P.op("act", lambda e, pm=pm: e.activation(out=sz[:, 640:1024], in_=pm[:, 0:384], func=AF.Silu), reads=[pmkey], writes=[szkey + "c"])
                    if j == 3:
                        hks = [hkey + f"_{jj}" for jj in range(4)]
                        for ti in range(6):
                            run_pend(2)
                            c0 = (CQ if ti < 3 else CK) + (ti % 3) * 128
                            pf, pfkey = psF.next()
                            for kc in range(8):
                                P.op("pe", lambda e, pf=pf, kc=kc, c0=c0: e.matmul(pf[:], lhsT=wsb[:, kc, c0:c0 + 128], rhs=hT[:, kc, :], start=(kc == 0), stop=(kc == 7)),
                                     reads=hks + wkeys(c0, c0 + 128), writes=[pfkey])
                            if ti < 3:
                                P.op("dve", lambda e, pf=pf, ti=ti: e.tensor_copy(out=qt_s[:, 5 + ti, :], in_=pf[:]), reads=[pfkey], writes=[qkey + f"_c{ti}"])
                            else:
                                P.op("act", lambda e, pf=pf, ti=ti: e.activation(out=kt_s[:, 4 + ti - 3, :], in_=pf[:], func=AF.Copy), reads=[pfkey], writes=[kkey + f"_c{ti}"])

                def s1_store(c):
                    j, blk, pn = c["j"], c["blk"], c["pn"]
                    t0 = blk * 128
                    vs, vkey = c["vs"]
                    sz, szkey = c["sz"]
                    qt_s, qkey = c["qt"]
                    kt_s, kkey = c["kt"]
                    P.op("sp", lambda e: e.dma_start(out=dr["v_" + pn][t0:t0 + 128, :], in_=vs[:].rearrange("p h d -> p (h d)")),
                         reads=[vkey + "a", vkey + "b", vkey + "b2", vkey + "c", vkey + "one"], dma=vkey)
                    P.op("sp", lambda e: e.dma_start(out=dr["sz_" + pn][t0:t0 + 128, :], in_=sz[:].bitcast(F32)),
                         reads=[szkey + "a", szkey + "b", szkey + "c"], dma=szkey)
                    if j == 3:
                        tt0 = c["tg"] * 512
                        qr = [qkey + f"_{jj}a" for jj in range(4)] + [qkey + f"_{jj}b" for jj in range(4)] + [qkey + f"_c{ti}" for ti in range(3)]
                        kr = [kkey + f"_{jj}a" for jj in range(4)] + [kkey + f"_{jj}b" for jj in range(4)] + [kkey + f"_c{ti}" for ti in range(3, 6)]
                        P.op("sp", lambda e: e.dma_start(out=dr["qt_" + pn][:, tt0:tt0 + 512].rearrange("(k p) t -> p k t", p=128), in_=qt_s[:]), reads=qr, dma=qkey)
                        P.op("sp", lambda e: e.dma_start(out=dr["kt_" + pn][:, tt0:tt0 + 512].rearrange("(k p) t -> p k t", p=128), in_=kt_s[:]), reads=kr, dma=kkey)

                nb1 = len(blocks1)
                for i in range(-3, nb1 + 2):
                    if 0 <= i + 3 < nb1:
                        s1_load(blocks1[i + 3])
                    if 0 <= i + 2 < nb1:
                        s1_fa(blocks1[i + 2])
                    if 0 <= i + 1 < nb1:
                        s1_front(blocks1[i + 1])
                    if 0 <= i < nb1:
                        s1_main(blocks1[i])
                        if blocks1[i]["j"] == 3:
                            run_pend(0)
                    if 0 <= i - 2 < nb1:
                        s1_store(blocks1[i - 2])
                run_pend(0)
                P.flush(final=(STOP == "s1"))
            if STOP == "s1":
                return nc

            with ExitStack() as st:
                SMAXP = max(S for (_, S, _) in parts)
                qpr = Ring(P, st, "qpr", 2, [128, SMAXP], BF16)
                kpr = Ring(P, st, "kpr", 2, [128, SMAXP], BF16)
                vbr = Ring(P, st, "vbr", 2, [128, SMAXP // 128, 192], BF16)
                eab = st.enter_context(nc.sbuf_tensor(U("eab"), [128, 5, 128], BF16))
                m16 = st.enter_context(nc.sbuf_tensor(U("m16"), [128, 5, 128], BF16))
                accs = [(st.enter_context(nc.sbuf_tensor(U(f"acc{hh}"), [128, 2048], F32)), f"acc{hh}") for hh in range(2)]
                v16r = Ring(P, st, "v16r", 1, [128, SMAXP // 128, 192], BF16)
                ecr = Ring(P, st, "ecr", 1, [128, 2, 4, 6, 128], BF16)
                est = Ring(P, st, "est", 2, [128, 1024], F32)
                efb = st.enter_context(nc.sbuf_tensor(U("efb"), [128, 6, 7, 128], BF16))
                eib = st.enter_context(nc.sbuf_tensor(U("eib"), [128, 6, 5, 128], BF16))
                ptr = Ring(P, st, "ptr", 6, [128, 512], BF16)
                osr = Ring(P, st, "osr", 2, [128, 512], F32)
                ogr = Ring(P, st, "ogr", 2, [128, 4, 128], F32)
                rlr = Ring(P, st, "rlr", 2, [128, 4], F32)
                psS = Ring(P, st, "psS", 6, [128, 512], F32, psum=True)
                psO = Ring(P, st, "psO", 2, [128, 512], F32, psum=True)
                psR = None

                eb, ekey = est.next()
                P.op("sp", lambda e, eb=eb: e.dma_start(out=eb[:, 0:640], in_=dr["ea"][:, 0:5, :].rearrange("p a b -> p (a b)")), writes=[ekey], dma=ekey)
                P.op("dve", lambda e, eb=eb: e.tensor_copy(out=eab[:].rearrange("p a b -> p (a b)"), in_=eb[:, 0:5 * 128]), reads=[ekey], writes=["eab"])
                eb, ekey = est.next()
                P.op("sp", lambda e, eb=eb: e.dma_start(out=eb[:, 0:640], in_=dr["ea"][:, 5:10, :].rearrange("p a b -> p (a b)")), writes=[ekey], dma=ekey)
                P.op("dve", lambda e, eb=eb: e.tensor_copy(out=m16[:].rearrange("p a b -> p (a b)"), in_=eb[:, 0:5 * 128]), reads=[ekey], writes=["m16"])
                for h in range(6):
                    eb, ekey = est.next()
                    P.op("sp", lambda e, eb=eb, h=h: e.dma_start(out=eb[:, 0:7 * 128], in_=dr["efraw"][l, h].rearrange("p a b -> p (a b)")), writes=[ekey], dma=ekey)
                    P.op("act", lambda e, eb=eb, h=h: e.activation(out=efb[:, h].rearrange("p a b -> p (a b)"), in_=eb[:, 0:7 * 128], func=AF.Exp),
                         reads=[ekey], writes=[f"efb{h}"])
                    P.op("dve", lambda e, h=h: e.tensor_copy(out=eib[:, h], in_=efb[:, h, 1:6, :]), reads=[f"efb{h}"], writes=[f"eib{h}"])
                    P.op("dve", lambda e, h=h: e.memset(eib[0:64, h, 0, 64:128], 0.0), reads=[f"eib{h}"], writes=[f"eib{h}"])
                    P.op("dve", lambda e, h=h: e.memset(eib[64:128, h, 4, :], 0.0), reads=[f"eib{h}"], writes=[f"eib{h}"])
                    P.op("dve", lambda e, h=h: e.memset(eib[0:64, h, 4, 0:64], 0.0), reads=[f"eib{h}"], writes=[f"eib{h}"])

                jobs = []
                for (pn, S, src) in parts:
                    for gp in range(8):
                        jobs.append(dict(pn=pn, S=S, gp=gp, dyn=(last and pn == QPN and 3 <= gp < 5), dynA=(last and pn == QPN and gp < 3), dynC=(last and pn == QPN and gp >= 5)))
                if last and QPN is not None:
                    P.op("sp", lambda e: e.dma_start(out=dr["ktc"][0:384, :], in_=dyn_ap(e, rq0, 0, dr["ktpad"], [[SQ + 2048, 384], [1, 4096]])), writes=["ktc"], dma="cq5")
                    P.op("sp", lambda e: e.dma_start(out=dr["vc"][:, 0:576], in_=dyn_ap(e, rv, 0, dr["vpad"], [[1536, 4096], [1, 576]])), writes=["vc"], dma="cq6")
                    P.op("sp", lambda e: e.dma_start(out=dr["ktc"][384:768, :], in_=dyn_ap(e, rq0, 512 * (SQ + 2048), dr["ktpad"], [[SQ + 2048, 384], [1, 4096]])), writes=["ktc2"], dma="cq7")
                    P.op("sp", lambda e: e.dma_start(out=dr["vc"][:, 576:1152], in_=dyn_ap(e, rv, 960, dr["vpad"], [[1536, 4096], [1, 576]])), writes=["vc2"], dma="cq8")

                def load_job(jb):
                    pn, S, gp = jb["pn"], jb["S"], jb["gp"]
                    NB = S // 128
                    qtd, ktd, vd = dr["qt_" + pn], dr["kt_" + pn], dr["v_" + pn]
                    if gp < 3:
                        pr = gp
                        qrow, krows = pr * 128, [pr * 128, pr * 128 + 64]
                    elif gp < 5:
                        pr = gp - 3
                        qrow, krows = 384 + pr * 128, [384 + pr * 64, 384 + pr * 64]
                    else:
                        pr = gp - 5
                        qrow, krows = 640 + pr * 128, [512 + pr * 128, 512 + pr * 128 + 64]
                    vb, vbkey = vbr.next()
                    dynA = jb.get("dynA")
                    dynC = jb.get("dynC")
                    if dynA or dynC:
                        nch = 1
                        vcol = gp * 192 if dynA else 576 + pr * 192
                        P.op("sp", lambda e: e.dma_start(out=vb[:, 0:32, :], in_=dr["vc"][:, vcol:vcol + 192].rearrange("(b p) c -> p b c", p=128)),
                             reads=["vc", "vc2"], writes=[vbkey + "_0"], dma=f"{vbkey}_0")
                    else:
                        nch = 4 if S > 2048 else 1
                        for ch in range(nch):
                            b0 = ch * (NB // nch)
                            b1 = (ch + 1) * (NB // nch)
                            P.op("sp", lambda e, b0=b0, b1=b1: e.dma_start(
                                out=vb[:, b0:b1, :], in_=vd[b0 * 128:b1 * 128, gp * 192:(gp + 1) * 192].rearrange("(b p) c -> p b c", p=128)),
                                writes=[vbkey + f"_{ch}"], dma=f"{vbkey}_{ch}")
                    jb["vb"] = (vb, vbkey, [vbkey + f"_{ch}" for ch in range(nch)])

                    qp, qpkey = qpr.next()
                    kp, kpkey = kpr.next()
                    if jb.get("dyn") or dynA or dynC:
                        P.op("sp", lambda e: e.dma_start(out=qp[:, 0:2048], in_=dyn_ap(e, rq0, qrow * S, qtd, [[S, 128], [1, 2048]])), writes=[qpkey], dma=qpkey)
                    else:
                        P.op("sp", lambda e: e.dma_start(out=qp[:, 0:S], in_=qtd[qrow:qrow + 128, :]), writes=[qpkey], dma=qpkey)
                    for hh in range(2):
                        if dynA or dynC:
                            kr0 = krows[hh] if dynA else krows[hh] - 512 + 384
                            P.op("sp", lambda e, hh=hh, kr0=kr0: e.dma_start(out=kp[hh * 64:(hh + 1) * 64, 0:4096], in_=dr["ktc"][kr0:kr0 + 64, :]),
                                 reads=["ktc", "ktc2"], writes=[kpkey + f"_{hh}"], dma=f"{kpkey}_{hh}")
                        else:
                            P.op("sp", lambda e, hh=hh: e.dma_start(out=kp[hh * 64:(hh + 1) * 64, 0:S], in_=ktd[krows[hh]:krows[hh] + 64, :]),
                                 writes=[kpkey + f"_{hh}"], dma=f"{kpkey}_{hh}")
                    jb["qp"] = (qp, qpkey)
                    jb["kp"] = (kp, kpkey)

                def load_ec(jb):
                    pr = jb["gp"] - 5
                    ecp, eckey = ecr.next()
                    for hh in range(2):
                        for ch in range(3):
                            eb, ekey = est.next()
                            P.op("sp", lambda e, eb=eb, hh=hh, ch=ch: e.dma_start(out=eb[:, 0:1024], in_=dr["cedge"][2 * pr + hh][:, ch * 1024:(ch + 1) * 1024]), writes=[ekey], dma=ekey)
                            P.op("act", lambda e, eb=eb, hh=hh, ch=ch: e.activation(out=ecp[:, hh].rearrange("p a b c -> p (a b c)")[:, ch * 1024:(ch + 1) * 1024], in_=eb[:, 0:1024], func=AF.Exp),
                                 reads=[ekey], writes=[eckey + f"_{hh}_{ch}"])
                    jb["ec"] = (ecp, [eckey + f"_{hh}_{ch}" for hh in range(2) for ch in range(3)])

                load_job(jobs[0])
                for ji, jb in enumerate(jobs):
                    if ji + 1 < len(jobs):
                        load_job(jobs[ji + 1])
                    pn, S, gp = jb["pn"], jb["S"], jb["gp"]
                    NB = S // 128
                    NW = S // 512
                    od = dr["o_" + pn]
                    if jb.get("dynC"):
                        load_ec(jb)
                    if True:
                        if gp < 3:
                            br, pr = "A", gp
                            ocol = pr * 128
                        elif gp < 5:
                            br, pr = "B", gp - 3
                            ocol = 384 + pr * 128
                        else:
                            br, pr = "C", gp - 5
                            ocol = 640 + pr * 128
                        vb, vbkey, vkeys = jb["vb"]
                        qp, qpkey = jb["qp"]
                        kp, kpkey = jb["kp"]
                        groups = []

                        def dense_groups(w):
                            for qb in range(4):
                                b = w * 4 + qb
                                if br == "A":
                                    alist = list(range(max(0, b - 2), min(NB, b + 3)))
                                    kind, off = "ea", 2
                                elif b <= 1:
                                    alist, kind, off = list(range(0, 4)), "ef", 3
                                elif b >= NB - 2:
                                    alist, kind, off = list(range(NB - 4, NB)), "ef", 3
                                else:
                                    alist, kind, off = list(range(b - 2, b + 3)), "ei", 2
                                nu = len(alist)
                                gi = 0
                                while gi < nu:
                                    gn = min(4, nu - gi)
                                    if nu - gi == 5:
                                        gn = 3
                                    us = alist[gi:gi + gn]
                                    groups.append(dict(units=[dict(k=(a * 128, 1), q=(b * 128, 1, 128), pc0=ui * 128, v=("nat", a), oc0=qb * 128,
                                                                   st=(gi + ui == 0), sp=(gi + ui == nu - 1)) for ui, a in enumerate(us)],
                                                       e=(kind, us[0] - b + off, gn, False), tail=("win" if (qb == 3 and gi + gn == nu) else None), w=w))
                                    gi += gn

                        dyn = jb.get("dyn")
                        if br == "B":
                            for w in range(4 if dyn else NW):
                                for kb in range(NB):
                                    groups.append(dict(units=[dict(k=(kb * 128, 1), q=(w * 512, 1, 512), pc0=0, v=("nat", kb), oc0=0, st=(kb == 0), sp=(kb == NB - 1))],
                                                       e=None, tail=("win" if kb == NB - 1 else None), w=w))
                        elif jb.get("dynC"):
                            for w in range(4):
                                for qb in range(4):
                                    b = w * 4 + qb
                                    if b <= 1 or b >= 14:
                                        sb = b if b <= 1 else 2 + (b - 14)
                                        a0 = 6 if b <= 1 else 20
                                        for gi in (0, 3):
                                            groups.append(dict(units=[dict(k=((a0 + gi + ui) * 128, 1), q=(b * 128, 1, 128), pc0=ui * 128, v=("nat", a0 + gi + ui), oc0=qb * 128,
                                                                           st=(gi + ui == 0), sp=(gi + ui == 5)) for ui in range(3)],
                                                               e=("ec", (sb, gi), 3, False), tail=("win" if (qb == 3 and gi == 3) else None), w=w))
                                    else:
                                        alist = [b + 8 + d_ for d_ in range(-2, 3)]
                                        for (gi, gn) in ((0, 3), (3, 2)):
                                            us = alist[gi:gi + gn]
                                            groups.append(dict(units=[dict(k=(a * 128, 1), q=(b * 128, 1, 128), pc0=ui * 128, v=("nat", a), oc0=qb * 128,
                                                                           st=(gi + ui == 0), sp=(gi + ui == 4)) for ui, a in enumerate(us)],
                                                               e=("ei", gi, gn, False), tail=("win" if (qb == 3 and gi == 3) else None), w=w))
                        elif br == "C":
                            for w in range(NW):
                                dense_groups(w)
                        elif jb.get("dynA"):
                            for rq in range(4):
                                for ri in range(4):
                                    r = 4 * rq + ri
                                    units = [dict(k=(jj * 2048 + r, 16), q=(r, 16, 128), pc0=jj * 128, v=("v16", r * 2 + jj), oc0=ri * 128, st=(jj == 0), sp=(jj == 1)) for jj in range(2)]
                                    groups.append(dict(units=units, e=("m16", 3, 2, False), tail=("quad" if ri == 3 else None), sw=0, rq=rq))
                            for w in range(4):
                                for qb in range(4):
                                    b = w * 4 + qb
                                    alist = [b + 8 + d_ for d_ in range(-2, 3)]
                                    for (gi, gn) in ((0, 3), (3, 2)):
                                        us = alist[gi:gi + gn]
                                        groups.append(dict(units=[dict(k=(a * 128, 1), q=(b * 128, 1, 128), pc0=ui * 128, v=("nat", a), oc0=qb * 128,
                                                                       st=(gi + ui == 0), sp=(gi + ui == 4)) for ui, a in enumerate(us)],
                                                           e=("ea", gi, gn, False), tail=("win" if (qb == 3 and gi == 3) else None), w=w))
                        else:
                            nj = NB // 16
                            for sw in range(nj):
                                for rq in range(4):
                                    ulist = [u for u in (-1, 0, 1) if 0 <= sw + u < nj]
                                    if len(ulist) == 1:
                                        units = []
                                        for ri in range(4):
                                            r = 4 * rq + ri
                                            units.append(dict(k=(sw * 2048 + r, 16), q=(sw * 2048 + r, 16, 128), pc0=ri * 128, v=("v16", r * nj + sw), oc0=ri * 128, st=True, sp=True))
                                        groups.append(dict(units=units, e=("m16", 1, 4, True), tail="quad", sw=sw, rq=rq))
                                    else:
                                        for ri in range(4):
                                            r = 4 * rq + ri
                                            units = []
                                            for ui, u in enumerate(ulist):
                                                units.append(dict(k=((sw + u) * 2048 + r, 16), q=(sw * 2048 + r, 16, 128), pc0=ui * 128, v=("v16", r * nj + sw + u), oc0=ri * 128,
                                                                  st=(ui == 0), sp=(ui == len(ulist) - 1)))
                                            groups.append(dict(units=units, e=("m16", ulist[0] + 1, len(ulist), False), tail=("quad" if ri == 3 else None), sw=sw, rq=rq))
                                for w in range(sw * 4, sw * 4 + 4):
                                    dense_groups(w)
                        ng = len(groups)
                        pos = [psO.next(), psO.next()]
                        state = {}
                        v16 = None
                        def load_v16(jbx):
                            pnx, Sx, gpx = jbx["pn"], jbx["S"], jbx["gp"]
                            v16b, v16key = v16r.next()
                            if jbx.get("dynA"):
                                nj_ = 2
                                vsrc = dr["vc"][:, gpx * 192:(gpx + 1) * 192].rearrange("(jj i r) c -> r i jj c", i=128, r=16)
                                v16reads = ["vc"]
                            else:
                                nj_ = (Sx // 128) // 16
                                vsrc = dr["v_" + pnx][:, gpx * 192:(gpx + 1) * 192].rearrange("(jj i r) c -> r i jj c", i=128, r=16)
                                v16reads = []
                            for r in range(16):
                                P.op("sp", lambda e, r=r, v16b=v16b, vsrc=vsrc, nj_=nj_: e.dma_start(out=v16b[:, r * nj_:(r + 1) * nj_, :], in_=vsrc[r]),
                                     reads=v16reads, writes=[v16key + f"_{r}"], dma=f"{v16key}_{r}")
                            jbx["v16"] = (v16b, [v16key + f"_{r}" for r in range(16)])

                        if br == "A":
                            if "v16" not in jb:
                                load_v16(jb)
                            v16 = jb["v16"]
                        nxt = jobs[ji + 1] if ji + 1 < len(jobs) else None
                        pre_at = None
                        if br == "A" and nxt is not None and nxt["gp"] < 3:
                            lastpat = max(gi_ for gi_, g_ in enumerate(groups) if g_["units"][0]["v"][0] == "v16")
                            pre_at = lastpat + 3

                        def sl(start, stride, n):
                            return slice(start, start + (n - 1) * stride + 1, stride) if stride != 1 else slice(start, start + n)

                        def emit_front2(g, qp=qp, kp=kp, qpkey=qpkey, kpkey=kpkey, pr=pr, jb=jb):
                            pss = [psS.next(), psS.next()]
                            ncols = 0
                            for u in g["units"]:
                                ks, kst = u["k"]
                                qs, qst, n = u["q"]
                                pc0 = u["pc0"]
                                for hh in range(2):
                                    ps, pskey = pss[hh]
                                    P.op("pe", lambda e, ps=ps, ks=ks, kst=kst, qs=qs, qst=qst, n=n, pc0=pc0, hh=hh: e.matmul(
                                        ps[:, pc0:pc0 + n], lhsT=kp[hh * 64:(hh + 1) * 64, sl(ks, kst, 128)], rhs=qp[hh * 64:(hh + 1) * 64, sl(qs, qst, n)], start=True, stop=True),
                                        reads=[qpkey, kpkey + f"_{hh}"], writes=[pskey])
                                ncols = max(ncols, pc0 + n)
                            for hh in range(2):
                                ps, pskey = pss[hh]
                                pt, ptkey = ptr.next()
                                P.op("act", lambda e, ps=ps, pt=pt, ncols=ncols: e.activation(out=pt[:, 0:ncols], in_=ps[:, 0:ncols], func=AF.Exp, scale=0.125),
                                     reads=[pskey], writes=[ptkey])
                                if g["e"] is not None:
                                    kind, i0_, gn, bc = g["e"]
                                    hglob = 2 * pr + hh
                                    if kind == "ea":
                                        e_ap, ekeys = eab[:, i0_:i0_ + gn, :], ["eab"]
                                    elif kind == "m16":
                                        if bc:
                                            e_ap, ekeys = m16[:, i0_:i0_ + 1, :].to_broadcast([128, gn, 128]), ["m16"]
                                        else:
                                            e_ap, ekeys = m16[:, i0_:i0_ + gn, :], ["m16"]
                                    elif kind == "ec":
                                        ecp_, eckeys_ = jb["ec"]
                                        e_ap, ekeys = ecp_[:, hh, i0_[0], i0_[1]:i0_[1] + gn, :], eckeys_
                                    elif kind == "ef":
                                        e_ap, ekeys = efb[:, hglob, i0_:i0_ + gn, :], [f"efb{hglob}"]
                                    else:
                                        e_ap, ekeys = eib[:, hglob, i0_:i0_ + gn, :], [f"eib{hglob}"]
                                    P.op("dve", lambda e, pt=pt, e_ap=e_ap, gn=gn: e.tensor_tensor(
                                        out=pt[:, 0:gn * 128].rearrange("p (a b) -> p a b", b=128), in0=pt[:, 0:gn * 128].rearrange("p (a b) -> p a b", b=128), in1=e_ap, op=ALU.mult),
                                        reads=[ptkey] + ekeys, writes=[ptkey])
                                g["pt%d" % hh] = (pt, ptkey)

                        def emit_pv2(g, vb=vb, vkeys=vkeys):
                            for u in g["units"]:
                                vkind, vblk = u["v"]
                                pc0, oc0, st_, sp_ = u["pc0"], u["oc0"], u["st"], u["sp"]
                                n = u["q"][2]
                                if vkind == "nat":
                                    vt, vks = vb, vkeys
                                else:
                                    vt, vks = v16[0], v16[1]
                                for hh in range(2):
                                    pt, ptkey = g["pt%d" % hh]
                                    po, pokey = pos[hh]
                                    P.op("pe", lambda e, po=po, vt=vt, vblk=vblk, pc0=pc0, n=n, oc0=oc0, st_=st_, sp_=sp_, pt=pt, hh=hh: e.matmul(
                                        po[:, oc0:oc0 + n], lhsT=vt[:, vblk, hh * 64:hh * 64 + 128], rhs=pt[:, pc0:pc0 + n], start=st_, stop=sp_),
                                        reads=[ptkey] + vks, writes=[pokey])

                        def emit_back(g, hh, ocol=ocol, od=od, br=br, dyn=dyn, dynA=jb.get("dynA"), dynC=jb.get("dynC")):
                            po, pokey = pos[hh]
                            if g["tail"] == "quad":
                                rq = g["rq"]
                                ac, ackey = accs[hh]
                                dst = ac[:].rearrange("p (l r) -> p r l", r=16)[:, 4 * rq:4 * rq + 4, :]
                                P.op("act", lambda e, po=po, dst=dst: e.activation(out=dst, in_=po[:].rearrange("p (a b) -> p a b", b=128), func=AF.Copy),
                                     reads=[pokey], writes=[ackey + f"_{rq}"])
                            if g["tail"] == "win":
                                osb, oskey = osr.next()
                                w = g["w"]
                                if br == "A":
                                    ac, ackey = accs[hh]
                                    wl = (w % 4) * 512
                                    P.op("dve", lambda e, osb=osb, po=po, ac=ac, wl=wl: e.tensor_tensor(out=osb[:], in0=po[:], in1=ac[:, wl:wl + 512], op=ALU.add),
                                         reads=[pokey] + [ackey + f"_{q}" for q in range(4)], writes=[oskey])
                                else:
                                    P.op("act", lambda e, osb=osb, po=po: e.activation(out=osb[:], in_=po[:], func=AF.Copy), reads=[pokey], writes=[oskey])
                                prr_, prkey = psS.next()
                                prr = prr_[:].rearrange("p (a b) -> p a b", b=128)
                                for jj in range(4):
                                    P.op("pe", lambda e, prr=prr, osb=osb, jj=jj: e.transpose(out=prr[:, jj, :], in_=osb[:, jj * 128:(jj + 1) * 128], identity=identf[:]),
                                         reads=[oskey, "identf"], writes=[prkey])
                                rl, rlkey = rlr.next()
                                lcol = 64 if hh == 0 else 0
                                P.op("dve", lambda e, rl=rl, prr=prr, lcol=lcol: e.reciprocal(out=rl[:], in_=prr[:, :, lcol]), reads=[prkey], writes=[rlkey])
                                if hh == 0:
                                    state["og"] = ogr.next()
                                og, ogkey = state["og"]
                                P.op("dve", lambda e, og=og, prr=prr, rl=rl, hh=hh: e.tensor_tensor(
                                    out=og[:, :, hh * 64:(hh + 1) * 64], in0=prr[:, :, hh * 64:(hh + 1) * 64], in1=rl[:].unsqueeze(2).to_broadcast([128, 4, 64]), op=ALU.mult),
                                    reads=[prkey, rlkey], writes=[ogkey + f"_{hh}"])
                                if hh == 1 and dynC:
                                    P.op("sp", lambda e, og=og, w=w: e.dma_start(
                                        out=dr["ocq"][w * 512:(w + 1) * 512, ocol - 640:ocol - 640 + 128].rearrange("(b p) c -> p b c", p=128), in_=og[:]),
                                        reads=[ogkey + "_0", ogkey + "_1"], dma=ogkey)
                                elif hh == 1 and dynA:
                                    P.op("sp", lambda e, og=og, w=w: e.dma_start(
                                        out=dr["oaq"][w * 512:(w + 1) * 512, ocol:ocol + 128].rearrange("(b p) c -> p b c", p=128), in_=og[:]),
                                        reads=[ogkey + "_0", ogkey + "_1"], dma=ogkey)
                                elif hh == 1 and dyn:
                                    P.op("sp", lambda e, og=og, w=w: e.dma_start(
                                        out=dr["obq"][w * 512:(w + 1) * 512, ocol - 384:ocol - 384 + 128].rearrange("(b p) c -> p b c", p=128), in_=og[:]),
                                        reads=[ogkey + "_0", ogkey + "_1"], dma=ogkey)
                                elif hh == 1:
                                    P.op("sp", lambda e, og=og, w=w: e.dma_start(
                                        out=od[w * 512:(w + 1) * 512, ocol:ocol + 128].rearrange("(b p) c -> p b c", p=128), in_=og[:]),
                                        reads=[ogkey + "_0", ogkey + "_1"], dma=ogkey)

                        LOOK = 2
                        for i in range(ng + LOOK):
                            if pre_at is not None and i == pre_at:
                                load_v16(nxt)
                            if i < ng:
                                emit_front2(groups[i])
                            if i - LOOK >= 0:
                                emit_pv2(groups[i - LOOK])
                                emit_back(groups[i - LOOK], 0)
                                emit_back(groups[i - LOOK], 1)
                P.flush(final=(STOP == "s2"))
            if STOP == "s2":
                return nc

            with ExitStack() as st:
                wo = st.enter_context(nc.sbuf_tensor(U("wo"), [128, 8, D], BF16))
                gbr = st.enter_context(nc.sbuf_tensor(U("gbr"), [128, D], F32))
                gpo = st.enter_context(nc.sbuf_tensor(U("gpo"), [128, D], F32))
                invw = st.enter_context(nc.sbuf_tensor(U("invw"), [128, 3], F32))
                o_r = Ring(P, st, "o_r", 4, [128, D], F32)
                z_r = Ring(P, st, "z_r", 4, [128, D // 2], F32)
                x_r = Ring(P, st, "x_r", 5, [128, D], F32)
                g_r = Ring(P, st, "g_r", 6, [128, D], F32)
                pc_r = Ring(P, st, "pc_r", 6, [128, D], F32)
                junk3 = st.enter_context(nc.sbuf_tensor(U("junk3"), [128, D], BF16))
                s3r = Ring(P, st, "s3r", 6, [128, 3], F32)
                s2r = Ring(P, st, "s2r", 6, [128, 2], F32)
                ybr = Ring(P, st, "ybr", 3, [128, D], BF16)
                yTr = Ring(P, st, "yTr", 3, [128, 8, 128], BF16)
                t_r = Ring(P, st, "t_r", 5, [128, D], F32)
                psT3 = Ring(P, st, "psT3", 2, [128, 8, 128], BF16, psum=True)
                psY = Ring(P, st, "psY", 3, [128, D], F32, psum=True)

                for ci in range(4):
                    c0 = ci * 256
                    P.op("pool", lambda e, c0=c0: e.dma_start(out=wo[:, :, c0:c0 + 256], in_=dr["w_out"][l][:, c0:c0 + 256].rearrange("(k p) c -> p k c", p=128)),
                         writes=[f"wo{ci}"], dma=f"wold{ci}")
                WO = [f"wo{ci}" for ci in range(4)]
                P.op("sp", lambda e: e.dma_start(out=gbr[:], in_=dr["branch_gain"][l].partition_broadcast(128), allow_slow_non_contiguous=True), writes=["gbr"], dma="c5")
                P.op("sp", lambda e: e.dma_start(out=gpo[:], in_=dr["norm_post"][l].partition_broadcast(128), allow_slow_non_contiguous=True), writes=["gpo"], dma="c6")
                P.op("dve", lambda e: e.memset(invw[:, 0:1], 1.0 / 384), writes=["invw"])
                P.op("dve", lambda e: e.memset(invw[:, 1:2], 1.0 / 256), writes=["invw"])
                P.op("dve", lambda e: e.memset(invw[:, 2:3], 1.0 / 384), writes=["invw"])
                BR = ((0, 384), (384, 640), (640, 1024))
                blocks3 = []
                qblocks = []
                for (pn, S, src) in parts:
                    xsrc = dr["x_" + pn] if l == 0 else dr["y1_" + pn]
                    if last and pn == QPN:
                        P.op("sp", lambda e: e.dma_start(out=dr["oq"][:, 0:384], in_=dr["oaq"]), writes=["oq"], dma="cq0")
                        P.op("sp", lambda e: e.dma_start(out=dr["oq"][:, 640:1024], in_=dr["ocq"]), reads=["oq"], writes=["oq"], dma="cq4")
                        P.op("sp", lambda e, pn=pn: e.dma_start(out=dr["szq"], in_=dyn_ap(e, rrow2, 0, dr["sz_" + pn], [[D // 2, 2048], [1, D // 2]])), writes=["szq"], dma="cq1")
                        P.op("sp", lambda e, xsrc=xsrc: e.dma_start(out=dr["xq"], in_=dyn_ap(e, rrow, 0, xsrc, [[D, 2048], [1, D]])), writes=["xq"], dma="cq2")
                        P.op("sp", lambda e: e.dma_start(out=dr["oq"][:, 384:640], in_=dr["obq"]), reads=["oq"], writes=["oq"], dma="cq3")
                        qblocks = [dict(pn=pn, t0=blk * 128, osrc=dr["oq"], zsrc=dr["szq"], xsrc=dr["xq"], ydst=dr["yq_" + pn], keys=["oq", "szq", "xq"]) for blk in range(16)]
                        continue
                    ydst = dr["y_" + pn] if last else dr["y1_" + pn]
                    for blk in range(S // 128):
                        blocks3.append(dict(pn=pn, t0=blk * 128, osrc=dr["o_" + pn], zsrc=dr["sz_" + pn], xsrc=xsrc, ydst=ydst, keys=[]))
                blocks3 = blocks3 + qblocks

                def p_load(c):
                    pn, t0 = c["pn"], c["t0"]
                    ob, okey = o_r.next()
                    zb, zkey = z_r.next()
                    c["o"], c["z"] = (ob, okey), (zb, zkey)
                    osrc, zsrc = c["osrc"], c["zsrc"]
                    P.op("sp", lambda e: e.dma_start(out=ob[:], in_=osrc[t0:t0 + 128, :]), reads=c["keys"][0:1], writes=[okey], dma=okey)
                    P.op("sp", lambda e: e.dma_start(out=zb[:], in_=zsrc[t0:t0 + 128, :]), reads=c["keys"][1:2], writes=[zkey], dma=zkey)

                def p_g(c):
                    ob, okey = c["o"]
                    zb, zkey = c["z"]
                    gb, gkey = g_r.next()
                    c["g"] = (gb, gkey)
                    P.op("pool", lambda e: e.tensor_tensor(out=gb[:], in0=ob[:], in1=zb[:].bitcast(BF16), op=ALU.mult), reads=[okey, zkey], writes=[gkey])

                def p_sq(c):
                    gb, gkey = c["g"]
                    s3, s3key = s3r.next()
                    c["s3"] = (s3, s3key)
                    for bi, (c0, c1) in enumerate(BR):
                        P.op("act", lambda e, bi=bi, c0=c0, c1=c1: e.activation(out=junk3[:, c0:c1], in_=gb[:, c0:c1], func=AF.Square, accum_out=s3[:, bi:bi + 1]),
                             reads=[gkey], writes=[s3key, "junk3"])

                def p_r1(c):
                    s3, s3key = c["s3"]
                    P.op("dve", lambda e: e.tensor_tensor(out=s3[:], in0=s3[:], in1=invw[:], op=ALU.mult), reads=[s3key, "invw"], writes=[s3key])
                    P.op("dve", lambda e: e.tensor_scalar(out=s3[:], in0=s3[:], scalar1=1.0, scalar2=float(EPS), op0=ALU.mult, op1=ALU.add), reads=[s3key], writes=[s3key])

                def p_r2(c):
                    s3, s3key = c["s3"]
                    P.op("pool", lambda e: e.tensor_tensor(out=s3[:], in0=s3[:], in1=nhalf[:, 0:3], op=ALU.pow), reads=[s3key], writes=[s3key])

                def p_y(c):
                    gb, gkey = c["g"]
                    s3, s3key = c["s3"]
                    yb, ykey = ybr.next()
                    c["y"] = (yb, ykey)
                    for bi, (c0, c1) in enumerate(BR):
                        P.op("dve", lambda e, bi=bi, c0=c0, c1=c1: e.scalar_tensor_tensor(
                            out=yb[:, c0:c1], in0=gb[:, c0:c1], scalar=s3[:, bi:bi + 1], in1=gbr[:, c0:c1], op0=ALU.mult, op1=ALU.mult),
                            reads=[gkey, s3key, "gbr"], writes=[ykey])

                def p_T(c):
                    yb, ykey = c["y"]
                    pT, pTkey = psT3.next()
                    c["pT"] = (pT, pTkey)
                    for kc in range(8):
                        P.op("pe", lambda e, kc=kc: e.transpose(out=pT[:, kc, :], in_=yb[:, kc * 128:(kc + 1) * 128], identity=identb[:]),
                             reads=[ykey, "identb"], writes=[pTkey])

                def p_yT(c):
                    pT, pTkey = c["pT"]
                    yT, yTkey = yTr.next()
                    c["yT"] = (yT, yTkey)
                    P.op("act", lambda e: e.activation(out=yT[:], in_=pT[:], func=AF.Copy), reads=[pTkey], writes=[yTkey])

                def p_mm(c):
                    yT, yTkey = c["yT"]
                    py, pykey = psY.next()
                    c["py"] = (py, pykey)
                    for n in range(2):
                        for kc in range(8):
                            P.op("pe", lambda e, n=n, kc=kc: e.matmul(py[:, n * 512:(n + 1) * 512], lhsT=yT[:, kc, :], rhs=wo[:, kc, n * 512:(n + 1) * 512],
                                                                      start=(kc == 0), stop=(kc == 7)),
                                 reads=[yTkey] + WO, writes=[pykey])

                def p_ev(c):
                    py, pykey = c["py"]
                    s2, s2key = s2r.next()
                    c["s2"] = (s2, s2key)
                    pc, pckey = pc_r.next()
                    c["pc"] = (pc, pckey)
                    for n in range(2):
                        P.op("act", lambda e, n=n: e.activation(out=junk3[:, n * 512:(n + 1) * 512], in_=py[:, n * 512:(n + 1) * 512], func=AF.Square, accum_out=s2[:, n:n + 1]),
                             reads=[pykey], writes=[s2key, "junk3"])
                    P.op("act", lambda e: e.activation(out=pc[:], in_=py[:], func=AF.Copy), reads=[pykey], writes=[pckey])

                def p_r3(c):
                    s2, s2key = c["s2"]
                    P.op("dve", lambda e: e.tensor_tensor(out=s2[:, 0:1], in0=s2[:, 0:1], in1=s2[:, 1:2], op=ALU.add), reads=[s2key], writes=[s2key])
                    P.op("dve", lambda e: e.tensor_scalar(out=s2[:, 0:1], in0=s2[:, 0:1], scalar1=1.0 / D, scalar2=float(EPS), op0=ALU.mult, op1=ALU.add), reads=[s2key], writes=[s2key])

                def p_r4(c):
                    s2, s2key = c["s2"]
                    P.op("pool", lambda e: e.tensor_tensor(out=s2[:, 0:1], in0=s2[:, 0:1], in1=nhalf[:, 0:1], op=ALU.pow), reads=[s2key], writes=[s2key])
                    t0, xsrc = c["t0"], c["xsrc"]
                    xb, xkey = x_r.next()
                    c["x"] = (xb, xkey)
                    P.op("sp", lambda e: e.dma_start(out=xb[:], in_=xsrc[t0:t0 + 128, :]), reads=c["keys"][2:3], writes=[xkey], dma=xkey)

                def p_stt(c):
                    pc, pckey = c["pc"]
                    s2, s2key = c["s2"]
                    tb, tkey = t_r.next()
                    c["t"] = (tb, tkey)
                    P.op("dve", lambda e: e.scalar_tensor_tensor(out=tb[:], in0=pc[:], scalar=s2[:, 0:1], in1=gpo[:], op0=ALU.mult, op1=ALU.mult),
                         reads=[pckey, s2key, "gpo"], writes=[tkey])

                def p_add(c):
                    tb, tkey = c["t"]
                    xb, xkey = c["x"]
                    P.op("pool", lambda e: e.tensor_tensor(out=tb[:], in0=tb[:], in1=xb[:], op=ALU.add), reads=[tkey, xkey], writes=[tkey])

                def p_st(c):
                    tb, tkey = c["t"]
                    t0, ydst = c["t0"], c["ydst"]
                    P.op("sp", lambda e: e.dma_start(out=ydst[t0:t0 + 128, :], in_=tb[:]), reads=[tkey], dma=tkey)

                phases = [(p_load, 0), (p_g, 2), (p_sq, 3), (p_r1, 4), (p_r2, 5), (p_y, 6), (p_T, 7), (p_yT, 8), (p_mm, 9), (p_ev, 10),
                          (p_r3, 11), (p_r4, 12), (p_stt, 14), (p_add, 15), (p_st, 17)]
                nb3 = len(blocks3)
                for i in range(nb3 + 18):
                    for fn, dly in phases:
                        if 0 <= i - dly < nb3:
                            fn(blocks3[i - dly])
                P.flush(final=last)
        print(f"[build] instructions: {P.n_ins}", flush=True)
    return nc


_PARTS = [("p", 8192, "xp"), ("s0", 2048, "xs0"), ("s1", 2048, "xs1")]


def kernel(x_prompt, x_sample, norm_pre, w_in, q_norm, k_norm, rel_bias, branch_gain, w_out, norm_post):
    f = lambda a: np.ascontiguousarray(np.asarray(a, dtype=np.float32))
    x_prompt, x_sample = f(x_prompt), f(x_sample)
    ropea, ropeb, ea, ident = _const_tables()
    efraw = _c_bias_tables(f(rel_bias))
    nc = build(_PARTS, 2)
    shared = dict(w_in=f(w_in), w_out=f(w_out), norm_pre=f(norm_pre), norm_post=f(norm_post), branch_gain=f(branch_gain),
                  q_norm=f(q_norm), k_norm=f(k_norm), ropea=ropea, ropeb=ropeb, ea=ea, ident=ident, efraw=efraw)
    in_maps = []
    for c in range(8):
        m = dict(shared)
        q0 = (c % 4) * 2048
        m["qoff"] = np.array([[q0, q0 * D, q0 * (D // 2), q0 * 1536]], dtype=np.int32)
        m["cedge"] = np.ascontiguousarray(_c_edge_tables(efraw[-1], c % 4).reshape(6, 128, 24 * 128))
        m["x_p"] = x_prompt[c // 4]
        m["x_s0"] = x_sample[2 * c]
        m["x_s1"] = x_sample[2 * c + 1]
        in_maps.append(m)
    res = run_bass_kernel_spmd(nc, in_maps, core_ids=list(range(8)))
    r = res.results
    y_prompt = np.stack([np.concatenate([np.asarray(r[4 * b + q]["yq_p"], dtype=np.float32) for q in range(4)], axis=0) for b in range(2)], axis=0)
    ys = []
    for c in range(8):
        ys.append(np.asarray(r[c]["y_s0"], dtype=np.float32))
        ys.append(np.asarray(r[c]["y_s1"], dtype=np.float32))
    y_sample = np.stack(ys, axis=0)
    return (y_prompt, y_sample)
```
